# Optimizing a Trainium2 kernel written in Bass

```python
import math
import jax, jax.numpy as jnp
from jax import lax
import numpy as np

D_MODEL = 2048
BATCH = 8
SEQ = 2048
DEPTH = 2
DEC_BATCH = 128
DEC_SEQ = 4
PAST_LEN = 8192
PAGE_SIZE = 128

N_EVEN = (DEPTH + 1) // 2
N_ODD = DEPTH // 2
MIX_WIDTH = D_MODEL
M_HEADS = 4
M_DV = MIX_WIDTH // 2 // M_HEADS
M_DK = M_DV // 2
M_CHUNK = 64
S_HEADS = 16
S_KV_HEADS = 4
S_GROUP = S_HEADS // S_KV_HEADS
S_HD = MIX_WIDTH // 2 // S_HEADS
WINDOW = 128
S5_WIDTH = MIX_WIDTH
S5_CH = 16
S5_GROUPS = S5_WIDTH // S5_CH
S5_STATE = 64
S5_CHUNK = 128
X_HEADS = 4
X_HD = 128
X_WIDTH = X_HEADS * X_HD
MEM_LEN = 256
D_FF = 5632
EPS = 1e-6
M_QK = M_HEADS * M_DK
M_V = M_HEADS * M_DV
S_Q = S_HEADS * S_HD
S_KV = S_KV_HEADS * S_HD
_O1 = M_QK
_O2 = _O1 + M_QK
_O3 = _O2 + M_V
_O4 = _O3 + M_V
_O5 = _O4 + M_HEADS
_O6 = _O5 + M_HEADS
_O7 = _O6 + S_Q
_O8 = _O7 + S_KV
EVEN_IN = _O8 + S_KV
EVEN_SPLITS = (_O1, _O2, _O3, _O4, _O5, _O6, _O7, _O8)

kernel_name = 'hybrid_mlstm_swa_s5_decoder_step'


def rmsnorm(x, g):
    xf = x.astype(jnp.float32)
    y = xf * lax.rsqrt(jnp.mean(xf * xf, axis=-1, keepdims=True) + EPS) * g.astype(jnp.float32)
    return y.astype(x.dtype)


def swiglu(x, w_gate, w_up, w_down):
    return (jax.nn.silu(x @ w_gate) * (x @ w_up)) @ w_down


def alibi_slopes():
    return 2.0 ** (-8.0 * jnp.arange(1, S_HEADS + 1, dtype=jnp.float32) / S_HEADS)


def mlstm(q, k, v, i_pre, f_pre, C0, n0, m0):
    B, S = q.shape[:2]
    L = M_CHUNK if S % M_CHUNK == 0 else S
    nc = S // L

    def to_chunks(a):
        a = a.astype(jnp.float32).reshape((B, nc, L) + a.shape[2:])
        return jnp.swapaxes(jnp.moveaxis(a, 1, 0), 2, 3)

    causal = jnp.tril(jnp.ones((L, L), dtype=bool))

    def step(carry, inp):
        C, n, m = carry
        qc, kc, vc, li, lf = inp
        b = jnp.cumsum(lf, axis=-1)
        logw = b[..., :, None] - b[..., None, :] + li[..., None, :]
        logw = jnp.where(causal, logw, -jnp.inf)
        log_state = b + m[..., None]
        m_t = jnp.maximum(log_state, jnp.max(logw, axis=-1))
        w = jnp.exp(logw - m_t[..., None])
        ws = jnp.exp(log_state - m_t)
        s = jnp.einsum('bhtd,bhsd->bhts', qc, kc) * w
        num = jnp.einsum('bhts,bhsv->bhtv', s, vc) + ws[..., None] * jnp.einsum('bhtd,bhdv->bhtv', qc, C)
        den = jnp.sum(s, axis=-1) + ws * jnp.einsum('bhtd,bhd->bht', qc, n)
        h = num / jnp.maximum(jnp.abs(den), jnp.exp(-m_t))[..., None]
        m_new = m_t[..., -1]
        wl = jnp.exp(b[..., -1:] - b + li - m_new[..., None])
        decay = jnp.exp(b[..., -1] + m - m_new)
        C_new = decay[..., None, None] * C + jnp.einsum('bhs,bhsd,bhsv->bhdv', wl, kc, vc)
        n_new = decay[..., None] * n + jnp.einsum('bhs,bhsd->bhd', wl, kc)
        return (C_new, n_new, m_new), h

    xs = tuple(to_chunks(a) for a in (q, k, v, i_pre, jax.nn.log_sigmoid(f_pre)))
    init = (C0.astype(jnp.float32), n0.astype(jnp.float32), m0.astype(jnp.float32))
    (C, n, m), h = lax.scan(step, init, xs)
    h = jnp.transpose(h, (1, 0, 3, 2, 4)).reshape(B, S, M_HEADS, M_DV)
    return h, C, n, m


def sink_attend(q, k, v, dist, valid, slopes, sinks):
    s = jnp.einsum('...qkgd,...skd->...kgqs', q, k).astype(jnp.float32) * (S_HD ** -0.5)
    s = s - slopes[:, :, None, None] * dist[..., None, None, :, :]
    s = jnp.where(valid[..., None, None, :, :], s, -jnp.inf)
    sink = sinks.astype(jnp.float32)[:, :, None, None]
    mx = jnp.maximum(jnp.max(s, axis=-1, keepdims=True), sink)
    p = jnp.exp(s - mx)
    p = p / (jnp.sum(p, axis=-1, keepdims=True) + jnp.exp(sink - mx))
    return jnp.einsum('...kgqs,...skd->...qkgd', p.astype(v.dtype), v)


def swa_prompt(q, k, v, slopes, sinks):
    B, S = q.shape[:2]
    T = WINDOW
    nb = S // T
    qb = q.reshape(B, nb, T, S_KV_HEADS, S_GROUP, S_HD)

    def band(a):
        ap = jnp.pad(a, ((0, 0), (T, 0), (0, 0), (0, 0))).reshape(B, nb + 1, T, S_KV_HEADS, S_HD)
        return jnp.concatenate([ap[:, :-1], ap[:, 1:]], axis=2)

    i = jnp.arange(T)[:, None]
    s = jnp.arange(2 * T)[None, :]
    dist = i + T - s
    blk = jnp.arange(nb)[:, None, None]
    valid = (dist >= 0) & (dist < WINDOW) & ((blk - 1) * T + s >= 0)
    o = sink_attend(qb, band(k), band(v), dist.astype(jnp.float32)[None], valid, slopes, sinks)
    return o.reshape(B, S, S_Q)


def swa_sample(q, k, v, kbuf, vbuf, slopes, sinks):
    B, Sd = q.shape[:2]
    kk = jnp.concatenate([kbuf.astype(k.dtype), k], axis=1)
    vv = jnp.concatenate([vbuf.astype(v.dtype), v], axis=1)
    i = jnp.arange(Sd)[:, None]
    s = jnp.arange(WINDOW + Sd)[None, :]
    dist = i + WINDOW - s
    valid = (dist >= 0) & (dist < WINDOW)
    o = sink_attend(q, kk, vv, dist.astype(jnp.float32), valid, slopes, sinks)
    return o.reshape(B, Sd, S_Q), kk[:, -WINDOW:], vv[:, -WINDOW:]


def even_mixer(h, C0, n0, m0, kbuf, vbuf, w_in, b_i, b_f, head_gain, sinks, w_out):
    B, S, _ = h.shape
    z = h @ w_in
    q_m, k_m, v_m, o_m, i_m, f_m, q_s, k_s, v_s = jnp.split(z, EVEN_SPLITS, axis=-1)
    hm, C, n, m = mlstm(q_m.reshape(B, S, M_HEADS, M_DK) * (M_DK ** -0.5),
                        k_m.reshape(B, S, M_HEADS, M_DK),
                        v_m.reshape(B, S, M_HEADS, M_DV),
                        i_m.astype(jnp.float32) + b_i.astype(jnp.float32),
                        f_m.astype(jnp.float32) + b_f.astype(jnp.float32),
                        C0, n0, m0)
    hm = rmsnorm(hm, head_gain).reshape(B, S, M_V).astype(h.dtype) * jax.nn.sigmoid(o_m)
    q = q_s.reshape(B, S, S_KV_HEADS, S_GROUP, S_HD)
    k = k_s.reshape(B, S, S_KV_HEADS, S_HD)
    v = v_s.reshape(B, S, S_KV_HEADS, S_HD)
    slopes = alibi_slopes().reshape(S_KV_HEADS, S_GROUP)
    sk = sinks.reshape(S_KV_HEADS, S_GROUP)
    if kbuf is None:
        hs = swa_prompt(q, k, v, slopes, sk)
        kw, vw = k[:, -WINDOW:], v[:, -WINDOW:]
    else:
        hs, kw, vw = swa_sample(q, k, v, kbuf, vbuf, slopes, sk)
    out = jnp.concatenate([hm, hs.astype(h.dtype)], axis=-1) @ w_out
    return out, (C, n, m, kw, vw)


def _cmul(ar, ai, br, bi):
    return ar * br - ai * bi, ar * bi + ai * br


def _combine(e1, e2):
    a1r, a1i, b1r, b1i = e1
    a2r, a2i, b2r, b2i = e2
    ar, ai = _cmul(a2r, a2i, a1r, a1i)
    br, bi = _cmul(a2r, a2i, b1r, b1i)
    return ar, ai, br + b2r, bi + b2i


def s5_scan(u, xr0, xi0, a_re, a_im, log_dt, b_re, b_im, c_re, c_im):
    B, S = u.shape[:2]
    L = S5_CHUNK if S % S5_CHUNK == 0 else S
    nc = S // L
    a_re = a_re.astype(jnp.float32)
    a_im = a_im.astype(jnp.float32)
    dt = jnp.exp(log_dt.astype(jnp.float32))[:, None]
    mag = jnp.exp(a_re * dt)
    ab_re, ab_im = mag * jnp.cos(a_im * dt), mag * jnp.sin(a_im * dt)
    den = a_re * a_re + a_im * a_im
    f_re = ((ab_re - 1.0) * a_re + ab_im * a_im) / den
    f_im = (ab_im * a_re - (ab_re - 1.0) * a_im) / den
    bb_re, bb_im = _cmul(f_re[..., None], f_im[..., None], b_re.astype(jnp.float32), b_im.astype(jnp.float32))
    c_re = c_re.astype(jnp.float32)
    c_im = c_im.astype(jnp.float32)
    a_seq_re = jnp.broadcast_to(ab_re, (1, L) + ab_re.shape)
    a_seq_im = jnp.broadcast_to(ab_im, (1, L) + ab_im.shape)

    def step(carry, u_c):
        xr, xi = carry
        bur = jnp.einsum('gpc,blgc->blgp', bb_re, u_c)
        bui = jnp.einsum('gpc,blgc->blgp', bb_im, u_c)
        Ar, Ai, Xr, Xi = lax.associative_scan(_combine, (a_seq_re, a_seq_im, bur, bui), axis=1)
        sr, si = _cmul(Ar, Ai, xr[:, None], xi[:, None])
        Xr = Xr + sr
        Xi = Xi + si
        y = jnp.einsum('gcp,blgp->blgc', c_re, Xr) - jnp.einsum('gcp,blgp->blgc', c_im, Xi)
        return (Xr[:, -1], Xi[:, -1]), y

    uc = jnp.moveaxis(u.reshape(B, nc, L, S5_GROUPS, S5_CH), 1, 0)
    (xr, xi), y = lax.scan(step, (xr0.astype(jnp.float32), xi0.astype(jnp.float32)), uc)
    return jnp.moveaxis(y, 0, 1).reshape(B, S, S5_WIDTH), xr, xi


def odd_mixer(h, xr0, xi0, w_in, a_re, a_im, log_dt, b_re, b_im, c_re, c_im, d_skip, w_out):
    B, S, _ = h.shape
    u = (h @ w_in).astype(jnp.float32)
    y, xr, xi = s5_scan(u.reshape(B, S, S5_GROUPS, S5_CH), xr0, xi0, a_re, a_im, log_dt, b_re, b_im, c_re, c_im)
    y = y + d_skip.astype(jnp.float32) * u
    z = jax.nn.gelu(y).astype(h.dtype) @ w_out
    return z[..., :D_MODEL] * jax.nn.sigmoid(z[..., D_MODEL:]), (xr, xi)


def cross_attend(h, mk, mv, w_q, w_o):
    B, S, _ = h.shape
    q = (h @ w_q).reshape(B, S, X_HEADS, X_HD)
    s = jnp.einsum('bqhd,bkhd->bhqk', q, mk.astype(q.dtype)).astype(jnp.float32) * (X_HD ** -0.5)
    p = jax.nn.softmax(s, axis=-1)
    o = jnp.einsum('bhqk,bkhd->bqhd', p.astype(mv.dtype), mv)
    return o.reshape(B, S, X_WIDTH).astype(h.dtype) @ w_o


def setup_inputs(seed: int = 0) -> dict:
    key = jax.random.key(seed)
    ks = iter(jax.random.split(key, 48))

    def nrm(shape, scale=1.0):
        return scale * jax.random.normal(next(ks), shape, dtype=jnp.float32)

    x_prompt = nrm((BATCH, SEQ, D_MODEL))
    x_sample = nrm((DEC_BATCH, DEC_SEQ, D_MODEL))
    mem_prompt = nrm((BATCH, MEM_LEN, D_MODEL))
    cache_mem_k = nrm((DEPTH, DEC_BATCH, MEM_LEN, X_HEADS, X_HD))
    cache_mem_v = nrm((DEPTH, DEC_BATCH, MEM_LEN, X_HEADS, X_HD))
    state_mlstm_C = nrm((N_EVEN, DEC_BATCH, M_HEADS, M_DK, M_DV))
    state_mlstm_n = nrm((N_EVEN, DEC_BATCH, M_HEADS, M_DK))
    state_mlstm_m = nrm((N_EVEN, DEC_BATCH, M_HEADS))
    cache_swa_k = nrm((N_EVEN, DEC_BATCH, WINDOW, S_KV_HEADS, S_HD))
    cache_swa_v = nrm((N_EVEN, DEC_BATCH, WINDOW, S_KV_HEADS, S_HD))
    state_s5_re = nrm((N_ODD, DEC_BATCH, S5_GROUPS, S5_STATE), 0.5)
    state_s5_im = nrm((N_ODD, DEC_BATCH, S5_GROUPS, S5_STATE), 0.5)
    norm_gain = 1.0 + nrm((DEPTH, 4, D_MODEL), 0.02)
    final_gain = 1.0 + nrm((D_MODEL,), 0.02)
    ffn_w_gate = nrm((DEPTH, 2, D_MODEL, D_FF), D_MODEL ** -0.5)
    ffn_w_up = nrm((DEPTH, 2, D_MODEL, D_FF), D_MODEL ** -0.5)
    ffn_w_down = nrm((DEPTH, 2, D_FF, D_MODEL), D_FF ** -0.5)
    even_w_in = nrm((N_EVEN, D_MODEL, EVEN_IN), D_MODEL ** -0.5)
    mlstm_b_i = nrm((N_EVEN, M_HEADS), 0.1)
    mlstm_b_f = jnp.linspace(3.0, 6.0, M_HEADS, dtype=jnp.float32)[None] + nrm((N_EVEN, M_HEADS), 0.1)
    mlstm_head_gain = 1.0 + nrm((N_EVEN, M_HEADS, M_DV), 0.02)
    swa_sinks = nrm((N_EVEN, S_HEADS))
    even_w_out = nrm((N_EVEN, M_V + S_Q, D_MODEL), (M_V + S_Q) ** -0.5)
    odd_w_in = nrm((N_ODD, D_MODEL, S5_WIDTH), D_MODEL ** -0.5)
    s5_a_re = -0.5 + nrm((N_ODD, S5_GROUPS, S5_STATE), 0.01)
    s5_a_im = jnp.pi * jnp.arange(S5_STATE, dtype=jnp.float32) + nrm((N_ODD, S5_GROUPS, S5_STATE), 0.01)
    s5_log_dt = jax.random.uniform(next(ks), (N_ODD, S5_GROUPS), dtype=jnp.float32,
                                   minval=math.log(0.001), maxval=math.log(0.1))
    s5_b_re = nrm((N_ODD, S5_GROUPS, S5_STATE, S5_CH), (2 * S5_CH) ** -0.5)
    s5_b_im = nrm((N_ODD, S5_GROUPS, S5_STATE, S5_CH), (2 * S5_CH) ** -0.5)
    s5_c_re = nrm((N_ODD, S5_GROUPS, S5_CH, S5_STATE), (2 * S5_STATE) ** -0.5)
    s5_c_im = nrm((N_ODD, S5_GROUPS, S5_CH, S5_STATE), (2 * S5_STATE) ** -0.5)
    s5_d = nrm((N_ODD, S5_WIDTH), 0.5)
    odd_w_out = nrm((N_ODD, S5_WIDTH, 2 * D_MODEL), S5_WIDTH ** -0.5)
    xattn_w_q = nrm((DEPTH, D_MODEL, X_WIDTH), D_MODEL ** -0.5)
    xattn_w_k = nrm((DEPTH, D_MODEL, X_WIDTH), D_MODEL ** -0.5)
    xattn_w_v = nrm((DEPTH, D_MODEL, X_WIDTH), D_MODEL ** -0.5)
    xattn_w_o = nrm((DEPTH, X_WIDTH, D_MODEL), X_WIDTH ** -0.5)
    return {'x_prompt': x_prompt, 'x_sample': x_sample, 'mem_prompt': mem_prompt,
            'cache_mem_k': cache_mem_k, 'cache_mem_v': cache_mem_v,
            'state_mlstm_C': state_mlstm_C, 'state_mlstm_n': state_mlstm_n, 'state_mlstm_m': state_mlstm_m,
            'cache_swa_k': cache_swa_k, 'cache_swa_v': cache_swa_v,
            'state_s5_re': state_s5_re, 'state_s5_im': state_s5_im,
            'norm_gain': norm_gain, 'final_gain': final_gain,
            'ffn_w_gate': ffn_w_gate, 'ffn_w_up': ffn_w_up, 'ffn_w_down': ffn_w_down,
            'even_w_in': even_w_in, 'mlstm_b_i': mlstm_b_i, 'mlstm_b_f': mlstm_b_f,
            'mlstm_head_gain': mlstm_head_gain, 'swa_sinks': swa_sinks, 'even_w_out': even_w_out,
            'odd_w_in': odd_w_in, 's5_a_re': s5_a_re, 's5_a_im': s5_a_im, 's5_log_dt': s5_log_dt,
            's5_b_re': s5_b_re, 's5_b_im': s5_b_im, 's5_c_re': s5_c_re, 's5_c_im': s5_c_im,
            's5_d': s5_d, 'odd_w_out': odd_w_out,
            'xattn_w_q': xattn_w_q, 'xattn_w_k': xattn_w_k, 'xattn_w_v': xattn_w_v, 'xattn_w_o': xattn_w_o}


def reference(x_prompt, x_sample, mem_prompt, cache_mem_k, cache_mem_v, state_mlstm_C, state_mlstm_n,
              state_mlstm_m, cache_swa_k, cache_swa_v, state_s5_re, state_s5_im, norm_gain, final_gain,
              ffn_w_gate, ffn_w_up, ffn_w_down, even_w_in, mlstm_b_i, mlstm_b_f, mlstm_head_gain, swa_sinks,
              even_w_out, odd_w_in, s5_a_re, s5_a_im, s5_log_dt, s5_b_re, s5_b_im, s5_c_re, s5_c_im, s5_d,
              odd_w_out, xattn_w_q, xattn_w_k, xattn_w_v, xattn_w_o):

    def trunk(x, mem_k, mem_v, C0, n0, m0, kbuf, vbuf, sr0, si0):
        st = {name: [] for name in ('C', 'n', 'm', 'k', 'v', 'sr', 'si')}
        for l in range(DEPTH):
            x = x + 0.5 * swiglu(rmsnorm(x, norm_gain[l, 0]), ffn_w_gate[l, 0], ffn_w_up[l, 0], ffn_w_down[l, 0])
            h = rmsnorm(x, norm_gain[l, 1])
            if l % 2 == 0:
                e = l // 2
                mix, (C, n, m, kw, vw) = even_mixer(
                    h, C0[e], n0[e], m0[e],
                    None if kbuf is None else kbuf[e], None if vbuf is None else vbuf[e],
                    even_w_in[e], mlstm_b_i[e], mlstm_b_f[e], mlstm_head_gain[e], swa_sinks[e], even_w_out[e])
                st['C'].append(C)
                st['n'].append(n)
                st['m'].append(m)
                st['k'].append(kw)
                st['v'].append(vw)
            else:
                o = l // 2
                mix, (sr, si) = odd_mixer(h, sr0[o], si0[o], odd_w_in[o], s5_a_re[o], s5_a_im[o], s5_log_dt[o],
                                          s5_b_re[o], s5_b_im[o], s5_c_re[o], s5_c_im[o], s5_d[o], odd_w_out[o])
                st['sr'].append(sr)
                st['si'].append(si)
            x = x + mix
            x = x + cross_attend(rmsnorm(x, norm_gain[l, 2]), mem_k[l], mem_v[l], xattn_w_q[l], xattn_w_o[l])
            x = x + 0.5 * swiglu(rmsnorm(x, norm_gain[l, 3]), ffn_w_gate[l, 1], ffn_w_up[l, 1], ffn_w_down[l, 1])
        return rmsnorm(x, final_gain), {name: jnp.stack(vals) for name, vals in st.items()}

    Bp = x_prompt.shape[0]
    mem_k_prompt = jnp.stack([(mem_prompt @ xattn_w_k[l]).reshape(Bp, MEM_LEN, X_HEADS, X_HD) for l in range(DEPTH)])
    mem_v_prompt = jnp.stack([(mem_prompt @ xattn_w_v[l]).reshape(Bp, MEM_LEN, X_HEADS, X_HD) for l in range(DEPTH)])
    y_prompt, sp = trunk(x_prompt, mem_k_prompt, mem_v_prompt,
                         jnp.zeros((N_EVEN, Bp, M_HEADS, M_DK, M_DV), jnp.float32),
                         jnp.zeros((N_EVEN, Bp, M_HEADS, M_DK), jnp.float32),
                         jnp.zeros((N_EVEN, Bp, M_HEADS), jnp.float32),
                         None, None,
                         jnp.zeros((N_ODD, Bp, S5_GROUPS, S5_STATE), jnp.float32),
                         jnp.zeros((N_ODD, Bp, S5_GROUPS, S5_STATE), jnp.float32))
    y_sample, ss = trunk(x_sample, cache_mem_k, cache_mem_v, state_mlstm_C, state_mlstm_n, state_mlstm_m,
                         cache_swa_k, cache_swa_v, state_s5_re, state_s5_im)
    return (y_prompt, y_sample, mem_k_prompt, mem_v_prompt,
            sp['C'], ss['C'], sp['n'], ss['n'], sp['m'], ss['m'],
            sp['k'], ss['k'], sp['v'], ss['v'],
            sp['sr'], ss['sr'], sp['si'], ss['si'])
```

```python
import numpy as np
import concourse.bass as bass
import concourse.mybir as mybir
from concourse.bass_utils import run_bass_kernel_spmd

F32 = mybir.dt.float32
BF16 = mybir.dt.bfloat16
AF = mybir.ActivationFunctionType
ALU = mybir.AluOpType
AX = mybir.AxisListType

D = 2048
KC = D // 128
DFF = 5632
NF = DFF // 128
SEQ = 2048
NB = 16
DSEQ = 4
NS = NB * DSEQ
NTOK = SEQ + NS
EPS = 1e-6


class Res:
    __slots__ = ("name", "w", "rs", "dsem", "dkey")

    def __init__(self, name):
        self.name = name
        self.w = {}
        self.rs = {}
        self.dsem = None
        self.dkey = None


class Ctx:
    def __init__(self, nc):
        self.nc = nc
        self.eng = {"pe": nc.tensor, "act": nc.scalar, "dve": nc.vector, "pool": nc.gpsimd, "sp": nc.sync}
        self.sem = {}
        self.cnt = {}
        for k in ("pe", "act", "dve", "pool"):
            self.sem[k] = nc.alloc_semaphore("sem_" + k)
            self.cnt[k] = 0
        self.waited = {k: {} for k in self.eng}
        self.nres = 0
        self.n_inst = 0
        self.free_d = []
        self.phase_d = []
        self.dead = set()
        self.semG = nc.alloc_semaphore("bar_gather")
        self.semR = nc.alloc_semaphore("bar_release")
        self.bar_n = 0

    def res(self, name=None):
        self.nres += 1
        return Res(name or f"r{self.nres}")

    def sb(self, name, shape, dtype):
        t = self.nc.alloc_sbuf_tensor(name, list(shape), dtype)
        return t, self.res(name)

    def _deps(self, reads, writes, skip=None):
        d = {}
        for r in reads:
            for k, v in r.w.items():
                if d.get(k, 0) < v:
                    d[k] = v
        for w in writes:
            for k, v in w.w.items():
                if d.get(k, 0) < v:
                    d[k] = v
            for k, v in w.rs.items():
                if d.get(k, 0) < v:
                    d[k] = v
        if skip is not None:
            d.pop(skip, None)
        return d

    def _wait(self, eng, deps):
        wd = self.waited[eng]
        e = self.eng[eng]
        for k, v in deps.items():
            if k in self.dead or wd.get(k, 0) >= v:
                continue
            if k.startswith("dma:"):
                v = self.cnt[k]
            e.wait_ge(self.sem[k], v)
            wd[k] = v
            self.n_inst += 1

    def _mark(self, tok, reads, writes):
        k, v = tok
        for r in reads:
            r.rs[k] = v
        for w in writes:
            w.w[k] = v

    def op(self, eng, fn, reads=(), writes=()):
        self._wait(eng, self._deps(reads, writes))
        inst = fn(self.eng[eng])
        self.cnt[eng] += 1
        inst.then_inc(self.sem[eng], 1)
        self.n_inst += 1
        self._mark((eng, self.cnt[eng]), reads, writes)

    def group(self, eng, fns, reads=(), writes=()):
        self._wait(eng, self._deps(reads, writes, skip="pe" if eng == "pe" else None))
        inst = None
        for fn in fns:
            inst = fn(self.eng[eng])
            self.n_inst += 1
        self.cnt[eng] += 1
        inst.then_inc(self.sem[eng], 1)
        self._mark((eng, self.cnt[eng]), reads, writes)

    def dma(self, q, out, in_, sres, reads=(), writes=(), **kw):
        if sres.dsem is None:
            if self.free_d:
                sres.dsem = self.free_d.pop()
            else:
                sres.dsem = self.nc.alloc_semaphore("d_" + str(self.nres))
            sres.dkey = "dma:" + str(self.nres)
            self.nres += 1
            self.sem[sres.dkey] = sres.dsem
            self.cnt[sres.dkey] = 0
            self.phase_d.append(sres)
        self._wait(q, self._deps(reads, writes))
        inst = self.eng[q].dma_start(out=out, in_=in_, **kw)
        self.cnt[sres.dkey] += 16
        inst.then_inc(sres.dsem, 16)
        self.n_inst += 1
        self._mark((sres.dkey, self.cnt[sres.dkey]), reads, writes)

    def finish(self, eng, resources):
        d = {}
        for r in resources:
            for k, v in r.w.items():
                if d.get(k, 0) < v:
                    d[k] = v
        self._wait(eng, d)

    def barrier(self):
        allk = {k: v for k, v in self.cnt.items() if v > 0 and k not in self.dead}
        for e in ("sp", "pe", "act", "dve", "pool"):
            self._wait(e, dict(allk))
        self.bar_n += 1
        for e in ("sp", "pe", "act", "dve"):
            self.eng[e].sem_inc(self.semG, 1)
        pool = self.eng["pool"]
        pool.wait_ge(self.semG, 4 * self.bar_n)
        for res in self.phase_d:
            pool.sem_clear(res.dsem)
        pool.sem_inc(self.semR, 1)
        for e in ("sp", "pe", "act", "dve"):
            self.eng[e].wait_ge(self.semR, self.bar_n)
        for res in self.phase_d:
            self.dead.add(res.dkey)
            self.free_d.append(res.dsem)
            res.dsem = None
            res.dkey = None
        self.phase_d = []


def blocks_of(r0, r1):
    out = []
    r = r0
    while r < r1:
        n = min(128, r1 - r)
        out.append((r, n))
        r += n
    return out


class Prog:
    def __init__(self, cfg=None):
        self.cfg = cfg or {}
        self.nc = bass.Bass("TRN2", target_bir_lowering=False)
        self.cx = Ctx(self.nc)
        self.dram = {}
        self.dres = {}
        nc = self.nc
        self.bank = []
        self.bres = []
        self.psum = nc.alloc_psum_tensor("psum_all", [128, 4096], F32)
        for i in range(8):
            self.bank.append(self.psum[:, i * 512:(i + 1) * 512])
            self.bres.append(self.cx.res(f"bank{i}"))
        self.ident = None
        self.outputs = []

    def din(self, name, shape):
        t = self.nc.dram_tensor(name, list(shape), F32, kind="ExternalInput")
        self.dram[name] = t
        self.dres[name] = self.cx.res(name)
        return t

    def dout(self, name, shape):
        t = self.nc.dram_tensor(name, list(shape), F32, kind="ExternalOutput")
        self.dram[name] = t
        self.dres[name] = self.cx.res(name)
        self.outputs.append(name)
        return t

    def dscr(self, name, shape, dtype=F32):
        t = self.nc.dram_tensor(name, list(shape), dtype, kind="Internal")
        self.dram[name] = t
        self.dres[name] = self.cx.res(name)
        return t

    def setup_consts(self):
        nc, cx = self.nc, self.cx
        self.ident_f, self.ident_f_r = cx.sb("ident_f", [128, 128], F32)
        self.ident_b, self.ident_b_r = cx.sb("ident_b", [128, 128], BF16)
        cx.op("pool", lambda e: e.memset(self.ident_f[:], 1.0), writes=[self.ident_f_r])
        cx.op("pool", lambda e: e.affine_select(out=self.ident_f[:], in_=self.ident_f[:], pattern=[[-1, 128]],
                                               compare_op=ALU.is_equal, fill=0.0, base=0, channel_multiplier=1),
              reads=[self.ident_f_r], writes=[self.ident_f_r])
        cx.op("pool", lambda e: e.tensor_copy(out=self.ident_b[:], in_=self.ident_f[:]),
              reads=[self.ident_f_r], writes=[self.ident_b_r])
        self.eps_t, self.eps_r = cx.sb("eps_t", [128, 1], F32)
        cx.op("pool", lambda e: e.memset(self.eps_t[:], EPS), writes=[self.eps_r])
        self.tp_bank = 0
        self._nb = 0
        self._norm_bufs = None

    def norm_blocks(self, es, srcs, gain_ap, hT, hT_r, tag):
        nc, cx = self.nc, self.cx
        gbc, gbc_r = self._sb(es, tag + "gbc", [128, D], F32)
        cx.dma("sp", gbc[:], gain_ap.partition_broadcast(128), gbc_r, writes=[gbc_r])
        xt, xt_r = self._sb(es, tag + "xt", [128, D], F32)
        sq, sq_r = self._sb(es, tag + "sq", [128, D], BF16)
        xn, xn_r = self._sb(es, tag + "xn", [128, D], BF16)
        st, st_r = self._sb(es, tag + "st", [128, 4], F32)
        for (src, src_r, n, col0) in srcs:
            cx.dma("sp", xt[0:n, :], src, xt_r, reads=[src_r], writes=[xt_r])
            cx.op("act", lambda e: e.activation(out=sq[0:n, :], in_=xt[0:n, :], func=AF.Square,
                                                accum_out=st[0:n, 0:1]),
                  reads=[xt_r], writes=[sq_r, st_r])
            cx.op("act", lambda e: e.activation(out=st[0:n, 1:2], in_=st[0:n, 0:1], func=AF.Sqrt,
                                                scale=1.0 / D, bias=self.eps_t[0:n, 0:1]),
                  reads=[st_r, self.eps_r], writes=[st_r])
            cx.op("dve", lambda e: e.reciprocal(out=st[0:n, 2:3], in_=st[0:n, 1:2]), reads=[st_r], writes=[st_r])
            cx.op("dve", lambda e: e.scalar_tensor_tensor(out=xn[0:n, :], in0=xt[0:n, :], scalar=st[0:n, 2:3],
                                                          in1=gbc[0:n, :], op0=ALU.mult, op1=ALU.mult),
                  reads=[xt_r, st_r, gbc_r], writes=[xn_r])
            self.transpose_into(xn, xn_r, n, hT, hT_r, col0)

    def transpose_into(self, xn, xn_r, n, hT, hT_r, col0, nk=KC):
        cx = self.cx
        for half in range((nk + 7) // 8):
            k0 = half * 8
            k1 = min(nk, k0 + 8)
            b = self.nb()
            pb = self.bank[b][:].bitcast(BF16)
            fns = []
            for kc in range(k0, k1):
                fns.append(lambda e, kc=kc: e.transpose(out=pb[:, (kc - k0) * 128:(kc - k0) * 128 + n],
                                                        in_=xn[0:n, kc * 128:(kc + 1) * 128],
                                                        identity=self.ident_b[0:n, 0:n]))
            cx.group("pe", fns, reads=[xn_r, self.ident_b_r], writes=[self.bres[b]])
            src = pb.rearrange("p (k c) -> p k c", c=128)[:, 0:k1 - k0, 0:n]
            eng = "act" if half % 2 == 0 else "dve"
            if eng == "act":
                cx.op("act", lambda e: e.copy(out=hT[:, k0:k1, col0:col0 + n], in_=src),
                      reads=[self.bres[b]], writes=[hT_r])
            else:
                cx.op("dve", lambda e: e.tensor_copy(out=hT[:, k0:k1, col0:col0 + n], in_=src),
                      reads=[self.bres[b]], writes=[hT_r])

    def _sb(self, es, name, shape, dtype):
        t = es.enter_context(self.nc.sbuf_tensor(name, list(shape), dtype))
        return t, self.cx.res(name)

    def ffn_phase(self, tag, wgu, wd, gain_ap, tiles, src_fn, dst_fn):
        import contextlib
        nc, cx = self.nc, self.cx
        wgu_r = self.dres[wgu.name]
        wd_r = self.dres[wd.name]
        Tmax = max(sum(n for _, n in t) for t in tiles)
        CW = 256
        NCB = D // CW
        with contextlib.ExitStack() as es:
            aT, aT_r = self._sb(es, tag + "aT", [128, NF, Tmax], BF16)
            hT, hT_r = self._sb(es, tag + "hT", [128, KC, Tmax], BF16)
            for ti, tile in enumerate(tiles):
                cols = []
                c = 0
                for (r0, n) in tile:
                    cols.append(c)
                    c += n
                T = c
                with contextlib.ExitStack() as es2:
                    srcs = []
                    for (r0, n), c0 in zip(tile, cols):
                        ap, r = src_fn(r0, n)
                        srcs.append((ap, r, n, c0))
                    self._norm_bufs = None
                    self.norm_blocks_cached(es2, srcs, gain_ap, hT, hT_r, f"{tag}{ti}")
                    cx.barrier()
                with contextlib.ExitStack() as es2:
                    NST = 3
                    wst = [self._sb(es2, f"{tag}{ti}wst{i}", [128, 2 * KC * 128], F32) for i in range(NST)]
                    wbf = [self._sb(es2, f"{tag}{ti}wbf{i}", [128, 2 * KC * 128], BF16) for i in range(2)]
                    sil = [self._sb(es2, f"{tag}{ti}sil{i}", [128, 512], F32) for i in range(2)]
                    subs = self.sub_tiles(T)
                    u = 0
                    H = KC * 128
                    for f in range(NF):
                        s_t, s_r = wst[f % NST]
                        b_t, b_r = wbf[f % 2]
                        cx.dma("sp", s_t[:], wgu[f], s_r, reads=[wgu_r], writes=[s_r])
                        cx.op("act", lambda e: e.copy(out=b_t[:, 0:H], in_=s_t[:, 0:H]), reads=[s_r], writes=[b_r])
                        cx.op("dve", lambda e: e.tensor_copy(out=b_t[:, H:2 * H], in_=s_t[:, H:2 * H]), reads=[s_r], writes=[b_r])
                        for (c0, n) in subs:
                            bg = 2 * (u % 4)
                            bu = bg + 1
                            fns = []
                            for gu, bk in ((0, bg), (1, bu)):
                                for kc in range(KC):
                                    fns.append(lambda e, gu=gu, bk=bk, kc=kc: e.matmul(
                                        self.bank[bk][:, 0:n],
                                        lhsT=b_t[:, (gu * KC + kc) * 128:(gu * KC + kc + 1) * 128],
                                        rhs=hT[:, kc, c0:c0 + n], start=(kc == 0), stop=(kc == KC - 1)))
                            cx.group("pe", fns, reads=[b_r, hT_r], writes=[self.bres[bg], self.bres[bu]])
                            sl_t, sl_r = sil[u % 2]
                            cx.op("act", lambda e: e.activation(out=sl_t[:, 0:n], in_=self.bank[bg][:, 0:n], func=AF.Silu),
                                  reads=[self.bres[bg]], writes=[sl_r])
                            cx.op("dve", lambda e: e.tensor_tensor(out=aT[:, f, c0:c0 + n], in0=sl_t[:, 0:n],
                                                                   in1=self.bank[bu][:, 0:n], op=ALU.mult),
                                  reads=[sl_r, self.bres[bu]], writes=[aT_r])
                            u += 1
                    cx.barrier()
                if self.cfg.get('skipD'):
                    continue
                with contextlib.ExitStack() as es2:
                    wres = [self._sb(es2, f"{tag}{ti}wres{i}", [128, NF, CW], BF16) for i in range(2)]
                    dst_ = [self._sb(es2, f"{tag}{ti}dst{i}", [128, 11, CW], F32) for i in range(2)]
                    xo = [self._sb(es2, f"{tag}{ti}xo{i}", [128, CW], F32) for i in range(4)]
                    xw = [self._sb(es2, f"{tag}{ti}xw{i}", [128, CW], F32) for i in range(4)]
                    g = 0
                    ev = 0
                    hb = 0
                    hres = [cx.res(f"{tag}{ti}hb{i}") for i in range(16)]
                    def load_wd(cb):
                        nonlocal g
                        w_t, w_r = wres[cb % 2]
                        for q in range(4):
                            s_t, s_r = dst_[g % 2]
                            g += 1
                            src = wd[q * 11 * 128:(q + 1) * 11 * 128, cb * CW:(cb + 1) * CW].rearrange("(j p) c -> p j c", p=128)
                            cx.dma("sp", s_t[:], src, s_r, reads=[wd_r], writes=[s_r])
                            self.cast(w_t[:, q * 11:(q + 1) * 11, :], s_t[:], [s_r], [w_r])

                    load_wd(0)
                    for cb in range(NCB):
                        w_t, w_r = wres[cb % 2]
                        if cb + 1 < NCB:
                            load_wd(cb + 1)
                        for bi, ((r0, n), c0) in enumerate(zip(tile, cols)):
                            bk, half = hb % 8, hb // 8
                            h_r = hres[hb % 8]
                            hb = (hb + 1) % 16
                            o = self.bank[bk][0:n, half * CW:(half + 1) * CW]
                            fns = [lambda e, f=f: e.matmul(o, lhsT=aT[:, f, c0:c0 + n], rhs=w_t[:, f, :],
                                                           start=(f == 0), stop=(f == NF - 1)) for f in range(NF)]
                            cx.group("pe", fns, reads=[w_r, aT_r], writes=[h_r])
                            xo_t, xo_r = xo[ev % 4]
                            xw_t, xw_r = xw[ev % 4]
                            ev += 1
                            sap, sr = src_fn(r0, n)
                            dap, dr = dst_fn(r0, n)
                            cx.dma("sp", xo_t[0:n, :], sap[:, cb * CW:(cb + 1) * CW], xo_r, reads=[sr], writes=[xo_r])
                            cx.op("dve", lambda e: e.scalar_tensor_tensor(out=xw_t[0:n, :], in0=o, scalar=0.5,
                                                                          in1=xo_t[0:n, :], op0=ALU.mult, op1=ALU.add),
                                  reads=[h_r, xo_r], writes=[xw_r])
                            cx.dma("act", dap[:, cb * CW:(cb + 1) * CW], xw_t[0:n, :], xw_r, reads=[xw_r], writes=[dr])
                    cx.barrier()

    def norm_blocks_cached(self, es, srcs, gain_ap, hT, hT_r, tag):
        cx = self.cx
        if self._norm_bufs is None:
            gbc, gbc_r = self._sb(es, tag + "gbc", [128, D], F32)
            cx.dma("sp", gbc[:], gain_ap.partition_broadcast(128), gbc_r, writes=[gbc_r])
            xt = self._sb(es, tag + "xt", [128, D], F32)
            sq = self._sb(es, tag + "sq", [128, D], BF16)
            xn = self._sb(es, tag + "xn", [128, D], BF16)
            st = self._sb(es, tag + "st", [128, 4], F32)
            self._norm_bufs = (gbc, gbc_r, xt, sq, xn, st)
        gbc, gbc_r, (xt, xt_r), (sq, sq_r), (xn, xn_r), (st, st_r) = self._norm_bufs
        for (src, src_r, n, col0) in srcs:
            cx.dma("sp", xt[0:n, :], src, xt_r, reads=[src_r], writes=[xt_r])
            cx.op("act", lambda e: e.activation(out=sq[0:n, :], in_=xt[0:n, :], func=AF.Square,
                                                accum_out=st[0:n, 0:1]),
                  reads=[xt_r], writes=[sq_r, st_r])
            cx.op("act", lambda e: e.activation(out=st[0:n, 1:2], in_=st[0:n, 0:1], func=AF.Sqrt,
                                                scale=1.0 / D, bias=self.eps_t[0:n, 0:1]),
                  reads=[st_r, self.eps_r], writes=[st_r])
            cx.op("dve", lambda e: e.reciprocal(out=st[0:n, 2:3], in_=st[0:n, 1:2]), reads=[st_r], writes=[st_r])
            cx.op("dve", lambda e: e.scalar_tensor_tensor(out=xn[0:n, :], in0=xt[0:n, :], scalar=st[0:n, 2:3],
                                                          in1=gbc[0:n, :], op0=ALU.mult, op1=ALU.mult),
                  reads=[xt_r, st_r, gbc_r], writes=[xn_r])
            self.transpose_into(xn, xn_r, n, hT, hT_r, col0)

    def cast(self, out, in_, reads, writes, eng=None):
        if eng is None:
            self._ce = 1 - getattr(self, "_ce", 0)
            eng = "act" if self._ce else "dve"
        if eng == "act":
            self.cx.op("act", lambda e: e.copy(out=out, in_=in_), reads=reads, writes=writes)
        else:
            self.cx.op("dve", lambda e: e.tensor_copy(out=out, in_=in_), reads=reads, writes=writes)

    def nb(self):
        b = self._nb
        self._nb = (self._nb + 1) % 8
        return b

    def sub_tiles(self, T, w=512):
        out = []
        c = 0
        while c < T:
            n = min(w, T - c)
            out.append((c, n))
            c += n
        return out

    def wbufs(self, es, tag, nk, w, nst=2, nbf=2):
        return {"st": [self._sb(es, f"{tag}wS{i}", [128, nk, w], F32) for i in range(nst)],
                "bf": [self._sb(es, f"{tag}wB{i}", [128, nk, w], BF16) for i in range(nbf)], "i": 0}

    def load_w(self, bufs, src_ap, src_r, nk, w):
        cx = self.cx
        i = bufs["i"]
        bufs["i"] += 1
        st, st_r = bufs["st"][i % len(bufs["st"])]
        bf, bf_r = bufs["bf"][i % len(bufs["bf"])]
        cx.dma("sp", st[:, 0:nk, 0:w], src_ap.rearrange("(k p) c -> p k c", p=128), st_r, reads=[src_r], writes=[st_r])
        h = max(1, nk // 2)
        self.cast(bf[:, 0:h, 0:w], st[:, 0:h, 0:w], [st_r], [bf_r], eng="act")
        if nk > h:
            self.cast(bf[:, h:nk, 0:w], st[:, h:nk, 0:w], [st_r], [bf_r], eng="dve")
        return bf, bf_r

    def load_w_resident(self, bufs, W_ap, W_r, nk, N, dst, dst_r):
        cx = self.cx
        for k0 in range(0, nk, 4):
            k1 = min(nk, k0 + 4)
            for c0 in range(0, N, 512):
                c1 = min(N, c0 + 512)
                i = bufs["i"]
                bufs["i"] += 1
                st, st_r = bufs["st"][i % len(bufs["st"])]
                cx.dma("sp", st[:, 0:k1 - k0, 0:c1 - c0],
                       W_ap[k0 * 128:k1 * 128, c0:c1].rearrange("(k p) c -> p k c", p=128), st_r,
                       reads=[W_r], writes=[st_r])
                self.cast(dst[:, k0:k1, c0:c1], st[:, 0:k1 - k0, 0:c1 - c0], [st_r], [dst_r])

    def norm_all(self, es, tag, src_fn, gain_ap, blocks):
        T = max(r0 + n for r0, n in blocks)
        hT, hT_r = self._sb(es, tag + "hT", [128, KC, T], BF16)
        self._norm_bufs = None
        srcs = []
        for (r0, n) in blocks:
            ap, r = src_fn(r0, n)
            srcs.append((ap, r, n, r0))
        import contextlib
        with contextlib.ExitStack() as es2:
            self.norm_blocks_cached(es2, srcs, gain_ap, hT, hT_r, tag)
            self.cx.barrier()
        return hT, hT_r

    def mm_tok(self, hT, hT_r, nk, col0, n, wb, wb_r, wc0, w, bank):
        fns = [lambda e, kc=kc: e.matmul(self.bank[bank][0:n, 0:w], lhsT=hT[:, kc, col0:col0 + n],
                                         rhs=wb[:, kc, wc0:wc0 + w], start=(kc == 0), stop=(kc == nk - 1))
               for kc in range(nk)]
        self.cx.group("pe", fns, reads=[hT_r, wb_r], writes=[self.bres[bank]])

    def mm_feat(self, hT, hT_r, nk, c0, n, wb, wb_r, wc0, m, bank, bcol=0):
        fns = [lambda e, kc=kc: e.matmul(self.bank[bank][0:m, bcol:bcol + n], lhsT=wb[:, kc, wc0:wc0 + m],
                                         rhs=hT[:, kc, c0:c0 + n], start=(kc == 0), stop=(kc == nk - 1))
               for kc in range(nk)]
        self.cx.group("pe", fns, reads=[hT_r, wb_r], writes=[self.bres[bank]])

    def out_proj(self, es, tag, oT, oT_r, nk, W_ap, W_r, blocks, src_fn, dst_fn):
        cx = self.cx
        wres, wres_r = self._sb(es, tag + "Wres", [128, nk, D], BF16)
        bufs = self.wbufs(es, tag + "op", 4, 512, nst=2, nbf=0)
        self.load_w_resident(bufs, W_ap, W_r, nk, D, wres, wres_r)
        xo = [self._sb(es, f"{tag}oxo{i}", [128, 512], F32) for i in range(3)]
        xw = [self._sb(es, f"{tag}oxw{i}", [128, 512], F32) for i in range(3)]
        ev = 0
        for (r0, n, col0) in blocks:
            sap, sr = src_fn(r0, n)
            dap, dr = dst_fn(r0, n)
            for cb in range(4):
                b = self.nb()
                self.mm_tok(oT, oT_r, nk, col0, n, wres, wres_r, cb * 512, 512, b)
                xo_t, xo_r = xo[ev % 3]
                xw_t, xw_r = xw[ev % 3]
                ev += 1
                cx.dma("sp", xo_t[0:n, :], sap[:, cb * 512:(cb + 1) * 512], xo_r, reads=[sr], writes=[xo_r])
                cx.op("dve", lambda e: e.tensor_tensor(out=xw_t[0:n, :], in0=self.bank[b][0:n, :], in1=xo_t[0:n, :], op=ALU.add),
                      reads=[self.bres[b], xo_r], writes=[xw_r])
                cx.dma("sp", dap[:, cb * 512:(cb + 1) * 512], xw_t[0:n, :], xw_r, reads=[xw_r], writes=[dr])

    def transpose_blk(self, src, src_r, n, ncols, dst_fn, dst_r, dt=BF16, evac="act", scale=None):
        cx = self.cx
        nk = ncols // 128
        per = 8 if dt == BF16 else 4
        ident, ident_r = (self.ident_b, self.ident_b_r) if dt == BF16 else (self.ident_f, self.ident_f_r)
        for k0 in range(0, nk, per):
            k1 = min(nk, k0 + per)
            b = self.nb()
            pb = self.bank[b][:].bitcast(BF16) if dt == BF16 else self.bank[b][:]
            fns = [lambda e, kc=kc: e.transpose(out=pb[:, (kc - k0) * 128:(kc - k0) * 128 + n],
                                                in_=src[0:n, kc * 128:(kc + 1) * 128], identity=ident[0:n, 0:n])
                   for kc in range(k0, k1)]
            cx.group("pe", fns, reads=[src_r, ident_r], writes=[self.bres[b]])
            for kc in range(k0, k1):
                o = dst_fn(kc)
                i_ = pb[:, (kc - k0) * 128:(kc - k0) * 128 + n]
                if evac == "act":
                    cx.op("act", lambda e: e.copy(out=o, in_=i_), reads=[self.bres[b]], writes=[dst_r])
                else:
                    cx.op("dve", lambda e: e.tensor_copy(out=o, in_=i_), reads=[self.bres[b]], writes=[dst_r])

    def memkv_phase(self):
        import contextlib
        cx = self.cx
        mem = self.dram["mem_p"]
        with contextlib.ExitStack() as es:
            mT, mT_r = self._sb(es, "mkT", [128, KC, 256], BF16)
            xt, xt_r = self._sb(es, "mkx", [128, D], F32)
            xb, xb_r = self._sb(es, "mkxb", [128, D], BF16)
            for blk in range(2):
                cx.dma("sp", xt[:], mem.ap()[blk * 128:(blk + 1) * 128, :], xt_r, reads=[self.dres["mem_p"]], writes=[xt_r])
                cx.op("act", lambda e: e.copy(out=xb[:], in_=xt[:]), reads=[xt_r], writes=[xb_r])
                self.transpose_into(xb, xb_r, 128, mT, mT_r, blk * 128)
            bufs = self.wbufs(es, "mk", KC, 512)
            ev = [self._sb(es, f"mkev{i}", [128, 512], F32) for i in range(2)]
            k = 0
            for l in range(2):
                for nm, wn in (("mem_k_p", "xwk"), ("mem_v_p", "xwv")):
                    wb, wb_r = self.load_w(bufs, self.dram[wn].ap()[l], self.dres[wn], KC, 512)
                    for blk in range(2):
                        b = self.nb()
                        self.mm_tok(mT, mT_r, KC, blk * 128, 128, wb, wb_r, 0, 512, b)
                        e_t, e_r = ev[k % 2]
                        k += 1
                        cx.op("act", lambda e: e.copy(out=e_t[:], in_=self.bank[b][:]), reads=[self.bres[b]], writes=[e_r])
                        cx.dma("sp", self.dram[nm].ap()[l, blk * 128:(blk + 1) * 128, :], e_t[:], e_r,
                               reads=[e_r], writes=[self.dres[nm]])
        cx.barrier()

    def final_phase(self, src_fn, dst_fn, gain_ap, blocks):
        import contextlib
        cx = self.cx
        with contextlib.ExitStack() as es:
            gbc, gbc_r = self._sb(es, "fgbc", [128, D], F32)
            cx.dma("sp", gbc[:], gain_ap.partition_broadcast(128), gbc_r, writes=[gbc_r])
            xt = [self._sb(es, f"fx{i}", [128, D], F32) for i in range(2)]
            sq, sq_r = self._sb(es, "fsq", [128, D], BF16)
            yo = [self._sb(es, f"fy{i}", [128, D], F32) for i in range(2)]
            st, st_r = self._sb(es, "fst", [128, 4], F32)
            for i, (r0, n) in enumerate(blocks):
                x_t, x_r = xt[i % 2]
                y_t, y_r = yo[i % 2]
                sap, sr = src_fn(r0, n)
                dap, dr = dst_fn(r0, n)
                cx.dma("sp", x_t[0:n, :], sap, x_r, reads=[sr], writes=[x_r])
                cx.op("act", lambda e: e.activation(out=sq[0:n, :], in_=x_t[0:n, :], func=AF.Square, accum_out=st[0:n, 0:1]),
                      reads=[x_r], writes=[sq_r, st_r])
                cx.op("act", lambda e: e.activation(out=st[0:n, 1:2], in_=st[0:n, 0:1], func=AF.Sqrt, scale=1.0 / D,
                                                    bias=self.eps_t[0:n, 0:1]), reads=[st_r, self.eps_r], writes=[st_r])
                cx.op("dve", lambda e: e.reciprocal(out=st[0:n, 2:3], in_=st[0:n, 1:2]), reads=[st_r], writes=[st_r])
                cx.op("dve", lambda e: e.scalar_tensor_tensor(out=y_t[0:n, :], in0=x_t[0:n, :], scalar=st[0:n, 2:3],
                                                              in1=gbc[0:n, :], op0=ALU.mult, op1=ALU.mult),
                      reads=[x_r, st_r, gbc_r], writes=[y_r])
                cx.dma("sp", dap, y_t[0:n, :], y_r, reads=[y_r], writes=[dr])
        cx.barrier()

    def xattn_phase(self, l, src_fn, dst_fn):
        import contextlib
        cx = self.cx
        tag = f"xa{l}"
        blocks = blocks_of(0, SEQ) + [(SEQ, NS)]
        scale = 128 ** -0.5
        with contextlib.ExitStack() as es:
            hT, hT_r = self.norm_all(es, tag, src_fn, self.dram["norm_gain"].ap()[l, 2], blocks)
            qT, qT_r = self._sb(es, tag + "qT", [128, 4, NTOK], BF16)
            oT, oT_r = self._sb(es, tag + "oT", [128, 4, NTOK], BF16)
            with contextlib.ExitStack() as es2:
                bufs = self.wbufs(es2, tag + "q", KC, 512, nst=1, nbf=1)
                wb, wb_r = self.load_w(bufs, self.dram["xwq"].ap()[l], self.dres["xwq"], KC, 512)
                for hd in range(4):
                    for (c0, n) in self.sub_tiles(NTOK):
                        b = self.nb()
                        self.mm_feat(hT, hT_r, KC, c0, n, wb, wb_r, hd * 128, 128, b)
                        cx.op("act", lambda e: e.copy(out=qT[:, hd, c0:c0 + n], in_=self.bank[b][:, 0:n]),
                              reads=[self.bres[b]], writes=[qT_r])
                cx.barrier()
            kf = [self._sb(es, f"{tag}kf{i}", [128, 2, 512], F32) for i in range(2)]
            vf = [self._sb(es, f"{tag}vf{i}", [128, 2, 512], F32) for i in range(2)]
            kb = [self._sb(es, f"{tag}kb{i}", [128, 2, 512], BF16) for i in range(2)]
            vb = [self._sb(es, f"{tag}vb{i}", [128, 2, 512], BF16) for i in range(2)]
            kT = [self._sb(es, f"{tag}kT{i}", [128, 4, 256], BF16) for i in range(2)]
            pp = [self._sb(es, f"{tag}p{i}", [128, 256], F32) for i in range(2)]
            pn = [self._sb(es, f"{tag}pn{i}", [128, 256], BF16) for i in range(2)]
            pT = [self._sb(es, f"{tag}pT{i}", [128, 2, 128], BF16) for i in range(2)]
            sm = [self._sb(es, f"{tag}sm{i}", [128, 4], F32) for i in range(2)]
            u = 0

            def load_kv(i, k_ap, k_r, v_ap, v_r):
                kf_t, kf_r = kf[i % 2]
                vf_t, vf_r = vf[i % 2]
                kb_t, kb_r = kb[i % 2]
                vb_t, vb_r = vb[i % 2]
                kT_t, kT_r = kT[i % 2]
                cx.dma("sp", kf_t[:], k_ap.rearrange("(c p) f -> p c f", p=128), kf_r, reads=[k_r], writes=[kf_r])
                cx.dma("sp", vf_t[:], v_ap.rearrange("(c p) f -> p c f", p=128), vf_r, reads=[v_r], writes=[vf_r])
                self.cast(kb_t[:], kf_t[:], [kf_r], [kb_r], eng="act")
                self.cast(vb_t[:], vf_t[:], [vf_r], [vb_r], eng="dve")
                for mc in range(2):
                    self.transpose_blk(kb_t[:, mc, :], kb_r, 128, 512,
                                       lambda hd: kT_t[:, hd, mc * 128:(mc + 1) * 128], kT_r, evac="dve")
                return kT_t, kT_r, vb_t, vb_r

            def s1(c0, n, kT_t, kT_r, vb_t, vb_r, hd, uu):
                p_t, p_r = pp[uu % 2]
                sm_t, sm_r = sm[uu % 2]
                b = self.nb()
                cx.group("pe", [lambda e: e.matmul(self.bank[b][0:n, 0:256], lhsT=qT[:, hd, c0:c0 + n],
                                                   rhs=kT_t[:, hd, :], start=True, stop=True)],
                         reads=[qT_r, kT_r], writes=[self.bres[b]])
                cx.op("dve", lambda e: e.reduce_max(out=sm_t[0:n, 0:1], in_=self.bank[b][0:n, 0:256], axis=AX.X),
                      reads=[self.bres[b]], writes=[sm_r])
                cx.op("dve", lambda e: e.tensor_scalar_mul(out=sm_t[0:n, 1:2], in0=sm_t[0:n, 0:1], scalar1=-scale),
                      reads=[sm_r], writes=[sm_r])
                cx.op("act", lambda e: e.activation(out=p_t[0:n, :], in_=self.bank[b][0:n, 0:256], func=AF.Exp,
                                                    scale=scale, bias=sm_t[0:n, 1:2], accum_out=sm_t[0:n, 2:3]),
                      reads=[self.bres[b], sm_r], writes=[p_r, sm_r])

            def s2(c0, n, kT_t, kT_r, vb_t, vb_r, hd, uu):
                p_t, p_r = pp[uu % 2]
                pn_t, pn_r = pn[uu % 2]
                pT_t, pT_r = pT[uu % 2]
                sm_t, sm_r = sm[uu % 2]
                cx.op("dve", lambda e: e.reciprocal(out=sm_t[0:n, 3:4], in_=sm_t[0:n, 2:3]), reads=[sm_r], writes=[sm_r])
                cx.op("dve", lambda e: e.tensor_scalar_mul(out=pn_t[0:n, :], in0=p_t[0:n, :], scalar1=sm_t[0:n, 3:4]),
                      reads=[p_r, sm_r], writes=[pn_r])
                self.transpose_blk(pn_t, pn_r, n, 256, lambda mc: pT_t[:, mc, 0:n], pT_r, evac="act")
                b2 = self.nb()
                cx.group("pe", [lambda e, mc=mc: e.matmul(self.bank[b2][:, 0:n], lhsT=vb_t[:, mc, hd * 128:(hd + 1) * 128],
                                                          rhs=pT_t[:, mc, 0:n], start=(mc == 0), stop=(mc == 1))
                                for mc in range(2)],
                         reads=[vb_r, pT_r], writes=[self.bres[b2]])
                cx.op("act", lambda e: e.copy(out=oT[:, hd, c0:c0 + n], in_=self.bank[b2][:, 0:n]),
                      reads=[self.bres[b2]], writes=[oT_r])

            pend = [None]

            def unit(c0, n, kT_t, kT_r, vb_t, vb_r):
                nonlocal u
                for hd in range(4):
                    item = (c0, n, kT_t, kT_r, vb_t, vb_r, hd, u)
                    u += 1
                    s1(*item)
                    if pend[0] is not None:
                        s2(*pend[0])
                    pend[0] = item

            kvp = load_kv(0, self.dram["mem_k_p"].ap()[l], self.dres["mem_k_p"], self.dram["mem_v_p"].ap()[l], self.dres["mem_v_p"])
            for (r0, n) in blocks_of(0, SEQ):
                unit(r0, n, *kvp)
            for bi in range(NB):
                kv = load_kv(bi + 1, self.dram["cmk"].ap()[l, bi], self.dres["cmk"], self.dram["cmv"].ap()[l, bi], self.dres["cmv"])
                unit(SEQ + bi * DSEQ, DSEQ, *kv)
            s2(*pend[0])
            self.out_proj(es, tag, oT, oT_r, 4, self.dram["xwo"].ap()[l], self.dres["xwo"],
                          [(r0, n, r0) for (r0, n) in blocks], src_fn, dst_fn)
        cx.barrier()

    def even_in_phase(self, src_fn):
        import contextlib
        cx = self.cx
        tag = "ei"
        blocks = blocks_of(0, SEQ) + [(SEQ, NS)]
        d, r = self.dram, self.dres
        with contextlib.ExitStack() as es:
            hT, hT_r = self.norm_all(es, tag, src_fn, d["norm_gain"].ap()[0, 1], blocks)
            bufs = self.wbufs(es, tag, KC, 512)
            evf = [self._sb(es, f"{tag}evf{i}", [128, 512], F32) for i in range(3)]
            evb = [self._sb(es, f"{tag}evb{i}", [128, 512], BF16) for i in range(3)]
            k = 0
            tok_dst = [("KM", 0, BF16), ("VM", 0, BF16), ("VM", 512, BF16), ("OM", 0, F32), ("OM", 512, F32), (None, 0, F32)]
            for cb in range(6):
                wb, wb_r = self.load_w(bufs, d["wtok"].ap()[:, cb * 512:(cb + 1) * 512], r["wtok"], KC, 512)
                nm, dc, dt = tok_dst[cb]
                for (r0, n) in blocks:
                    b = self.nb()
                    self.mm_tok(hT, hT_r, KC, r0, n, wb, wb_r, 0, 512, b)
                    e_t, e_r = (evf if dt == F32 else evb)[k % 3]
                    k += 1
                    cx.op("act", lambda e: e.copy(out=e_t[0:n, :], in_=self.bank[b][0:n, :]), reads=[self.bres[b]], writes=[e_r])
                    if nm is not None:
                        cx.dma("sp", d[nm].ap()[r0:r0 + n, dc:dc + 512], e_t[0:n, :], e_r, reads=[e_r], writes=[r[nm]])
                    else:
                        cx.dma("sp", d["KS"].ap()[r0:r0 + n, :], e_t[0:n, 0:256], e_r, reads=[e_r], writes=[r["KS"]])
                        cx.dma("sp", d["VS"].ap()[r0:r0 + n, :], e_t[0:n, 256:512], e_r, reads=[e_r], writes=[r["VS"]])
            feat_dst = [("QMT", 0), ("KMT", 0), ("QST", 0), ("QST", 512), ("KSTD", 0)]
            for jb in range(5):
                wb, wb_r = self.load_w(bufs, d["wfeat"].ap()[:, jb * 512:(jb + 1) * 512], r["wfeat"], KC, 512)
                nm, dr0 = feat_dst[jb]
                for jj in range(4):
                    for (c0, n) in self.sub_tiles(NTOK):
                        b = self.nb()
                        self.mm_feat(hT, hT_r, KC, c0, n, wb, wb_r, jj * 128, 128, b)
                        e_t, e_r = evb[k % 3]
                        k += 1
                        if nm == "QMT":
                            cx.op("act", lambda e: e.mul(out=e_t[:, 0:n], in_=self.bank[b][:, 0:n], mul=128 ** -0.5),
                                  reads=[self.bres[b]], writes=[e_r])
                        else:
                            cx.op("act", lambda e: e.copy(out=e_t[:, 0:n], in_=self.bank[b][:, 0:n]), reads=[self.bres[b]], writes=[e_r])
                        rr = dr0 + jj * 128
                        cx.dma("sp", d[nm].ap()[rr:rr + 128, c0:c0 + n], e_t[:, 0:n], e_r, reads=[e_r], writes=[r[nm]])
            wif_s, wif_sr = self._sb(es, tag + "wifs", [128, KC, 8], F32)
            wif, wif_r = self._sb(es, tag + "wif", [128, KC, 8], BF16)
            cx.dma("sp", wif_s[:], d["wif"].ap().rearrange("(k p) c -> p k c", p=128), wif_sr, reads=[r["wif"]], writes=[wif_sr])
            self.cast(wif[:], wif_s[:], [wif_sr], [wif_r])
            ifr, ifr_r = self._sb(es, tag + "ifr", [4, 2, NTOK], F32)
            for g in range(2):
                for (c0, n) in self.sub_tiles(NTOK):
                    b = self.nb()
                    self.mm_feat(hT, hT_r, KC, c0, n, wif, wif_r, g * 4, 4, b)
                    cx.op("act", lambda e: e.copy(out=ifr[:, g, c0:c0 + n], in_=self.bank[b][0:4, 0:n]),
                          reads=[self.bres[b]], writes=[ifr_r])
            cx.dma("sp", d["IF"].ap().rearrange("g h t -> h g t"), ifr[:], ifr_r, reads=[ifr_r], writes=[r["IF"]])
        cx.barrier()

    def mlstm_chunks(self):
        ch = [(c * 64, 64, None) for c in range(SEQ // 64)]
        ch += [(SEQ + b * DSEQ, DSEQ, b) for b in range(NB)]
        return ch

    def mlstm_scal_phase(self):
        import contextlib
        cx = self.cx
        d, r = self.dram, self.dres
        tag = "ms"
        with contextlib.ExitStack() as es:
            def T(nm, w=NTOK):
                return self._sb(es, tag + nm, [4, w], F32)
            ifr, ifr_r = self._sb(es, tag + "ifr", [4, 2, NTOK], F32)
            cx.dma("sp", ifr[:], d["IF"].ap().rearrange("g h t -> h g t"), ifr_r, reads=[r["IF"]], writes=[ifr_r])
            bi, bi_r = T("bi", 2)
            cx.dma("sp", bi[:, 0:1], d["mlstm_b_i"].ap().rearrange("o h -> h o"), bi_r, reads=[r["mlstm_b_i"]], writes=[bi_r])
            cx.dma("sp", bi[:, 1:2], d["mlstm_b_f"].ap().rearrange("o h -> h o"), bi_r, reads=[r["mlstm_b_f"]], writes=[bi_r])
            one, one_r = T("one")
            cx.op("pool", lambda e: e.memset(one[:], 1.0), writes=[one_r])
            ninf, ninf_r = T("ninf", 64)
            cx.op("pool", lambda e: e.memset(ninf[:], -1e30), writes=[ninf_r])
            li, li_r = T("li")
            x, x_r = T("x")
            ax, ax_r = T("ax")
            lf, lf_r = T("lf")
            bb, bb_r = T("b")
            a, a_r = T("a")
            mu, mu_r = T("mu")
            ws, ws_r = T("ws")
            M, M_r = T("M", SEQ // 64 + 1)
            Ms, Ms_r = T("Ms", 2 * NB)
            cx.op("dve", lambda e: e.tensor_scalar(out=li[:], in0=ifr[:, 0, :], scalar1=bi[:, 0:1], scalar2=None, op0=ALU.add),
                  reads=[ifr_r, bi_r], writes=[li_r])
            cx.op("dve", lambda e: e.tensor_scalar(out=x[:], in0=ifr[:, 1, :], scalar1=bi[:, 1:2], scalar2=None, op0=ALU.add),
                  reads=[ifr_r, bi_r], writes=[x_r])
            cx.op("act", lambda e: e.activation(out=ax[:], in_=x[:], func=AF.Abs), reads=[x_r], writes=[ax_r])
            cx.op("act", lambda e: e.activation(out=ax[:], in_=ax[:], func=AF.Exp, scale=-1.0), reads=[ax_r], writes=[ax_r])
            cx.op("act", lambda e: e.activation(out=ax[:], in_=ax[:], func=AF.Ln, bias=one[:, 0:1]), reads=[ax_r, one_r], writes=[ax_r])
            cx.op("dve", lambda e: e.tensor_scalar_min(out=lf[:], in0=x[:], scalar1=0.0), reads=[x_r], writes=[lf_r])
            cx.op("dve", lambda e: e.tensor_tensor(out=lf[:], in0=lf[:], in1=ax[:], op=ALU.subtract), reads=[lf_r, ax_r], writes=[lf_r])
            for (t0, L, sb) in self.mlstm_chunks():
                cx.op("dve", lambda e: e.tensor_tensor_scan(out=bb[:, t0:t0 + L], data0=one[:, t0:t0 + L], data1=lf[:, t0:t0 + L],
                                                            initial=0.0, op0=ALU.mult, op1=ALU.add),
                      reads=[one_r, lf_r], writes=[bb_r])
            cx.op("dve", lambda e: e.tensor_tensor(out=a[:], in0=li[:], in1=bb[:], op=ALU.subtract), reads=[li_r, bb_r], writes=[a_r])
            cx.op("pool", lambda e: e.memset(M[:], 0.0), writes=[M_r])
            cx.dma("sp", Ms[:, 0:NB], d["mm"].ap().rearrange("b h -> h b"), Ms_r, reads=[r["mm"]], writes=[Ms_r])
            for ci, (t0, L, sb) in enumerate(self.mlstm_chunks()):
                m_in = M[:, ci:ci + 1] if sb is None else Ms[:, sb:sb + 1]
                m_out = M[:, ci + 1:ci + 2] if sb is None else Ms[:, NB + sb:NB + sb + 1]
                mr = M_r if sb is None else Ms_r
                cx.op("dve", lambda e: e.tensor_tensor_scan(out=mu[:, t0:t0 + L], data0=ninf[:, 0:L], data1=a[:, t0:t0 + L],
                                                            initial=m_in, op0=ALU.max, op1=ALU.max),
                      reads=[ninf_r, a_r, mr], writes=[mu_r])
                cx.op("act", lambda e: e.activation(out=ws[:, t0:t0 + L], in_=mu[:, t0:t0 + L], func=AF.Exp, scale=-1.0, bias=m_in),
                      reads=[mu_r, mr], writes=[ws_r])
                cx.op("dve", lambda e: e.tensor_tensor(out=m_out, in0=bb[:, t0 + L - 1:t0 + L], in1=mu[:, t0 + L - 1:t0 + L], op=ALU.add),
                      reads=[bb_r, mu_r], writes=[mr])
            cx.op("dve", lambda e: e.tensor_tensor(out=bb[:], in0=bb[:], in1=mu[:], op=ALU.add), reads=[bb_r, mu_r], writes=[bb_r])
            cx.op("dve", lambda e: e.tensor_scalar_mul(out=mu[:], in0=mu[:], scalar1=-1.0), reads=[mu_r], writes=[mu_r])
            cx.dma("sp", d["NMU"].ap(), mu[:], mu_r, reads=[mu_r], writes=[r["NMU"]])
            cx.dma("sp", d["WS"].ap(), ws[:], ws_r, reads=[ws_r], writes=[r["WS"]])
            with self.nc.allow_non_contiguous_dma(reason="tiny transposed gate-scalar scratch"):
                cx.dma("sp", d["SCT"].ap()[:, 0:4].rearrange("t h -> h t"), a[:], a_r, reads=[a_r], writes=[r["SCT"]])
                cx.dma("sp", d["SCT"].ap()[:, 4:8].rearrange("t h -> h t"), bb[:], bb_r, reads=[bb_r], writes=[r["SCT"]])
                cx.dma("sp", d["mm_p"].ap().rearrange("(h o) -> h o", o=1), M[:, SEQ // 64:SEQ // 64 + 1], M_r, reads=[M_r], writes=[r["mm_p"]])
                cx.dma("sp", d["mm_s"].ap().rearrange("b h -> h b"), Ms[:, NB:2 * NB], Ms_r, reads=[Ms_r], writes=[r["mm_s"]])
        cx.barrier()

    def mlstm_phase(self):
        import contextlib
        cx = self.cx
        d, r = self.dram, self.dres
        tag = "ml"
        with contextlib.ExitStack() as es:
            S = lambda nm, shape, dt: self._sb(es, tag + nm, shape, dt)
            NBUF = 2
            qT = [S(f"qT{i}", [128, 4, 64], BF16) for i in range(NBUF)]
            kT = [S(f"kT{i}", [128, 4, 64], BF16) for i in range(NBUF)]
            kt = [S(f"kt{i}", [64, 512], BF16) for i in range(NBUF)]
            vv = [S(f"v{i}", [64, 4, 256], BF16) for i in range(NBUF)]
            nmu = [S(f"nmu{i}", [64, 4, 64], F32) for i in range(NBUF)]
            wsb = [S(f"wsb{i}", [128, 4, 64], F32) for i in range(NBUF)]
            col = [S(f"col{i}", [64, 8], F32) for i in range(NBUF)]
            WT = [S(f"WT{i}", [64, 4, 64], F32) for i in range(2)]
            ST = [S(f"ST{i}", [64, 4, 64], BF16) for i in range(2)]
            qs = [S(f"qs{i}", [128, 4, 64], BF16) for i in range(2)]
            vw = [S(f"vw{i}", [64, 4, 256], BF16) for i in range(2)]
            wlb = [S(f"wlb{i}", [64, 4], BF16) for i in range(2)]
            hm = [S(f"hm{i}", [64, 4, 256], F32) for i in range(2)]
            sm = [S(f"sm{i}", [64, 4, 4], F32) for i in range(2)]
            C, C_r = S("C", [128, 4, 256], F32)
            Cb, Cb_r = S("Cb", [128, 4, 256], BF16)
            nn, nn_r = S("n", [128, 4], F32)
            nb_, nb_r = S("nb", [128, 4], BF16)
            ones, ones_r = S("ones", [64, 1], BF16)
            cx.op("pool", lambda e: e.memset(ones[:], 1.0), writes=[ones_r])
            cx.op("pool", lambda e: e.memset(C[:], 0.0), writes=[C_r])
            cx.op("pool", lambda e: e.memset(nn[:], 0.0), writes=[nn_r])
            cx.op("pool", lambda e: e.memset(Cb[:], 0.0), writes=[Cb_r])
            cx.op("pool", lambda e: e.memset(nb_[:], 0.0), writes=[nb_r])
            chunks = self.mlstm_chunks()
            for ci, (t0, L, sb) in enumerate(chunks):
                i = ci % NBUF
                j = ci % 2
                (qT_t, qT_r), (kT_t, kT_r), (kt_t, kt_r), (v_t, v_r) = qT[i], kT[i], kt[i], vv[i]
                (nmu_t, nmu_r), (wsb_t, wsb_r), (col_t, col_r) = nmu[i], wsb[i], col[i]
                (WT_t, WT_r), (ST_t, ST_r), (qs_t, qs_r), (vw_t, vw_r) = WT[j], ST[j], qs[j], vw[j]
                (wlb_t, wlb_r), (hm_t, hm_r), (sm_t, sm_r) = wlb[j], hm[j], sm[j]
                cx.dma("sp", qT_t[:, :, 0:L], d["QMT"].ap()[:, t0:t0 + L].rearrange("(h p) t -> p h t", p=128), qT_r, reads=[r["QMT"]], writes=[qT_r])
                cx.dma("sp", kT_t[:, :, 0:L], d["KMT"].ap()[:, t0:t0 + L].rearrange("(h p) t -> p h t", p=128), kT_r, reads=[r["KMT"]], writes=[kT_r])
                cx.dma("sp", kt_t[0:L, :], d["KM"].ap()[t0:t0 + L, :], kt_r, reads=[r["KM"]], writes=[kt_r])
                cx.dma("sp", v_t[0:L], d["VM"].ap()[t0:t0 + L, :].rearrange("t (h v) -> t h v", h=4), v_r, reads=[r["VM"]], writes=[v_r])
                cx.dma("sp", nmu_t[0:L, :, 0:L], d["NMU"].ap()[:, t0:t0 + L].partition_broadcast(L), nmu_r, reads=[r["NMU"]], writes=[nmu_r])
                cx.dma("sp", wsb_t[:, :, 0:L], d["WS"].ap()[:, t0:t0 + L].partition_broadcast(128), wsb_r, reads=[r["WS"]], writes=[wsb_r])
                cx.dma("sp", col_t[0:L, :], d["SCT"].ap()[t0:t0 + L, :], col_r, reads=[r["SCT"]], writes=[col_r])
                if sb is not None:
                    cx.dma("sp", C[:], d["mC"].ap()[sb].rearrange("h k v -> k h v"), C_r, reads=[r["mC"]], writes=[C_r])
                    cx.dma("sp", nn[:], d["mn"].ap()[sb].rearrange("h k -> k h"), nn_r, reads=[r["mn"]], writes=[nn_r])
                    self.cast(Cb[:], C[:], [C_r], [Cb_r], eng="act")
                    self.cast(nb_[:], nn[:], [nn_r], [nb_r], eng="dve")
                for h in range(4):
                    cx.op("act", lambda e: e.activation(out=WT_t[0:L, h, 0:L], in_=nmu_t[0:L, h, 0:L], func=AF.Exp, bias=col_t[0:L, h:h + 1]),
                          reads=[nmu_r, col_r], writes=[WT_r])
                cx.op("pool", lambda e: e.affine_select(out=WT_t[0:L, :, 0:L], in_=WT_t[0:L, :, 0:L], pattern=[[0, 4], [1, L]],
                                                       compare_op=ALU.is_ge, fill=0.0, base=0, channel_multiplier=-1),
                      reads=[WT_r], writes=[WT_r])
                bq = self.nb()
                kq = self.bank[bq][0:L, 0:256].rearrange("s (h t) -> s h t", h=4)
                cx.group("pe", [lambda e, h=h: e.matmul(kq[:, h, 0:L], lhsT=kT_t[:, h, 0:L], rhs=qT_t[:, h, 0:L], start=True, stop=True)
                                for h in range(4)], reads=[kT_r, qT_r], writes=[self.bres[bq]])
                cx.op("dve", lambda e: e.tensor_tensor(out=ST_t[0:L, :, 0:L], in0=kq[:, :, 0:L], in1=WT_t[0:L, :, 0:L], op=ALU.mult),
                      reads=[self.bres[bq], WT_r], writes=[ST_r])
                cx.op("dve", lambda e: e.tensor_tensor(out=qs_t[:, :, 0:L], in0=qT_t[:, :, 0:L], in1=wsb_t[:, :, 0:L], op=ALU.mult),
                      reads=[qT_r, wsb_r], writes=[qs_r])
                bn = [self.nb(), self.nb()]
                bd = self.nb()
                fns = []
                for h in range(4):
                    o = self.bank[bn[h // 2]][0:L, (h % 2) * 256:(h % 2 + 1) * 256]
                    fns.append(lambda e, h=h, o=o: e.matmul(o, lhsT=ST_t[0:L, h, 0:L], rhs=v_t[0:L, h, :], start=True, stop=False))
                    fns.append(lambda e, h=h, o=o: e.matmul(o, lhsT=qs_t[:, h, 0:L], rhs=Cb[:, h, :], start=False, stop=True))
                cx.group("pe", fns, reads=[ST_r, v_r, qs_r, Cb_r], writes=[self.bres[bn[0]], self.bres[bn[1]]])
                fns = []
                for h in range(4):
                    o = self.bank[bd][0:L, h:h + 1]
                    fns.append(lambda e, h=h, o=o: e.matmul(o, lhsT=ST_t[0:L, h, 0:L], rhs=ones[0:L, :], start=True, stop=False))
                    fns.append(lambda e, h=h, o=o: e.matmul(o, lhsT=qs_t[:, h, 0:L], rhs=nb_[:, h:h + 1], start=False, stop=True))
                cx.group("pe", fns, reads=[ST_r, ones_r, qs_r, nb_r], writes=[self.bres[bd]])
                cx.op("act", lambda e: e.activation(out=sm_t[0:L, :, 0], in_=self.bank[bd][0:L, 0:4], func=AF.Abs),
                      reads=[self.bres[bd]], writes=[sm_r])
                cx.op("act", lambda e: e.activation(out=sm_t[0:L, :, 1], in_=col_t[0:L, 4:8], func=AF.Exp, scale=-1.0),
                      reads=[col_r], writes=[sm_r])
                cx.op("dve", lambda e: e.tensor_tensor(out=sm_t[0:L, :, 2], in0=sm_t[0:L, :, 0], in1=sm_t[0:L, :, 1], op=ALU.max),
                      reads=[sm_r], writes=[sm_r])
                cx.op("dve", lambda e: e.reciprocal(out=sm_t[0:L, :, 3], in_=sm_t[0:L, :, 2]), reads=[sm_r], writes=[sm_r])
                for h in range(4):
                    o = self.bank[bn[h // 2]][0:L, (h % 2) * 256:(h % 2 + 1) * 256]
                    cx.op("act", lambda e: e.activation(out=hm_t[0:L, h, :], in_=o, func=AF.Copy, scale=sm_t[0:L, h, 3:4]),
                          reads=[self.bres[bn[h // 2]], sm_r], writes=[hm_r])
                cx.dma("sp", d["HM"].ap()[t0:t0 + L, :], hm_t[0:L].rearrange("t h v -> t (h v)"), hm_r, reads=[hm_r], writes=[r["HM"]])
                for h in range(4):
                    cx.op("act", lambda e: e.activation(out=vw_t[0:L, h, :], in_=v_t[0:L, h, :], func=AF.Copy, scale=WT_t[0:L, h, L - 1:L]),
                          reads=[v_r, WT_r], writes=[vw_r])
                cx.op("dve", lambda e: e.tensor_copy(out=wlb_t[0:L, :], in_=WT_t[0:L, :, L - 1]), reads=[WT_r], writes=[wlb_r])
                bu = [self.nb(), self.nb()]
                bnu = self.nb()
                fns = []
                for h in range(4):
                    o = self.bank[bu[h // 2]][:, (h % 2) * 256:(h % 2 + 1) * 256]
                    fns.append(lambda e, h=h, o=o: e.matmul(o, lhsT=kt_t[0:L, h * 128:(h + 1) * 128], rhs=vw_t[0:L, h, :], start=True, stop=True))
                for h in range(4):
                    fns.append(lambda e, h=h: e.matmul(self.bank[bnu][:, h:h + 1], lhsT=kt_t[0:L, h * 128:(h + 1) * 128], rhs=wlb_t[0:L, h:h + 1],
                                                       start=True, stop=True))
                cx.group("pe", fns, reads=[kt_r, vw_r, wlb_r], writes=[self.bres[bu[0]], self.bres[bu[1]], self.bres[bnu]])
                for h in range(4):
                    o = self.bank[bu[h // 2]][:, (h % 2) * 256:(h % 2 + 1) * 256]
                    cx.op("dve", lambda e: e.scalar_tensor_tensor(out=C[:, h, :], in0=C[:, h, :], scalar=wsb_t[:, h, L - 1:L], in1=o,
                                                                  op0=ALU.mult, op1=ALU.add),
                          reads=[C_r, wsb_r, self.bres[bu[h // 2]]], writes=[C_r])
                cx.op("dve", lambda e: e.tensor_tensor(out=nn[:], in0=nn[:], in1=wsb_t[:, :, L - 1], op=ALU.mult), reads=[nn_r, wsb_r], writes=[nn_r])
                cx.op("dve", lambda e: e.tensor_tensor(out=nn[:], in0=nn[:], in1=self.bank[bnu][:, 0:4], op=ALU.add),
                      reads=[nn_r, self.bres[bnu]], writes=[nn_r])
                self.cast(Cb[:], C[:], [C_r], [Cb_r], eng="act")
                self.cast(nb_[:], nn[:], [nn_r], [nb_r], eng="dve")
                last_prompt = (sb is None and ci == SEQ // 64 - 1)
                if last_prompt:
                    cx.dma("sp", d["mC_p"].ap().rearrange("h k v -> k h v"), C[:], C_r, reads=[C_r], writes=[r["mC_p"]])
                    cx.dma("sp", d["mn_p"].ap().rearrange("h k -> k h"), nn[:], nn_r, reads=[nn_r], writes=[r["mn_p"]])
                if sb is not None:
                    cx.dma("sp", d["mC_s"].ap()[sb].rearrange("h k v -> k h v"), C[:], C_r, reads=[C_r], writes=[r["mC_s"]])
                    cx.dma("sp", d["mn_s"].ap()[sb].rearrange("h k -> k h"), nn[:], nn_r, reads=[nn_r], writes=[r["mn_s"]])
        cx.barrier()

    def swa_phase(self):
        import contextlib
        cx = self.cx
        d, r = self.dram, self.dres
        tag = "sw"
        NEG = -30000.0
        with contextlib.ExitStack() as es:
            S = lambda nm, shape, dt: self._sb(es, tag + nm, shape, dt)
            dist, dist_r = S("dist", [128, 256], F32)
            Mall, Mall_r = S("Mall", [128, 16, 256], F32)
            disti, disti_r = S("disti", [128, 256], mybir.dt.int32)
            cx.op("pool", lambda e: e.iota(disti[:], pattern=[[-1, 256]], base=128, channel_multiplier=1), writes=[disti_r])
            cx.op("pool", lambda e: e.tensor_copy(out=dist[:], in_=disti[:]), reads=[disti_r], writes=[dist_r])
            for h in range(16):
                slope = 2.0 ** (-8.0 * (h + 1) / 16)
                cx.op("dve", lambda e: e.tensor_scalar_mul(out=Mall[:, h, :], in0=dist[:], scalar1=-slope), reads=[dist_r], writes=[Mall_r])
            cx.op("pool", lambda e: e.affine_select(out=Mall[:], in_=Mall[:], pattern=[[0, 16], [-1, 256]], compare_op=ALU.is_ge,
                                                   fill=NEG, base=128, channel_multiplier=1), reads=[Mall_r], writes=[Mall_r])
            cx.op("pool", lambda e: e.affine_select(out=Mall[:], in_=Mall[:], pattern=[[0, 16], [1, 256]], compare_op=ALU.is_ge,
                                                   fill=NEG, base=-1, channel_multiplier=-1), reads=[Mall_r], writes=[Mall_r])
            snk, snk_r = S("snk", [128, 16], F32)
            cx.dma("sp", snk[:], d["swa_sinks"].ap()[0].partition_broadcast(128), snk_r, reads=[r["swa_sinks"]], writes=[snk_r])
            qT = [S(f"qT{i}", [128, 8, 128], BF16) for i in range(2)]
            kTd = [S(f"kTd{i}", [128, 4, 128], BF16) for i in range(2)]
            kT2 = [S(f"kT2{i}", [128, 4, 128], BF16) for i in range(2)]
            vf = [S(f"vf{i}", [128, 256], F32) for i in range(2)]
            vb = [S(f"vb{i}", [128, 256], BF16) for i in range(2)]
            v2f = [S(f"v2f{i}", [128, 256], F32) for i in range(2)]
            v2b = [S(f"v2b{i}", [128, 256], BF16) for i in range(2)]
            ckf = [S(f"ckf{i}", [128, 256], F32) for i in range(2)]
            ckd = [S(f"ckd{i}", [128, 4, 2, 64], BF16) for i in range(2)]
            sc = [S(f"sc{i}", [128, 256], F32) for i in range(2)]
            pb = [S(f"p{i}", [128, 256], BF16) for i in range(2)]
            pT = [S(f"pT{i}", [128, 2, 128], BF16) for i in range(2)]
            sm = [S(f"sm{i}", [128, 8], F32) for i in range(2)]
            hs = [S(f"hs{i}", [128, 1024], F32) for i in range(2)]
            u = 0

            def attend(nq, qT_t, qT_r, k1, k1_r, k2, k2_r, v1, v1_r, v2, v2_r, hs_t, hs_r):
                nonlocal u
                lo = 0 if k1 is not None else 128
                hi = 128 + nq

                def s1(h, uu):
                    c, chunk, half = h // 4, h // 2, h % 2
                    ps = slice(half * 64, (half + 1) * 64)
                    sc_t, sc_r = sc[uu % 2]
                    p_t, p_r = pb[uu % 2]
                    sm_t, sm_r = sm[uu % 2]
                    b = self.nb()
                    fns = []
                    rds = [qT_r, k2_r]
                    if k1 is not None:
                        fns.append(lambda e: e.matmul(self.bank[b][0:nq, 0:128], lhsT=qT_t[ps, chunk, 0:nq], rhs=k1[ps, c, :], start=True, stop=True))
                        rds.append(k1_r)
                    fns.append(lambda e: e.matmul(self.bank[b][0:nq, 128:hi], lhsT=qT_t[ps, chunk, 0:nq], rhs=k2[ps, c, 0:nq], start=True, stop=True))
                    cx.group("pe", fns, reads=rds, writes=[self.bres[b]])
                    cx.op("dve", lambda e: e.scalar_tensor_tensor(out=sc_t[0:nq, lo:hi], in0=self.bank[b][0:nq, lo:hi], scalar=0.125,
                                                                  in1=Mall[0:nq, h, lo:hi], op0=ALU.mult, op1=ALU.add),
                          reads=[self.bres[b], Mall_r], writes=[sc_r])
                    cx.op("dve", lambda e: e.reduce_max(out=sm_t[0:nq, 0:1], in_=sc_t[0:nq, lo:hi], axis=AX.X), reads=[sc_r], writes=[sm_r])
                    cx.op("dve", lambda e: e.tensor_scalar(out=sm_t[0:nq, 1:2], in0=sm_t[0:nq, 0:1], scalar1=snk[0:nq, h:h + 1], scalar2=-1.0,
                                                           op0=ALU.max, op1=ALU.mult), reads=[sm_r, snk_r], writes=[sm_r])
                    cx.op("act", lambda e: e.activation(out=p_t[0:nq, lo:hi], in_=sc_t[0:nq, lo:hi], func=AF.Exp, bias=sm_t[0:nq, 1:2],
                                                        accum_out=sm_t[0:nq, 2:3]), reads=[sc_r, sm_r], writes=[p_r, sm_r])
                    cx.op("act", lambda e: e.activation(out=sm_t[0:nq, 3:4], in_=sm_t[0:nq, 1:2], func=AF.Exp, bias=snk[0:nq, h:h + 1]),
                          reads=[sm_r, snk_r], writes=[sm_r])

                def s2(h, uu):
                    c = h // 4
                    p_t, p_r = pb[uu % 2]
                    pT_t, pT_r = pT[uu % 2]
                    sm_t, sm_r = sm[uu % 2]
                    cx.op("dve", lambda e: e.tensor_tensor(out=sm_t[0:nq, 4:5], in0=sm_t[0:nq, 2:3], in1=sm_t[0:nq, 3:4], op=ALU.add),
                          reads=[sm_r], writes=[sm_r])
                    cx.op("dve", lambda e: e.reciprocal(out=sm_t[0:nq, 5:6], in_=sm_t[0:nq, 4:5]), reads=[sm_r], writes=[sm_r])
                    bt = self.nb()
                    ptb = self.bank[bt][:].bitcast(BF16)
                    fns = []
                    if k1 is not None:
                        fns.append(lambda e: e.transpose(out=ptb[:, 0:nq], in_=p_t[0:nq, 0:128], identity=self.ident_b[0:nq, 0:nq]))
                    fns.append(lambda e: e.transpose(out=ptb[0:nq, 128:128 + nq], in_=p_t[0:nq, 128:hi], identity=self.ident_b[0:nq, 0:nq]))
                    cx.group("pe", fns, reads=[p_r, self.ident_b_r], writes=[self.bres[bt]])
                    if k1 is not None:
                        cx.op("act", lambda e: e.copy(out=pT_t[:, 0, 0:nq], in_=ptb[:, 0:nq]), reads=[self.bres[bt]], writes=[pT_r])
                    cx.op("dve", lambda e: e.tensor_copy(out=pT_t[0:nq, 1, 0:nq], in_=ptb[0:nq, 128:128 + nq]), reads=[self.bres[bt]], writes=[pT_r])
                    bo = self.nb()
                    fns = []
                    rds = [pT_r, v2_r]
                    if k1 is not None:
                        fns.append(lambda e: e.matmul(self.bank[bo][0:nq, 0:64], lhsT=pT_t[:, 0, 0:nq], rhs=v1[:, c * 64:(c + 1) * 64], start=True, stop=False))
                        rds.append(v1_r)
                    fns.append(lambda e: e.matmul(self.bank[bo][0:nq, 0:64], lhsT=pT_t[0:nq, 1, 0:nq], rhs=v2[0:nq, c * 64:(c + 1) * 64],
                                                  start=(k1 is None), stop=True))
                    cx.group("pe", fns, reads=rds, writes=[self.bres[bo]])
                    cx.op("act", lambda e: e.activation(out=hs_t[0:nq, h * 64:(h + 1) * 64], in_=self.bank[bo][0:nq, 0:64], func=AF.Copy,
                                                        scale=sm_t[0:nq, 5:6]), reads=[self.bres[bo], sm_r], writes=[hs_r])

                u0 = u
                u += 16
                s1(0, u0)
                for h in range(16):
                    if h + 1 < 16:
                        s1(h + 1, u0 + h + 1)
                    s2(h, u0 + h)

            prev = None
            for j in range(SEQ // 128):
                t0 = j * 128
                i = j % 2
                (q_t, q_r), (k_t, k_r), (vf_t, vf_r), (vb_t, vb_r), (hs_t, hs_r) = qT[i], kTd[i], vf[i], vb[i], hs[i]
                cx.dma("sp", q_t[:], d["QST"].ap()[:, t0:t0 + 128].rearrange("(c p) t -> p c t", p=128), q_r, reads=[r["QST"]], writes=[q_r])
                cx.dma("sp", k_t[:], d["KSTD"].ap()[:, t0:t0 + 128].rearrange("(c p) t -> p c t", p=128), k_r, reads=[r["KSTD"]], writes=[k_r])
                cx.dma("sp", vf_t[:], d["VS"].ap()[t0:t0 + 128, :], vf_r, reads=[r["VS"]], writes=[vf_r])
                self.cast(vb_t[:], vf_t[:], [vf_r], [vb_r])
                if prev is None:
                    attend(128, q_t, q_r, None, None, k_t, k_r, None, None, vb_t, vb_r, hs_t, hs_r)
                else:
                    attend(128, q_t, q_r, prev[0], prev[1], k_t, k_r, prev[2], prev[3], vb_t, vb_r, hs_t, hs_r)
                prev = (k_t, k_r, vb_t, vb_r)
                cx.dma("sp", d["HS"].ap()[t0:t0 + 128, :], hs_t[:], hs_r, reads=[hs_r], writes=[r["HS"]])
            for sb in range(NB):
                t0 = SEQ + sb * DSEQ
                i = sb % 2
                (q_t, q_r), (k1_t, k1_r), (k2_t, k2_r) = qT[i], kTd[i], kT2[i]
                (vf_t, vf_r), (vb_t, vb_r), (v2f_t, v2f_r), (v2b_t, v2b_r) = vf[i], vb[i], v2f[i], v2b[i]
                (ckf_t, ckf_r), (ckd_t, ckd_r), (hs_t, hs_r) = ckf[i], ckd[i], hs[i]
                cx.dma("sp", q_t[:, :, 0:DSEQ], d["QST"].ap()[:, t0:t0 + DSEQ].rearrange("(c p) t -> p c t", p=128), q_r, reads=[r["QST"]], writes=[q_r])
                cx.dma("sp", k2_t[:, :, 0:DSEQ], d["KSTD"].ap()[:, t0:t0 + DSEQ].rearrange("(c p) t -> p c t", p=128), k2_r, reads=[r["KSTD"]], writes=[k2_r])
                cx.dma("sp", ckf_t[:], d["swk"].ap()[sb], ckf_r, reads=[r["swk"]], writes=[ckf_r])
                cx.dma("sp", vf_t[:], d["swv"].ap()[sb], vf_r, reads=[r["swv"]], writes=[vf_r])
                cx.dma("sp", v2f_t[0:DSEQ, :], d["VS"].ap()[t0:t0 + DSEQ, :], v2f_r, reads=[r["VS"]], writes=[v2f_r])
                self.cast(vb_t[:], vf_t[:], [vf_r], [vb_r])
                self.cast(v2b_t[0:DSEQ, :], v2f_t[0:DSEQ, :], [v2f_r], [v2b_r])
                ck3 = ckf_t[:].rearrange("s (c d) -> s c d", c=4)
                for dup in range(2):
                    self.cast(ckd_t[:, :, dup, :], ck3, [ckf_r], [ckd_r])
                self.transpose_blk(ckd_t[:].rearrange("s c u d -> s (c u d)"), ckd_r, 128, 512, lambda c: k1_t[:, c, :], k1_r, evac="dve")
                attend(DSEQ, q_t, q_r, k1_t, k1_r, k2_t, k2_r, vb_t, vb_r, v2b_t, v2b_r, hs_t, hs_r)
                cx.dma("sp", d["HS"].ap()[t0:t0 + DSEQ, :], hs_t[0:DSEQ, :], hs_r, reads=[hs_r], writes=[r["HS"]])
            dm = cx.res("swcopy")
            cx.dma("sp", d["swk_p"].ap(), d["KS"].ap()[SEQ - 128:SEQ, :], dm, reads=[r["KS"]], writes=[r["swk_p"]])
            cx.dma("sp", d["swv_p"].ap(), d["VS"].ap()[SEQ - 128:SEQ, :], dm, reads=[r["VS"]], writes=[r["swv_p"]])
            cx.dma("sp", d["swk_s"].ap()[:, 0:128 - DSEQ, :], d["swk"].ap()[:, DSEQ:128, :], dm, reads=[r["swk"]], writes=[r["swk_s"]])
            cx.dma("sp", d["swv_s"].ap()[:, 0:128 - DSEQ, :], d["swv"].ap()[:, DSEQ:128, :], dm, reads=[r["swv"]], writes=[r["swv_s"]])
            cx.dma("sp", d["swk_s"].ap()[:, 128 - DSEQ:128, :], d["KS"].ap()[SEQ:NTOK, :].rearrange("(b t) f -> b t f", t=DSEQ), dm,
                   reads=[r["KS"]], writes=[r["swk_s"]])
            cx.dma("sp", d["swv_s"].ap()[:, 128 - DSEQ:128, :], d["VS"].ap()[SEQ:NTOK, :].rearrange("(b t) f -> b t f", t=DSEQ), dm,
                   reads=[r["VS"]], writes=[r["swv_s"]])
        cx.barrier()

    def even_out_phase(self, src_fn, dst_fn):
        import contextlib
        cx = self.cx
        d, r = self.dram, self.dres
        tag = "eo"
        blocks = blocks_of(0, SEQ) + [(SEQ, NS)]
        with contextlib.ExitStack() as es:
            S = lambda nm, shape, dt: self._sb(es, tag + nm, shape, dt)
            oT, oT_r = S("oT", [128, KC, NTOK], BF16)
            with contextlib.ExitStack() as es2:
                S2 = lambda nm, shape, dt: self._sb(es2, tag + nm, shape, dt)
                hg, hg_r = S2("hg", [128, 1024], F32)
                cx.dma("sp", hg[:], d["mlstm_head_gain"].ap().rearrange("o h v -> (o h v)").partition_broadcast(128), hg_r,
                       reads=[r["mlstm_head_gain"]], writes=[hg_r])
                hm = [S2(f"hm{i}", [128, 1024], F32) for i in range(2)]
                om = [S2(f"om{i}", [128, 1024], F32) for i in range(2)]
                hsf = [S2(f"hs{i}", [128, 1024], F32) for i in range(2)]
                sq, sq_r = S2("sq", [128, 256], BF16)
                cat = [S2(f"cat{i}", [128, D], BF16) for i in range(2)]
                st = [S2(f"st{i}", [128, 12], F32) for i in range(2)]
                for bi, (r0, n) in enumerate(blocks):
                    i = bi % 2
                    (hm_t, hm_r), (om_t, om_r), (hs_t, hs_r), (cat_t, cat_r), (st_t, st_r) = hm[i], om[i], hsf[i], cat[i], st[i]
                    cx.dma("sp", hm_t[0:n, :], d["HM"].ap()[r0:r0 + n, :], hm_r, reads=[r["HM"]], writes=[hm_r])
                    cx.dma("sp", om_t[0:n, :], d["OM"].ap()[r0:r0 + n, :], om_r, reads=[r["OM"]], writes=[om_r])
                    cx.dma("sp", hs_t[0:n, :], d["HS"].ap()[r0:r0 + n, :], hs_r, reads=[r["HS"]], writes=[hs_r])
                    for h in range(4):
                        cx.op("act", lambda e: e.activation(out=sq[0:n, :], in_=hm_t[0:n, h * 256:(h + 1) * 256], func=AF.Square,
                                                            accum_out=st_t[0:n, h:h + 1]), reads=[hm_r], writes=[sq_r, st_r])
                    cx.op("act", lambda e: e.activation(out=st_t[0:n, 4:8], in_=st_t[0:n, 0:4], func=AF.Sqrt, scale=1.0 / 256,
                                                        bias=self.eps_t[0:n, 0:1]), reads=[st_r, self.eps_r], writes=[st_r])
                    cx.op("dve", lambda e: e.reciprocal(out=st_t[0:n, 8:12], in_=st_t[0:n, 4:8]), reads=[st_r], writes=[st_r])
                    cx.op("act", lambda e: e.activation(out=om_t[0:n, :], in_=om_t[0:n, :], func=AF.Sigmoid), reads=[om_r], writes=[om_r])
                    for h in range(4):
                        cx.op("dve", lambda e: e.scalar_tensor_tensor(out=hm_t[0:n, h * 256:(h + 1) * 256], in0=hm_t[0:n, h * 256:(h + 1) * 256],
                                                                      scalar=st_t[0:n, 8 + h:9 + h], in1=hg[0:n, h * 256:(h + 1) * 256],
                                                                      op0=ALU.mult, op1=ALU.mult), reads=[hm_r, st_r, hg_r], writes=[hm_r])
                    cx.op("dve", lambda e: e.tensor_tensor(out=cat_t[0:n, 0:1024], in0=hm_t[0:n, :], in1=om_t[0:n, :], op=ALU.mult),
                          reads=[hm_r, om_r], writes=[cat_r])
                    self.cast(cat_t[0:n, 1024:2048], hs_t[0:n, :], [hs_r], [cat_r], eng="act")
                    self.transpose_into(cat_t, cat_r, n, oT, oT_r, r0)
                cx.barrier()
            self.out_proj(es, tag, oT, oT_r, KC, d["even_w_out"].ap()[0], r["even_w_out"],
                          [(r0, n, r0) for (r0, n) in blocks], src_fn, dst_fn)
        cx.barrier()

    def odd_in_phase(self, src_fn):
        import contextlib
        cx = self.cx
        d, r = self.dram, self.dres
        tag = "oi"
        blocks = blocks_of(0, SEQ) + [(SEQ, NS)]
        with contextlib.ExitStack() as es:
            hT, hT_r = self.norm_all(es, tag, src_fn, d["norm_gain"].ap()[1, 1], blocks)
            bufs = self.wbufs(es, tag, KC, 512)
            evf = [self._sb(es, f"{tag}evf{i}", [128, 512], F32) for i in range(3)]
            evb = [self._sb(es, f"{tag}evb{i}", [128, 512], BF16) for i in range(3)]
            k = 0
            for cb in range(4):
                wb, wb_r = self.load_w(bufs, d["odd_w_in"].ap()[0][:, cb * 512:(cb + 1) * 512], r["odd_w_in"], KC, 512)
                for (r0, n) in blocks:
                    b = self.nb()
                    self.mm_tok(hT, hT_r, KC, r0, n, wb, wb_r, 0, 512, b)
                    e_t, e_r = evf[k % 3]
                    k += 1
                    cx.op("act", lambda e: e.copy(out=e_t[0:n, :], in_=self.bank[b][0:n, :]), reads=[self.bres[b]], writes=[e_r])
                    cx.dma("sp", d["U"].ap()[r0:r0 + n, cb * 512:(cb + 1) * 512], e_t[0:n, :], e_r, reads=[e_r], writes=[r["U"]])
                for jj in range(4):
                    for (c0, n) in self.sub_tiles(NTOK):
                        b = self.nb()
                        self.mm_feat(hT, hT_r, KC, c0, n, wb, wb_r, jj * 128, 128, b)
                        e_t, e_r = evb[k % 3]
                        k += 1
                        cx.op("act", lambda e: e.copy(out=e_t[:, 0:n], in_=self.bank[b][:, 0:n]), reads=[self.bres[b]], writes=[e_r])
                        rr = cb * 512 + jj * 128
                        cx.dma("sp", d["UT"].ap()[rr:rr + 128, c0:c0 + n], e_t[:, 0:n], e_r, reads=[e_r], writes=[r["UT"]])
        cx.barrier()

    def _range_reduce(self, out, out_r, in_, in_r, tmpi, tmpi_r, tmpf, tmpf_r, shift):
        cx = self.cx
        TWO_PI = 2.0 * np.pi
        cx.op("dve", lambda e: e.tensor_scalar(out=tmpf, in0=in_, scalar1=shift, scalar2=1.0 / TWO_PI, op0=ALU.add, op1=ALU.mult),
              reads=[in_r], writes=[tmpf_r])
        cx.op("dve", lambda e: e.tensor_copy(out=tmpi, in_=tmpf), reads=[tmpf_r], writes=[tmpi_r])
        cx.op("dve", lambda e: e.tensor_copy(out=tmpf, in_=tmpi), reads=[tmpi_r], writes=[tmpf_r])
        cx.op("dve", lambda e: e.tensor_scalar(out=tmpf, in0=tmpf, scalar1=-TWO_PI, scalar2=shift, op0=ALU.mult, op1=ALU.add),
              reads=[tmpf_r], writes=[tmpf_r])
        cx.op("dve", lambda e: e.tensor_tensor(out=out, in0=tmpf, in1=in_, op=ALU.add), reads=[tmpf_r, in_r], writes=[out_r])
        cx.op("dve", lambda e: e.tensor_scalar(out=tmpf, in0=out, scalar1=np.pi, scalar2=-TWO_PI, op0=ALU.is_gt, op1=ALU.mult),
              reads=[out_r], writes=[tmpf_r])
        cx.op("dve", lambda e: e.tensor_tensor(out=out, in0=out, in1=tmpf, op=ALU.add), reads=[out_r, tmpf_r], writes=[out_r])
        cx.op("dve", lambda e: e.tensor_scalar(out=tmpf, in0=out, scalar1=-np.pi, scalar2=TWO_PI, op0=ALU.is_lt, op1=ALU.mult),
              reads=[out_r], writes=[tmpf_r])
        cx.op("dve", lambda e: e.tensor_tensor(out=out, in0=out, in1=tmpf, op=ALU.add), reads=[out_r, tmpf_r], writes=[out_r])

    def s5_setup_phase(self):
        import contextlib
        cx = self.cx
        d, r = self.dram, self.dres
        tag = "s5s"
        I32 = mybir.dt.int32
        with contextlib.ExitStack() as es:
            S = lambda nm, shape, dt: self._sb(es, tag + nm, shape, dt)
            are, are_r = S("are", [128, 64], F32)
            aim, aim_r = S("aim", [128, 64], F32)
            ldt, ldt_r = S("ldt", [128, 1], F32)
            cx.dma("sp", are[:], d["s5_a_re"].ap()[0], are_r, reads=[r["s5_a_re"]], writes=[are_r])
            cx.dma("sp", aim[:], d["s5_a_im"].ap()[0], aim_r, reads=[r["s5_a_im"]], writes=[aim_r])
            cx.dma("sp", ldt[:], d["s5_log_dt"].ap().rearrange("o g -> g o"), ldt_r, reads=[r["s5_log_dt"]], writes=[ldt_r])
            cx.op("act", lambda e: e.activation(out=ldt[:], in_=ldt[:], func=AF.Exp), reads=[ldt_r], writes=[ldt_r])
            lm, lm_r = S("lm", [128, 64], F32)
            th, th_r = S("th", [128, 64], F32)
            cx.op("dve", lambda e: e.tensor_scalar_mul(out=lm[:], in0=are[:], scalar1=ldt[:, 0:1]), reads=[are_r, ldt_r], writes=[lm_r])
            cx.op("dve", lambda e: e.tensor_scalar_mul(out=th[:], in0=aim[:], scalar1=ldt[:, 0:1]), reads=[aim_r, ldt_r], writes=[th_r])
            cx.dma("sp", d["LM"].ap().rearrange("(g p) -> g p", p=64), lm[:], lm_r, reads=[lm_r], writes=[r["LM"]])
            cx.dma("sp", d["TH"].ap().rearrange("(g p) -> g p", p=64), th[:], th_r, reads=[th_r], writes=[r["TH"]])
            ti, ti_r = S("ti", [128, 64], I32)
            tf, tf_r = S("tf", [128, 64], F32)
            sn, sn_r = S("sn", [128, 64], F32)
            cs, cs_r = S("cs", [128, 64], F32)
            mg, mg_r = S("mg", [128, 64], F32)
            self._range_reduce(sn[:], sn_r, th[:], th_r, ti[:], ti_r, tf[:], tf_r, 0.0)
            self._range_reduce(cs[:], cs_r, th[:], th_r, ti[:], ti_r, tf[:], tf_r, np.pi / 2)
            cx.op("act", lambda e: e.activation(out=sn[:], in_=sn[:], func=AF.Sin), reads=[sn_r], writes=[sn_r])
            cx.op("act", lambda e: e.activation(out=cs[:], in_=cs[:], func=AF.Sin), reads=[cs_r], writes=[cs_r])
            cx.op("act", lambda e: e.activation(out=mg[:], in_=lm[:], func=AF.Exp), reads=[lm_r], writes=[mg_r])
            abr, abr_r = S("abr", [128, 64], F32)
            abi, abi_r = S("abi", [128, 64], F32)
            cx.op("dve", lambda e: e.tensor_tensor(out=abr[:], in0=mg[:], in1=cs[:], op=ALU.mult), reads=[mg_r, cs_r], writes=[abr_r])
            cx.op("dve", lambda e: e.tensor_scalar_add(out=abr[:], in0=abr[:], scalar1=-1.0), reads=[abr_r], writes=[abr_r])
            cx.op("dve", lambda e: e.tensor_tensor(out=abi[:], in0=mg[:], in1=sn[:], op=ALU.mult), reads=[mg_r, sn_r], writes=[abi_r])
            den, den_r = S("den", [128, 64], F32)
            t1, t1_r = S("t1", [128, 64], F32)
            cx.op("dve", lambda e: e.tensor_tensor(out=den[:], in0=are[:], in1=are[:], op=ALU.mult), reads=[are_r], writes=[den_r])
            cx.op("dve", lambda e: e.tensor_tensor(out=t1[:], in0=aim[:], in1=aim[:], op=ALU.mult), reads=[aim_r], writes=[t1_r])
            cx.op("dve", lambda e: e.tensor_tensor(out=den[:], in0=den[:], in1=t1[:], op=ALU.add), reads=[den_r, t1_r], writes=[den_r])
            cx.op("dve", lambda e: e.reciprocal(out=den[:], in_=den[:]), reads=[den_r], writes=[den_r])
            fre, fre_r = S("fre", [128, 64], F32)
            fim, fim_r = S("fim", [128, 64], F32)
            cx.op("dve", lambda e: e.tensor_tensor(out=fre[:], in0=abr[:], in1=are[:], op=ALU.mult), reads=[abr_r, are_r], writes=[fre_r])
            cx.op("dve", lambda e: e.tensor_tensor(out=t1[:], in0=abi[:], in1=aim[:], op=ALU.mult), reads=[abi_r, aim_r], writes=[t1_r])
            cx.op("dve", lambda e: e.tensor_tensor(out=fre[:], in0=fre[:], in1=t1[:], op=ALU.add), reads=[fre_r, t1_r], writes=[fre_r])
            cx.op("dve", lambda e: e.tensor_tensor(out=fre[:], in0=fre[:], in1=den[:], op=ALU.mult), reads=[fre_r, den_r], writes=[fre_r])
            cx.op("dve", lambda e: e.tensor_tensor(out=fim[:], in0=abi[:], in1=are[:], op=ALU.mult), reads=[abi_r, are_r], writes=[fim_r])
            cx.op("dve", lambda e: e.tensor_tensor(out=t1[:], in0=abr[:], in1=aim[:], op=ALU.mult), reads=[abr_r, aim_r], writes=[t1_r])
            cx.op("dve", lambda e: e.tensor_tensor(out=fim[:], in0=fim[:], in1=t1[:], op=ALU.subtract), reads=[fim_r, t1_r], writes=[fim_r])
            cx.op("dve", lambda e: e.tensor_tensor(out=fim[:], in0=fim[:], in1=den[:], op=ALU.mult), reads=[fim_r, den_r], writes=[fim_r])
            bre, bre_r = S("bre", [128, 64, 16], F32)
            bim, bim_r = S("bim", [128, 64, 16], F32)
            cx.dma("sp", bre[:], d["s5_b_re"].ap()[0], bre_r, reads=[r["s5_b_re"]], writes=[bre_r])
            cx.dma("sp", bim[:], d["s5_b_im"].ap()[0], bim_r, reads=[r["s5_b_im"]], writes=[bim_r])
            bbo, bbo_r = S("bbo", [128, 2, 64, 16], F32)
            t3, t3_r = S("t3", [128, 64, 16], F32)
            frb = fre[:].unsqueeze(2).to_broadcast([128, 64, 16])
            fib = fim[:].unsqueeze(2).to_broadcast([128, 64, 16])
            cx.op("dve", lambda e: e.tensor_tensor(out=bbo[:, 0], in0=bre[:], in1=frb, op=ALU.mult), reads=[bre_r, fre_r], writes=[bbo_r])
            cx.op("dve", lambda e: e.tensor_tensor(out=t3[:], in0=bim[:], in1=fib, op=ALU.mult), reads=[bim_r, fim_r], writes=[t3_r])
            cx.op("dve", lambda e: e.tensor_tensor(out=bbo[:, 0], in0=bbo[:, 0], in1=t3[:], op=ALU.subtract), reads=[bbo_r, t3_r], writes=[bbo_r])
            cx.op("dve", lambda e: e.tensor_tensor(out=bbo[:, 1], in0=bim[:], in1=frb, op=ALU.mult), reads=[bim_r, fre_r], writes=[bbo_r])
            cx.op("dve", lambda e: e.tensor_tensor(out=t3[:], in0=bre[:], in1=fib, op=ALU.mult), reads=[bre_r, fim_r], writes=[t3_r])
            cx.op("dve", lambda e: e.tensor_tensor(out=bbo[:, 1], in0=bbo[:, 1], in1=t3[:], op=ALU.add), reads=[bbo_r, t3_r], writes=[bbo_r])
            cx.dma("sp", d["BBS"].ap(), bbo[:], bbo_r, reads=[bbo_r], writes=[r["BBS"]])
            cx.barrier()
        with contextlib.ExitStack() as es:
            S = lambda nm, shape, dt: self._sb(es, tag + nm, shape, dt)
            bbt, bbt_r = S("bbt", [128, 128, 16], F32)
            cx.dma("sp", bbt[:], d["BBS"].ap().rearrange("g r p c -> (r p) g c"), bbt_r, reads=[r["BBS"]], writes=[bbt_r])
            mask, mask_r = S("mask", [128, 8], F32)
            cx.op("pool", lambda e: e.memset(mask[:], 1.0), writes=[mask_r])
            cx.op("pool", lambda e: e.affine_select(out=mask[:], in_=mask[:], pattern=[[-16, 8]], compare_op=ALU.is_ge, fill=0.0, base=0,
                                                   channel_multiplier=1), reads=[mask_r], writes=[mask_r])
            cx.op("pool", lambda e: e.affine_select(out=mask[:], in_=mask[:], pattern=[[16, 8]], compare_op=ALU.is_ge, fill=0.0, base=15,
                                                   channel_multiplier=-1), reads=[mask_r], writes=[mask_r])
            dj = [S(f"dj{i}", [128, 128], F32) for i in range(2)]
            bd = [S(f"bd{i}", [128, 8, 128], BF16) for i in range(2)]
            for j in range(16):
                b = self.nb()
                dj_t, dj_r = dj[j % 2]
                bd_t, bd_r = bd[j % 2]
                cx.group("pe", [lambda e: e.transpose(out=self.bank[b][:, 0:128], in_=bbt[:, 8 * j:8 * j + 8, :].rearrange("q g c -> q (g c)"),
                                                      identity=self.ident_f[:])], reads=[bbt_r, self.ident_f_r], writes=[self.bres[b]])
                cx.op("act", lambda e: e.copy(out=dj_t[:], in_=self.bank[b][:, 0:128]), reads=[self.bres[b]], writes=[dj_r])
                cx.op("dve", lambda e: e.tensor_tensor(out=bd_t[:], in0=dj_t[:].unsqueeze(1).to_broadcast([128, 8, 128]),
                                                       in1=mask[:].unsqueeze(2).to_broadcast([128, 8, 128]), op=ALU.mult),
                      reads=[dj_r, mask_r], writes=[bd_r])
                cx.dma("sp", d["BBD"].ap()[:, j], bd_t[:], bd_r, reads=[bd_r], writes=[r["BBD"]])
            cn, cn_r = S("cn", [128, 16, 2, 64], F32)
            cx.dma("sp", cn[:, :, 0, :], d["s5_c_re"].ap()[0].rearrange("(j g) c p -> (g c) j p", g=8), cn_r, reads=[r["s5_c_re"]], writes=[cn_r])
            cx.dma("sp", cn[:, :, 1, :], d["s5_c_im"].ap()[0].rearrange("(j g) c p -> (g c) j p", g=8), cn_r, reads=[r["s5_c_im"]], writes=[cn_r])
            cm, cm_r = S("cm", [128, 16, 128], BF16)
            for j in range(16):
                b = self.nb()
                cx.group("pe", [lambda e: e.transpose(out=self.bank[b][:, 0:128], in_=cn[:, j].rearrange("q r p -> q (r p)"),
                                                      identity=self.ident_f[:])], reads=[cn_r, self.ident_f_r], writes=[self.bres[b]])
                cx.op("act", lambda e: e.copy(out=cm[0:64, j, :], in_=self.bank[b][0:64, 0:128]), reads=[self.bres[b]], writes=[cm_r])
                cx.op("act", lambda e: e.mul(out=cm[64:128, j, :], in_=self.bank[b][64:128, 0:128], mul=-1.0), reads=[self.bres[b]], writes=[cm_r])
            cx.dma("sp", d["CM"].ap(), cm[:], cm_r, reads=[cm_r], writes=[r["CM"]])
            cx.barrier()
        with contextlib.ExitStack() as es:
            S = lambda nm, shape, dt: self._sb(es, tag + nm, shape, dt)
            W = 2048
            kcol, kcol_r = S("kcol", [128, 2], F32)
            kci, kci_r = S("kci", [128, 1], I32)
            cx.op("pool", lambda e: e.iota(kci[:], pattern=[[0, 1]], base=1, channel_multiplier=1), writes=[kci_r])
            cx.op("pool", lambda e: e.tensor_copy(out=kcol[:, 0:1], in_=kci[:]), reads=[kci_r], writes=[kcol_r])
            cx.op("dve", lambda e: e.tensor_scalar_mul(out=kcol[:, 1:2], in0=kcol[:, 0:1], scalar1=-1.0), reads=[kcol_r], writes=[kcol_r])
            thb, thb_r = S("thb", [128, W], F32)
            lmb, lmb_r = S("lmb", [128, W], F32)
            ang, ang_r = S("ang", [128, W], F32)
            ti, ti_r = S("tti", [128, W], I32)
            tf, tf_r = S("ttf", [128, W], F32)
            sn, sn_r = S("tsn", [128, W], F32)
            cs, cs_r = S("tcs", [128, W], F32)
            ep, ep_r = S("tep", [128, W], F32)
            en, en_r = S("ten", [128, W], F32)
            o4 = [S(f"to{i}", [128, W], F32) for i in range(4)]
            for q in range(8192 // W):
                cs_ = slice(q * W, (q + 1) * W)
                cx.dma("sp", thb[:], d["TH"].ap()[cs_].partition_broadcast(128), thb_r, reads=[r["TH"]], writes=[thb_r])
                cx.dma("sp", lmb[:], d["LM"].ap()[cs_].partition_broadcast(128), lmb_r, reads=[r["LM"]], writes=[lmb_r])
                cx.op("dve", lambda e: e.tensor_scalar_mul(out=ang[:], in0=thb[:], scalar1=kcol[:, 0:1]), reads=[thb_r, kcol_r], writes=[ang_r])
                self._range_reduce(sn[:], sn_r, ang[:], ang_r, ti[:], ti_r, tf[:], tf_r, 0.0)
                self._range_reduce(cs[:], cs_r, ang[:], ang_r, ti[:], ti_r, tf[:], tf_r, np.pi / 2)
                cx.op("act", lambda e: e.activation(out=sn[:], in_=sn[:], func=AF.Sin), reads=[sn_r], writes=[sn_r])
                cx.op("act", lambda e: e.activation(out=cs[:], in_=cs[:], func=AF.Sin), reads=[cs_r], writes=[cs_r])
                cx.op("act", lambda e: e.activation(out=ep[:], in_=lmb[:], func=AF.Exp, scale=kcol[:, 0:1]), reads=[lmb_r, kcol_r], writes=[ep_r])
                cx.op("act", lambda e: e.activation(out=en[:], in_=lmb[:], func=AF.Exp, scale=kcol[:, 1:2]), reads=[lmb_r, kcol_r], writes=[en_r])
                (o0, o0r), (o1, o1r), (o2, o2r), (o3, o3r) = o4
                cx.op("dve", lambda e: e.tensor_tensor(out=o0[:], in0=en[:], in1=cs[:], op=ALU.mult), reads=[en_r, cs_r], writes=[o0r])
                cx.op("dve", lambda e: e.scalar_tensor_tensor(out=o1[:], in0=en[:], scalar=-1.0, in1=sn[:], op0=ALU.mult, op1=ALU.mult),
                      reads=[en_r, sn_r], writes=[o1r])
                cx.op("dve", lambda e: e.tensor_tensor(out=o2[:], in0=ep[:], in1=cs[:], op=ALU.mult), reads=[ep_r, cs_r], writes=[o2r])
                cx.op("dve", lambda e: e.tensor_tensor(out=o3[:], in0=ep[:], in1=sn[:], op=ALU.mult), reads=[ep_r, sn_r], writes=[o3r])
                for k_, (o, orr) in enumerate(o4):
                    cx.dma("sp", d["TAB"].ap()[k_, :, cs_], o[:], orr, reads=[orr], writes=[r["TAB"]])
        cx.barrier()

    def s5_scan_phase(self):
        import contextlib
        cx = self.cx
        d, r = self.dram, self.dres
        tag = "s5"
        with contextlib.ExitStack() as es:
            S = lambda nm, shape, dt: self._sb(es, tag + nm, shape, dt)
            tri, tri_r = S("tri", [128, 128], BF16)
            sel, sel_r = S("sel", [128, 128], BF16)
            tri64, tri64_r = S("tri64", [64, 16, 4], BF16)
            selb, selb_r = S("selb", [16, 16, 4], BF16)
            for t_, tr_ in ((tri, tri_r), (sel, sel_r), (tri64, tri64_r), (selb, selb_r)):
                cx.op("pool", lambda e: e.memset(t_[:], 1.0), writes=[tr_])
            AS = lambda t_, tr_, pat, base, cm: cx.op("pool", lambda e: e.affine_select(
                out=t_[:], in_=t_[:], pattern=pat, compare_op=ALU.is_ge, fill=0.0, base=base, channel_multiplier=cm), reads=[tr_], writes=[tr_])
            AS(tri, tri_r, [[1, 128]], 0, -1)
            AS(sel, sel_r, [[0, 128]], -127, 1)
            AS(tri64, tri64_r, [[4, 16], [1, 4]], 0, -1)
            AS(tri64, tri64_r, [[-4, 16], [0, 4]], 0, 1)
            AS(tri64, tri64_r, [[4, 16], [0, 4]], 3, -1)
            AS(selb, selb_r, [[-1, 16], [0, 4]], 0, 1)
            AS(selb, selb_r, [[1, 16], [0, 4]], 0, -1)
            tri64f = tri64[:].rearrange("s b t -> s (b t)")
            selbf = selb[:].rearrange("s b t -> s (b t)")
            tab, tab_r = S("tab", [128, 4, 1024], F32)
            tabs, tabs_r = S("tabs", [64, 4, 1024], F32)
            uT, uT_r = S("uT", [128, 2, NTOK], BF16)
            bbd, bbd_r = S("bbd", [128, 2, 8, 128], BF16)
            cm, cm_r = S("cm", [128, 2, 128], BF16)
            x0f, x0f_r = S("x0f", [16, 16, 2, 64], F32)
            x0b, x0b_r = S("x0b", [16, 2048], BF16)
            bus = [S(f"bus{i}", [128, 2048], F32) for i in range(2)]
            BH = [S(f"BH{i}", [128, 2048], BF16) for i in range(2)]
            tt = [S(f"tt{i}", [128, 1024], F32) for i in range(4)]
            tq = [S(f"tq{i}", [128, 1024], F32) for i in range(4)]
            XF = [S(f"XF{i}", [128, 2048], F32) for i in range(2)]
            Xbs = [S(f"Xb{i}", [128, 2048], BF16) for i in range(2)]
            XT = [S(f"XT{i}", [128, 16, 128], BF16) for i in range(2)]
            ys = [S(f"ys{i}", [128, 256], F32) for i in range(2)]
            ps = self.psum
            v4 = lambda ap, L: ap.rearrange("t (g r p) -> t g r p", g=16, r=2)
            v3 = lambda ap, L: ap.rearrange("t (g p) -> t g p", g=16)
            chunks = [(c * 128, 128, False) for c in range(SEQ // 128)] + [(SEQ, NS, True)]
            it = 0
            for e8 in range(8):
                gc = slice(e8 * 1024, (e8 + 1) * 1024)
                cx.dma("sp", tab[:], d["TAB"].ap()[:, :, gc].rearrange("k t c -> t k c"), tab_r, reads=[r["TAB"]], writes=[tab_r])
                for b_ in range(NB):
                    cx.dma("sp", tabs[4 * b_:4 * b_ + 4], d["TAB"].ap()[:, 0:4, gc].rearrange("k t c -> t k c"), tabs_r, reads=[r["TAB"]], writes=[tabs_r])
                cx.dma("sp", uT[:], d["UT"].ap()[e8 * 256:(e8 + 1) * 256, :].rearrange("(j q) t -> q j t", q=128), uT_r, reads=[r["UT"]], writes=[uT_r])
                cx.dma("sp", bbd[:], d["BBD"].ap()[:, 2 * e8:2 * e8 + 2], bbd_r, reads=[r["BBD"]], writes=[bbd_r])
                cx.dma("sp", cm[:], d["CM"].ap()[:, 2 * e8:2 * e8 + 2, :], cm_r, reads=[r["CM"]], writes=[cm_r])
                cx.dma("sp", x0f[:, :, 0, :], d["s5re"].ap()[:, e8 * 16:(e8 + 1) * 16, :], x0f_r, reads=[r["s5re"]], writes=[x0f_r])
                cx.dma("sp", x0f[:, :, 1, :], d["s5im"].ap()[:, e8 * 16:(e8 + 1) * 16, :], x0f_r, reads=[r["s5im"]], writes=[x0f_r])
                self.cast(x0b[:], x0f[:].rearrange("b g r p -> b (g r p)"), [x0f_r], [x0b_r])
                def stage_a(ci, t0, L, smp, it):
                    (BH_t, BH_r) = BH[it % 2]
                    tb_t, tb_r = (tabs, tabs_r) if smp else (tab, tab_r)
                    fns = []
                    for q in range(4):
                        fns.append(lambda e, q=q: e.matmul(ps[0:L, q * 512:(q + 1) * 512], lhsT=uT[:, q // 2, t0:t0 + L],
                                                           rhs=bbd[:, q // 2, (q % 2) * 4:(q % 2) * 4 + 4, :], start=True, stop=True))
                    cx.group("pe", fns, reads=[uT_r, bbd_r], writes=self.bres[0:4])
                    buv = v4(ps[0:L, 0:2048], L)
                    bur, bui = buv[:, :, 0, :], buv[:, :, 1, :]
                    Pr, Pi = v3(tb_t[0:L, 0, :], L), v3(tb_t[0:L, 1, :], L)
                    T = [(v3(t_[0:L, :], L), tr_) for (t_, tr_) in tt]
                    BHv = v4(BH_t[0:L, :], L)
                    rb = self.bres[0:4] + [tb_r]
                    cx.op("dve", lambda e: e.tensor_tensor(out=T[0][0], in0=bur, in1=Pr, op=ALU.mult), reads=rb, writes=[T[0][1]])
                    cx.op("dve", lambda e: e.tensor_tensor(out=T[1][0], in0=bui, in1=Pi, op=ALU.mult), reads=rb, writes=[T[1][1]])
                    cx.op("dve", lambda e: e.tensor_tensor(out=T[2][0], in0=bur, in1=Pi, op=ALU.mult), reads=rb, writes=[T[2][1]])
                    cx.op("dve", lambda e: e.tensor_tensor(out=T[3][0], in0=bui, in1=Pr, op=ALU.mult), reads=rb, writes=[T[3][1]])
                    cx.op("pool", lambda e: e.tensor_tensor(out=BHv[:, :, 0, :], in0=T[0][0], in1=T[1][0], op=ALU.subtract),
                          reads=[T[0][1], T[1][1]], writes=[BH_r])
                    cx.op("dve", lambda e: e.tensor_tensor(out=BHv[:, :, 1, :], in0=T[2][0], in1=T[3][0], op=ALU.add),
                          reads=[T[2][1], T[3][1]], writes=[BH_r])

                def stage_b1(ci, t0, L, smp, it):
                    Xb, Xb_r = Xbs[it % 2]
                    Xp, Xp_r = Xbs[(it + 1) % 2]
                    (BH_t, BH_r), (XF_t, XF_r), (XT_t, XT_r), (ys_t, ys_r) = BH[it % 2], XF[it % 2], XT[it % 2], ys[it % 2]
                    tb_t, tb_r = (tabs, tabs_r) if smp else (tab, tab_r)
                    fns = []
                    rds = [BH_r]
                    carry = smp or ci > 0
                    for q in range(4):
                        o = ps[0:L, 2048 + q * 512:2048 + (q + 1) * 512]
                        if smp:
                            fns.append(lambda e, q=q, o=o: e.matmul(o, lhsT=tri64f, rhs=BH_t[0:L, q * 512:(q + 1) * 512], start=True, stop=False))
                            fns.append(lambda e, q=q, o=o: e.matmul(o, lhsT=selbf, rhs=x0b[:, q * 512:(q + 1) * 512], start=False, stop=True))
                        else:
                            fns.append(lambda e, q=q, o=o: e.matmul(o, lhsT=tri[:], rhs=BH_t[:, q * 512:(q + 1) * 512], start=True, stop=not carry))
                            if carry:
                                fns.append(lambda e, q=q, o=o: e.matmul(o, lhsT=sel[:], rhs=Xp[:, q * 512:(q + 1) * 512], start=False, stop=True))
                    if smp:
                        rds += [tri64_r, selb_r, x0b_r]
                    else:
                        rds += [tri_r, sel_r] + ([Xp_r] if carry else [])
                    cx.group("pe", fns, reads=rds, writes=self.bres[4:8])
                    xh = v4(ps[0:L, 2048:4096], L)
                    xr, xi = xh[:, :, 0, :], xh[:, :, 1, :]
                    Qr, Qi = v3(tb_t[0:L, 2, :], L), v3(tb_t[0:L, 3, :], L)
                    Q = [(v3(t_[0:L, :], L), tr_) for (t_, tr_) in tq]
                    XFv = v4(XF_t[0:L, :], L)
                    rb = self.bres[4:8] + [tb_r]
                    cx.op("dve", lambda e: e.tensor_tensor(out=Q[0][0], in0=xr, in1=Qr, op=ALU.mult), reads=rb, writes=[Q[0][1]])
                    cx.op("dve", lambda e: e.tensor_tensor(out=Q[1][0], in0=xi, in1=Qi, op=ALU.mult), reads=rb, writes=[Q[1][1]])
                    cx.op("dve", lambda e: e.tensor_tensor(out=Q[2][0], in0=xr, in1=Qi, op=ALU.mult), reads=rb, writes=[Q[2][1]])
                    cx.op("dve", lambda e: e.tensor_tensor(out=Q[3][0], in0=xi, in1=Qr, op=ALU.mult), reads=rb, writes=[Q[3][1]])
                    Xbv = v4(Xb[0:L, :], L)
                    need_f32 = smp or ci == SEQ // 128 - 1
                    cx.op("dve", lambda e: e.tensor_tensor(out=Xbv[:, :, 0, :], in0=Q[0][0], in1=Q[1][0], op=ALU.subtract),
                          reads=[Q[0][1], Q[1][1]], writes=[Xb_r])
                    cx.op("dve", lambda e: e.tensor_tensor(out=Xbv[:, :, 1, :], in0=Q[2][0], in1=Q[3][0], op=ALU.add),
                          reads=[Q[2][1], Q[3][1]], writes=[Xb_r])
                    if need_f32:
                        cx.op("pool", lambda e: e.tensor_tensor(out=XFv[:, :, 0, :], in0=Q[0][0], in1=Q[1][0], op=ALU.subtract),
                              reads=[Q[0][1], Q[1][1]], writes=[XF_r])
                        cx.op("pool", lambda e: e.tensor_tensor(out=XFv[:, :, 1, :], in0=Q[2][0], in1=Q[3][0], op=ALU.add),
                              reads=[Q[2][1], Q[3][1]], writes=[XF_r])
                    gs = slice(e8 * 16, (e8 + 1) * 16)
                    if (not smp) and ci == SEQ // 128 - 1:
                        cx.dma("sp", d["s5re_p"].ap()[gs, :].rearrange("(o g) p -> o g p", o=1), XFv[L - 1:L, :, 0, :], XF_r, reads=[XF_r], writes=[r["s5re_p"]])
                        cx.dma("sp", d["s5im_p"].ap()[gs, :].rearrange("(o g) p -> o g p", o=1), XFv[L - 1:L, :, 1, :], XF_r, reads=[XF_r], writes=[r["s5im_p"]])
                    if smp:
                        for b_ in range(NB):
                            row = b_ * DSEQ + DSEQ - 1
                            cx.dma("sp", d["s5re_s"].ap()[b_, gs, :].rearrange("(o g) p -> o g p", o=1), XFv[row:row + 1, :, 0, :], XF_r, reads=[XF_r], writes=[r["s5re_s"]])
                            cx.dma("sp", d["s5im_s"].ap()[b_, gs, :].rearrange("(o g) p -> o g p", o=1), XFv[row:row + 1, :, 1, :], XF_r, reads=[XF_r], writes=[r["s5im_s"]])
                def stage_b2(ci, t0, L, smp, it):
                    Xb, Xb_r = Xbs[it % 2]
                    (XT_t, XT_r), (ys_t, ys_r) = XT[it % 2], ys[it % 2]
                    for k0 in (0, 8):
                        bt = self.nb()
                        pb = self.bank[bt][:].bitcast(BF16)
                        cx.group("pe", [lambda e, kc=kc: e.transpose(out=pb[:, (kc - k0) * 128:(kc - k0) * 128 + L],
                                                                     in_=Xb[0:L, kc * 128:(kc + 1) * 128], identity=self.ident_b[0:L, 0:L])
                                        for kc in range(k0, k0 + 8)], reads=[Xb_r, self.ident_b_r], writes=[self.bres[bt]])
                        src = pb.rearrange("p (k c) -> p k c", c=128)[:, :, 0:L]
                        self.cast(XT_t[:, k0:k0 + 8, 0:L], src, [self.bres[bt]], [XT_r], eng="act")
                    by = self.nb()
                    cx.group("pe", [lambda e, g=g: e.matmul(self.bank[by][0:L, g * 16:(g + 1) * 16], lhsT=XT_t[:, g, 0:L],
                                                            rhs=cm[:, g // 8, (g % 8) * 16:(g % 8 + 1) * 16], start=True, stop=True)
                                    for g in range(16)], reads=[XT_r, cm_r], writes=[self.bres[by]])
                    cx.op("act", lambda e: e.copy(out=ys_t[0:L, :], in_=self.bank[by][0:L, 0:256]), reads=[self.bres[by]], writes=[ys_r])
                    cx.dma("sp", d["Y"].ap()[t0:t0 + L, e8 * 256:(e8 + 1) * 256], ys_t[0:L, :], ys_r, reads=[ys_r], writes=[r["Y"]])

                units = [(ci, t0, L, smp, it + ci) for ci, (t0, L, smp) in enumerate(chunks)]
                it += len(chunks)
                nU = len(units)
                stage_a(*units[0])
                stage_a(*units[1])
                stage_b1(*units[0])
                for ui in range(nU):
                    if ui + 2 < nU:
                        stage_a(*units[ui + 2])
                    if ui + 1 < nU:
                        stage_b1(*units[ui + 1])
                    stage_b2(*units[ui])
        cx.barrier()

    def odd_out_phase(self, src_fn, dst_fn):
        import contextlib
        cx = self.cx
        d, r = self.dram, self.dres
        tag = "oo"
        blocks = blocks_of(0, SEQ) + [(SEQ, NS)]
        with contextlib.ExitStack() as es:
            S = lambda nm, shape, dt: self._sb(es, tag + nm, shape, dt)
            yT, yT_r = S("yT", [128, KC, NTOK], BF16)
            with contextlib.ExitStack() as es2:
                S2 = lambda nm, shape, dt: self._sb(es2, tag + nm, shape, dt)
                dsk, dsk_r = S2("dsk", [128, D], F32)
                cx.dma("sp", dsk[:], d["s5_d"].ap()[0].partition_broadcast(128), dsk_r, reads=[r["s5_d"]], writes=[dsk_r])
                yb = [S2(f"y{i}", [128, D], F32) for i in range(2)]
                ub = [S2(f"u{i}", [128, D], F32) for i in range(2)]
                gb = [S2(f"g{i}", [128, D], BF16) for i in range(2)]
                for bi, (r0, n) in enumerate(blocks):
                    (y_t, y_r), (u_t, u_r), (g_t, g_r) = yb[bi % 2], ub[bi % 2], gb[bi % 2]
                    cx.dma("sp", y_t[0:n, :], d["Y"].ap()[r0:r0 + n, :], y_r, reads=[r["Y"]], writes=[y_r])
                    cx.dma("sp", u_t[0:n, :], d["U"].ap()[r0:r0 + n, :], u_r, reads=[r["U"]], writes=[u_r])
                    cx.op("dve", lambda e: e.tensor_tensor(out=u_t[0:n, :], in0=u_t[0:n, :], in1=dsk[0:n, :], op=ALU.mult), reads=[u_r, dsk_r], writes=[u_r])
                    cx.op("dve", lambda e: e.tensor_tensor(out=y_t[0:n, :], in0=y_t[0:n, :], in1=u_t[0:n, :], op=ALU.add), reads=[y_r, u_r], writes=[y_r])
                    GC = 2.0 * (2.0 / np.pi) ** 0.5
                    cx.op("dve", lambda e: e.tensor_tensor(out=u_t[0:n, :], in0=y_t[0:n, :], in1=y_t[0:n, :], op=ALU.mult), reads=[y_r, u_r], writes=[u_r])
                    cx.op("dve", lambda e: e.tensor_scalar(out=u_t[0:n, :], in0=u_t[0:n, :], scalar1=0.044715, scalar2=1.0, op0=ALU.mult, op1=ALU.add),
                          reads=[u_r], writes=[u_r])
                    cx.op("dve", lambda e: e.tensor_tensor(out=u_t[0:n, :], in0=u_t[0:n, :], in1=y_t[0:n, :], op=ALU.mult), reads=[y_r, u_r], writes=[u_r])
                    cx.op("act", lambda e: e.activation(out=u_t[0:n, :], in_=u_t[0:n, :], func=AF.Sigmoid, scale=GC), reads=[u_r], writes=[u_r])
                    cx.op("dve", lambda e: e.tensor_tensor(out=g_t[0:n, :], in0=u_t[0:n, :], in1=y_t[0:n, :], op=ALU.mult), reads=[y_r, u_r], writes=[g_r])
                    self.transpose_into(g_t, g_r, n, yT, yT_r, r0)
                cx.barrier()
            bufs = self.wbufs(es, tag, KC, 512, nst=2, nbf=2)
            sg = [S(f"sg{i}", [128, 512], F32) for i in range(2)]
            xo = [S(f"xo{i}", [128, 512], F32) for i in range(3)]
            xw = [S(f"xw{i}", [128, 512], F32) for i in range(3)]
            W = d["odd_w_out"].ap()[0]
            ev = 0
            for cb in range(4):
                wv, wv_r = self.load_w(bufs, W[:, cb * 512:(cb + 1) * 512], r["odd_w_out"], KC, 512)
                wg, wg_r = self.load_w(bufs, W[:, D + cb * 512:D + (cb + 1) * 512], r["odd_w_out"], KC, 512)
                for (r0, n) in blocks:
                    bv, bg = self.nb(), self.nb()
                    self.mm_tok(yT, yT_r, KC, r0, n, wv, wv_r, 0, 512, bv)
                    self.mm_tok(yT, yT_r, KC, r0, n, wg, wg_r, 0, 512, bg)
                    (sg_t, sg_r), (xo_t, xo_r), (xw_t, xw_r) = sg[ev % 2], xo[ev % 3], xw[ev % 3]
                    ev += 1
                    sap, sr = src_fn(r0, n)
                    dap, dr = dst_fn(r0, n)
                    cx.dma("sp", xo_t[0:n, :], sap[:, cb * 512:(cb + 1) * 512], xo_r, reads=[sr], writes=[xo_r])
                    cx.op("act", lambda e: e.activation(out=sg_t[0:n, :], in_=self.bank[bg][0:n, :], func=AF.Sigmoid), reads=[self.bres[bg]], writes=[sg_r])
                    cx.op("dve", lambda e: e.tensor_tensor(out=sg_t[0:n, :], in0=self.bank[bv][0:n, :], in1=sg_t[0:n, :], op=ALU.mult),
                          reads=[self.bres[bv], sg_r], writes=[sg_r])
                    cx.op("dve", lambda e: e.tensor_tensor(out=xw_t[0:n, :], in0=sg_t[0:n, :], in1=xo_t[0:n, :], op=ALU.add),
                          reads=[sg_r, xo_r], writes=[xw_r])
                    cx.dma("sp", dap[:, cb * 512:(cb + 1) * 512], xw_t[0:n, :], xw_r, reads=[xw_r], writes=[dr])
        cx.barrier()

    def declare_io(self):
        di, do, ds = self.din, self.dout, self.dscr
        di("x_p", [SEQ, D]); di("x_s", [NS, D]); di("mem_p", [256, D])
        di("cmk", [2, NB, 256, 512]); di("cmv", [2, NB, 256, 512])
        di("mC", [NB, 4, 128, 256]); di("mn", [NB, 4, 128]); di("mm", [NB, 4])
        di("swk", [NB, 128, 256]); di("swv", [NB, 128, 256])
        di("s5re", [NB, 128, 64]); di("s5im", [NB, 128, 64])
        di("norm_gain", [2, 4, D]); di("final_gain", [D])
        for l in range(2):
            for i in range(2):
                di(f"wgu{l}{i}", [NF, 128, 2 * KC * 128]); di(f"wd{l}{i}", [DFF, D])
        di("wtok", [D, 3072]); di("wfeat", [D, 2560]); di("wif", [D, 8])
        di("mlstm_b_i", [1, 4]); di("mlstm_b_f", [1, 4]); di("mlstm_head_gain", [1, 4, 256]); di("swa_sinks", [1, 16])
        di("even_w_out", [1, D, D]); di("odd_w_in", [1, D, D])
        di("s5_a_re", [1, 128, 64]); di("s5_a_im", [1, 128, 64]); di("s5_log_dt", [1, 128])
        di("s5_b_re", [1, 128, 64, 16]); di("s5_b_im", [1, 128, 64, 16]); di("s5_c_re", [1, 128, 16, 64]); di("s5_c_im", [1, 128, 16, 64])
        di("s5_d", [1, D]); di("odd_w_out", [1, D, 2 * D])
        di("xwq", [2, D, 512]); di("xwk", [2, D, 512]); di("xwv", [2, D, 512]); di("xwo", [2, 512, D])
        do("y_p", [SEQ, D]); do("y_s", [NS, D]); do("mem_k_p", [2, 256, 512]); do("mem_v_p", [2, 256, 512])
        do("mC_p", [4, 128, 256]); do("mC_s", [NB, 4, 128, 256]); do("mn_p", [4, 128]); do("mn_s", [NB, 4, 128])
        do("mm_p", [4]); do("mm_s", [NB, 4])
        do("swk_p", [128, 256]); do("swk_s", [NB, 128, 256]); do("swv_p", [128, 256]); do("swv_s", [NB, 128, 256])
        do("s5re_p", [128, 64]); do("s5re_s", [NB, 128, 64]); do("s5im_p", [128, 64]); do("s5im_s", [NB, 128, 64])
        ds("XA", [NTOK, D]); ds("XB", [NTOK, D])
        ds("KM", [NTOK, 512], BF16); ds("VM", [NTOK, 1024], BF16); ds("OM", [NTOK, 1024]); ds("KS", [NTOK, 256]); ds("VS", [NTOK, 256])
        ds("QMT", [512, NTOK], BF16); ds("KMT", [512, NTOK], BF16); ds("QST", [1024, NTOK], BF16); ds("KSTD", [512, NTOK], BF16)
        ds("IF", [2, 4, NTOK]); ds("NMU", [4, NTOK]); ds("WS", [4, NTOK]); ds("SCT", [NTOK, 8])
        ds("HM", [NTOK, 1024]); ds("HS", [NTOK, 1024])
        ds("U", [NTOK, D]); ds("UT", [D, NTOK], BF16); ds("LM", [8192]); ds("TH", [8192])
        ds("BBS", [128, 2, 64, 16]); ds("BBD", [128, 16, 8, 128], BF16); ds("CM", [128, 16, 128], BF16)
        ds("TAB", [4, 128, 8192]); ds("Y", [NTOK, D])

    def xio(self, name):
        if name == "in":
            def f(r0, n):
                if r0 < SEQ:
                    return self.dram["x_p"].ap()[r0:r0 + n, :], self.dres["x_p"]
                return self.dram["x_s"].ap()[r0 - SEQ:r0 - SEQ + n, :], self.dres["x_s"]
            return f
        if name == "out":
            def f(r0, n):
                if r0 < SEQ:
                    return self.dram["y_p"].ap()[r0:r0 + n, :], self.dres["y_p"]
                return self.dram["y_s"].ap()[r0 - SEQ:r0 - SEQ + n, :], self.dres["y_s"]
            return f
        return lambda r0, n: (self.dram[name].ap()[r0:r0 + n, :], self.dres[name])

    def build(self, phases=None):
        d = self.dram
        self.declare_io()
        self.setup_consts()
        self.cx.barrier()
        tiles = [blocks_of(0, 1024), blocks_of(1024, SEQ) + [(SEQ, NS)]]
        allb = blocks_of(0, SEQ) + [(SEQ, NS)]
        ng = d["norm_gain"].ap()
        P = phases

        def on(p):
            return P is None or p in P
        if on("memkv"):
            self.memkv_phase()
        if on("ffn00"):
            self.ffn_phase("fa0", d["wgu00"].ap(), d["wd00"].ap(), ng[0, 0], tiles, self.xio("in"), self.xio("XA"))
        if on("even"):
            self.even_in_phase(self.xio("XA"))
            self.mlstm_scal_phase()
            self.mlstm_phase()
            self.swa_phase()
            self.even_out_phase(self.xio("XA"), self.xio("XB"))
        if on("xa0"):
            self.xattn_phase(0, self.xio("XB"), self.xio("XA"))
        if on("ffn01"):
            self.ffn_phase("fb0", d["wgu01"].ap(), d["wd01"].ap(), ng[0, 3], tiles, self.xio("XA"), self.xio("XB"))
        if on("ffn10"):
            self.ffn_phase("fa1", d["wgu10"].ap(), d["wd10"].ap(), ng[1, 0], tiles, self.xio("XB"), self.xio("XA"))
        if on("odd"):
            self.odd_in_phase(self.xio("XA"))
            self.s5_setup_phase()
            self.s5_scan_phase()
            self.odd_out_phase(self.xio("XA"), self.xio("XB"))
        if on("xa1"):
            self.xattn_phase(1, self.xio("XB"), self.xio("XA"))
        if on("ffn11"):
            self.ffn_phase("fb1", d["wgu11"].ap(), d["wd11"].ap(), ng[1, 3], tiles, self.xio("XA"), self.xio("XB"))
        if on("final"):
            self.final_phase(self.xio("XB"), self.xio("out"), d["final_gain"].ap(), allb)
        self.cx.finish("sp", [self.dres[n] for n in self.outputs])
        return self.nc


def _lay_wgu(wg, wu):
    a = np.stack([wg, wu], 0).reshape(2, KC, 128, NF, 128)
    return np.ascontiguousarray(a.transpose(3, 2, 0, 1, 4)).reshape(NF, 128, 2 * KC * 128)


def host_weights(inp):
    f = lambda a: np.ascontiguousarray(np.asarray(a, dtype=np.float32))
    w = {}
    for l in range(2):
        for i in range(2):
            w[f"wgu{l}{i}"] = _lay_wgu(np.asarray(inp["ffn_w_gate"][l, i]), np.asarray(inp["ffn_w_up"][l, i]))
            w[f"wd{l}{i}"] = f(inp["ffn_w_down"][l, i])
    win = np.asarray(inp["even_w_in"][0])
    q_m, k_m, v_m, o_m = win[:, 0:512], win[:, 512:1024], win[:, 1024:2048], win[:, 2048:3072]
    i_m, f_m = win[:, 3072:3076], win[:, 3076:3080]
    q_s, k_s, v_s = win[:, 3080:4104], win[:, 4104:4360], win[:, 4360:4616]
    w["wtok"] = f(np.concatenate([k_m, v_m, o_m, k_s, v_s], 1))
    ksd = np.concatenate([np.concatenate([k_s[:, c * 64:(c + 1) * 64]] * 2, 1) for c in range(4)], 1)
    w["wfeat"] = f(np.concatenate([q_m, k_m, q_s, ksd], 1))
    w["wif"] = f(np.concatenate([i_m, f_m], 1))
    for k in ("norm_gain", "final_gain", "mlstm_b_i", "mlstm_b_f", "mlstm_head_gain", "swa_sinks", "even_w_out", "odd_w_in",
              "s5_a_re", "s5_a_im", "s5_log_dt", "s5_b_re", "s5_b_im", "s5_c_re", "s5_c_im", "s5_d", "odd_w_out"):
        w[k] = f(inp[k])
    w["xwq"], w["xwk"], w["xwv"], w["xwo"] = f(inp["xattn_w_q"]), f(inp["xattn_w_k"]), f(inp["xattn_w_v"]), f(inp["xattn_w_o"])
    return w


def host_core_inputs(inp, c):
    f = lambda a: np.ascontiguousarray(np.asarray(a, dtype=np.float32))
    sb = slice(c * NB, (c + 1) * NB)
    m = {}
    m["x_p"] = f(inp["x_prompt"][c])
    m["x_s"] = f(np.asarray(inp["x_sample"][sb]).reshape(NS, D))
    m["mem_p"] = f(inp["mem_prompt"][c])
    m["cmk"] = f(np.asarray(inp["cache_mem_k"][:, sb]).reshape(2, NB, 256, 512))
    m["cmv"] = f(np.asarray(inp["cache_mem_v"][:, sb]).reshape(2, NB, 256, 512))
    m["mC"] = f(inp["state_mlstm_C"][0, sb])
    m["mn"] = f(inp["state_mlstm_n"][0, sb])
    m["mm"] = f(inp["state_mlstm_m"][0, sb])
    m["swk"] = f(np.asarray(inp["cache_swa_k"][0, sb]).reshape(NB, 128, 256))
    m["swv"] = f(np.asarray(inp["cache_swa_v"][0, sb]).reshape(NB, 128, 256))
    m["s5re"] = f(inp["state_s5_re"][0, sb])
    m["s5im"] = f(inp["state_s5_im"][0, sb])
    return m


def assemble(results, ncores=8):
    R = results
    cat = lambda k: [np.asarray(R[c][k]) for c in range(ncores)]
    y_p = np.stack(cat("y_p"), 0)
    y_s = np.concatenate([a.reshape(NB, DSEQ, D) for a in cat("y_s")], 0)
    mk = np.stack([a.reshape(2, 256, 4, 128) for a in cat("mem_k_p")], 1)
    mv = np.stack([a.reshape(2, 256, 4, 128) for a in cat("mem_v_p")], 1)
    C_p = np.stack(cat("mC_p"), 0)[None]
    C_s = np.concatenate(cat("mC_s"), 0)[None]
    n_p = np.stack(cat("mn_p"), 0)[None]
    n_s = np.concatenate(cat("mn_s"), 0)[None]
    m_p = np.stack(cat("mm_p"), 0)[None]
    m_s = np.concatenate(cat("mm_s"), 0)[None]
    k_p = np.stack([a.reshape(128, 4, 64) for a in cat("swk_p")], 0)[None]
    k_s = np.concatenate([a.reshape(NB, 128, 4, 64) for a in cat("swk_s")], 0)[None]
    v_p = np.stack([a.reshape(128, 4, 64) for a in cat("swv_p")], 0)[None]
    v_s = np.concatenate([a.reshape(NB, 128, 4, 64) for a in cat("swv_s")], 0)[None]
    sr_p = np.stack(cat("s5re_p"), 0)[None]
    sr_s = np.concatenate(cat("s5re_s"), 0)[None]
    si_p = np.stack(cat("s5im_p"), 0)[None]
    si_s = np.concatenate(cat("s5im_s"), 0)[None]
    outs = (y_p, y_s, mk, mv, C_p, C_s, n_p, n_s, m_p, m_s, k_p, k_s, v_p, v_s, sr_p, sr_s, si_p, si_s)
    return tuple(np.ascontiguousarray(o, dtype=np.float32) for o in outs)


def kernel(**inputs):
    ncores = 8
    w = host_weights(inputs)
    in_maps = []
    for c in range(ncores):
        m = host_core_inputs(inputs, c)
        m.update(w)
        in_maps.append(m)
    prog = Prog()
    with prog.nc.allow_non_contiguous_dma(reason="small strided state / scratch transfers"):
        nc = prog.build()
    res = run_bass_kernel_spmd(nc, in_maps, core_ids=list(range(ncores)))
    return assemble(res.results, ncores)
```

```python
import numpy as np
import concourse.bass as bass
import concourse.mybir as mybir
from concourse.bass_utils import run_bass_kernel_spmd

F32 = mybir.dt.float32
BF16 = mybir.dt.bfloat16
AF = mybir.ActivationFunctionType
ALU = mybir.AluOpType
AX = mybir.AxisListType

D = 2048
KC = D // 128
DFF = 5632
NF = DFF // 128
SEQ = 2048
NB = 16
DSEQ = 4
NS = NB * DSEQ
NTOK = SEQ + NS
EPS = 1e-6


class Res:
    __slots__ = ("name", "w", "rs", "dsem", "dkey")

    def __init__(self, name):
        self.name = name
        self.w = {}
        self.rs = {}
        self.dsem = None
        self.dkey = None


class Ctx:
    def __init__(self, nc):
        self.nc = nc
        self.eng = {"pe": nc.tensor, "act": nc.scalar, "dve": nc.vector, "pool": nc.gpsimd, "sp": nc.sync}
        self.sem = {}
        self.cnt = {}
        for k in ("pe", "act", "dve", "pool"):
            self.sem[k] = nc.alloc_semaphore("sem_" + k)
            self.cnt[k] = 0
        self.waited = {k: {} for k in self.eng}
        self.nres = 0
        self.n_inst = 0
        self.free_d = []
        self.phase_d = []
        self.dead = set()
        self.semG = nc.alloc_semaphore("bar_gather")
        self.semR = nc.alloc_semaphore("bar_release")
        self.bar_n = 0

    def res(self, name=None):
        self.nres += 1
        return Res(name or f"r{self.nres}")

    def sb(self, name, shape, dtype):
        t = self.nc.alloc_sbuf_tensor(name, list(shape), dtype)
        return t, self.res(name)

    def _deps(self, reads, writes, skip=None):
        d = {}
        for r in reads:
            for k, v in r.w.items():
                if d.get(k, 0) < v:
                    d[k] = v
        for w in writes:
            for k, v in w.w.items():
                if d.get(k, 0) < v:
                    d[k] = v
            for k, v in w.rs.items():
                if d.get(k, 0) < v:
                    d[k] = v
        if skip is not None:
            d.pop(skip, None)
        return d

    def _wait(self, eng, deps):
        wd = self.waited[eng]
        e = self.eng[eng]
        for k, v in deps.items():
            if k in self.dead or wd.get(k, 0) >= v:
                continue
            if k.startswith("dma:"):
                v = self.cnt[k]
            e.wait_ge(self.sem[k], v)
            wd[k] = v
            self.n_inst += 1

    def _mark(self, tok, reads, writes):
        k, v = tok
        for r in reads:
            r.rs[k] = v
        for w in writes:
            w.w[k] = v

    def op(self, eng, fn, reads=(), writes=()):
        self._wait(eng, self._deps(reads, writes))
        inst = fn(self.eng[eng])
        self.cnt[eng] += 1
        inst.then_inc(self.sem[eng], 1)
        self.n_inst += 1
        self._mark((eng, self.cnt[eng]), reads, writes)

    def group(self, eng, fns, reads=(), writes=()):
        self._wait(eng, self._deps(reads, writes, skip="pe" if eng == "pe" else None))
        inst = None
        for fn in fns:
            inst = fn(self.eng[eng])
            self.n_inst += 1
        self.cnt[eng] += 1
        inst.then_inc(self.sem[eng], 1)
        self._mark((eng, self.cnt[eng]), reads, writes)

    def dma(self, q, out, in_, sres, reads=(), writes=(), **kw):
        if sres.dsem is None:
            if self.free_d:
                sres.dsem = self.free_d.pop()
            else:
                sres.dsem = self.nc.alloc_semaphore("d_" + str(self.nres))
            sres.dkey = "dma:" + str(self.nres)
            self.nres += 1
            self.sem[sres.dkey] = sres.dsem
            self.cnt[sres.dkey] = 0
            self.phase_d.append(sres)
        self._wait(q, self._deps(reads, writes))
        inst = self.eng[q].dma_start(out=out, in_=in_, **kw)
        self.cnt[sres.dkey] += 16
        inst.then_inc(sres.dsem, 16)
        self.n_inst += 1
        self._mark((sres.dkey, self.cnt[sres.dkey]), reads, writes)

    def finish(self, eng, resources):
        d = {}
        for r in resources:
            for k, v in r.w.items():
                if d.get(k, 0) < v:
                    d[k] = v
        self._wait(eng, d)

    def barrier(self):
        allk = {k: v for k, v in self.cnt.items() if v > 0 and k not in self.dead}
        for e in ("sp", "pe", "act", "dve", "pool"):
            self._wait(e, dict(allk))
        self.bar_n += 1
        for e in ("sp", "pe", "act", "dve"):
            self.eng[e].sem_inc(self.semG, 1)
        pool = self.eng["pool"]
        pool.wait_ge(self.semG, 4 * self.bar_n)
        for res in self.phase_d:
            pool.sem_clear(res.dsem)
        pool.sem_inc(self.semR, 1)
        for e in ("sp", "pe", "act", "dve"):
            self.eng[e].wait_ge(self.semR, self.bar_n)
        for res in self.phase_d:
            self.dead.add(res.dkey)
            self.free_d.append(res.dsem)
            res.dsem = None
            res.dkey = None
        self.phase_d = []


def blocks_of(r0, r1):
    out = []
    r = r0
    while r < r1:
        n = min(128, r1 - r)
        out.append((r, n))
        r += n
    return out


class Prog:
    def __init__(self, cfg=None):
        self.cfg = cfg or {}
        self.nc = bass.Bass("TRN2", target_bir_lowering=False)
        self.cx = Ctx(self.nc)
        self.dram = {}
        self.dres = {}
        nc = self.nc
        self.bank = []
        self.bres = []
        self.psum = nc.alloc_psum_tensor("psum_all", [128, 4096], F32)
        for i in range(8):
            self.bank.append(self.psum[:, i * 512:(i + 1) * 512])
            self.bres.append(self.cx.res(f"bank{i}"))
        self.ident = None
        self.outputs = []

    def din(self, name, shape):
        t = self.nc.dram_tensor(name, list(shape), F32, kind="ExternalInput")
        self.dram[name] = t
        self.dres[name] = self.cx.res(name)
        return t

    def dout(self, name, shape):
        t = self.nc.dram_tensor(name, list(shape), F32, kind="ExternalOutput")
        self.dram[name] = t
        self.dres[name] = self.cx.res(name)
        self.outputs.append(name)
        return t

    def dscr(self, name, shape, dtype=F32):
        t = self.nc.dram_tensor(name, list(shape), dtype, kind="Internal")
        self.dram[name] = t
        self.dres[name] = self.cx.res(name)
        return t

    def setup_consts(self):
        nc, cx = self.nc, self.cx
        self.ident_f, self.ident_f_r = cx.sb("ident_f", [128, 128], F32)
        self.ident_b, self.ident_b_r = cx.sb("ident_b", [128, 128], BF16)
        cx.op("pool", lambda e: e.memset(self.ident_f[:], 1.0), writes=[self.ident_f_r])
        cx.op("pool", lambda e: e.affine_select(out=self.ident_f[:], in_=self.ident_f[:], pattern=[[-1, 128]],
                                               compare_op=ALU.is_equal, fill=0.0, base=0, channel_multiplier=1),
              reads=[self.ident_f_r], writes=[self.ident_f_r])
        cx.op("pool", lambda e: e.tensor_copy(out=self.ident_b[:], in_=self.ident_f[:]),
              reads=[self.ident_f_r], writes=[self.ident_b_r])
        self.eps_t, self.eps_r = cx.sb("eps_t", [128, 1], F32)
        cx.op("pool", lambda e: e.memset(self.eps_t[:], EPS), writes=[self.eps_r])
        self.tp_bank = 0
        self._nb = 0
        self._norm_bufs = None

    def norm_blocks(self, es, srcs, gain_ap, hT, hT_r, tag):
        nc, cx = self.nc, self.cx
        gbc, gbc_r = self._sb(es, tag + "gbc", [128, D], F32)
        cx.dma("sp", gbc[:], gain_ap.partition_broadcast(128), gbc_r, writes=[gbc_r])
        xt, xt_r = self._sb(es, tag + "xt", [128, D], F32)
        sq, sq_r = self._sb(es, tag + "sq", [128, D], BF16)
        xn, xn_r = self._sb(es, tag + "xn", [128, D], BF16)
        st, st_r = self._sb(es, tag + "st", [128, 4], F32)
        for (src, src_r, n, col0) in srcs:
            cx.dma("sp", xt[0:n, :], src, xt_r, reads=[src_r], writes=[xt_r])
            cx.op("act", lambda e: e.activation(out=sq[0:n, :], in_=xt[0:n, :], func=AF.Square,
                                                accum_out=st[0:n, 0:1]),
                  reads=[xt_r], writes=[sq_r, st_r])
            cx.op("act", lambda e: e.activation(out=st[0:n, 1:2], in_=st[0:n, 0:1], func=AF.Sqrt,
                                                scale=1.0 / D, bias=self.eps_t[0:n, 0:1]),
                  reads=[st_r, self.eps_r], writes=[st_r])
            cx.op("dve", lambda e: e.reciprocal(out=st[0:n, 2:3], in_=st[0:n, 1:2]), reads=[st_r], writes=[st_r])
            cx.op("dve", lambda e: e.scalar_tensor_tensor(out=xn[0:n, :], in0=xt[0:n, :], scalar=st[0:n, 2:3],
                                                          in1=gbc[0:n, :], op0=ALU.mult, op1=ALU.mult),
                  reads=[xt_r, st_r, gbc_r], writes=[xn_r])
            self.transpose_into(xn, xn_r, n, hT, hT_r, col0)

    def transpose_into(self, xn, xn_r, n, hT, hT_r, col0, nk=KC):
        cx = self.cx
        for half in range((nk + 7) // 8):
            k0 = half * 8
            k1 = min(nk, k0 + 8)
            b = self.nb()
            pb = self.bank[b][:].bitcast(BF16)
            fns = []
            for kc in range(k0, k1):
                fns.append(lambda e, kc=kc: e.transpose(out=pb[:, (kc - k0) * 128:(kc - k0) * 128 + n],
                                                        in_=xn[0:n, kc * 128:(kc + 1) * 128],
                                                        identity=self.ident_b[0:n, 0:n]))
            cx.group("pe", fns, reads=[xn_r, self.ident_b_r], writes=[self.bres[b]])
            src = pb.rearrange("p (k c) -> p k c", c=128)[:, 0:k1 - k0, 0:n]
            eng = "act" if half % 2 == 0 else "dve"
            if eng == "act":
                cx.op("act", lambda e: e.copy(out=hT[:, k0:k1, col0:col0 + n], in_=src),
                      reads=[self.bres[b]], writes=[hT_r])
            else:
                cx.op("dve", lambda e: e.tensor_copy(out=hT[:, k0:k1, col0:col0 + n], in_=src),
                      reads=[self.bres[b]], writes=[hT_r])

    def _sb(self, es, name, shape, dtype):
        t = es.enter_context(self.nc.sbuf_tensor(name, list(shape), dtype))
        return t, self.cx.res(name)

    def ffn_phase(self, tag, wgu, wd, gain_ap, tiles, src_fn, dst_fn):
        import contextlib
        nc, cx = self.nc, self.cx
        wgu_r = self.dres[wgu.name]
        wd_r = self.dres[wd.name]
        Tmax = max(sum(n for _, n in t) for t in tiles)
        CW = 256
        NCB = D // CW
        with contextlib.ExitStack() as es:
            aT, aT_r = self._sb(es, tag + "aT", [128, NF, Tmax], BF16)
            hT, hT_r = self._sb(es, tag + "hT", [128, KC, Tmax], BF16)
            for ti, tile in enumerate(tiles):
                cols = []
                c = 0
                for (r0, n) in tile:
                    cols.append(c)
                    c += n
                T = c
                with contextlib.ExitStack() as es2:
                    srcs = []
                    for (r0, n), c0 in zip(tile, cols):
                        ap, r = src_fn(r0, n)
                        srcs.append((ap, r, n, c0))
                    self._norm_bufs = None
                    self.norm_blocks_cached(es2, srcs, gain_ap, hT, hT_r, f"{tag}{ti}")
                    cx.barrier()
                with contextlib.ExitStack() as es2:
                    NST = 3
                    wst = [self._sb(es2, f"{tag}{ti}wst{i}", [128, 2 * KC * 128], F32) for i in range(NST)]
                    wbf = [self._sb(es2, f"{tag}{ti}wbf{i}", [128, 2 * KC * 128], BF16) for i in range(2)]
                    sil = [self._sb(es2, f"{tag}{ti}sil{i}", [128, 512], F32) for i in range(2)]
                    subs = self.sub_tiles(T)
                    u = 0
                    H = KC * 128
                    for f in range(NF):
                        s_t, s_r = wst[f % NST]
                        b_t, b_r = wbf[f % 2]
                        cx.dma("sp", s_t[:], wgu[f], s_r, reads=[wgu_r], writes=[s_r])
                        cx.op("act", lambda e: e.copy(out=b_t[:, 0:H], in_=s_t[:, 0:H]), reads=[s_r], writes=[b_r])
                        cx.op("dve", lambda e: e.tensor_copy(out=b_t[:, H:2 * H], in_=s_t[:, H:2 * H]), reads=[s_r], writes=[b_r])
                        for (c0, n) in subs:
                            bg = 2 * (u % 4)
                            bu = bg + 1
                            fns = []
                            for gu, bk in ((0, bg), (1, bu)):
                                for kc in range(KC):
                                    fns.append(lambda e, gu=gu, bk=bk, kc=kc: e.matmul(
                                        self.bank[bk][:, 0:n],
                                        lhsT=b_t[:, (gu * KC + kc) * 128:(gu * KC + kc + 1) * 128],
                                        rhs=hT[:, kc, c0:c0 + n], start=(kc == 0), stop=(kc == KC - 1)))
                            cx.group("pe", fns, reads=[b_r, hT_r], writes=[self.bres[bg], self.bres[bu]])
                            sl_t, sl_r = sil[u % 2]
                            cx.op("act", lambda e: e.activation(out=sl_t[:, 0:n], in_=self.bank[bg][:, 0:n], func=AF.Silu),
                                  reads=[self.bres[bg]], writes=[sl_r])
                            cx.op("dve", lambda e: e.tensor_tensor(out=aT[:, f, c0:c0 + n], in0=sl_t[:, 0:n],
                                                                   in1=self.bank[bu][:, 0:n], op=ALU.mult),
                                  reads=[sl_r, self.bres[bu]], writes=[aT_r])
                            u += 1
                    cx.barrier()
                if self.cfg.get('skipD'):
                    continue
                with contextlib.ExitStack() as es2:
                    wres = [self._sb(es2, f"{tag}{ti}wres{i}", [128, NF, CW], BF16) for i in range(2)]
                    dst_ = [self._sb(es2, f"{tag}{ti}dst{i}", [128, 11, CW], F32) for i in range(2)]
                    xo = [self._sb(es2, f"{tag}{ti}xo{i}", [128, CW], F32) for i in range(4)]
                    xw = [self._sb(es2, f"{tag}{ti}xw{i}", [128, CW], F32) for i in range(4)]
                    g = 0
                    ev = 0
                    hb = 0
                    hres = [cx.res(f"{tag}{ti}hb{i}") for i in range(16)]
                    def load_wd(cb):
                        nonlocal g
                        w_t, w_r = wres[cb % 2]
                        for q in range(4):
                            s_t, s_r = dst_[g % 2]
                            g += 1
                            src = wd[q * 11 * 128:(q + 1) * 11 * 128, cb * CW:(cb + 1) * CW].rearrange("(j p) c -> p j c", p=128)
                            cx.dma("sp", s_t[:], src, s_r, reads=[wd_r], writes=[s_r])
                            self.cast(w_t[:, q * 11:(q + 1) * 11, :], s_t[:], [s_r], [w_r])

                    load_wd(0)
                    for cb in range(NCB):
                        w_t, w_r = wres[cb % 2]
                        if cb + 1 < NCB:
                            load_wd(cb + 1)
                        for bi, ((r0, n), c0) in enumerate(zip(tile, cols)):
                            bk, half = hb % 8, hb // 8
                            h_r = hres[hb % 8]
                            hb = (hb + 1) % 16
                            o = self.bank[bk][0:n, half * CW:(half + 1) * CW]
                            fns = [lambda e, f=f: e.matmul(o, lhsT=aT[:, f, c0:c0 + n], rhs=w_t[:, f, :],
                                                           start=(f == 0), stop=(f == NF - 1)) for f in range(NF)]
                            cx.group("pe", fns, reads=[w_r, aT_r], writes=[h_r])
                            xo_t, xo_r = xo[ev % 4]
                            xw_t, xw_r = xw[ev % 4]
                            ev += 1
                            sap, sr = src_fn(r0, n)
                            dap, dr = dst_fn(r0, n)
                            cx.dma("sp", xo_t[0:n, :], sap[:, cb * CW:(cb + 1) * CW], xo_r, reads=[sr], writes=[xo_r])
                            cx.op("dve", lambda e: e.scalar_tensor_tensor(out=xw_t[0:n, :], in0=o, scalar=0.5,
                                                                          in1=xo_t[0:n, :], op0=ALU.mult, op1=ALU.add),
                                  reads=[h_r, xo_r], writes=[xw_r])
                            cx.dma("sp", dap[:, cb * CW:(cb + 1) * CW], xw_t[0:n, :], xw_r, reads=[xw_r], writes=[dr])
                    cx.barrier()

    def norm_blocks_cached(self, es, srcs, gain_ap, hT, hT_r, tag):
        cx = self.cx
        if self._norm_bufs is None:
            gbc, gbc_r = self._sb(es, tag + "gbc", [128, D], F32)
            cx.dma("sp", gbc[:], gain_ap.partition_broadcast(128), gbc_r, writes=[gbc_r])
            xt = self._sb(es, tag + "xt", [128, D], F32)
            sq = self._sb(es, tag + "sq", [128, D], BF16)
            xn = self._sb(es, tag + "xn", [128, D], BF16)
            st = self._sb(es, tag + "st", [128, 4], F32)
            self._norm_bufs = (gbc, gbc_r, xt, sq, xn, st)
        gbc, gbc_r, (xt, xt_r), (sq, sq_r), (xn, xn_r), (st, st_r) = self._norm_bufs
        for (src, src_r, n, col0) in srcs:
            cx.dma("sp", xt[0:n, :], src, xt_r, reads=[src_r], writes=[xt_r])
            cx.op("act", lambda e: e.activation(out=sq[0:n, :], in_=xt[0:n, :], func=AF.Square,
                                                accum_out=st[0:n, 0:1]),
                  reads=[xt_r], writes=[sq_r, st_r])
            cx.op("act", lambda e: e.activation(out=st[0:n, 1:2], in_=st[0:n, 0:1], func=AF.Sqrt,
                                                scale=1.0 / D, bias=self.eps_t[0:n, 0:1]),
                  reads=[st_r, self.eps_r], writes=[st_r])
            cx.op("dve", lambda e: e.reciprocal(out=st[0:n, 2:3], in_=st[0:n, 1:2]), reads=[st_r], writes=[st_r])
            cx.op("dve", lambda e: e.scalar_tensor_tensor(out=xn[0:n, :], in0=xt[0:n, :], scalar=st[0:n, 2:3],
                                                          in1=gbc[0:n, :], op0=ALU.mult, op1=ALU.mult),
                  reads=[xt_r, st_r, gbc_r], writes=[xn_r])
            self.transpose_into(xn, xn_r, n, hT, hT_r, col0)

    def cast(self, out, in_, reads, writes, eng=None):
        if eng is None:
            self._ce = 1 - getattr(self, "_ce", 0)
            eng = "act" if self._ce else "dve"
        if eng == "act":
            self.cx.op("act", lambda e: e.copy(out=out, in_=in_), reads=reads, writes=writes)
        else:
            self.cx.op("dve", lambda e: e.tensor_copy(out=out, in_=in_), reads=reads, writes=writes)

    def nb(self):
        b = self._nb
        self._nb = (self._nb + 1) % 8
        return b

    def sub_tiles(self, T, w=512):
        out = []
        c = 0
        while c < T:
            n = min(w, T - c)
            out.append((c, n))
            c += n
        return out

    def wbufs(self, es, tag, nk, w, nst=2, nbf=2):
        return {"st": [self._sb(es, f"{tag}wS{i}", [128, nk, w], F32) for i in range(nst)],
                "bf": [self._sb(es, f"{tag}wB{i}", [128, nk, w], BF16) for i in range(nbf)], "i": 0}

    def load_w(self, bufs, src_ap, src_r, nk, w):
        cx = self.cx
        i = bufs["i"]
        bufs["i"] += 1
        st, st_r = bufs["st"][i % len(bufs["st"])]
        bf, bf_r = bufs["bf"][i % len(bufs["bf"])]
        cx.dma("sp", st[:, 0:nk, 0:w], src_ap.rearrange("(k p) c -> p k c", p=128), st_r, reads=[src_r], writes=[st_r])
        h = max(1, nk // 2)
        self.cast(bf[:, 0:h, 0:w], st[:, 0:h, 0:w], [st_r], [bf_r], eng="act")
        if nk > h:
            self.cast(bf[:, h:nk, 0:w], st[:, h:nk, 0:w], [st_r], [bf_r], eng="dve")
        return bf, bf_r

    def load_w_resident(self, bufs, W_ap, W_r, nk, N, dst, dst_r):
        cx = self.cx
        for k0 in range(0, nk, 4):
            k1 = min(nk, k0 + 4)
            for c0 in range(0, N, 512):
                c1 = min(N, c0 + 512)
                i = bufs["i"]
                bufs["i"] += 1
                st, st_r = bufs["st"][i % len(bufs["st"])]
                cx.dma("sp", st[:, 0:k1 - k0, 0:c1 - c0],
                       W_ap[k0 * 128:k1 * 128, c0:c1].rearrange("(k p) c -> p k c", p=128), st_r,
                       reads=[W_r], writes=[st_r])
                self.cast(dst[:, k0:k1, c0:c1], st[:, 0:k1 - k0, 0:c1 - c0], [st_r], [dst_r])

    def norm_all(self, es, tag, src_fn, gain_ap, blocks):
        T = max(r0 + n for r0, n in blocks)
        hT, hT_r = self._sb(es, tag + "hT", [128, KC, T], BF16)
        self._norm_bufs = None
        srcs = []
        for (r0, n) in blocks:
            ap, r = src_fn(r0, n)
            srcs.append((ap, r, n, r0))
        import contextlib
        with contextlib.ExitStack() as es2:
            self.norm_blocks_cached(es2, srcs, gain_ap, hT, hT_r, tag)
            self.cx.barrier()
        return hT, hT_r

    def mm_tok(self, hT, hT_r, nk, col0, n, wb, wb_r, wc0, w, bank):
        fns = [lambda e, kc=kc: e.matmul(self.bank[bank][0:n, 0:w], lhsT=hT[:, kc, col0:col0 + n],
                                         rhs=wb[:, kc, wc0:wc0 + w], start=(kc == 0), stop=(kc == nk - 1))
               for kc in range(nk)]
        self.cx.group("pe", fns, reads=[hT_r, wb_r], writes=[self.bres[bank]])

    def mm_feat(self, hT, hT_r, nk, c0, n, wb, wb_r, wc0, m, bank, bcol=0):
        fns = [lambda e, kc=kc: e.matmul(self.bank[bank][0:m, bcol:bcol + n], lhsT=wb[:, kc, wc0:wc0 + m],
                                         rhs=hT[:, kc, c0:c0 + n], start=(kc == 0), stop=(kc == nk - 1))
               for kc in range(nk)]
        self.cx.group("pe", fns, reads=[hT_r, wb_r], writes=[self.bres[bank]])

    def out_proj(self, es, tag, oT, oT_r, nk, W_ap, W_r, blocks, src_fn, dst_fn):
        cx = self.cx
        wres, wres_r = self._sb(es, tag + "Wres", [128, nk, D], BF16)
        bufs = self.wbufs(es, tag + "op", 4, 512, nst=2, nbf=0)
        self.load_w_resident(bufs, W_ap, W_r, nk, D, wres, wres_r)
        xo = [self._sb(es, f"{tag}oxo{i}", [128, 512], F32) for i in range(3)]
        xw = [self._sb(es, f"{tag}oxw{i}", [128, 512], F32) for i in range(3)]
        ev = 0
        for (r0, n, col0) in blocks:
            sap, sr = src_fn(r0, n)
            dap, dr = dst_fn(r0, n)
            for cb in range(4):
                b = self.nb()
                self.mm_tok(oT, oT_r, nk, col0, n, wres, wres_r, cb * 512, 512, b)
                xo_t, xo_r = xo[ev % 3]
                xw_t, xw_r = xw[ev % 3]
                ev += 1
                cx.dma("sp", xo_t[0:n, :], sap[:, cb * 512:(cb + 1) * 512], xo_r, reads=[sr], writes=[xo_r])
                cx.op("dve", lambda e: e.tensor_tensor(out=xw_t[0:n, :], in0=self.bank[b][0:n, :], in1=xo_t[0:n, :], op=ALU.add),
                      reads=[self.bres[b], xo_r], writes=[xw_r])
                cx.dma("sp", dap[:, cb * 512:(cb + 1) * 512], xw_t[0:n, :], xw_r, reads=[xw_r], writes=[dr])

    def transpose_blk(self, src, src_r, n, ncols, dst_fn, dst_r, dt=BF16, evac="act", scale=None):
        cx = self.cx
        nk = ncols // 128
        per = 8 if dt == BF16 else 4
        ident, ident_r = (self.ident_b, self.ident_b_r) if dt == BF16 else (self.ident_f, self.ident_f_r)
        for k0 in range(0, nk, per):
            k1 = min(nk, k0 + per)
            b = self.nb()
            pb = self.bank[b][:].bitcast(BF16) if dt == BF16 else self.bank[b][:]
            fns = [lambda e, kc=kc: e.transpose(out=pb[:, (kc - k0) * 128:(kc - k0) * 128 + n],
                                                in_=src[0:n, kc * 128:(kc + 1) * 128], identity=ident[0:n, 0:n])
                   for kc in range(k0, k1)]
            cx.group("pe", fns, reads=[src_r, ident_r], writes=[self.bres[b]])
            for kc in range(k0, k1):
                o = dst_fn(kc)
                i_ = pb[:, (kc - k0) * 128:(kc - k0) * 128 + n]
                if evac == "act":
                    cx.op("act", lambda e: e.copy(out=o, in_=i_), reads=[self.bres[b]], writes=[dst_r])
                else:
                    cx.op("dve", lambda e: e.tensor_copy(out=o, in_=i_), reads=[self.bres[b]], writes=[dst_r])

    def memkv_phase(self):
        import contextlib
        cx = self.cx
        mem = self.dram["mem_p"]
        with contextlib.ExitStack() as es:
            mT, mT_r = self._sb(es, "mkT", [128, KC, 256], BF16)
            xt, xt_r = self._sb(es, "mkx", [128, D], F32)
            xb, xb_r = self._sb(es, "mkxb", [128, D], BF16)
            for blk in range(2):
                cx.dma("sp", xt[:], mem.ap()[blk * 128:(blk + 1) * 128, :], xt_r, reads=[self.dres["mem_p"]], writes=[xt_r])
                cx.op("act", lambda e: e.copy(out=xb[:], in_=xt[:]), reads=[xt_r], writes=[xb_r])
                self.transpose_into(xb, xb_r, 128, mT, mT_r, blk * 128)
            bufs = self.wbufs(es, "mk", KC, 512)
            ev = [self._sb(es, f"mkev{i}", [128, 512], F32) for i in range(2)]
            k = 0
            for l in range(2):
                for nm, wn in (("mem_k_p", "xwk"), ("mem_v_p", "xwv")):
                    wb, wb_r = self.load_w(bufs, self.dram[wn].ap()[l], self.dres[wn], KC, 512)
                    for blk in range(2):
                        b = self.nb()
                        self.mm_tok(mT, mT_r, KC, blk * 128, 128, wb, wb_r, 0, 512, b)
                        e_t, e_r = ev[k % 2]
                        k += 1
                        cx.op("act", lambda e: e.copy(out=e_t[:], in_=self.bank[b][:]), reads=[self.bres[b]], writes=[e_r])
                        cx.dma("sp", self.dram[nm].ap()[l, blk * 128:(blk + 1) * 128, :], e_t[:], e_r,
                               reads=[e_r], writes=[self.dres[nm]])
        cx.barrier()

    def final_phase(self, src_fn, dst_fn, gain_ap, blocks):
        import contextlib
        cx = self.cx
        with contextlib.ExitStack() as es:
            gbc, gbc_r = self._sb(es, "fgbc", [128, D], F32)
            cx.dma("sp", gbc[:], gain_ap.partition_broadcast(128), gbc_r, writes=[gbc_r])
            xt = [self._sb(es, f"fx{i}", [128, D], F32) for i in range(2)]
            sq, sq_r = self._sb(es, "fsq", [128, D], BF16)
            yo = [self._sb(es, f"fy{i}", [128, D], F32) for i in range(2)]
            st, st_r = self._sb(es, "fst", [128, 4], F32)
            for i, (r0, n) in enumerate(blocks):
                x_t, x_r = xt[i % 2]
                y_t, y_r = yo[i % 2]
                sap, sr = src_fn(r0, n)
                dap, dr = dst_fn(r0, n)
                cx.dma("sp", x_t[0:n, :], sap, x_r, reads=[sr], writes=[x_r])
                cx.op("act", lambda e: e.activation(out=sq[0:n, :], in_=x_t[0:n, :], func=AF.Square, accum_out=st[0:n, 0:1]),
                      reads=[x_r], writes=[sq_r, st_r])
                cx.op("act", lambda e: e.activation(out=st[0:n, 1:2], in_=st[0:n, 0:1], func=AF.Sqrt, scale=1.0 / D,
                                                    bias=self.eps_t[0:n, 0:1]), reads=[st_r, self.eps_r], writes=[st_r])
                cx.op("dve", lambda e: e.reciprocal(out=st[0:n, 2:3], in_=st[0:n, 1:2]), reads=[st_r], writes=[st_r])
                cx.op("dve", lambda e: e.scalar_tensor_tensor(out=y_t[0:n, :], in0=x_t[0:n, :], scalar=st[0:n, 2:3],
                                                              in1=gbc[0:n, :], op0=ALU.mult, op1=ALU.mult),
                      reads=[x_r, st_r, gbc_r], writes=[y_r])
                cx.dma("sp", dap, y_t[0:n, :], y_r, reads=[y_r], writes=[dr])
        cx.barrier()

    def xattn_phase(self, l, src_fn, dst_fn):
        import contextlib
        cx = self.cx
        tag = f"xa{l}"
        blocks = blocks_of(0, SEQ) + [(SEQ, NS)]
        scale = 128 ** -0.5
        with contextlib.ExitStack() as es:
            hT, hT_r = self.norm_all(es, tag, src_fn, self.dram["norm_gain"].ap()[l, 2], blocks)
            qT, qT_r = self._sb(es, tag + "qT", [128, 4, NTOK], BF16)
            oT, oT_r = self._sb(es, tag + "oT", [128, 4, NTOK], BF16)
            with contextlib.ExitStack() as es2:
                bufs = self.wbufs(es2, tag + "q", KC, 512, nst=1, nbf=1)
                wb, wb_r = self.load_w(bufs, self.dram["xwq"].ap()[l], self.dres["xwq"], KC, 512)
                for hd in range(4):
                    for (c0, n) in self.sub_tiles(NTOK):
                        b = self.nb()
                        self.mm_feat(hT, hT_r, KC, c0, n, wb, wb_r, hd * 128, 128, b)
                        cx.op("act", lambda e: e.copy(out=qT[:, hd, c0:c0 + n], in_=self.bank[b][:, 0:n]),
                              reads=[self.bres[b]], writes=[qT_r])
                cx.barrier()
            kf = [self._sb(es, f"{tag}kf{i}", [128, 2, 512], F32) for i in range(2)]
            vf = [self._sb(es, f"{tag}vf{i}", [128, 2, 512], F32) for i in range(2)]
            kb = [self._sb(es, f"{tag}kb{i}", [128, 2, 512], BF16) for i in range(2)]
            vb = [self._sb(es, f"{tag}vb{i}", [128, 2, 512], BF16) for i in range(2)]
            kT = [self._sb(es, f"{tag}kT{i}", [128, 4, 256], BF16) for i in range(2)]
            pp = [self._sb(es, f"{tag}p{i}", [128, 256], F32) for i in range(2)]
            pn = [self._sb(es, f"{tag}pn{i}", [128, 256], BF16) for i in range(2)]
            pT = [self._sb(es, f"{tag}pT{i}", [128, 2, 128], BF16) for i in range(2)]
            sm = [self._sb(es, f"{tag}sm{i}", [128, 4], F32) for i in range(2)]
            u = 0

            def load_kv(i, k_ap, k_r, v_ap, v_r):
                kf_t, kf_r = kf[i % 2]
                vf_t, vf_r = vf[i % 2]
                kb_t, kb_r = kb[i % 2]
                vb_t, vb_r = vb[i % 2]
                kT_t, kT_r = kT[i % 2]
                cx.dma("sp", kf_t[:], k_ap.rearrange("(c p) f -> p c f", p=128), kf_r, reads=[k_r], writes=[kf_r])
                cx.dma("sp", vf_t[:], v_ap.rearrange("(c p) f -> p c f", p=128), vf_r, reads=[v_r], writes=[vf_r])
                self.cast(kb_t[:], kf_t[:], [kf_r], [kb_r], eng="act")
                self.cast(vb_t[:], vf_t[:], [vf_r], [vb_r], eng="dve")
                for mc in range(2):
                    self.transpose_blk(kb_t[:, mc, :], kb_r, 128, 512,
                                       lambda hd: kT_t[:, hd, mc * 128:(mc + 1) * 128], kT_r, evac="dve")
                return kT_t, kT_r, vb_t, vb_r

            def s1(c0, n, kT_t, kT_r, vb_t, vb_r, hd, uu):
                p_t, p_r = pp[uu % 2]
                sm_t, sm_r = sm[uu % 2]
                b = self.nb()
                cx.group("pe", [lambda e: e.matmul(self.bank[b][0:n, 0:256], lhsT=qT[:, hd, c0:c0 + n],
                                                   rhs=kT_t[:, hd, :], start=True, stop=True)],
                         reads=[qT_r, kT_r], writes=[self.bres[b]])
                cx.op("dve", lambda e: e.reduce_max(out=sm_t[0:n, 0:1], in_=self.bank[b][0:n, 0:256], axis=AX.X),
                      reads=[self.bres[b]], writes=[sm_r])
                cx.op("dve", lambda e: e.tensor_scalar_mul(out=sm_t[0:n, 1:2], in0=sm_t[0:n, 0:1], scalar1=-scale),
                      reads=[sm_r], writes=[sm_r])
                cx.op("act", lambda e: e.activation(out=p_t[0:n, :], in_=self.bank[b][0:n, 0:256], func=AF.Exp,
                                                    scale=scale, bias=sm_t[0:n, 1:2], accum_out=sm_t[0:n, 2:3]),
                      reads=[self.bres[b], sm_r], writes=[p_r, sm_r])

            def s2(c0, n, kT_t, kT_r, vb_t, vb_r, hd, uu):
                p_t, p_r = pp[uu % 2]
                pn_t, pn_r = pn[uu % 2]
                pT_t, pT_r = pT[uu % 2]
                sm_t, sm_r = sm[uu % 2]
                cx.op("dve", lambda e: e.reciprocal(out=sm_t[0:n, 3:4], in_=sm_t[0:n, 2:3]), reads=[sm_r], writes=[sm_r])
                cx.op("dve", lambda e: e.tensor_scalar_mul(out=pn_t[0:n, :], in0=p_t[0:n, :], scalar1=sm_t[0:n, 3:4]),
                      reads=[p_r, sm_r], writes=[pn_r])
                self.transpose_blk(pn_t, pn_r, n, 256, lambda mc: pT_t[:, mc, 0:n], pT_r, evac="act")
                b2 = self.nb()
                cx.group("pe", [lambda e, mc=mc: e.matmul(self.bank[b2][:, 0:n], lhsT=vb_t[:, mc, hd * 128:(hd + 1) * 128],
                                                          rhs=pT_t[:, mc, 0:n], start=(mc == 0), stop=(mc == 1))
                                for mc in range(2)],
                         reads=[vb_r, pT_r], writes=[self.bres[b2]])
                cx.op("act", lambda e: e.copy(out=oT[:, hd, c0:c0 + n], in_=self.bank[b2][:, 0:n]),
                      reads=[self.bres[b2]], writes=[oT_r])

            pend = [None]

            def unit(c0, n, kT_t, kT_r, vb_t, vb_r):
                nonlocal u
                for hd in range(4):
                    item = (c0, n, kT_t, kT_r, vb_t, vb_r, hd, u)
                    u += 1
                    s1(*item)
                    if pend[0] is not None:
                        s2(*pend[0])
                    pend[0] = item

            kvp = load_kv(0, self.dram["mem_k_p"].ap()[l], self.dres["mem_k_p"], self.dram["mem_v_p"].ap()[l], self.dres["mem_v_p"])
            for (r0, n) in blocks_of(0, SEQ):
                unit(r0, n, *kvp)
            for bi in range(NB):
                kv = load_kv(bi + 1, self.dram["cmk"].ap()[l, bi], self.dres["cmk"], self.dram["cmv"].ap()[l, bi], self.dres["cmv"])
                unit(SEQ + bi * DSEQ, DSEQ, *kv)
            s2(*pend[0])
            self.out_proj(es, tag, oT, oT_r, 4, self.dram["xwo"].ap()[l], self.dres["xwo"],
                          [(r0, n, r0) for (r0, n) in blocks], src_fn, dst_fn)
        cx.barrier()

    def even_in_phase(self, src_fn):
        import contextlib
        cx = self.cx
        tag = "ei"
        blocks = blocks_of(0, SEQ) + [(SEQ, NS)]
        d, r = self.dram, self.dres
        with contextlib.ExitStack() as es:
            hT, hT_r = self.norm_all(es, tag, src_fn, d["norm_gain"].ap()[0, 1], blocks)
            bufs = self.wbufs(es, tag, KC, 512)
            evf = [self._sb(es, f"{tag}evf{i}", [128, 512], F32) for i in range(3)]
            evb = [self._sb(es, f"{tag}evb{i}", [128, 512], BF16) for i in range(3)]
            k = 0
            tok_dst = [("KM", 0, BF16), ("VM", 0, BF16), ("VM", 512, BF16), ("OM", 0, F32), ("OM", 512, F32), (None, 0, F32)]
            for cb in range(6):
                wb, wb_r = self.load_w(bufs, d["wtok"].ap()[:, cb * 512:(cb + 1) * 512], r["wtok"], KC, 512)
                nm, dc, dt = tok_dst[cb]
                for (r0, n) in blocks:
                    b = self.nb()
                    self.mm_tok(hT, hT_r, KC, r0, n, wb, wb_r, 0, 512, b)
                    e_t, e_r = (evf if dt == F32 else evb)[k % 3]
                    k += 1
                    cx.op("act", lambda e: e.copy(out=e_t[0:n, :], in_=self.bank[b][0:n, :]), reads=[self.bres[b]], writes=[e_r])
                    if nm is not None:
                        cx.dma("sp", d[nm].ap()[r0:r0 + n, dc:dc + 512], e_t[0:n, :], e_r, reads=[e_r], writes=[r[nm]])
                    else:
                        cx.dma("sp", d["KS"].ap()[r0:r0 + n, :], e_t[0:n, 0:256], e_r, reads=[e_r], writes=[r["KS"]])
                        cx.dma("sp", d["VS"].ap()[r0:r0 + n, :], e_t[0:n, 256:512], e_r, reads=[e_r], writes=[r["VS"]])
            feat_dst = [("QMT", 0), ("KMT", 0), ("QST", 0), ("QST", 512), ("KSTD", 0)]
            for jb in range(5):
                wb, wb_r = self.load_w(bufs, d["wfeat"].ap()[:, jb * 512:(jb + 1) * 512], r["wfeat"], KC, 512)
                nm, dr0 = feat_dst[jb]
                for jj in range(4):
                    for (c0, n) in self.sub_tiles(NTOK):
                        b = self.nb()
                        self.mm_feat(hT, hT_r, KC, c0, n, wb, wb_r, jj * 128, 128, b)
                        e_t, e_r = evb[k % 3]
                        k += 1
                        if nm == "QMT":
                            cx.op("act", lambda e: e.mul(out=e_t[:, 0:n], in_=self.bank[b][:, 0:n], mul=128 ** -0.5),
                                  reads=[self.bres[b]], writes=[e_r])
                        else:
                            cx.op("act", lambda e: e.copy(out=e_t[:, 0:n], in_=self.bank[b][:, 0:n]), reads=[self.bres[b]], writes=[e_r])
                        rr = dr0 + jj * 128
                        cx.dma("sp", d[nm].ap()[rr:rr + 128, c0:c0 + n], e_t[:, 0:n], e_r, reads=[e_r], writes=[r[nm]])
            wif_s, wif_sr = self._sb(es, tag + "wifs", [128, KC, 8], F32)
            wif, wif_r = self._sb(es, tag + "wif", [128, KC, 8], BF16)
            cx.dma("sp", wif_s[:], d["wif"].ap().rearrange("(k p) c -> p k c", p=128), wif_sr, reads=[r["wif"]], writes=[wif_sr])
            self.cast(wif[:], wif_s[:], [wif_sr], [wif_r])
            ifr, ifr_r = self._sb(es, tag + "ifr", [4, 2, NTOK], F32)
            for g in range(2):
                for (c0, n) in self.sub_tiles(NTOK):
                    b = self.nb()
                    self.mm_feat(hT, hT_r, KC, c0, n, wif, wif_r, g * 4, 4, b)
                    cx.op("act", lambda e: e.copy(out=ifr[:, g, c0:c0 + n], in_=self.bank[b][0:4, 0:n]),
                          reads=[self.bres[b]], writes=[ifr_r])
            cx.dma("sp", d["IF"].ap().rearrange("g h t -> h g t"), ifr[:], ifr_r, reads=[ifr_r], writes=[r["IF"]])
        cx.barrier()

    def mlstm_chunks(self):
        ch = [(c * 64, 64, None) for c in range(SEQ // 64)]
        ch += [(SEQ + b * DSEQ, DSEQ, b) for b in range(NB)]
        return ch

    def mlstm_scal_phase(self):
        import contextlib
        cx = self.cx
        d, r = self.dram, self.dres
        tag = "ms"
        with contextlib.ExitStack() as es:
            def T(nm, w=NTOK):
                return self._sb(es, tag + nm, [4, w], F32)
            ifr, ifr_r = self._sb(es, tag + "ifr", [4, 2, NTOK], F32)
            cx.dma("sp", ifr[:], d["IF"].ap().rearrange("g h t -> h g t"), ifr_r, reads=[r["IF"]], writes=[ifr_r])
            bi, bi_r = T("bi", 2)
            cx.dma("sp", bi[:, 0:1], d["mlstm_b_i"].ap().rearrange("o h -> h o"), bi_r, reads=[r["mlstm_b_i"]], writes=[bi_r])
            cx.dma("sp", bi[:, 1:2], d["mlstm_b_f"].ap().rearrange("o h -> h o"), bi_r, reads=[r["mlstm_b_f"]], writes=[bi_r])
            one, one_r = T("one")
            cx.op("pool", lambda e: e.memset(one[:], 1.0), writes=[one_r])
            ninf, ninf_r = T("ninf", 64)
            cx.op("pool", lambda e: e.memset(ninf[:], -1e30), writes=[ninf_r])
            li, li_r = T("li")
            x, x_r = T("x")
            ax, ax_r = T("ax")
            lf, lf_r = T("lf")
            bb, bb_r = T("b")
            a, a_r = T("a")
            mu, mu_r = T("mu")
            ws, ws_r = T("ws")
            M, M_r = T("M", SEQ // 64 + 1)
            Ms, Ms_r = T("Ms", 2 * NB)
            cx.op("dve", lambda e: e.tensor_scalar(out=li[:], in0=ifr[:, 0, :], scalar1=bi[:, 0:1], scalar2=None, op0=ALU.add),
                  reads=[ifr_r, bi_r], writes=[li_r])
            cx.op("dve", lambda e: e.tensor_scalar(out=x[:], in0=ifr[:, 1, :], scalar1=bi[:, 1:2], scalar2=None, op0=ALU.add),
                  reads=[ifr_r, bi_r], writes=[x_r])
            cx.op("act", lambda e: e.activation(out=ax[:], in_=x[:], func=AF.Abs), reads=[x_r], writes=[ax_r])
            cx.op("act", lambda e: e.activation(out=ax[:], in_=ax[:], func=AF.Exp, scale=-1.0), reads=[ax_r], writes=[ax_r])
            cx.op("act", lambda e: e.activation(out=ax[:], in_=ax[:], func=AF.Ln, bias=one[:, 0:1]), reads=[ax_r, one_r], writes=[ax_r])
            cx.op("dve", lambda e: e.tensor_scalar_min(out=lf[:], in0=x[:], scalar1=0.0), reads=[x_r], writes=[lf_r])
            cx.op("dve", lambda e: e.tensor_tensor(out=lf[:], in0=lf[:], in1=ax[:], op=ALU.subtract), reads=[lf_r, ax_r], writes=[lf_r])
            for (t0, L, sb) in self.mlstm_chunks():
                cx.op("dve", lambda e: e.tensor_tensor_scan(out=bb[:, t0:t0 + L], data0=one[:, t0:t0 + L], data1=lf[:, t0:t0 + L],
                                                            initial=0.0, op0=ALU.mult, op1=ALU.add),
                      reads=[one_r, lf_r], writes=[bb_r])
            cx.op("dve", lambda e: e.tensor_tensor(out=a[:], in0=li[:], in1=bb[:], op=ALU.subtract), reads=[li_r, bb_r], writes=[a_r])
            cx.op("pool", lambda e: e.memset(M[:], 0.0), writes=[M_r])
            cx.dma("sp", Ms[:, 0:NB], d["mm"].ap().rearrange("b h -> h b"), Ms_r, reads=[r["mm"]], writes=[Ms_r])
            for ci, (t0, L, sb) in enumerate(self.mlstm_chunks()):
                m_in = M[:, ci:ci + 1] if sb is None else Ms[:, sb:sb + 1]
                m_out = M[:, ci + 1:ci + 2] if sb is None else Ms[:, NB + sb:NB + sb + 1]
                mr = M_r if sb is None else Ms_r
                cx.op("dve", lambda e: e.tensor_tensor_scan(out=mu[:, t0:t0 + L], data0=ninf[:, 0:L], data1=a[:, t0:t0 + L],
                                                            initial=m_in, op0=ALU.max, op1=ALU.max),
                      reads=[ninf_r, a_r, mr], writes=[mu_r])
                cx.op("act", lambda e: e.activation(out=ws[:, t0:t0 + L], in_=mu[:, t0:t0 + L], func=AF.Exp, scale=-1.0, bias=m_in),
                      reads=[mu_r, mr], writes=[ws_r])
                cx.op("dve", lambda e: e.tensor_tensor(out=m_out, in0=bb[:, t0 + L - 1:t0 + L], in1=mu[:, t0 + L - 1:t0 + L], op=ALU.add),
                      reads=[bb_r, mu_r], writes=[mr])
            cx.op("dve", lambda e: e.tensor_tensor(out=bb[:], in0=bb[:], in1=mu[:], op=ALU.add), reads=[bb_r, mu_r], writes=[bb_r])
            cx.op("dve", lambda e: e.tensor_scalar_mul(out=mu[:], in0=mu[:], scalar1=-1.0), reads=[mu_r], writes=[mu_r])
            cx.dma("sp", d["NMU"].ap(), mu[:], mu_r, reads=[mu_r], writes=[r["NMU"]])
            cx.dma("sp", d["WS"].ap(), ws[:], ws_r, reads=[ws_r], writes=[r["WS"]])
            with self.nc.allow_non_contiguous_dma(reason="tiny transposed gate-scalar scratch"):
                cx.dma("sp", d["SCT"].ap()[:, 0:4].rearrange("t h -> h t"), a[:], a_r, reads=[a_r], writes=[r["SCT"]])
                cx.dma("sp", d["SCT"].ap()[:, 4:8].rearrange("t h -> h t"), bb[:], bb_r, reads=[bb_r], writes=[r["SCT"]])
                cx.dma("sp", d["mm_p"].ap().rearrange("(h o) -> h o", o=1), M[:, SEQ // 64:SEQ // 64 + 1], M_r, reads=[M_r], writes=[r["mm_p"]])
                cx.dma("sp", d["mm_s"].ap().rearrange("b h -> h b"), Ms[:, NB:2 * NB], Ms_r, reads=[Ms_r], writes=[r["mm_s"]])
        cx.barrier()

    def mlstm_phase(self):
        import contextlib
        cx = self.cx
        d, r = self.dram, self.dres
        tag = "ml"
        with contextlib.ExitStack() as es:
            S = lambda nm, shape, dt: self._sb(es, tag + nm, shape, dt)
            NBUF = 2
            qT = [S(f"qT{i}", [128, 4, 64], BF16) for i in range(NBUF)]
            kT = [S(f"kT{i}", [128, 4, 64], BF16) for i in range(NBUF)]
            kt = [S(f"kt{i}", [64, 512], BF16) for i in range(NBUF)]
            vv = [S(f"v{i}", [64, 4, 256], BF16) for i in range(NBUF)]
            nmu = [S(f"nmu{i}", [64, 4, 64], F32) for i in range(NBUF)]
            wsb = [S(f"wsb{i}", [128, 4, 64], F32) for i in range(NBUF)]
            col = [S(f"col{i}", [64, 8], F32) for i in range(NBUF)]
            WT = [S(f"WT{i}", [64, 4, 64], F32) for i in range(2)]
            ST = [S(f"ST{i}", [64, 4, 64], BF16) for i in range(2)]
            qs = [S(f"qs{i}", [128, 4, 64], BF16) for i in range(2)]
            vw = [S(f"vw{i}", [64, 4, 256], BF16) for i in range(2)]
            wlb = [S(f"wlb{i}", [64, 4], BF16) for i in range(2)]
            hm = [S(f"hm{i}", [64, 4, 256], F32) for i in range(2)]
            sm = [S(f"sm{i}", [64, 4, 4], F32) for i in range(2)]
            C, C_r = S("C", [128, 4, 256], F32)
            Cb, Cb_r = S("Cb", [128, 4, 256], BF16)
            nn, nn_r = S("n", [128, 4], F32)
            nb_, nb_r = S("nb", [128, 4], BF16)
            ones, ones_r = S("ones", [64, 1], BF16)
            cx.op("pool", lambda e: e.memset(ones[:], 1.0), writes=[ones_r])
            cx.op("pool", lambda e: e.memset(C[:], 0.0), writes=[C_r])
            cx.op("pool", lambda e: e.memset(nn[:], 0.0), writes=[nn_r])
            cx.op("pool", lambda e: e.memset(Cb[:], 0.0), writes=[Cb_r])
            cx.op("pool", lambda e: e.memset(nb_[:], 0.0), writes=[nb_r])
            chunks = self.mlstm_chunks()
            for ci, (t0, L, sb) in enumerate(chunks):
                i = ci % NBUF
                j = ci % 2
                (qT_t, qT_r), (kT_t, kT_r), (kt_t, kt_r), (v_t, v_r) = qT[i], kT[i], kt[i], vv[i]
                (nmu_t, nmu_r), (wsb_t, wsb_r), (col_t, col_r) = nmu[i], wsb[i], col[i]
                (WT_t, WT_r), (ST_t, ST_r), (qs_t, qs_r), (vw_t, vw_r) = WT[j], ST[j], qs[j], vw[j]
                (wlb_t, wlb_r), (hm_t, hm_r), (sm_t, sm_r) = wlb[j], hm[j], sm[j]
                cx.dma("sp", qT_t[:, :, 0:L], d["QMT"].ap()[:, t0:t0 + L].rearrange("(h p) t -> p h t", p=128), qT_r, reads=[r["QMT"]], writes=[qT_r])
                cx.dma("sp", kT_t[:, :, 0:L], d["KMT"].ap()[:, t0:t0 + L].rearrange("(h p) t -> p h t", p=128), kT_r, reads=[r["KMT"]], writes=[kT_r])
                cx.dma("sp", kt_t[0:L, :], d["KM"].ap()[t0:t0 + L, :], kt_r, reads=[r["KM"]], writes=[kt_r])
                cx.dma("sp", v_t[0:L], d["VM"].ap()[t0:t0 + L, :].rearrange("t (h v) -> t h v", h=4), v_r, reads=[r["VM"]], writes=[v_r])
                cx.dma("sp", nmu_t[0:L, :, 0:L], d["NMU"].ap()[:, t0:t0 + L].partition_broadcast(L), nmu_r, reads=[r["NMU"]], writes=[nmu_r])
                cx.dma("sp", wsb_t[:, :, 0:L], d["WS"].ap()[:, t0:t0 + L].partition_broadcast(128), wsb_r, reads=[r["WS"]], writes=[wsb_r])
                cx.dma("sp", col_t[0:L, :], d["SCT"].ap()[t0:t0 + L, :], col_r, reads=[r["SCT"]], writes=[col_r])
                if sb is not None:
                    cx.dma("sp", C[:], d["mC"].ap()[sb].rearrange("h k v -> k h v"), C_r, reads=[r["mC"]], writes=[C_r])
                    cx.dma("sp", nn[:], d["mn"].ap()[sb].rearrange("h k -> k h"), nn_r, reads=[r["mn"]], writes=[nn_r])
                    self.cast(Cb[:], C[:], [C_r], [Cb_r], eng="act")
                    self.cast(nb_[:], nn[:], [nn_r], [nb_r], eng="dve")
                for h in range(4):
                    cx.op("act", lambda e: e.activation(out=WT_t[0:L, h, 0:L], in_=nmu_t[0:L, h, 0:L], func=AF.Exp, bias=col_t[0:L, h:h + 1]),
                          reads=[nmu_r, col_r], writes=[WT_r])
                cx.op("pool", lambda e: e.affine_select(out=WT_t[0:L, :, 0:L], in_=WT_t[0:L, :, 0:L], pattern=[[0, 4], [1, L]],
                                                       compare_op=ALU.is_ge, fill=0.0, base=0, channel_multiplier=-1),
                      reads=[WT_r], writes=[WT_r])
                bq = self.nb()
                kq = self.bank[bq][0:L, 0:256].rearrange("s (h t) -> s h t", h=4)
                cx.group("pe", [lambda e, h=h: e.matmul(kq[:, h, 0:L], lhsT=kT_t[:, h, 0:L], rhs=qT_t[:, h, 0:L], start=True, stop=True)
                                for h in range(4)], reads=[kT_r, qT_r], writes=[self.bres[bq]])
                cx.op("dve", lambda e: e.tensor_tensor(out=ST_t[0:L, :, 0:L], in0=kq[:, :, 0:L], in1=WT_t[0:L, :, 0:L], op=ALU.mult),
                      reads=[self.bres[bq], WT_r], writes=[ST_r])
                cx.op("dve", lambda e: e.tensor_tensor(out=qs_t[:, :, 0:L], in0=qT_t[:, :, 0:L], in1=wsb_t[:, :, 0:L], op=ALU.mult),
                      reads=[qT_r, wsb_r], writes=[qs_r])
                bn = [self.nb(), self.nb()]
                bd = self.nb()
                fns = []
                for h in range(4):
                    o = self.bank[bn[h // 2]][0:L, (h % 2) * 256:(h % 2 + 1) * 256]
                    fns.append(lambda e, h=h, o=o: e.matmul(o, lhsT=ST_t[0:L, h, 0:L], rhs=v_t[0:L, h, :], start=True, stop=False))
                    fns.append(lambda e, h=h, o=o: e.matmul(o, lhsT=qs_t[:, h, 0:L], rhs=Cb[:, h, :], start=False, stop=True))
                cx.group("pe", fns, reads=[ST_r, v_r, qs_r, Cb_r], writes=[self.bres[bn[0]], self.bres[bn[1]]])
                fns = []
                for h in range(4):
                    o = self.bank[bd][0:L, h:h + 1]
                    fns.append(lambda e, h=h, o=o: e.matmul(o, lhsT=ST_t[0:L, h, 0:L], rhs=ones[0:L, :], start=True, stop=False))
                    fns.append(lambda e, h=h, o=o: e.matmul(o, lhsT=qs_t[:, h, 0:L], rhs=nb_[:, h:h + 1], start=False, stop=True))
                cx.group("pe", fns, reads=[ST_r, ones_r, qs_r, nb_r], writes=[self.bres[bd]])
                cx.op("act", lambda e: e.activation(out=sm_t[0:L, :, 0], in_=self.bank[bd][0:L, 0:4], func=AF.Abs),
                      reads=[self.bres[bd]], writes=[sm_r])
                cx.op("act", lambda e: e.activation(out=sm_t[0:L, :, 1], in_=col_t[0:L, 4:8], func=AF.Exp, scale=-1.0),
                      reads=[col_r], writes=[sm_r])
                cx.op("dve", lambda e: e.tensor_tensor(out=sm_t[0:L, :, 2], in0=sm_t[0:L, :, 0], in1=sm_t[0:L, :, 1], op=ALU.max),
                      reads=[sm_r], writes=[sm_r])
                cx.op("dve", lambda e: e.reciprocal(out=sm_t[0:L, :, 3], in_=sm_t[0:L, :, 2]), reads=[sm_r], writes=[sm_r])
                for h in range(4):
                    o = self.bank[bn[h // 2]][0:L, (h % 2) * 256:(h % 2 + 1) * 256]
                    cx.op("act", lambda e: e.activation(out=hm_t[0:L, h, :], in_=o, func=AF.Copy, scale=sm_t[0:L, h, 3:4]),
                          reads=[self.bres[bn[h // 2]], sm_r], writes=[hm_r])
                cx.dma("sp", d["HM"].ap()[t0:t0 + L, :], hm_t[0:L].rearrange("t h v -> t (h v)"), hm_r, reads=[hm_r], writes=[r["HM"]])
                for h in range(4):
                    cx.op("act", lambda e: e.activation(out=vw_t[0:L, h, :], in_=v_t[0:L, h, :], func=AF.Copy, scale=WT_t[0:L, h, L - 1:L]),
                          reads=[v_r, WT_r], writes=[vw_r])
                cx.op("dve", lambda e: e.tensor_copy(out=wlb_t[0:L, :], in_=WT_t[0:L, :, L - 1]), reads=[WT_r], writes=[wlb_r])
                bu = [self.nb(), self.nb()]
                bnu = self.nb()
                fns = []
                for h in range(4):
                    o = self.bank[bu[h // 2]][:, (h % 2) * 256:(h % 2 + 1) * 256]
                    fns.append(lambda e, h=h, o=o: e.matmul(o, lhsT=kt_t[0:L, h * 128:(h + 1) * 128], rhs=vw_t[0:L, h, :], start=True, stop=True))
                for h in range(4):
                    fns.append(lambda e, h=h: e.matmul(self.bank[bnu][:, h:h + 1], lhsT=kt_t[0:L, h * 128:(h + 1) * 128], rhs=wlb_t[0:L, h:h + 1],
                                                       start=True, stop=True))
                cx.group("pe", fns, reads=[kt_r, vw_r, wlb_r], writes=[self.bres[bu[0]], self.bres[bu[1]], self.bres[bnu]])
                for h in range(4):
                    o = self.bank[bu[h // 2]][:, (h % 2) * 256:(h % 2 + 1) * 256]
                    cx.op("dve", lambda e: e.scalar_tensor_tensor(out=C[:, h, :], in0=C[:, h, :], scalar=wsb_t[:, h, L - 1:L], in1=o,
                                                                  op0=ALU.mult, op1=ALU.add),
                          reads=[C_r, wsb_r, self.bres[bu[h // 2]]], writes=[C_r])
                cx.op("dve", lambda e: e.tensor_tensor(out=nn[:], in0=nn[:], in1=wsb_t[:, :, L - 1], op=ALU.mult), reads=[nn_r, wsb_r], writes=[nn_r])
                cx.op("dve", lambda e: e.tensor_tensor(out=nn[:], in0=nn[:], in1=self.bank[bnu][:, 0:4], op=ALU.add),
                      reads=[nn_r, self.bres[bnu]], writes=[nn_r])
                self.cast(Cb[:], C[:], [C_r], [Cb_r], eng="act")
                self.cast(nb_[:], nn[:], [nn_r], [nb_r], eng="dve")
                last_prompt = (sb is None and ci == SEQ // 64 - 1)
                if last_prompt:
                    cx.dma("sp", d["mC_p"].ap().rearrange("h k v -> k h v"), C[:], C_r, reads=[C_r], writes=[r["mC_p"]])
                    cx.dma("sp", d["mn_p"].ap().rearrange("h k -> k h"), nn[:], nn_r, reads=[nn_r], writes=[r["mn_p"]])
                if sb is not None:
                    cx.dma("sp", d["mC_s"].ap()[sb].rearrange("h k v -> k h v"), C[:], C_r, reads=[C_r], writes=[r["mC_s"]])
                    cx.dma("sp", d["mn_s"].ap()[sb].rearrange("h k -> k h"), nn[:], nn_r, reads=[nn_r], writes=[r["mn_s"]])
        cx.barrier()

    def swa_phase(self):
        import contextlib
        cx = self.cx
        d, r = self.dram, self.dres
        tag = "sw"
        NEG = -30000.0
        with contextlib.ExitStack() as es:
            S = lambda nm, shape, dt: self._sb(es, tag + nm, shape, dt)
            dist, dist_r = S("dist", [128, 256], F32)
            Mall, Mall_r = S("Mall", [128, 16, 256], F32)
            disti, disti_r = S("disti", [128, 256], mybir.dt.int32)
            cx.op("pool", lambda e: e.iota(disti[:], pattern=[[-1, 256]], base=128, channel_multiplier=1), writes=[disti_r])
            cx.op("pool", lambda e: e.tensor_copy(out=dist[:], in_=disti[:]), reads=[disti_r], writes=[dist_r])
            for h in range(16):
                slope = 2.0 ** (-8.0 * (h + 1) / 16)
                cx.op("dve", lambda e: e.tensor_scalar_mul(out=Mall[:, h, :], in0=dist[:], scalar1=-slope), reads=[dist_r], writes=[Mall_r])
            cx.op("pool", lambda e: e.affine_select(out=Mall[:], in_=Mall[:], pattern=[[0, 16], [-1, 256]], compare_op=ALU.is_ge,
                                                   fill=NEG, base=128, channel_multiplier=1), reads=[Mall_r], writes=[Mall_r])
            cx.op("pool", lambda e: e.affine_select(out=Mall[:], in_=Mall[:], pattern=[[0, 16], [1, 256]], compare_op=ALU.is_ge,
                                                   fill=NEG, base=-1, channel_multiplier=-1), reads=[Mall_r], writes=[Mall_r])
            snk, snk_r = S("snk", [128, 16], F32)
            cx.dma("sp", snk[:], d["swa_sinks"].ap()[0].partition_broadcast(128), snk_r, reads=[r["swa_sinks"]], writes=[snk_r])
            qT = [S(f"qT{i}", [128, 8, 128], BF16) for i in range(2)]
            kTd = [S(f"kTd{i}", [128, 4, 128], BF16) for i in range(2)]
            kT2 = [S(f"kT2{i}", [128, 4, 128], BF16) for i in range(2)]
            vf = [S(f"vf{i}", [128, 256], F32) for i in range(2)]
            vb = [S(f"vb{i}", [128, 256], BF16) for i in range(2)]
            v2f = [S(f"v2f{i}", [128, 256], F32) for i in range(2)]
            v2b = [S(f"v2b{i}", [128, 256], BF16) for i in range(2)]
            ckf = [S(f"ckf{i}", [128, 256], F32) for i in range(2)]
            ckd = [S(f"ckd{i}", [128, 4, 2, 64], BF16) for i in range(2)]
            sc = [S(f"sc{i}", [128, 256], F32) for i in range(2)]
            pb = [S(f"p{i}", [128, 256], BF16) for i in range(2)]
            pT = [S(f"pT{i}", [128, 2, 128], BF16) for i in range(2)]
            sm = [S(f"sm{i}", [128, 8], F32) for i in range(2)]
            hs = [S(f"hs{i}", [128, 1024], F32) for i in range(2)]
            u = 0

            def attend(nq, qT_t, qT_r, k1, k1_r, k2, k2_r, v1, v1_r, v2, v2_r, hs_t, hs_r):
                nonlocal u
                lo = 0 if k1 is not None else 128
                hi = 128 + nq

                def s1(h, uu):
                    c, chunk, half = h // 4, h // 2, h % 2
                    ps = slice(half * 64, (half + 1) * 64)
                    sc_t, sc_r = sc[uu % 2]
                    p_t, p_r = pb[uu % 2]
                    sm_t, sm_r = sm[uu % 2]
                    b = self.nb()
                    fns = []
                    rds = [qT_r, k2_r]
                    if k1 is not None:
                        fns.append(lambda e: e.matmul(self.bank[b][0:nq, 0:128], lhsT=qT_t[ps, chunk, 0:nq], rhs=k1[ps, c, :], start=True, stop=True))
                        rds.append(k1_r)
                    fns.append(lambda e: e.matmul(self.bank[b][0:nq, 128:hi], lhsT=qT_t[ps, chunk, 0:nq], rhs=k2[ps, c, 0:nq], start=True, stop=True))
                    cx.group("pe", fns, reads=rds, writes=[self.bres[b]])
                    cx.op("dve", lambda e: e.scalar_tensor_tensor(out=sc_t[0:nq, lo:hi], in0=self.bank[b][0:nq, lo:hi], scalar=0.125,
                                                                  in1=Mall[0:nq, h, lo:hi], op0=ALU.mult, op1=ALU.add),
                          reads=[self.bres[b], Mall_r], writes=[sc_r])
                    cx.op("dve", lambda e: e.reduce_max(out=sm_t[0:nq, 0:1], in_=sc_t[0:nq, lo:hi], axis=AX.X), reads=[sc_r], writes=[sm_r])
                    cx.op("dve", lambda e: e.tensor_scalar(out=sm_t[0:nq, 1:2], in0=sm_t[0:nq, 0:1], scalar1=snk[0:nq, h:h + 1], scalar2=-1.0,
                                                           op0=ALU.max, op1=ALU.mult), reads=[sm_r, snk_r], writes=[sm_r])
                    cx.op("act", lambda e: e.activation(out=p_t[0:nq, lo:hi], in_=sc_t[0:nq, lo:hi], func=AF.Exp, bias=sm_t[0:nq, 1:2],
                                                        accum_out=sm_t[0:nq, 2:3]), reads=[sc_r, sm_r], writes=[p_r, sm_r])
                    cx.op("act", lambda e: e.activation(out=sm_t[0:nq, 3:4], in_=sm_t[0:nq, 1:2], func=AF.Exp, bias=snk[0:nq, h:h + 1]),
                          reads=[sm_r, snk_r], writes=[sm_r])

                def s2(h, uu):
                    c = h // 4
                    p_t, p_r = pb[uu % 2]
                    pT_t, pT_r = pT[uu % 2]
                    sm_t, sm_r = sm[uu % 2]
                    cx.op("dve", lambda e: e.tensor_tensor(out=sm_t[0:nq, 4:5], in0=sm_t[0:nq, 2:3], in1=sm_t[0:nq, 3:4], op=ALU.add),
                          reads=[sm_r], writes=[sm_r])
                    cx.op("dve", lambda e: e.reciprocal(out=sm_t[0:nq, 5:6], in_=sm_t[0:nq, 4:5]), reads=[sm_r], writes=[sm_r])
                    bt = self.nb()
                    ptb = self.bank[bt][:].bitcast(BF16)
                    fns = []
                    if k1 is not None:
                        fns.append(lambda e: e.transpose(out=ptb[:, 0:nq], in_=p_t[0:nq, 0:128], identity=self.ident_b[0:nq, 0:nq]))
                    fns.append(lambda e: e.transpose(out=ptb[0:nq, 128:128 + nq], in_=p_t[0:nq, 128:hi], identity=self.ident_b[0:nq, 0:nq]))
                    cx.group("pe", fns, reads=[p_r, self.ident_b_r], writes=[self.bres[bt]])
                    if k1 is not None:
                        cx.op("act", lambda e: e.copy(out=pT_t[:, 0, 0:nq], in_=ptb[:, 0:nq]), reads=[self.bres[bt]], writes=[pT_r])
                    cx.op("dve", lambda e: e.tensor_copy(out=pT_t[0:nq, 1, 0:nq], in_=ptb[0:nq, 128:128 + nq]), reads=[self.bres[bt]], writes=[pT_r])
                    bo = self.nb()
                    fns = []
                    rds = [pT_r, v2_r]
                    if k1 is not None:
                        fns.append(lambda e: e.matmul(self.bank[bo][0:nq, 0:64], lhsT=pT_t[:, 0, 0:nq], rhs=v1[:, c * 64:(c + 1) * 64], start=True, stop=False))
                        rds.append(v1_r)
                    fns.append(lambda e: e.matmul(self.bank[bo][0:nq, 0:64], lhsT=pT_t[0:nq, 1, 0:nq], rhs=v2[0:nq, c * 64:(c + 1) * 64],
                                                  start=(k1 is None), stop=True))
                    cx.group("pe", fns, reads=rds, writes=[self.bres[bo]])
                    cx.op("act", lambda e: e.activation(out=hs_t[0:nq, h * 64:(h + 1) * 64], in_=self.bank[bo][0:nq, 0:64], func=AF.Copy,
                                                        scale=sm_t[0:nq, 5:6]), reads=[self.bres[bo], sm_r], writes=[hs_r])

                u0 = u
                u += 16
                s1(0, u0)
                for h in range(16):
                    if h + 1 < 16:
                        s1(h + 1, u0 + h + 1)
                    s2(h, u0 + h)

            prev = None
            for j in range(SEQ // 128):
                t0 = j * 128
                i = j % 2
                (q_t, q_r), (k_t, k_r), (vf_t, vf_r), (vb_t, vb_r), (hs_t, hs_r) = qT[i], kTd[i], vf[i], vb[i], hs[i]
                cx.dma("sp", q_t[:], d["QST"].ap()[:, t0:t0 + 128].rearrange("(c p) t -> p c t", p=128), q_r, reads=[r["QST"]], writes=[q_r])
                cx.dma("sp", k_t[:], d["KSTD"].ap()[:, t0:t0 + 128].rearrange("(c p) t -> p c t", p=128), k_r, reads=[r["KSTD"]], writes=[k_r])
                cx.dma("sp", vf_t[:], d["VS"].ap()[t0:t0 + 128, :], vf_r, reads=[r["VS"]], writes=[vf_r])
                self.cast(vb_t[:], vf_t[:], [vf_r], [vb_r])
                if prev is None:
                    attend(128, q_t, q_r, None, None, k_t, k_r, None, None, vb_t, vb_r, hs_t, hs_r)
                else:
                    attend(128, q_t, q_r, prev[0], prev[1], k_t, k_r, prev[2], prev[3], vb_t, vb_r, hs_t, hs_r)
                prev = (k_t, k_r, vb_t, vb_r)
                cx.dma("sp", d["HS"].ap()[t0:t0 + 128, :], hs_t[:], hs_r, reads=[hs_r], writes=[r["HS"]])
            for sb in range(NB):
                t0 = SEQ + sb * DSEQ
                i = sb % 2
                (q_t, q_r), (k1_t, k1_r), (k2_t, k2_r) = qT[i], kTd[i], kT2[i]
                (vf_t, vf_r), (vb_t, vb_r), (v2f_t, v2f_r), (v2b_t, v2b_r) = vf[i], vb[i], v2f[i], v2b[i]
                (ckf_t, ckf_r), (ckd_t, ckd_r), (hs_t, hs_r) = ckf[i], ckd[i], hs[i]
                cx.dma("sp", q_t[:, :, 0:DSEQ], d["QST"].ap()[:, t0:t0 + DSEQ].rearrange("(c p) t -> p c t", p=128), q_r, reads=[r["QST"]], writes=[q_r])
                cx.dma("sp", k2_t[:, :, 0:DSEQ], d["KSTD"].ap()[:, t0:t0 + DSEQ].rearrange("(c p) t -> p c t", p=128), k2_r, reads=[r["KSTD"]], writes=[k2_r])
                cx.dma("sp", ckf_t[:], d["swk"].ap()[sb], ckf_r, reads=[r["swk"]], writes=[ckf_r])
                cx.dma("sp", vf_t[:], d["swv"].ap()[sb], vf_r, reads=[r["swv"]], writes=[vf_r])
                cx.dma("sp", v2f_t[0:DSEQ, :], d["VS"].ap()[t0:t0 + DSEQ, :], v2f_r, reads=[r["VS"]], writes=[v2f_r])
                self.cast(vb_t[:], vf_t[:], [vf_r], [vb_r])
                self.cast(v2b_t[0:DSEQ, :], v2f_t[0:DSEQ, :], [v2f_r], [v2b_r])
                ck3 = ckf_t[:].rearrange("s (c d) -> s c d", c=4)
                for dup in range(2):
                    self.cast(ckd_t[:, :, dup, :], ck3, [ckf_r], [ckd_r])
                self.transpose_blk(ckd_t[:].rearrange("s c u d -> s (c u d)"), ckd_r, 128, 512, lambda c: k1_t[:, c, :], k1_r, evac="dve")
                attend(DSEQ, q_t, q_r, k1_t, k1_r, k2_t, k2_r, vb_t, vb_r, v2b_t, v2b_r, hs_t, hs_r)
                cx.dma("sp", d["HS"].ap()[t0:t0 + DSEQ, :], hs_t[0:DSEQ, :], hs_r, reads=[hs_r], writes=[r["HS"]])
            dm = cx.res("swcopy")
            cx.dma("sp", d["swk_p"].ap(), d["KS"].ap()[SEQ - 128:SEQ, :], dm, reads=[r["KS"]], writes=[r["swk_p"]])
            cx.dma("sp", d["swv_p"].ap(), d["VS"].ap()[SEQ - 128:SEQ, :], dm, reads=[r["VS"]], writes=[r["swv_p"]])
            cx.dma("sp", d["swk_s"].ap()[:, 0:128 - DSEQ, :], d["swk"].ap()[:, DSEQ:128, :], dm, reads=[r["swk"]], writes=[r["swk_s"]])
            cx.dma("sp", d["swv_s"].ap()[:, 0:128 - DSEQ, :], d["swv"].ap()[:, DSEQ:128, :], dm, reads=[r["swv"]], writes=[r["swv_s"]])
            cx.dma("sp", d["swk_s"].ap()[:, 128 - DSEQ:128, :], d["KS"].ap()[SEQ:NTOK, :].rearrange("(b t) f -> b t f", t=DSEQ), dm,
                   reads=[r["KS"]], writes=[r["swk_s"]])
            cx.dma("sp", d["swv_s"].ap()[:, 128 - DSEQ:128, :], d["VS"].ap()[SEQ:NTOK, :].rearrange("(b t) f -> b t f", t=DSEQ), dm,
                   reads=[r["VS"]], writes=[r["swv_s"]])
        cx.barrier()

    def even_out_phase(self, src_fn, dst_fn):
        import contextlib
        cx = self.cx
        d, r = self.dram, self.dres
        tag = "eo"
        blocks = blocks_of(0, SEQ) + [(SEQ, NS)]
        with contextlib.ExitStack() as es:
            S = lambda nm, shape, dt: self._sb(es, tag + nm, shape, dt)
            oT, oT_r = S("oT", [128, KC, NTOK], BF16)
            with contextlib.ExitStack() as es2:
                S2 = lambda nm, shape, dt: self._sb(es2, tag + nm, shape, dt)
                hg, hg_r = S2("hg", [128, 1024], F32)
                cx.dma("sp", hg[:], d["mlstm_head_gain"].ap().rearrange("o h v -> (o h v)").partition_broadcast(128), hg_r,
                       reads=[r["mlstm_head_gain"]], writes=[hg_r])
                hm = [S2(f"hm{i}", [128, 1024], F32) for i in range(2)]
                om = [S2(f"om{i}", [128, 1024], F32) for i in range(2)]
                hsf = [S2(f"hs{i}", [128, 1024], F32) for i in range(2)]
                sq, sq_r = S2("sq", [128, 256], BF16)
                cat = [S2(f"cat{i}", [128, D], BF16) for i in range(2)]
                st = [S2(f"st{i}", [128, 12], F32) for i in range(2)]
                for bi, (r0, n) in enumerate(blocks):
                    i = bi % 2
                    (hm_t, hm_r), (om_t, om_r), (hs_t, hs_r), (cat_t, cat_r), (st_t, st_r) = hm[i], om[i], hsf[i], cat[i], st[i]
                    cx.dma("sp", hm_t[0:n, :], d["HM"].ap()[r0:r0 + n, :], hm_r, reads=[r["HM"]], writes=[hm_r])
                    cx.dma("sp", om_t[0:n, :], d["OM"].ap()[r0:r0 + n, :], om_r, reads=[r["OM"]], writes=[om_r])
                    cx.dma("sp", hs_t[0:n, :], d["HS"].ap()[r0:r0 + n, :], hs_r, reads=[r["HS"]], writes=[hs_r])
                    for h in range(4):
                        cx.op("act", lambda e: e.activation(out=sq[0:n, :], in_=hm_t[0:n, h * 256:(h + 1) * 256], func=AF.Square,
                                                            accum_out=st_t[0:n, h:h + 1]), reads=[hm_r], writes=[sq_r, st_r])
                    cx.op("act", lambda e: e.activation(out=st_t[0:n, 4:8], in_=st_t[0:n, 0:4], func=AF.Sqrt, scale=1.0 / 256,
                                                        bias=self.eps_t[0:n, 0:1]), reads=[st_r, self.eps_r], writes=[st_r])
                    cx.op("dve", lambda e: e.reciprocal(out=st_t[0:n, 8:12], in_=st_t[0:n, 4:8]), reads=[st_r], writes=[st_r])
                    cx.op("act", lambda e: e.activation(out=om_t[0:n, :], in_=om_t[0:n, :], func=AF.Sigmoid), reads=[om_r], writes=[om_r])
                    for h in range(4):
                        cx.op("dve", lambda e: e.scalar_tensor_tensor(out=hm_t[0:n, h * 256:(h + 1) * 256], in0=hm_t[0:n, h * 256:(h + 1) * 256],
                                                                      scalar=st_t[0:n, 8 + h:9 + h], in1=hg[0:n, h * 256:(h + 1) * 256],
                                                                      op0=ALU.mult, op1=ALU.mult), reads=[hm_r, st_r, hg_r], writes=[hm_r])
                    cx.op("dve", lambda e: e.tensor_tensor(out=cat_t[0:n, 0:1024], in0=hm_t[0:n, :], in1=om_t[0:n, :], op=ALU.mult),
                          reads=[hm_r, om_r], writes=[cat_r])
                    self.cast(cat_t[0:n, 1024:2048], hs_t[0:n, :], [hs_r], [cat_r], eng="act")
                    self.transpose_into(cat_t, cat_r, n, oT, oT_r, r0)
                cx.barrier()
            self.out_proj(es, tag, oT, oT_r, KC, d["even_w_out"].ap()[0], r["even_w_out"],
                          [(r0, n, r0) for (r0, n) in blocks], src_fn, dst_fn)
        cx.barrier()

    def odd_in_phase(self, src_fn):
        import contextlib
        cx = self.cx
        d, r = self.dram, self.dres
        tag = "oi"
        blocks = blocks_of(0, SEQ) + [(SEQ, NS)]
        with contextlib.ExitStack() as es:
            hT, hT_r = self.norm_all(es, tag, src_fn, d["norm_gain"].ap()[1, 1], blocks)
            bufs = self.wbufs(es, tag, KC, 512)
            evf = [self._sb(es, f"{tag}evf{i}", [128, 512], F32) for i in range(3)]
            evb = [self._sb(es, f"{tag}evb{i}", [128, 512], BF16) for i in range(3)]
            k = 0
            for cb in range(4):
                wb, wb_r = self.load_w(bufs, d["odd_w_in"].ap()[0][:, cb * 512:(cb + 1) * 512], r["odd_w_in"], KC, 512)
                for (r0, n) in blocks:
                    b = self.nb()
                    self.mm_tok(hT, hT_r, KC, r0, n, wb, wb_r, 0, 512, b)
                    e_t, e_r = evf[k % 3]
                    k += 1
                    cx.op("act", lambda e: e.copy(out=e_t[0:n, :], in_=self.bank[b][0:n, :]), reads=[self.bres[b]], writes=[e_r])
                    cx.dma("sp", d["U"].ap()[r0:r0 + n, cb * 512:(cb + 1) * 512], e_t[0:n, :], e_r, reads=[e_r], writes=[r["U"]])
                for jj in range(4):
                    for (c0, n) in self.sub_tiles(NTOK):
                        b = self.nb()
                        self.mm_feat(hT, hT_r, KC, c0, n, wb, wb_r, jj * 128, 128, b)
                        e_t, e_r = evb[k % 3]
                        k += 1
                        cx.op("act", lambda e: e.copy(out=e_t[:, 0:n], in_=self.bank[b][:, 0:n]), reads=[self.bres[b]], writes=[e_r])
                        rr = cb * 512 + jj * 128
                        cx.dma("sp", d["UT"].ap()[rr:rr + 128, c0:c0 + n], e_t[:, 0:n], e_r, reads=[e_r], writes=[r["UT"]])
        cx.barrier()

    def _range_reduce(self, out, out_r, in_, in_r, tmpi, tmpi_r, tmpf, tmpf_r, shift):
        cx = self.cx
        TWO_PI = 2.0 * np.pi
        cx.op("dve", lambda e: e.tensor_scalar(out=tmpf, in0=in_, scalar1=shift, scalar2=1.0 / TWO_PI, op0=ALU.add, op1=ALU.mult),
              reads=[in_r], writes=[tmpf_r])
        cx.op("dve", lambda e: e.tensor_copy(out=tmpi, in_=tmpf), reads=[tmpf_r], writes=[tmpi_r])
        cx.op("dve", lambda e: e.tensor_copy(out=tmpf, in_=tmpi), reads=[tmpi_r], writes=[tmpf_r])
        cx.op("dve", lambda e: e.tensor_scalar(out=tmpf, in0=tmpf, scalar1=-TWO_PI, scalar2=shift, op0=ALU.mult, op1=ALU.add),
              reads=[tmpf_r], writes=[tmpf_r])
        cx.op("dve", lambda e: e.tensor_tensor(out=out, in0=tmpf, in1=in_, op=ALU.add), reads=[tmpf_r, in_r], writes=[out_r])
        cx.op("dve", lambda e: e.tensor_scalar(out=tmpf, in0=out, scalar1=np.pi, scalar2=-TWO_PI, op0=ALU.is_gt, op1=ALU.mult),
              reads=[out_r], writes=[tmpf_r])
        cx.op("dve", lambda e: e.tensor_tensor(out=out, in0=out, in1=tmpf, op=ALU.add), reads=[out_r, tmpf_r], writes=[out_r])
        cx.op("dve", lambda e: e.tensor_scalar(out=tmpf, in0=out, scalar1=-np.pi, scalar2=TWO_PI, op0=ALU.is_lt, op1=ALU.mult),
              reads=[out_r], writes=[tmpf_r])
        cx.op("dve", lambda e: e.tensor_tensor(out=out, in0=out, in1=tmpf, op=ALU.add), reads=[out_r, tmpf_r], writes=[out_r])

    def s5_setup_phase(self):
        import contextlib
        cx = self.cx
        d, r = self.dram, self.dres
        tag = "s5s"
        I32 = mybir.dt.int32
        with contextlib.ExitStack() as es:
            S = lambda nm, shape, dt: self._sb(es, tag + nm, shape, dt)
            are, are_r = S("are", [128, 64], F32)
            aim, aim_r = S("aim", [128, 64], F32)
            ldt, ldt_r = S("ldt", [128, 1], F32)
            cx.dma("sp", are[:], d["s5_a_re"].ap()[0], are_r, reads=[r["s5_a_re"]], writes=[are_r])
            cx.dma("sp", aim[:], d["s5_a_im"].ap()[0], aim_r, reads=[r["s5_a_im"]], writes=[aim_r])
            cx.dma("sp", ldt[:], d["s5_log_dt"].ap().rearrange("o g -> g o"), ldt_r, reads=[r["s5_log_dt"]], writes=[ldt_r])
            cx.op("act", lambda e: e.activation(out=ldt[:], in_=ldt[:], func=AF.Exp), reads=[ldt_r], writes=[ldt_r])
            lm, lm_r = S("lm", [128, 64], F32)
            th, th_r = S("th", [128, 64], F32)
            cx.op("dve", lambda e: e.tensor_scalar_mul(out=lm[:], in0=are[:], scalar1=ldt[:, 0:1]), reads=[are_r, ldt_r], writes=[lm_r])
            cx.op("dve", lambda e: e.tensor_scalar_mul(out=th[:], in0=aim[:], scalar1=ldt[:, 0:1]), reads=[aim_r, ldt_r], writes=[th_r])
            cx.dma("sp", d["LM"].ap().rearrange("(g p) -> g p", p=64), lm[:], lm_r, reads=[lm_r], writes=[r["LM"]])
            cx.dma("sp", d["TH"].ap().rearrange("(g p) -> g p", p=64), th[:], th_r, reads=[th_r], writes=[r["TH"]])
            ti, ti_r = S("ti", [128, 64], I32)
            tf, tf_r = S("tf", [128, 64], F32)
            sn, sn_r = S("sn", [128, 64], F32)
            cs, cs_r = S("cs", [128, 64], F32)
            mg, mg_r = S("mg", [128, 64], F32)
            self._range_reduce(sn[:], sn_r, th[:], th_r, ti[:], ti_r, tf[:], tf_r, 0.0)
            self._range_reduce(cs[:], cs_r, th[:], th_r, ti[:], ti_r, tf[:], tf_r, np.pi / 2)
            cx.op("act", lambda e: e.activation(out=sn[:], in_=sn[:], func=AF.Sin), reads=[sn_r], writes=[sn_r])
            cx.op("act", lambda e: e.activation(out=cs[:], in_=cs[:], func=AF.Sin), reads=[cs_r], writes=[cs_r])
            cx.op("act", lambda e: e.activation(out=mg[:], in_=lm[:], func=AF.Exp), reads=[lm_r], writes=[mg_r])
            abr, abr_r = S("abr", [128, 64], F32)
            abi, abi_r = S("abi", [128, 64], F32)
            cx.op("dve", lambda e: e.tensor_tensor(out=abr[:], in0=mg[:], in1=cs[:], op=ALU.mult), reads=[mg_r, cs_r], writes=[abr_r])
            cx.op("dve", lambda e: e.tensor_scalar_add(out=abr[:], in0=abr[:], scalar1=-1.0), reads=[abr_r], writes=[abr_r])
            cx.op("dve", lambda e: e.tensor_tensor(out=abi[:], in0=mg[:], in1=sn[:], op=ALU.mult), reads=[mg_r, sn_r], writes=[abi_r])
            den, den_r = S("den", [128, 64], F32)
            t1, t1_r = S("t1", [128, 64], F32)
            cx.op("dve", lambda e: e.tensor_tensor(out=den[:], in0=are[:], in1=are[:], op=ALU.mult), reads=[are_r], writes=[den_r])
            cx.op("dve", lambda e: e.tensor_tensor(out=t1[:], in0=aim[:], in1=aim[:], op=ALU.mult), reads=[aim_r], writes=[t1_r])
            cx.op("dve", lambda e: e.tensor_tensor(out=den[:], in0=den[:], in1=t1[:], op=ALU.add), reads=[den_r, t1_r], writes=[den_r])
            cx.op("dve", lambda e: e.reciprocal(out=den[:], in_=den[:]), reads=[den_r], writes=[den_r])
            fre, fre_r = S("fre", [128, 64], F32)
            fim, fim_r = S("fim", [128, 64], F32)
            cx.op("dve", lambda e: e.tensor_tensor(out=fre[:], in0=abr[:], in1=are[:], op=ALU.mult), reads=[abr_r, are_r], writes=[fre_r])
            cx.op("dve", lambda e: e.tensor_tensor(out=t1[:], in0=abi[:], in1=aim[:], op=ALU.mult), reads=[abi_r, aim_r], writes=[t1_r])
            cx.op("dve", lambda e: e.tensor_tensor(out=fre[:], in0=fre[:], in1=t1[:], op=ALU.add), reads=[fre_r, t1_r], writes=[fre_r])
            cx.op("dve", lambda e: e.tensor_tensor(out=fre[:], in0=fre[:], in1=den[:], op=ALU.mult), reads=[fre_r, den_r], writes=[fre_r])
            cx.op("dve", lambda e: e.tensor_tensor(out=fim[:], in0=abi[:], in1=are[:], op=ALU.mult), reads=[abi_r, are_r], writes=[fim_r])
            cx.op("dve", lambda e: e.tensor_tensor(out=t1[:], in0=abr[:], in1=aim[:], op=ALU.mult), reads=[abr_r, aim_r], writes=[t1_r])
            cx.op("dve", lambda e: e.tensor_tensor(out=fim[:], in0=fim[:], in1=t1[:], op=ALU.subtract), reads=[fim_r, t1_r], writes=[fim_r])
            cx.op("dve", lambda e: e.tensor_tensor(out=fim[:], in0=fim[:], in1=den[:], op=ALU.mult), reads=[fim_r, den_r], writes=[fim_r])
            bre, bre_r = S("bre", [128, 64, 16], F32)
            bim, bim_r = S("bim", [128, 64, 16], F32)
            cx.dma("sp", bre[:], d["s5_b_re"].ap()[0], bre_r, reads=[r["s5_b_re"]], writes=[bre_r])
            cx.dma("sp", bim[:], d["s5_b_im"].ap()[0], bim_r, reads=[r["s5_b_im"]], writes=[bim_r])
            bbo, bbo_r = S("bbo", [128, 2, 64, 16], F32)
            t3, t3_r = S("t3", [128, 64, 16], F32)
            frb = fre[:].unsqueeze(2).to_broadcast([128, 64, 16])
            fib = fim[:].unsqueeze(2).to_broadcast([128, 64, 16])
            cx.op("dve", lambda e: e.tensor_tensor(out=bbo[:, 0], in0=bre[:], in1=frb, op=ALU.mult), reads=[bre_r, fre_r], writes=[bbo_r])
            cx.op("dve", lambda e: e.tensor_tensor(out=t3[:], in0=bim[:], in1=fib, op=ALU.mult), reads=[bim_r, fim_r], writes=[t3_r])
            cx.op("dve", lambda e: e.tensor_tensor(out=bbo[:, 0], in0=bbo[:, 0], in1=t3[:], op=ALU.subtract), reads=[bbo_r, t3_r], writes=[bbo_r])
            cx.op("dve", lambda e: e.tensor_tensor(out=bbo[:, 1], in0=bim[:], in1=frb, op=ALU.mult), reads=[bim_r, fre_r], writes=[bbo_r])
            cx.op("dve", lambda e: e.tensor_tensor(out=t3[:], in0=bre[:], in1=fib, op=ALU.mult), reads=[bre_r, fim_r], writes=[t3_r])
            cx.op("dve", lambda e: e.tensor_tensor(out=bbo[:, 1], in0=bbo[:, 1], in1=t3[:], op=ALU.add), reads=[bbo_r, t3_r], writes=[bbo_r])
            cx.dma("sp", d["BBS"].ap(), bbo[:], bbo_r, reads=[bbo_r], writes=[r["BBS"]])
            cx.barrier()
        with contextlib.ExitStack() as es:
            S = lambda nm, shape, dt: self._sb(es, tag + nm, shape, dt)
            bbt, bbt_r = S("bbt", [128, 128, 16], F32)
            cx.dma("sp", bbt[:], d["BBS"].ap().rearrange("g r p c -> (r p) g c"), bbt_r, reads=[r["BBS"]], writes=[bbt_r])
            mask, mask_r = S("mask", [128, 8], F32)
            cx.op("pool", lambda e: e.memset(mask[:], 1.0), writes=[mask_r])
            cx.op("pool", lambda e: e.affine_select(out=mask[:], in_=mask[:], pattern=[[-16, 8]], compare_op=ALU.is_ge, fill=0.0, base=0,
                                                   channel_multiplier=1), reads=[mask_r], writes=[mask_r])
            cx.op("pool", lambda e: e.affine_select(out=mask[:], in_=mask[:], pattern=[[16, 8]], compare_op=ALU.is_ge, fill=0.0, base=15,
                                                   channel_multiplier=-1), reads=[mask_r], writes=[mask_r])
            dj = [S(f"dj{i}", [128, 128], F32) for i in range(2)]
            bd = [S(f"bd{i}", [128, 8, 128], BF16) for i in range(2)]
            for j in range(16):
                b = self.nb()
                dj_t, dj_r = dj[j % 2]
                bd_t, bd_r = bd[j % 2]
                cx.group("pe", [lambda e: e.transpose(out=self.bank[b][:, 0:128], in_=bbt[:, 8 * j:8 * j + 8, :].rearrange("q g c -> q (g c)"),
                                                      identity=self.ident_f[:])], reads=[bbt_r, self.ident_f_r], writes=[self.bres[b]])
                cx.op("act", lambda e: e.copy(out=dj_t[:], in_=self.bank[b][:, 0:128]), reads=[self.bres[b]], writes=[dj_r])
                cx.op("dve", lambda e: e.tensor_tensor(out=bd_t[:], in0=dj_t[:].unsqueeze(1).to_broadcast([128, 8, 128]),
                                                       in1=mask[:].unsqueeze(2).to_broadcast([128, 8, 128]), op=ALU.mult),
                      reads=[dj_r, mask_r], writes=[bd_r])
                cx.dma("sp", d["BBD"].ap()[:, j], bd_t[:], bd_r, reads=[bd_r], writes=[r["BBD"]])
            cn, cn_r = S("cn", [128, 16, 2, 64], F32)
            cx.dma("sp", cn[:, :, 0, :], d["s5_c_re"].ap()[0].rearrange("(j g) c p -> (g c) j p", g=8), cn_r, reads=[r["s5_c_re"]], writes=[cn_r])
            cx.dma("sp", cn[:, :, 1, :], d["s5_c_im"].ap()[0].rearrange("(j g) c p -> (g c) j p", g=8), cn_r, reads=[r["s5_c_im"]], writes=[cn_r])
            cm, cm_r = S("cm", [128, 16, 128], BF16)
            for j in range(16):
                b = self.nb()
                cx.group("pe", [lambda e: e.transpose(out=self.bank[b][:, 0:128], in_=cn[:, j].rearrange("q r p -> q (r p)"),
                                                      identity=self.ident_f[:])], reads=[cn_r, self.ident_f_r], writes=[self.bres[b]])
                cx.op("act", lambda e: e.copy(out=cm[0:64, j, :], in_=self.bank[b][0:64, 0:128]), reads=[self.bres[b]], writes=[cm_r])
                cx.op("act", lambda e: e.mul(out=cm[64:128, j, :], in_=self.bank[b][64:128, 0:128], mul=-1.0), reads=[self.bres[b]], writes=[cm_r])
            cx.dma("sp", d["CM"].ap(), cm[:], cm_r, reads=[cm_r], writes=[r["CM"]])
            cx.barrier()
        with contextlib.ExitStack() as es:
            S = lambda nm, shape, dt: self._sb(es, tag + nm, shape, dt)
            W = 2048
            kcol, kcol_r = S("kcol", [128, 2], F32)
            kci, kci_r = S("kci", [128, 1], I32)
            cx.op("pool", lambda e: e.iota(kci[:], pattern=[[0, 1]], base=1, channel_multiplier=1), writes=[kci_r])
            cx.op("pool", lambda e: e.tensor_copy(out=kcol[:, 0:1], in_=kci[:]), reads=[kci_r], writes=[kcol_r])
            cx.op("dve", lambda e: e.tensor_scalar_mul(out=kcol[:, 1:2], in0=kcol[:, 0:1], scalar1=-1.0), reads=[kcol_r], writes=[kcol_r])
            thb, thb_r = S("thb", [128, W], F32)
            lmb, lmb_r = S("lmb", [128, W], F32)
            ang, ang_r = S("ang", [128, W], F32)
            ti, ti_r = S("tti", [128, W], I32)
            tf, tf_r = S("ttf", [128, W], F32)
            sn, sn_r = S("tsn", [128, W], F32)
            cs, cs_r = S("tcs", [128, W], F32)
            ep, ep_r = S("tep", [128, W], F32)
            en, en_r = S("ten", [128, W], F32)
            o4 = [S(f"to{i}", [128, W], F32) for i in range(4)]
            for q in range(8192 // W):
                cs_ = slice(q * W, (q + 1) * W)
                cx.dma("sp", thb[:], d["TH"].ap()[cs_].partition_broadcast(128), thb_r, reads=[r["TH"]], writes=[thb_r])
                cx.dma("sp", lmb[:], d["LM"].ap()[cs_].partition_broadcast(128), lmb_r, reads=[r["LM"]], writes=[lmb_r])
                cx.op("dve", lambda e: e.tensor_scalar_mul(out=ang[:], in0=thb[:], scalar1=kcol[:, 0:1]), reads=[thb_r, kcol_r], writes=[ang_r])
                self._range_reduce(sn[:], sn_r, ang[:], ang_r, ti[:], ti_r, tf[:], tf_r, 0.0)
                self._range_reduce(cs[:], cs_r, ang[:], ang_r, ti[:], ti_r, tf[:], tf_r, np.pi / 2)
                cx.op("act", lambda e: e.activation(out=sn[:], in_=sn[:], func=AF.Sin), reads=[sn_r], writes=[sn_r])
                cx.op("act", lambda e: e.activation(out=cs[:], in_=cs[:], func=AF.Sin), reads=[cs_r], writes=[cs_r])
                cx.op("act", lambda e: e.activation(out=ep[:], in_=lmb[:], func=AF.Exp, scale=kcol[:, 0:1]), reads=[lmb_r, kcol_r], writes=[ep_r])
                cx.op("act", lambda e: e.activation(out=en[:], in_=lmb[:], func=AF.Exp, scale=kcol[:, 1:2]), reads=[lmb_r, kcol_r], writes=[en_r])
                (o0, o0r), (o1, o1r), (o2, o2r), (o3, o3r) = o4
                cx.op("dve", lambda e: e.tensor_tensor(out=o0[:], in0=en[:], in1=cs[:], op=ALU.mult), reads=[en_r, cs_r], writes=[o0r])
                cx.op("dve", lambda e: e.scalar_tensor_tensor(out=o1[:], in0=en[:], scalar=-1.0, in1=sn[:], op0=ALU.mult, op1=ALU.mult),
                      reads=[en_r, sn_r], writes=[o1r])
                cx.op("dve", lambda e: e.tensor_tensor(out=o2[:], in0=ep[:], in1=cs[:], op=ALU.mult), reads=[ep_r, cs_r], writes=[o2r])
                cx.op("dve", lambda e: e.tensor_tensor(out=o3[:], in0=ep[:], in1=sn[:], op=ALU.mult), reads=[ep_r, sn_r], writes=[o3r])
                for k_, (o, orr) in enumerate(o4):
                    cx.dma("sp", d["TAB"].ap()[k_, :, cs_], o[:], orr, reads=[orr], writes=[r["TAB"]])
        cx.barrier()

    def s5_scan_phase(self):
        import contextlib
        cx = self.cx
        d, r = self.dram, self.dres
        tag = "s5"
        with contextlib.ExitStack() as es:
            S = lambda nm, shape, dt: self._sb(es, tag + nm, shape, dt)
            tri, tri_r = S("tri", [128, 128], BF16)
            sel, sel_r = S("sel", [128, 128], BF16)
            tri64, tri64_r = S("tri64", [64, 16, 4], BF16)
            selb, selb_r = S("selb", [16, 16, 4], BF16)
            for t_, tr_ in ((tri, tri_r), (sel, sel_r), (tri64, tri64_r), (selb, selb_r)):
                cx.op("pool", lambda e: e.memset(t_[:], 1.0), writes=[tr_])
            AS = lambda t_, tr_, pat, base, cm: cx.op("pool", lambda e: e.affine_select(
                out=t_[:], in_=t_[:], pattern=pat, compare_op=ALU.is_ge, fill=0.0, base=base, channel_multiplier=cm), reads=[tr_], writes=[tr_])
            AS(tri, tri_r, [[1, 128]], 0, -1)
            AS(sel, sel_r, [[0, 128]], -127, 1)
            AS(tri64, tri64_r, [[4, 16], [1, 4]], 0, -1)
            AS(tri64, tri64_r, [[-4, 16], [0, 4]], 0, 1)
            AS(tri64, tri64_r, [[4, 16], [0, 4]], 3, -1)
            AS(selb, selb_r, [[-1, 16], [0, 4]], 0, 1)
            AS(selb, selb_r, [[1, 16], [0, 4]], 0, -1)
            tri64f = tri64[:].rearrange("s b t -> s (b t)")
            selbf = selb[:].rearrange("s b t -> s (b t)")
            sets = []
            for i in range(2):
                sets.append(dict(tab=S(f"tab{i}", [128, 4, 1024], F32), tabs=S(f"tabs{i}", [64, 4, 1024], F32),
                                 uT=S(f"uT{i}", [128, 2, NTOK], BF16), bbd=S(f"bbd{i}", [128, 2, 8, 128], BF16),
                                 cm=S(f"cm{i}", [128, 2, 128], BF16), x0f=S(f"x0f{i}", [16, 16, 2, 64], F32),
                                 x0b=S(f"x0b{i}", [16, 2048], BF16)))
            BH = [S(f"BH{i}", [128, 2048], BF16) for i in range(2)]
            tt = [S(f"tt{i}", [128, 1024], F32) for i in range(4)]
            tq = [S(f"tq{i}", [128, 1024], F32) for i in range(4)]
            XF = [S(f"XF{i}", [128, 2048], F32) for i in range(2)]
            Xbs = [S(f"Xb{i}", [128, 2048], BF16) for i in range(2)]
            XT = [S(f"XT{i}", [128, 16, 128], BF16) for i in range(2)]
            ys = [S(f"ys{i}", [128, 256], F32) for i in range(2)]
            ps = self.psum
            v4 = lambda ap, L: ap.rearrange("t (g r p) -> t g r p", g=16, r=2)
            v3 = lambda ap, L: ap.rearrange("t (g p) -> t g p", g=16)
            chunks = [(c * 128, 128, False) for c in range(SEQ // 128)] + [(SEQ, NS, True)]
            it = 0
            def load_eighth(e8):
                st = sets[e8 % 2]
                (tab, tab_r), (tabs, tabs_r), (uT, uT_r), (bbd, bbd_r) = st["tab"], st["tabs"], st["uT"], st["bbd"]
                (cm, cm_r), (x0f, x0f_r), (x0b, x0b_r) = st["cm"], st["x0f"], st["x0b"]
                gc = slice(e8 * 1024, (e8 + 1) * 1024)
                cx.dma("sp", tab[:], d["TAB"].ap()[:, :, gc].rearrange("k t c -> t k c"), tab_r, reads=[r["TAB"]], writes=[tab_r])
                for b_ in range(NB):
                    cx.dma("sp", tabs[4 * b_:4 * b_ + 4], d["TAB"].ap()[:, 0:4, gc].rearrange("k t c -> t k c"), tabs_r, reads=[r["TAB"]], writes=[tabs_r])
                cx.dma("sp", uT[:], d["UT"].ap()[e8 * 256:(e8 + 1) * 256, :].rearrange("(j q) t -> q j t", q=128), uT_r, reads=[r["UT"]], writes=[uT_r])
                cx.dma("sp", bbd[:], d["BBD"].ap()[:, 2 * e8:2 * e8 + 2], bbd_r, reads=[r["BBD"]], writes=[bbd_r])
                cx.dma("sp", cm[:], d["CM"].ap()[:, 2 * e8:2 * e8 + 2, :], cm_r, reads=[r["CM"]], writes=[cm_r])
                cx.dma("sp", x0f[:, :, 0, :], d["s5re"].ap()[:, e8 * 16:(e8 + 1) * 16, :], x0f_r, reads=[r["s5re"]], writes=[x0f_r])
                cx.dma("sp", x0f[:, :, 1, :], d["s5im"].ap()[:, e8 * 16:(e8 + 1) * 16, :], x0f_r, reads=[r["s5im"]], writes=[x0f_r])
                self.cast(x0b[:], x0f[:].rearrange("b g r p -> b (g r p)"), [x0f_r], [x0b_r])

            load_eighth(0)
            for e8 in range(8):
                st = sets[e8 % 2]
                (tab, tab_r), (tabs, tabs_r), (uT, uT_r), (bbd, bbd_r) = st["tab"], st["tabs"], st["uT"], st["bbd"]
                (cm, cm_r), (x0f, x0f_r), (x0b, x0b_r) = st["cm"], st["x0f"], st["x0b"]
                if e8 + 1 < 8:
                    load_eighth(e8 + 1)
                def stage_a(ci, t0, L, smp, it):
                    (BH_t, BH_r) = BH[it % 2]
                    tb_t, tb_r = (tabs, tabs_r) if smp else (tab, tab_r)
                    fns = []
                    for q in range(4):
                        fns.append(lambda e, q=q: e.matmul(ps[0:L, q * 512:(q + 1) * 512], lhsT=uT[:, q // 2, t0:t0 + L],
                                                           rhs=bbd[:, q // 2, (q % 2) * 4:(q % 2) * 4 + 4, :], start=True, stop=True))
                    cx.group("pe", fns, reads=[uT_r, bbd_r], writes=self.bres[0:4])
                    buv = v4(ps[0:L, 0:2048], L)
                    bur, bui = buv[:, :, 0, :], buv[:, :, 1, :]
                    Pr, Pi = v3(tb_t[0:L, 0, :], L), v3(tb_t[0:L, 1, :], L)
                    T = [(v3(t_[0:L, :], L), tr_) for (t_, tr_) in tt]
                    BHv = v4(BH_t[0:L, :], L)
                    rb = self.bres[0:4] + [tb_r]
                    cx.op("dve", lambda e: e.tensor_tensor(out=T[0][0], in0=bur, in1=Pr, op=ALU.mult), reads=rb, writes=[T[0][1]])
                    cx.op("dve", lambda e: e.tensor_tensor(out=T[1][0], in0=bui, in1=Pi, op=ALU.mult), reads=rb, writes=[T[1][1]])
                    cx.op("dve", lambda e: e.tensor_tensor(out=T[2][0], in0=bur, in1=Pi, op=ALU.mult), reads=rb, writes=[T[2][1]])
                    cx.op("dve", lambda e: e.tensor_tensor(out=T[3][0], in0=bui, in1=Pr, op=ALU.mult), reads=rb, writes=[T[3][1]])
                    cx.op("pool", lambda e: e.tensor_tensor(out=BHv[:, :, 0, :], in0=T[0][0], in1=T[1][0], op=ALU.subtract),
                          reads=[T[0][1], T[1][1]], writes=[BH_r])
                    cx.op("pool", lambda e: e.tensor_tensor(out=BHv[:, :, 1, :], in0=T[2][0], in1=T[3][0], op=ALU.add),
                          reads=[T[2][1], T[3][1]], writes=[BH_r])

                def stage_b1(ci, t0, L, smp, it):
                    Xb, Xb_r = Xbs[it % 2]
                    Xp, Xp_r = Xbs[(it + 1) % 2]
                    (BH_t, BH_r), (XF_t, XF_r), (XT_t, XT_r), (ys_t, ys_r) = BH[it % 2], XF[it % 2], XT[it % 2], ys[it % 2]
                    tb_t, tb_r = (tabs, tabs_r) if smp else (tab, tab_r)
                    fns = []
                    rds = [BH_r]
                    carry = smp or ci > 0
                    for q in range(4):
                        o = ps[0:L, 2048 + q * 512:2048 + (q + 1) * 512]
                        if smp:
                            fns.append(lambda e, q=q, o=o: e.matmul(o, lhsT=tri64f, rhs=BH_t[0:L, q * 512:(q + 1) * 512], start=True, stop=False))
                            fns.append(lambda e, q=q, o=o: e.matmul(o, lhsT=selbf, rhs=x0b[:, q * 512:(q + 1) * 512], start=False, stop=True))
                        else:
                            fns.append(lambda e, q=q, o=o: e.matmul(o, lhsT=tri[:], rhs=BH_t[:, q * 512:(q + 1) * 512], start=True, stop=not carry))
                            if carry:
                                fns.append(lambda e, q=q, o=o: e.matmul(o, lhsT=sel[:], rhs=Xp[:, q * 512:(q + 1) * 512], start=False, stop=True))
                    if smp:
                        rds += [tri64_r, selb_r, x0b_r]
                    else:
                        rds += [tri_r, sel_r] + ([Xp_r] if carry else [])
                    cx.group("pe", fns, reads=rds, writes=self.bres[4:8])
                    xh = v4(ps[0:L, 2048:4096], L)
                    xr, xi = xh[:, :, 0, :], xh[:, :, 1, :]
                    Qr, Qi = v3(tb_t[0:L, 2, :], L), v3(tb_t[0:L, 3, :], L)
                    Q = [(v3(t_[0:L, :], L), tr_) for (t_, tr_) in tq]
                    XFv = v4(XF_t[0:L, :], L)
                    rb = self.bres[4:8] + [tb_r]
                    cx.op("dve", lambda e: e.tensor_tensor(out=Q[0][0], in0=xr, in1=Qr, op=ALU.mult), reads=rb, writes=[Q[0][1]])
                    cx.op("dve", lambda e: e.tensor_tensor(out=Q[1][0], in0=xi, in1=Qi, op=ALU.mult), reads=rb, writes=[Q[1][1]])
                    Xbv = v4(Xb[0:L, :], L)
                    need_f32 = smp or ci == SEQ // 128 - 1
                    cx.op("pool", lambda e: e.tensor_tensor(out=Xbv[:, :, 0, :], in0=Q[0][0], in1=Q[1][0], op=ALU.subtract),
                          reads=[Q[0][1], Q[1][1]], writes=[Xb_r])
                    cx.op("dve", lambda e: e.tensor_tensor(out=Q[2][0], in0=xr, in1=Qi, op=ALU.mult), reads=rb, writes=[Q[2][1]])
                    cx.op("dve", lambda e: e.tensor_tensor(out=Q[3][0], in0=xi, in1=Qr, op=ALU.mult), reads=rb, writes=[Q[3][1]])
                    cx.op("dve", lambda e: e.tensor_tensor(out=Xbv[:, :, 1, :], in0=Q[2][0], in1=Q[3][0], op=ALU.add),
                          reads=[Q[2][1], Q[3][1]], writes=[Xb_r])
                    if need_f32:
                        cx.op("pool", lambda e: e.tensor_tensor(out=XFv[:, :, 0, :], in0=Q[0][0], in1=Q[1][0], op=ALU.subtract),
                              reads=[Q[0][1], Q[1][1]], writes=[XF_r])
                        cx.op("pool", lambda e: e.tensor_tensor(out=XFv[:, :, 1, :], in0=Q[2][0], in1=Q[3][0], op=ALU.add),
                              reads=[Q[2][1], Q[3][1]], writes=[XF_r])
                    gs = slice(e8 * 16, (e8 + 1) * 16)
                    if (not smp) and ci == SEQ // 128 - 1:
                        cx.dma("sp", d["s5re_p"].ap()[gs, :].rearrange("(o g) p -> o g p", o=1), XFv[L - 1:L, :, 0, :], XF_r, reads=[XF_r], writes=[r["s5re_p"]])
                        cx.dma("sp", d["s5im_p"].ap()[gs, :].rearrange("(o g) p -> o g p", o=1), XFv[L - 1:L, :, 1, :], XF_r, reads=[XF_r], writes=[r["s5im_p"]])
                    if smp:
                        for b_ in range(NB):
                            row = b_ * DSEQ + DSEQ - 1
                            cx.dma("sp", d["s5re_s"].ap()[b_, gs, :].rearrange("(o g) p -> o g p", o=1), XFv[row:row + 1, :, 0, :], XF_r, reads=[XF_r], writes=[r["s5re_s"]])
                            cx.dma("sp", d["s5im_s"].ap()[b_, gs, :].rearrange("(o g) p -> o g p", o=1), XFv[row:row + 1, :, 1, :], XF_r, reads=[XF_r], writes=[r["s5im_s"]])
                def stage_b2(ci, t0, L, smp, it):
                    Xb, Xb_r = Xbs[it % 2]
                    (XT_t, XT_r), (ys_t, ys_r) = XT[it % 2], ys[it % 2]
                    for k0 in (0, 8):
                        bt = self.nb()
                        pb = self.bank[bt][:].bitcast(BF16)
                        cx.group("pe", [lambda e, kc=kc: e.transpose(out=pb[:, (kc - k0) * 128:(kc - k0) * 128 + L],
                                                                     in_=Xb[0:L, kc * 128:(kc + 1) * 128], identity=self.ident_b[0:L, 0:L])
                                        for kc in range(k0, k0 + 8)], reads=[Xb_r, self.ident_b_r], writes=[self.bres[bt]])
                        src = pb.rearrange("p (k c) -> p k c", c=128)[:, :, 0:L]
                        self.cast(XT_t[:, k0:k0 + 8, 0:L], src, [self.bres[bt]], [XT_r], eng="act")
                    by = self.nb()
                    cx.group("pe", [lambda e, g=g: e.matmul(self.bank[by][0:L, g * 16:(g + 1) * 16], lhsT=XT_t[:, g, 0:L],
                                                            rhs=cm[:, g // 8, (g % 8) * 16:(g % 8 + 1) * 16], start=True, stop=True)
                                    for g in range(16)], reads=[XT_r, cm_r], writes=[self.bres[by]])
                    cx.op("act", lambda e: e.copy(out=ys_t[0:L, :], in_=self.bank[by][0:L, 0:256]), reads=[self.bres[by]], writes=[ys_r])
                    cx.dma("sp", d["Y"].ap()[t0:t0 + L, e8 * 256:(e8 + 1) * 256], ys_t[0:L, :], ys_r, reads=[ys_r], writes=[r["Y"]])

                units = [(ci, t0, L, smp, it + ci) for ci, (t0, L, smp) in enumerate(chunks)]
                it += len(chunks)
                nU = len(units)
                stage_a(*units[0])
                stage_a(*units[1])
                stage_b1(*units[0])
                for ui in range(nU):
                    if ui + 2 < nU:
                        stage_a(*units[ui + 2])
                    if ui + 1 < nU:
                        stage_b1(*units[ui + 1])
                    stage_b2(*units[ui])
        cx.barrier()

    def odd_out_phase(self, src_fn, dst_fn):
        import contextlib
        cx = self.cx
        d, r = self.dram, self.dres
        tag = "oo"
        blocks = blocks_of(0, SEQ) + [(SEQ, NS)]
        with contextlib.ExitStack() as es:
            S = lambda nm, shape, dt: self._sb(es, tag + nm, shape, dt)
            yT, yT_r = S("yT", [128, KC, NTOK], BF16)
            with contextlib.ExitStack() as es2:
                S2 = lambda nm, shape, dt: self._sb(es2, tag + nm, shape, dt)
                dsk, dsk_r = S2("dsk", [128, D], F32)
                cx.dma("sp", dsk[:], d["s5_d"].ap()[0].partition_broadcast(128), dsk_r, reads=[r["s5_d"]], writes=[dsk_r])
                yb = [S2(f"y{i}", [128, D], F32) for i in range(2)]
                ub = [S2(f"u{i}", [128, D], F32) for i in range(2)]
                gb = [S2(f"g{i}", [128, D], BF16) for i in range(2)]
                for bi, (r0, n) in enumerate(blocks):
                    (y_t, y_r), (u_t, u_r), (g_t, g_r) = yb[bi % 2], ub[bi % 2], gb[bi % 2]
                    cx.dma("sp", y_t[0:n, :], d["Y"].ap()[r0:r0 + n, :], y_r, reads=[r["Y"]], writes=[y_r])
                    cx.dma("sp", u_t[0:n, :], d["U"].ap()[r0:r0 + n, :], u_r, reads=[r["U"]], writes=[u_r])
                    cx.op("dve", lambda e: e.tensor_tensor(out=u_t[0:n, :], in0=u_t[0:n, :], in1=dsk[0:n, :], op=ALU.mult), reads=[u_r, dsk_r], writes=[u_r])
                    cx.op("dve", lambda e: e.tensor_tensor(out=y_t[0:n, :], in0=y_t[0:n, :], in1=u_t[0:n, :], op=ALU.add), reads=[y_r, u_r], writes=[y_r])
                    GC = 2.0 * (2.0 / np.pi) ** 0.5
                    cx.op("dve", lambda e: e.tensor_tensor(out=u_t[0:n, :], in0=y_t[0:n, :], in1=y_t[0:n, :], op=ALU.mult), reads=[y_r, u_r], writes=[u_r])
                    cx.op("dve", lambda e: e.tensor_scalar(out=u_t[0:n, :], in0=u_t[0:n, :], scalar1=0.044715, scalar2=1.0, op0=ALU.mult, op1=ALU.add),
                          reads=[u_r], writes=[u_r])
                    cx.op("dve", lambda e: e.tensor_tensor(out=u_t[0:n, :], in0=u_t[0:n, :], in1=y_t[0:n, :], op=ALU.mult), reads=[y_r, u_r], writes=[u_r])
                    cx.op("act", lambda e: e.activation(out=u_t[0:n, :], in_=u_t[0:n, :], func=AF.Sigmoid, scale=GC), reads=[u_r], writes=[u_r])
                    cx.op("dve", lambda e: e.tensor_tensor(out=g_t[0:n, :], in0=u_t[0:n, :], in1=y_t[0:n, :], op=ALU.mult), reads=[y_r, u_r], writes=[g_r])
                    self.transpose_into(g_t, g_r, n, yT, yT_r, r0)
                cx.barrier()
            bufs = self.wbufs(es, tag, KC, 512, nst=2, nbf=2)
            sg = [S(f"sg{i}", [128, 512], F32) for i in range(2)]
            xo = [S(f"xo{i}", [128, 512], F32) for i in range(3)]
            xw = [S(f"xw{i}", [128, 512], F32) for i in range(3)]
            W = d["odd_w_out"].ap()[0]
            ev = 0
            for cb in range(4):
                wv, wv_r = self.load_w(bufs, W[:, cb * 512:(cb + 1) * 512], r["odd_w_out"], KC, 512)
                wg, wg_r = self.load_w(bufs, W[:, D + cb * 512:D + (cb + 1) * 512], r["odd_w_out"], KC, 512)
                for (r0, n) in blocks:
                    bv, bg = self.nb(), self.nb()
                    self.mm_tok(yT, yT_r, KC, r0, n, wv, wv_r, 0, 512, bv)
                    self.mm_tok(yT, yT_r, KC, r0, n, wg, wg_r, 0, 512, bg)
                    (sg_t, sg_r), (xo_t, xo_r), (xw_t, xw_r) = sg[ev % 2], xo[ev % 3], xw[ev % 3]
                    ev += 1
                    sap, sr = src_fn(r0, n)
                    dap, dr = dst_fn(r0, n)
                    cx.dma("sp", xo_t[0:n, :], sap[:, cb * 512:(cb + 1) * 512], xo_r, reads=[sr], writes=[xo_r])
                    cx.op("act", lambda e: e.activation(out=sg_t[0:n, :], in_=self.bank[bg][0:n, :], func=AF.Sigmoid), reads=[self.bres[bg]], writes=[sg_r])
                    cx.op("dve", lambda e: e.tensor_tensor(out=sg_t[0:n, :], in0=self.bank[bv][0:n, :], in1=sg_t[0:n, :], op=ALU.mult),
                          reads=[self.bres[bv], sg_r], writes=[sg_r])
                    cx.op("dve", lambda e: e.tensor_tensor(out=xw_t[0:n, :], in0=sg_t[0:n, :], in1=xo_t[0:n, :], op=ALU.add),
                          reads=[sg_r, xo_r], writes=[xw_r])
                    cx.dma("sp", dap[:, cb * 512:(cb + 1) * 512], xw_t[0:n, :], xw_r, reads=[xw_r], writes=[dr])
        cx.barrier()

    def declare_io(self):
        di, do, ds = self.din, self.dout, self.dscr
        di("x_p", [SEQ, D]); di("x_s", [NS, D]); di("mem_p", [256, D])
        di("cmk", [2, NB, 256, 512]); di("cmv", [2, NB, 256, 512])
        di("mC", [NB, 4, 128, 256]); di("mn", [NB, 4, 128]); di("mm", [NB, 4])
        di("swk", [NB, 128, 256]); di("swv", [NB, 128, 256])
        di("s5re", [NB, 128, 64]); di("s5im", [NB, 128, 64])
        di("norm_gain", [2, 4, D]); di("final_gain", [D])
        for l in range(2):
            for i in range(2):
                di(f"wgu{l}{i}", [NF, 128, 2 * KC * 128]); di(f"wd{l}{i}", [DFF, D])
        di("wtok", [D, 3072]); di("wfeat", [D, 2560]); di("wif", [D, 8])
        di("mlstm_b_i", [1, 4]); di("mlstm_b_f", [1, 4]); di("mlstm_head_gain", [1, 4, 256]); di("swa_sinks", [1, 16])
        di("even_w_out", [1, D, D]); di("odd_w_in", [1, D, D])
        di("s5_a_re", [1, 128, 64]); di("s5_a_im", [1, 128, 64]); di("s5_log_dt", [1, 128])
        di("s5_b_re", [1, 128, 64, 16]); di("s5_b_im", [1, 128, 64, 16]); di("s5_c_re", [1, 128, 16, 64]); di("s5_c_im", [1, 128, 16, 64])
        di("s5_d", [1, D]); di("odd_w_out", [1, D, 2 * D])
        di("xwq", [2, D, 512]); di("xwk", [2, D, 512]); di("xwv", [2, D, 512]); di("xwo", [2, 512, D])
        do("y_p", [SEQ, D]); do("y_s", [NS, D]); do("mem_k_p", [2, 256, 512]); do("mem_v_p", [2, 256, 512])
        do("mC_p", [4, 128, 256]); do("mC_s", [NB, 4, 128, 256]); do("mn_p", [4, 128]); do("mn_s", [NB, 4, 128])
        do("mm_p", [4]); do("mm_s", [NB, 4])
        do("swk_p", [128, 256]); do("swk_s", [NB, 128, 256]); do("swv_p", [128, 256]); do("swv_s", [NB, 128, 256])
        do("s5re_p", [128, 64]); do("s5re_s", [NB, 128, 64]); do("s5im_p", [128, 64]); do("s5im_s", [NB, 128, 64])
        ds("XA", [NTOK, D]); ds("XB", [NTOK, D])
        ds("KM", [NTOK, 512], BF16); ds("VM", [NTOK, 1024], BF16); ds("OM", [NTOK, 1024]); ds("KS", [NTOK, 256]); ds("VS", [NTOK, 256])
        ds("QMT", [512, NTOK], BF16); ds("KMT", [512, NTOK], BF16); ds("QST", [1024, NTOK], BF16); ds("KSTD", [512, NTOK], BF16)
        ds("IF", [2, 4, NTOK]); ds("NMU", [4, NTOK]); ds("WS", [4, NTOK]); ds("SCT", [NTOK, 8])
        ds("HM", [NTOK, 1024]); ds("HS", [NTOK, 1024])
        ds("U", [NTOK, D]); ds("UT", [D, NTOK], BF16); ds("LM", [8192]); ds("TH", [8192])
        ds("BBS", [128, 2, 64, 16]); ds("BBD", [128, 16, 8, 128], BF16); ds("CM", [128, 16, 128], BF16)
        ds("TAB", [4, 128, 8192]); ds("Y", [NTOK, D])

    def xio(self, name):
        if name == "in":
            def f(r0, n):
                if r0 < SEQ:
                    return self.dram["x_p"].ap()[r0:r0 + n, :], self.dres["x_p"]
                return self.dram["x_s"].ap()[r0 - SEQ:r0 - SEQ + n, :], self.dres["x_s"]
            return f
        if name == "out":
            def f(r0, n):
                if r0 < SEQ:
                    return self.dram["y_p"].ap()[r0:r0 + n, :], self.dres["y_p"]
                return self.dram["y_s"].ap()[r0 - SEQ:r0 - SEQ + n, :], self.dres["y_s"]
            return f
        return lambda r0, n: (self.dram[name].ap()[r0:r0 + n, :], self.dres[name])

    def build(self, phases=None):
        d = self.dram
        self.declare_io()
        self.setup_consts()
        self.cx.barrier()
        tiles = [blocks_of(0, 1024), blocks_of(1024, SEQ) + [(SEQ, NS)]]
        allb = blocks_of(0, SEQ) + [(SEQ, NS)]
        ng = d["norm_gain"].ap()
        P = phases

        def on(p):
            return P is None or p in P
        if on("memkv"):
            self.memkv_phase()
        if on("ffn00"):
            self.ffn_phase("fa0", d["wgu00"].ap(), d["wd00"].ap(), ng[0, 0], tiles, self.xio("in"), self.xio("XA"))
        if on("even"):
            self.even_in_phase(self.xio("XA"))
            self.mlstm_scal_phase()
            self.mlstm_phase()
            self.swa_phase()
            self.even_out_phase(self.xio("XA"), self.xio("XB"))
        if on("xa0"):
            self.xattn_phase(0, self.xio("XB"), self.xio("XA"))
        if on("ffn01"):
            self.ffn_phase("fb0", d["wgu01"].ap(), d["wd01"].ap(), ng[0, 3], tiles, self.xio("XA"), self.xio("XB"))
        if on("ffn10"):
            self.ffn_phase("fa1", d["wgu10"].ap(), d["wd10"].ap(), ng[1, 0], tiles, self.xio("XB"), self.xio("XA"))
        if on("odd"):
            self.odd_in_phase(self.xio("XA"))
            self.s5_setup_phase()
            self.s5_scan_phase()
            self.odd_out_phase(self.xio("XA"), self.xio("XB"))
        if on("xa1"):
            self.xattn_phase(1, self.xio("XB"), self.xio("XA"))
        if on("ffn11"):
            self.ffn_phase("fb1", d["wgu11"].ap(), d["wd11"].ap(), ng[1, 3], tiles, self.xio("XA"), self.xio("XB"))
        if on("final"):
            self.final_phase(self.xio("XB"), self.xio("out"), d["final_gain"].ap(), allb)
        self.cx.finish("sp", [self.dres[n] for n in self.outputs])
        return self.nc


def _lay_wgu(wg, wu):
    a = np.stack([wg, wu], 0).reshape(2, KC, 128, NF, 128)
    return np.ascontiguousarray(a.transpose(3, 2, 0, 1, 4)).reshape(NF, 128, 2 * KC * 128)


def host_weights(inp):
    f = lambda a: np.ascontiguousarray(np.asarray(a, dtype=np.float32))
    w = {}
    for l in range(2):
        for i in range(2):
            w[f"wgu{l}{i}"] = _lay_wgu(np.asarray(inp["ffn_w_gate"][l, i]), np.asarray(inp["ffn_w_up"][l, i]))
            w[f"wd{l}{i}"] = f(inp["ffn_w_down"][l, i])
    win = np.asarray(inp["even_w_in"][0])
    q_m, k_m, v_m, o_m = win[:, 0:512], win[:, 512:1024], win[:, 1024:2048], win[:, 2048:3072]
    i_m, f_m = win[:, 3072:3076], win[:, 3076:3080]
    q_s, k_s, v_s = win[:, 3080:4104], win[:, 4104:4360], win[:, 4360:4616]
    w["wtok"] = f(np.concatenate([k_m, v_m, o_m, k_s, v_s], 1))
    ksd = np.concatenate([np.concatenate([k_s[:, c * 64:(c + 1) * 64]] * 2, 1) for c in range(4)], 1)
    w["wfeat"] = f(np.concatenate([q_m, k_m, q_s, ksd], 1))
    w["wif"] = f(np.concatenate([i_m, f_m], 1))
    for k in ("norm_gain", "final_gain", "mlstm_b_i", "mlstm_b_f", "mlstm_head_gain", "swa_sinks", "even_w_out", "odd_w_in",
              "s5_a_re", "s5_a_im", "s5_log_dt", "s5_b_re", "s5_b_im", "s5_c_re", "s5_c_im", "s5_d", "odd_w_out"):
        w[k] = f(inp[k])
    w["xwq"], w["xwk"], w["xwv"], w["xwo"] = f(inp["xattn_w_q"]), f(inp["xattn_w_k"]), f(inp["xattn_w_v"]), f(inp["xattn_w_o"])
    return w


def host_core_inputs(inp, c):
    f = lambda a: np.ascontiguousarray(np.asarray(a, dtype=np.float32))
    sb = slice(c * NB, (c + 1) * NB)
    m = {}
    m["x_p"] = f(inp["x_prompt"][c])
    m["x_s"] = f(np.asarray(inp["x_sample"][sb]).reshape(NS, D))
    m["mem_p"] = f(inp["mem_prompt"][c])
    m["cmk"] = f(np.asarray(inp["cache_mem_k"][:, sb]).reshape(2, NB, 256, 512))
    m["cmv"] = f(np.asarray(inp["cache_mem_v"][:, sb]).reshape(2, NB, 256, 512))
    m["mC"] = f(inp["state_mlstm_C"][0, sb])
    m["mn"] = f(inp["state_mlstm_n"][0, sb])
    m["mm"] = f(inp["state_mlstm_m"][0, sb])
    m["swk"] = f(np.asarray(inp["cache_swa_k"][0, sb]).reshape(NB, 128, 256))
    m["swv"] = f(np.asarray(inp["cache_swa_v"][0, sb]).reshape(NB, 128, 256))
    m["s5re"] = f(inp["state_s5_re"][0, sb])
    m["s5im"] = f(inp["state_s5_im"][0, sb])
    return m


def assemble(results, ncores=8):
    R = results
    cat = lambda k: [np.asarray(R[c][k]) for c in range(ncores)]
    y_p = np.stack(cat("y_p"), 0)
    y_s = np.concatenate([a.reshape(NB, DSEQ, D) for a in cat("y_s")], 0)
    mk = np.stack([a.reshape(2, 256, 4, 128) for a in cat("mem_k_p")], 1)
    mv = np.stack([a.reshape(2, 256, 4, 128) for a in cat("mem_v_p")], 1)
    C_p = np.stack(cat("mC_p"), 0)[None]
    C_s = np.concatenate(cat("mC_s"), 0)[None]
    n_p = np.stack(cat("mn_p"), 0)[None]
    n_s = np.concatenate(cat("mn_s"), 0)[None]
    m_p = np.stack(cat("mm_p"), 0)[None]
    m_s = np.concatenate(cat("mm_s"), 0)[None]
    k_p = np.stack([a.reshape(128, 4, 64) for a in cat("swk_p")], 0)[None]
    k_s = np.concatenate([a.reshape(NB, 128, 4, 64) for a in cat("swk_s")], 0)[None]
    v_p = np.stack([a.reshape(128, 4, 64) for a in cat("swv_p")], 0)[None]
    v_s = np.concatenate([a.reshape(NB, 128, 4, 64) for a in cat("swv_s")], 0)[None]
    sr_p = np.stack(cat("s5re_p"), 0)[None]
    sr_s = np.concatenate(cat("s5re_s"), 0)[None]
    si_p = np.stack(cat("s5im_p"), 0)[None]
    si_s = np.concatenate(cat("s5im_s"), 0)[None]
    outs = (y_p, y_s, mk, mv, C_p, C_s, n_p, n_s, m_p, m_s, k_p, k_s, v_p, v_s, sr_p, sr_s, si_p, si_s)
    return tuple(np.ascontiguousarray(o, dtype=np.float32) for o in outs)


def kernel(**inputs):
    ncores = 8
    w = host_weights(inputs)
    in_maps = []
    for c in range(ncores):
        m = host_core_inputs(inputs, c)
        m.update(w)
        in_maps.append(m)
    prog = Prog()
    with prog.nc.allow_non_contiguous_dma(reason="small strided state / scratch transfers"):
        nc = prog.build()
    res = run_bass_kernel_spmd(nc, in_maps, core_ids=list(range(ncores)))
    return assemble(res.results, ncores)
```

```python
import numpy as np
import concourse.bass as bass
import concourse.mybir as mybir
from concourse.bass_utils import run_bass_kernel_spmd

F32 = mybir.dt.float32
BF16 = mybir.dt.bfloat16
AF = mybir.ActivationFunctionType
ALU = mybir.AluOpType
AX = mybir.AxisListType

D = 2048
KC = D // 128
DFF = 5632
NF = DFF // 128
SEQ = 2048
NB = 16
DSEQ = 4
NS = NB * DSEQ
NTOK = SEQ + NS
EPS = 1e-6


class Res:
    __slots__ = ("name", "w", "rs", "dsem", "dkey")

    def __init__(self, name):
        self.name = name
        self.w = {}
        self.rs = {}
        self.dsem = None
        self.dkey = None


class Ctx:
    def __init__(self, nc):
        self.nc = nc
        self.eng = {"pe": nc.tensor, "act": nc.scalar, "dve": nc.vector, "pool": nc.gpsimd, "sp": nc.sync}
        self.sem = {}
        self.cnt = {}
        for k in ("pe", "act", "dve", "pool"):
            self.sem[k] = nc.alloc_semaphore("sem_" + k)
            self.cnt[k] = 0
        self.waited = {k: {} for k in self.eng}
        self.nres = 0
        self.n_inst = 0
        self.free_d = []
        self.phase_d = []
        self.dead = set()
        self.semG = nc.alloc_semaphore("bar_gather")
        self.semR = nc.alloc_semaphore("bar_release")
        self.bar_n = 0

    def res(self, name=None):
        self.nres += 1
        return Res(name or f"r{self.nres}")

    def sb(self, name, shape, dtype):
        t = self.nc.alloc_sbuf_tensor(name, list(shape), dtype)
        return t, self.res(name)

    def _deps(self, reads, writes, skip=None):
        d = {}
        for r in reads:
            for k, v in r.w.items():
                if d.get(k, 0) < v:
                    d[k] = v
        for w in writes:
            for k, v in w.w.items():
                if d.get(k, 0) < v:
                    d[k] = v
            for k, v in w.rs.items():
                if d.get(k, 0) < v:
                    d[k] = v
        if skip is not None:
            d.pop(skip, None)
        return d

    def _wait(self, eng, deps):
        wd = self.waited[eng]
        e = self.eng[eng]
        for k, v in deps.items():
            if k in self.dead or wd.get(k, 0) >= v:
                continue
            if k.startswith("dma:"):
                v = self.cnt[k]
            e.wait_ge(self.sem[k], v)
            wd[k] = v
            self.n_inst += 1

    def _mark(self, tok, reads, writes):
        k, v = tok
        for r in reads:
            r.rs[k] = v
        for w in writes:
            w.w[k] = v

    def op(self, eng, fn, reads=(), writes=()):
        self._wait(eng, self._deps(reads, writes))
        inst = fn(self.eng[eng])
        self.cnt[eng] += 1
        inst.then_inc(self.sem[eng], 1)
        self.n_inst += 1
        self._mark((eng, self.cnt[eng]), reads, writes)

    def group(self, eng, fns, reads=(), writes=()):
        self._wait(eng, self._deps(reads, writes, skip="pe" if eng == "pe" else None))
        inst = None
        for fn in fns:
            inst = fn(self.eng[eng])
            self.n_inst += 1
        self.cnt[eng] += 1
        inst.then_inc(self.sem[eng], 1)
        self._mark((eng, self.cnt[eng]), reads, writes)

    def dma(self, q, out, in_, sres, reads=(), writes=(), **kw):
        if sres.dsem is None:
            if self.free_d:
                sres.dsem = self.free_d.pop()
            else:
                sres.dsem = self.nc.alloc_semaphore("d_" + str(self.nres))
            sres.dkey = "dma:" + str(self.nres)
            self.nres += 1
            self.sem[sres.dkey] = sres.dsem
            self.cnt[sres.dkey] = 0
            self.phase_d.append(sres)
        self._wait(q, self._deps(reads, writes))
        inst = self.eng[q].dma_start(out=out, in_=in_, **kw)
        self.cnt[sres.dkey] += 16
        inst.then_inc(sres.dsem, 16)
        self.n_inst += 1
        self._mark((sres.dkey, self.cnt[sres.dkey]), reads, writes)

    def finish(self, eng, resources):
        d = {}
        for r in resources:
            for k, v in r.w.items():
                if d.get(k, 0) < v:
                    d[k] = v
        self._wait(eng, d)

    def barrier(self):
        allk = {k: v for k, v in self.cnt.items() if v > 0 and k not in self.dead}
        for e in ("sp", "pe", "act", "dve", "pool"):
            self._wait(e, dict(allk))
        self.bar_n += 1
        for e in ("sp", "pe", "act", "dve"):
            self.eng[e].sem_inc(self.semG, 1)
        pool = self.eng["pool"]
        pool.wait_ge(self.semG, 4 * self.bar_n)
        for res in self.phase_d:
            pool.sem_clear(res.dsem)
        pool.sem_inc(self.semR, 1)
        for e in ("sp", "pe", "act", "dve"):
            self.eng[e].wait_ge(self.semR, self.bar_n)
        for res in self.phase_d:
            self.dead.add(res.dkey)
            self.free_d.append(res.dsem)
            res.dsem = None
            res.dkey = None
        self.phase_d = []


def blocks_of(r0, r1):
    out = []
    r = r0
    while r < r1:
        n = min(128, r1 - r)
        out.append((r, n))
        r += n
    return out


class Prog:
    def __init__(self, cfg=None):
        self.cfg = cfg or {}
        self.nc = bass.Bass("TRN2", target_bir_lowering=False)
        self.cx = Ctx(self.nc)
        self.dram = {}
        self.dres = {}
        nc = self.nc
        self.bank = []
        self.bres = []
        self.psum = nc.alloc_psum_tensor("psum_all", [128, 4096], F32)
        for i in range(8):
            self.bank.append(self.psum[:, i * 512:(i + 1) * 512])
            self.bres.append(self.cx.res(f"bank{i}"))
        self.ident = None
        self.outputs = []

    def din(self, name, shape):
        t = self.nc.dram_tensor(name, list(shape), F32, kind="ExternalInput")
        self.dram[name] = t
        self.dres[name] = self.cx.res(name)
        return t

    def dout(self, name, shape):
        t = self.nc.dram_tensor(name, list(shape), F32, kind="ExternalOutput")
        self.dram[name] = t
        self.dres[name] = self.cx.res(name)
        self.outputs.append(name)
        return t

    def dscr(self, name, shape, dtype=F32):
        t = self.nc.dram_tensor(name, list(shape), dtype, kind="Internal")
        self.dram[name] = t
        self.dres[name] = self.cx.res(name)
        return t

    def setup_consts(self):
        nc, cx = self.nc, self.cx
        self.ident_f, self.ident_f_r = cx.sb("ident_f", [128, 128], F32)
        self.ident_b, self.ident_b_r = cx.sb("ident_b", [128, 128], BF16)
        cx.op("pool", lambda e: e.memset(self.ident_f[:], 1.0), writes=[self.ident_f_r])
        cx.op("pool", lambda e: e.affine_select(out=self.ident_f[:], in_=self.ident_f[:], pattern=[[-1, 128]],
                                               compare_op=ALU.is_equal, fill=0.0, base=0, channel_multiplier=1),
              reads=[self.ident_f_r], writes=[self.ident_f_r])
        cx.op("pool", lambda e: e.tensor_copy(out=self.ident_b[:], in_=self.ident_f[:]),
              reads=[self.ident_f_r], writes=[self.ident_b_r])
        self.eps_t, self.eps_r = cx.sb("eps_t", [128, 1], F32)
        cx.op("pool", lambda e: e.memset(self.eps_t[:], EPS), writes=[self.eps_r])
        self.tp_bank = 0
        self._nb = 0
        self._norm_bufs = None

    def norm_blocks(self, es, srcs, gain_ap, hT, hT_r, tag):
        nc, cx = self.nc, self.cx
        gbc, gbc_r = self._sb(es, tag + "gbc", [128, D], F32)
        cx.dma("sp", gbc[:], gain_ap.partition_broadcast(128), gbc_r, writes=[gbc_r])
        xt, xt_r = self._sb(es, tag + "xt", [128, D], F32)
        sq, sq_r = self._sb(es, tag + "sq", [128, D], BF16)
        xn, xn_r = self._sb(es, tag + "xn", [128, D], BF16)
        st, st_r = self._sb(es, tag + "st", [128, 4], F32)
        for (src, src_r, n, col0) in srcs:
            cx.dma("sp", xt[0:n, :], src, xt_r, reads=[src_r], writes=[xt_r])
            cx.op("act", lambda e: e.activation(out=sq[0:n, :], in_=xt[0:n, :], func=AF.Square,
                                                accum_out=st[0:n, 0:1]),
                  reads=[xt_r], writes=[sq_r, st_r])
            cx.op("act", lambda e: e.activation(out=st[0:n, 1:2], in_=st[0:n, 0:1], func=AF.Sqrt,
                                                scale=1.0 / D, bias=self.eps_t[0:n, 0:1]),
                  reads=[st_r, self.eps_r], writes=[st_r])
            cx.op("dve", lambda e: e.reciprocal(out=st[0:n, 2:3], in_=st[0:n, 1:2]), reads=[st_r], writes=[st_r])
            cx.op("dve", lambda e: e.scalar_tensor_tensor(out=xn[0:n, :], in0=xt[0:n, :], scalar=st[0:n, 2:3],
                                                          in1=gbc[0:n, :], op0=ALU.mult, op1=ALU.mult),
                  reads=[xt_r, st_r, gbc_r], writes=[xn_r])
            self.transpose_into(xn, xn_r, n, hT, hT_r, col0)

    def transpose_into(self, xn, xn_r, n, hT, hT_r, col0, nk=KC):
        cx = self.cx
        for half in range((nk + 7) // 8):
            k0 = half * 8
            k1 = min(nk, k0 + 8)
            b = self.nb()
            pb = self.bank[b][:].bitcast(BF16)
            fns = []
            for kc in range(k0, k1):
                fns.append(lambda e, kc=kc: e.transpose(out=pb[:, (kc - k0) * 128:(kc - k0) * 128 + n],
                                                        in_=xn[0:n, kc * 128:(kc + 1) * 128],
                                                        identity=self.ident_b[0:n, 0:n]))
            cx.group("pe", fns, reads=[xn_r, self.ident_b_r], writes=[self.bres[b]])
            src = pb.rearrange("p (k c) -> p k c", c=128)[:, 0:k1 - k0, 0:n]
            eng = "act" if half % 2 == 0 else "dve"
            if eng == "act":
                cx.op("act", lambda e: e.copy(out=hT[:, k0:k1, col0:col0 + n], in_=src),
                      reads=[self.bres[b]], writes=[hT_r])
            else:
                cx.op("dve", lambda e: e.tensor_copy(out=hT[:, k0:k1, col0:col0 + n], in_=src),
                      reads=[self.bres[b]], writes=[hT_r])

    def _sb(self, es, name, shape, dtype):
        t = es.enter_context(self.nc.sbuf_tensor(name, list(shape), dtype))
        return t, self.cx.res(name)

    def ffn_phase(self, tag, wgu, wd, gain_ap, tiles, src_fn, dst_fn):
        import contextlib
        nc, cx = self.nc, self.cx
        wgu_r = self.dres[wgu.name]
        wd_r = self.dres[wd.name]
        Tmax = max(sum(n for _, n in t) for t in tiles)
        CW = 256
        NCB = D // CW
        with contextlib.ExitStack() as es:
            aT, aT_r = self._sb(es, tag + "aT", [128, NF, Tmax], BF16)
            hT, hT_r = self._sb(es, tag + "hT", [128, KC, Tmax], BF16)
            for ti, tile in enumerate(tiles):
                cols = []
                c = 0
                for (r0, n) in tile:
                    cols.append(c)
                    c += n
                T = c
                with contextlib.ExitStack() as es2:
                    srcs = []
                    for (r0, n), c0 in zip(tile, cols):
                        ap, r = src_fn(r0, n)
                        srcs.append((ap, r, n, c0))
                    self._norm_bufs = None
                    self.norm_blocks_cached(es2, srcs, gain_ap, hT, hT_r, f"{tag}{ti}")
                    cx.barrier()
                with contextlib.ExitStack() as es2:
                    NST = 7
                    H = KC * 128
                    wst = [self._sb(es2, f"{tag}{ti}wst{i}", [128, H], F32) for i in range(NST)]
                    wbf = [self._sb(es2, f"{tag}{ti}wbf{i}", [128, 2 * KC * 128], BF16) for i in range(2)]
                    sil = [self._sb(es2, f"{tag}{ti}sil{i}", [128, 512], F32) for i in range(2)]
                    subs = self.sub_tiles(T)
                    u = 0
                    hq = 0

                    def load_chunk(f):
                        nonlocal hq
                        b_t, b_r = wbf[f % 2]
                        for half in range(2):
                            s_t, s_r = wst[hq % NST]
                            hq += 1
                            cx.dma("sp", s_t[:], wgu[f][:, half * H:(half + 1) * H], s_r, reads=[wgu_r], writes=[s_r])
                            self.cast(b_t[:, half * H:(half + 1) * H], s_t[:], [s_r], [b_r], eng=("act" if half == 0 else "dve"))

                    load_chunk(0)
                    for f in range(NF):
                        b_t, b_r = wbf[f % 2]
                        if f + 1 < NF:
                            load_chunk(f + 1)
                        for (c0, n) in subs:
                            bg = 2 * (u % 4)
                            bu = bg + 1
                            fns = []
                            for gu, bk in ((0, bg), (1, bu)):
                                for kc in range(KC):
                                    fns.append(lambda e, gu=gu, bk=bk, kc=kc: e.matmul(
                                        self.bank[bk][:, 0:n],
                                        lhsT=b_t[:, (gu * KC + kc) * 128:(gu * KC + kc + 1) * 128],
                                        rhs=hT[:, kc, c0:c0 + n], start=(kc == 0), stop=(kc == KC - 1)))
                            cx.group("pe", fns, reads=[b_r, hT_r], writes=[self.bres[bg], self.bres[bu]])
                            sl_t, sl_r = sil[u % 2]
                            cx.op("act", lambda e: e.activation(out=sl_t[:, 0:n], in_=self.bank[bg][:, 0:n], func=AF.Silu),
                                  reads=[self.bres[bg]], writes=[sl_r])
                            cx.op("dve", lambda e: e.tensor_tensor(out=aT[:, f, c0:c0 + n], in0=sl_t[:, 0:n],
                                                                   in1=self.bank[bu][:, 0:n], op=ALU.mult),
                                  reads=[sl_r, self.bres[bu]], writes=[aT_r])
                            u += 1
                    cx.barrier()
                if self.cfg.get('skipD'):
                    continue
                with contextlib.ExitStack() as es2:
                    wres = [self._sb(es2, f"{tag}{ti}wres{i}", [128, NF, CW], BF16) for i in range(2)]
                    dst_ = [self._sb(es2, f"{tag}{ti}dst{i}", [128, 11, CW], F32) for i in range(2)]
                    xo = [self._sb(es2, f"{tag}{ti}xo{i}", [128, CW], F32) for i in range(4)]
                    xw = [self._sb(es2, f"{tag}{ti}xw{i}", [128, CW], F32) for i in range(4)]
                    g = 0
                    ev = 0
                    hb = 0
                    hres = [cx.res(f"{tag}{ti}hb{i}") for i in range(16)]
                    def load_wd(cb):
                        nonlocal g
                        w_t, w_r = wres[cb % 2]
                        for q in range(4):
                            s_t, s_r = dst_[g % 2]
                            g += 1
                            src = wd[q * 11 * 128:(q + 1) * 11 * 128, cb * CW:(cb + 1) * CW].rearrange("(j p) c -> p j c", p=128)
                            cx.dma("sp", s_t[:], src, s_r, reads=[wd_r], writes=[s_r])
                            self.cast(w_t[:, q * 11:(q + 1) * 11, :], s_t[:], [s_r], [w_r])

                    load_wd(0)
                    for cb in range(NCB):
                        w_t, w_r = wres[cb % 2]
                        if cb + 1 < NCB:
                            load_wd(cb + 1)
                        for bi, ((r0, n), c0) in enumerate(zip(tile, cols)):
                            bk, half = hb % 8, hb // 8
                            h_r = hres[hb % 8]
                            hb = (hb + 1) % 16
                            o = self.bank[bk][0:n, half * CW:(half + 1) * CW]
                            fns = [lambda e, f=f: e.matmul(o, lhsT=aT[:, f, c0:c0 + n], rhs=w_t[:, f, :],
                                                           start=(f == 0), stop=(f == NF - 1)) for f in range(NF)]
                            cx.group("pe", fns, reads=[w_r, aT_r], writes=[h_r])
                            xo_t, xo_r = xo[ev % 4]
                            xw_t, xw_r = xw[ev % 4]
                            ev += 1
                            sap, sr = src_fn(r0, n)
                            dap, dr = dst_fn(r0, n)
                            cx.dma("sp", xo_t[0:n, :], sap[:, cb * CW:(cb + 1) * CW], xo_r, reads=[sr], writes=[xo_r])
                            cx.op("dve", lambda e: e.scalar_tensor_tensor(out=xw_t[0:n, :], in0=o, scalar=0.5,
                                                                          in1=xo_t[0:n, :], op0=ALU.mult, op1=ALU.add),
                                  reads=[h_r, xo_r], writes=[xw_r])
                            cx.dma("sp", dap[:, cb * CW:(cb + 1) * CW], xw_t[0:n, :], xw_r, reads=[xw_r], writes=[dr])
                    cx.barrier()

    def norm_blocks_cached(self, es, srcs, gain_ap, hT, hT_r, tag):
        cx = self.cx
        if self._norm_bufs is None:
            gbc, gbc_r = self._sb(es, tag + "gbc", [128, D], F32)
            cx.dma("sp", gbc[:], gain_ap.partition_broadcast(128), gbc_r, writes=[gbc_r])
            xt = self._sb(es, tag + "xt", [128, D], F32)
            sq = self._sb(es, tag + "sq", [128, D], BF16)
            xn = self._sb(es, tag + "xn", [128, D], BF16)
            st = self._sb(es, tag + "st", [128, 4], F32)
            self._norm_bufs = (gbc, gbc_r, xt, sq, xn, st)
        gbc, gbc_r, (xt, xt_r), (sq, sq_r), (xn, xn_r), (st, st_r) = self._norm_bufs
        for (src, src_r, n, col0) in srcs:
            cx.dma("sp", xt[0:n, :], src, xt_r, reads=[src_r], writes=[xt_r])
            cx.op("act", lambda e: e.activation(out=sq[0:n, :], in_=xt[0:n, :], func=AF.Square,
                                                accum_out=st[0:n, 0:1]),
                  reads=[xt_r], writes=[sq_r, st_r])
            cx.op("act", lambda e: e.activation(out=st[0:n, 1:2], in_=st[0:n, 0:1], func=AF.Sqrt,
                                                scale=1.0 / D, bias=self.eps_t[0:n, 0:1]),
                  reads=[st_r, self.eps_r], writes=[st_r])
            cx.op("dve", lambda e: e.reciprocal(out=st[0:n, 2:3], in_=st[0:n, 1:2]), reads=[st_r], writes=[st_r])
            cx.op("dve", lambda e: e.scalar_tensor_tensor(out=xn[0:n, :], in0=xt[0:n, :], scalar=st[0:n, 2:3],
                                                          in1=gbc[0:n, :], op0=ALU.mult, op1=ALU.mult),
                  reads=[xt_r, st_r, gbc_r], writes=[xn_r])
            self.transpose_into(xn, xn_r, n, hT, hT_r, col0)

    def cast(self, out, in_, reads, writes, eng=None):
        if eng is None:
            self._ce = 1 - getattr(self, "_ce", 0)
            eng = "act" if self._ce else "dve"
        if eng == "act":
            self.cx.op("act", lambda e: e.copy(out=out, in_=in_), reads=reads, writes=writes)
        else:
            self.cx.op("dve", lambda e: e.tensor_copy(out=out, in_=in_), reads=reads, writes=writes)

    def nb(self):
        b = self._nb
        self._nb = (self._nb + 1) % 8
        return b

    def sub_tiles(self, T, w=512):
        out = []
        c = 0
        while c < T:
            n = min(w, T - c)
            out.append((c, n))
            c += n
        return out

    def wbufs(self, es, tag, nk, w, nst=2, nbf=2):
        return {"st": [self._sb(es, f"{tag}wS{i}", [128, nk, w], F32) for i in range(nst)],
                "bf": [self._sb(es, f"{tag}wB{i}", [128, nk, w], BF16) for i in range(nbf)], "i": 0}

    def load_w(self, bufs, src_ap, src_r, nk, w):
        cx = self.cx
        i = bufs["i"]
        bufs["i"] += 1
        st, st_r = bufs["st"][i % len(bufs["st"])]
        bf, bf_r = bufs["bf"][i % len(bufs["bf"])]
        cx.dma("sp", st[:, 0:nk, 0:w], src_ap.rearrange("(k p) c -> p k c", p=128), st_r, reads=[src_r], writes=[st_r])
        h = max(1, nk // 2)
        self.cast(bf[:, 0:h, 0:w], st[:, 0:h, 0:w], [st_r], [bf_r], eng="act")
        if nk > h:
            self.cast(bf[:, h:nk, 0:w], st[:, h:nk, 0:w], [st_r], [bf_r], eng="dve")
        return bf, bf_r

    def load_w_resident(self, bufs, W_ap, W_r, nk, N, dst, dst_r):
        cx = self.cx
        for k0 in range(0, nk, 4):
            k1 = min(nk, k0 + 4)
            for c0 in range(0, N, 512):
                c1 = min(N, c0 + 512)
                i = bufs["i"]
                bufs["i"] += 1
                st, st_r = bufs["st"][i % len(bufs["st"])]
                cx.dma("sp", st[:, 0:k1 - k0, 0:c1 - c0],
                       W_ap[k0 * 128:k1 * 128, c0:c1].rearrange("(k p) c -> p k c", p=128), st_r,
                       reads=[W_r], writes=[st_r])
                self.cast(dst[:, k0:k1, c0:c1], st[:, 0:k1 - k0, 0:c1 - c0], [st_r], [dst_r])

    def norm_all(self, es, tag, src_fn, gain_ap, blocks):
        T = max(r0 + n for r0, n in blocks)
        hT, hT_r = self._sb(es, tag + "hT", [128, KC, T], BF16)
        self._norm_bufs = None
        srcs = []
        for (r0, n) in blocks:
            ap, r = src_fn(r0, n)
            srcs.append((ap, r, n, r0))
        import contextlib
        with contextlib.ExitStack() as es2:
            self.norm_blocks_cached(es2, srcs, gain_ap, hT, hT_r, tag)
            self.cx.barrier()
        return hT, hT_r

    def mm_tok(self, hT, hT_r, nk, col0, n, wb, wb_r, wc0, w, bank):
        fns = [lambda e, kc=kc: e.matmul(self.bank[bank][0:n, 0:w], lhsT=hT[:, kc, col0:col0 + n],
                                         rhs=wb[:, kc, wc0:wc0 + w], start=(kc == 0), stop=(kc == nk - 1))
               for kc in range(nk)]
        self.cx.group("pe", fns, reads=[hT_r, wb_r], writes=[self.bres[bank]])

    def mm_feat(self, hT, hT_r, nk, c0, n, wb, wb_r, wc0, m, bank, bcol=0):
        fns = [lambda e, kc=kc: e.matmul(self.bank[bank][0:m, bcol:bcol + n], lhsT=wb[:, kc, wc0:wc0 + m],
                                         rhs=hT[:, kc, c0:c0 + n], start=(kc == 0), stop=(kc == nk - 1))
               for kc in range(nk)]
        self.cx.group("pe", fns, reads=[hT_r, wb_r], writes=[self.bres[bank]])

    def out_proj(self, es, tag, oT, oT_r, nk, W_ap, W_r, blocks, src_fn, dst_fn):
        cx = self.cx
        wres, wres_r = self._sb(es, tag + "Wres", [128, nk, D], BF16)
        bufs = self.wbufs(es, tag + "op", 4, 512, nst=2, nbf=0)
        self.load_w_resident(bufs, W_ap, W_r, nk, D, wres, wres_r)
        xo = [self._sb(es, f"{tag}oxo{i}", [128, 512], F32) for i in range(3)]
        xw = [self._sb(es, f"{tag}oxw{i}", [128, 512], F32) for i in range(3)]
        ev = 0
        for (r0, n, col0) in blocks:
            sap, sr = src_fn(r0, n)
            dap, dr = dst_fn(r0, n)
            for cb in range(4):
                b = self.nb()
                self.mm_tok(oT, oT_r, nk, col0, n, wres, wres_r, cb * 512, 512, b)
                xo_t, xo_r = xo[ev % 3]
                xw_t, xw_r = xw[ev % 3]
                ev += 1
                cx.dma("sp", xo_t[0:n, :], sap[:, cb * 512:(cb + 1) * 512], xo_r, reads=[sr], writes=[xo_r])
                cx.op("dve", lambda e: e.tensor_tensor(out=xw_t[0:n, :], in0=self.bank[b][0:n, :], in1=xo_t[0:n, :], op=ALU.add),
                      reads=[self.bres[b], xo_r], writes=[xw_r])
                cx.dma("sp", dap[:, cb * 512:(cb + 1) * 512], xw_t[0:n, :], xw_r, reads=[xw_r], writes=[dr])

    def transpose_blk(self, src, src_r, n, ncols, dst_fn, dst_r, dt=BF16, evac="act", scale=None):
        cx = self.cx
        nk = ncols // 128
        per = 8 if dt == BF16 else 4
        ident, ident_r = (self.ident_b, self.ident_b_r) if dt == BF16 else (self.ident_f, self.ident_f_r)
        for k0 in range(0, nk, per):
            k1 = min(nk, k0 + per)
            b = self.nb()
            pb = self.bank[b][:].bitcast(BF16) if dt == BF16 else self.bank[b][:]
            fns = [lambda e, kc=kc: e.transpose(out=pb[:, (kc - k0) * 128:(kc - k0) * 128 + n],
                                                in_=src[0:n, kc * 128:(kc + 1) * 128], identity=ident[0:n, 0:n])
                   for kc in range(k0, k1)]
            cx.group("pe", fns, reads=[src_r, ident_r], writes=[self.bres[b]])
            for kc in range(k0, k1):
                o = dst_fn(kc)
                i_ = pb[:, (kc - k0) * 128:(kc - k0) * 128 + n]
                if evac == "act":
                    cx.op("act", lambda e: e.copy(out=o, in_=i_), reads=[self.bres[b]], writes=[dst_r])
                else:
                    cx.op("dve", lambda e: e.tensor_copy(out=o, in_=i_), reads=[self.bres[b]], writes=[dst_r])

    def memkv_phase(self):
        import contextlib
        cx = self.cx
        mem = self.dram["mem_p"]
        with contextlib.ExitStack() as es:
            mT, mT_r = self._sb(es, "mkT", [128, KC, 256], BF16)
            xt, xt_r = self._sb(es, "mkx", [128, D], F32)
            xb, xb_r = self._sb(es, "mkxb", [128, D], BF16)
            for blk in range(2):
                cx.dma("sp", xt[:], mem.ap()[blk * 128:(blk + 1) * 128, :], xt_r, reads=[self.dres["mem_p"]], writes=[xt_r])
                cx.op("act", lambda e: e.copy(out=xb[:], in_=xt[:]), reads=[xt_r], writes=[xb_r])
                self.transpose_into(xb, xb_r, 128, mT, mT_r, blk * 128)
            bufs = self.wbufs(es, "mk", KC, 512)
            ev = [self._sb(es, f"mkev{i}", [128, 512], F32) for i in range(2)]
            k = 0
            for l in range(2):
                for nm, wn in (("mem_k_p", "xwk"), ("mem_v_p", "xwv")):
                    wb, wb_r = self.load_w(bufs, self.dram[wn].ap()[l], self.dres[wn], KC, 512)
                    for blk in range(2):
                        b = self.nb()
                        self.mm_tok(mT, mT_r, KC, blk * 128, 128, wb, wb_r, 0, 512, b)
                        e_t, e_r = ev[k % 2]
                        k += 1
                        cx.op("act", lambda e: e.copy(out=e_t[:], in_=self.bank[b][:]), reads=[self.bres[b]], writes=[e_r])
                        cx.dma("sp", self.dram[nm].ap()[l, blk * 128:(blk + 1) * 128, :], e_t[:], e_r,
                               reads=[e_r], writes=[self.dres[nm]])
        cx.barrier()

    def final_phase(self, src_fn, dst_fn, gain_ap, blocks):
        import contextlib
        cx = self.cx
        with contextlib.ExitStack() as es:
            gbc, gbc_r = self._sb(es, "fgbc", [128, D], F32)
            cx.dma("sp", gbc[:], gain_ap.partition_broadcast(128), gbc_r, writes=[gbc_r])
            xt = [self._sb(es, f"fx{i}", [128, D], F32) for i in range(2)]
            sq, sq_r = self._sb(es, "fsq", [128, D], BF16)
            yo = [self._sb(es, f"fy{i}", [128, D], F32) for i in range(2)]
            st, st_r = self._sb(es, "fst", [128, 4], F32)
            for i, (r0, n) in enumerate(blocks):
                x_t, x_r = xt[i % 2]
                y_t, y_r = yo[i % 2]
                sap, sr = src_fn(r0, n)
                dap, dr = dst_fn(r0, n)
                cx.dma("sp", x_t[0:n, :], sap, x_r, reads=[sr], writes=[x_r])
                cx.op("act", lambda e: e.activation(out=sq[0:n, :], in_=x_t[0:n, :], func=AF.Square, accum_out=st[0:n, 0:1]),
                      reads=[x_r], writes=[sq_r, st_r])
                cx.op("act", lambda e: e.activation(out=st[0:n, 1:2], in_=st[0:n, 0:1], func=AF.Sqrt, scale=1.0 / D,
                                                    bias=self.eps_t[0:n, 0:1]), reads=[st_r, self.eps_r], writes=[st_r])
                cx.op("dve", lambda e: e.reciprocal(out=st[0:n, 2:3], in_=st[0:n, 1:2]), reads=[st_r], writes=[st_r])
                cx.op("dve", lambda e: e.scalar_tensor_tensor(out=y_t[0:n, :], in0=x_t[0:n, :], scalar=st[0:n, 2:3],
                                                              in1=gbc[0:n, :], op0=ALU.mult, op1=ALU.mult),
                      reads=[x_r, st_r, gbc_r], writes=[y_r])
                cx.dma("sp", dap, y_t[0:n, :], y_r, reads=[y_r], writes=[dr])
        cx.barrier()

    def xattn_phase(self, l, src_fn, dst_fn):
        import contextlib
        cx = self.cx
        tag = f"xa{l}"
        blocks = blocks_of(0, SEQ) + [(SEQ, NS)]
        scale = 128 ** -0.5
        with contextlib.ExitStack() as es:
            hT, hT_r = self.norm_all(es, tag, src_fn, self.dram["norm_gain"].ap()[l, 2], blocks)
            qT, qT_r = self._sb(es, tag + "qT", [128, 4, NTOK], BF16)
            oT, oT_r = self._sb(es, tag + "oT", [128, 4, NTOK], BF16)
            with contextlib.ExitStack() as es2:
                bufs = self.wbufs(es2, tag + "q", KC, 512, nst=1, nbf=1)
                wb, wb_r = self.load_w(bufs, self.dram["xwq"].ap()[l], self.dres["xwq"], KC, 512)
                for hd in range(4):
                    for (c0, n) in self.sub_tiles(NTOK):
                        b = self.nb()
                        self.mm_feat(hT, hT_r, KC, c0, n, wb, wb_r, hd * 128, 128, b)
                        cx.op("act", lambda e: e.copy(out=qT[:, hd, c0:c0 + n], in_=self.bank[b][:, 0:n]),
                              reads=[self.bres[b]], writes=[qT_r])
                cx.barrier()
            kf = [self._sb(es, f"{tag}kf{i}", [128, 2, 512], F32) for i in range(2)]
            vf = [self._sb(es, f"{tag}vf{i}", [128, 2, 512], F32) for i in range(2)]
            kb = [self._sb(es, f"{tag}kb{i}", [128, 2, 512], BF16) for i in range(2)]
            vb = [self._sb(es, f"{tag}vb{i}", [128, 2, 512], BF16) for i in range(2)]
            kT = [self._sb(es, f"{tag}kT{i}", [128, 4, 256], BF16) for i in range(2)]
            pp = [self._sb(es, f"{tag}p{i}", [128, 256], F32) for i in range(2)]
            pn = [self._sb(es, f"{tag}pn{i}", [128, 256], BF16) for i in range(2)]
            pT = [self._sb(es, f"{tag}pT{i}", [128, 2, 128], BF16) for i in range(2)]
            sm = [self._sb(es, f"{tag}sm{i}", [128, 4], F32) for i in range(2)]
            u = 0

            def load_kv(i, k_ap, k_r, v_ap, v_r):
                kf_t, kf_r = kf[i % 2]
                vf_t, vf_r = vf[i % 2]
                kb_t, kb_r = kb[i % 2]
                vb_t, vb_r = vb[i % 2]
                kT_t, kT_r = kT[i % 2]
                cx.dma("sp", kf_t[:], k_ap.rearrange("(c p) f -> p c f", p=128), kf_r, reads=[k_r], writes=[kf_r])
                cx.dma("sp", vf_t[:], v_ap.rearrange("(c p) f -> p c f", p=128), vf_r, reads=[v_r], writes=[vf_r])
                self.cast(kb_t[:], kf_t[:], [kf_r], [kb_r], eng="act")
                self.cast(vb_t[:], vf_t[:], [vf_r], [vb_r], eng="dve")
                for mc in range(2):
                    self.transpose_blk(kb_t[:, mc, :], kb_r, 128, 512,
                                       lambda hd: kT_t[:, hd, mc * 128:(mc + 1) * 128], kT_r, evac="dve")
                return kT_t, kT_r, vb_t, vb_r

            def s1(c0, n, kT_t, kT_r, vb_t, vb_r, hd, uu):
                p_t, p_r = pp[uu % 2]
                sm_t, sm_r = sm[uu % 2]
                b = self.nb()
                cx.group("pe", [lambda e: e.matmul(self.bank[b][0:n, 0:256], lhsT=qT[:, hd, c0:c0 + n],
                                                   rhs=kT_t[:, hd, :], start=True, stop=True)],
                         reads=[qT_r, kT_r], writes=[self.bres[b]])
                cx.op("dve", lambda e: e.reduce_max(out=sm_t[0:n, 0:1], in_=self.bank[b][0:n, 0:256], axis=AX.X),
                      reads=[self.bres[b]], writes=[sm_r])
                cx.op("dve", lambda e: e.tensor_scalar_mul(out=sm_t[0:n, 1:2], in0=sm_t[0:n, 0:1], scalar1=-scale),
                      reads=[sm_r], writes=[sm_r])
                cx.op("act", lambda e: e.activation(out=p_t[0:n, :], in_=self.bank[b][0:n, 0:256], func=AF.Exp,
                                                    scale=scale, bias=sm_t[0:n, 1:2], accum_out=sm_t[0:n, 2:3]),
                      reads=[self.bres[b], sm_r], writes=[p_r, sm_r])

            def s2(c0, n, kT_t, kT_r, vb_t, vb_r, hd, uu):
                p_t, p_r = pp[uu % 2]
                pn_t, pn_r = pn[uu % 2]
                pT_t, pT_r = pT[uu % 2]
                sm_t, sm_r = sm[uu % 2]
                cx.op("dve", lambda e: e.reciprocal(out=sm_t[0:n, 3:4], in_=sm_t[0:n, 2:3]), reads=[sm_r], writes=[sm_r])
                cx.op("dve", lambda e: e.tensor_scalar_mul(out=pn_t[0:n, :], in0=p_t[0:n, :], scalar1=sm_t[0:n, 3:4]),
                      reads=[p_r, sm_r], writes=[pn_r])
                self.transpose_blk(pn_t, pn_r, n, 256, lambda mc: pT_t[:, mc, 0:n], pT_r, evac="act")
                b2 = self.nb()
                cx.group("pe", [lambda e, mc=mc: e.matmul(self.bank[b2][:, 0:n], lhsT=vb_t[:, mc, hd * 128:(hd + 1) * 128],
                                                          rhs=pT_t[:, mc, 0:n], start=(mc == 0), stop=(mc == 1))
                                for mc in range(2)],
                         reads=[vb_r, pT_r], writes=[self.bres[b2]])
                cx.op("act", lambda e: e.copy(out=oT[:, hd, c0:c0 + n], in_=self.bank[b2][:, 0:n]),
                      reads=[self.bres[b2]], writes=[oT_r])

            pend = [None]

            def unit(c0, n, kT_t, kT_r, vb_t, vb_r):
                nonlocal u
                for hd in range(4):
                    item = (c0, n, kT_t, kT_r, vb_t, vb_r, hd, u)
                    u += 1
                    s1(*item)
                    if pend[0] is not None:
                        s2(*pend[0])
                    pend[0] = item

            kvp = load_kv(0, self.dram["mem_k_p"].ap()[l], self.dres["mem_k_p"], self.dram["mem_v_p"].ap()[l], self.dres["mem_v_p"])
            for (r0, n) in blocks_of(0, SEQ):
                unit(r0, n, *kvp)
            for bi in range(NB):
                kv = load_kv(bi + 1, self.dram["cmk"].ap()[l, bi], self.dres["cmk"], self.dram["cmv"].ap()[l, bi], self.dres["cmv"])
                unit(SEQ + bi * DSEQ, DSEQ, *kv)
            s2(*pend[0])
            self.out_proj(es, tag, oT, oT_r, 4, self.dram["xwo"].ap()[l], self.dres["xwo"],
                          [(r0, n, r0) for (r0, n) in blocks], src_fn, dst_fn)
        cx.barrier()

    def even_in_phase(self, src_fn):
        import contextlib
        cx = self.cx
        tag = "ei"
        blocks = blocks_of(0, SEQ) + [(SEQ, NS)]
        d, r = self.dram, self.dres
        with contextlib.ExitStack() as es:
            hT, hT_r = self.norm_all(es, tag, src_fn, d["norm_gain"].ap()[0, 1], blocks)
            bufs = self.wbufs(es, tag, KC, 512)
            evf = [self._sb(es, f"{tag}evf{i}", [128, 512], F32) for i in range(3)]
            evb = [self._sb(es, f"{tag}evb{i}", [128, 512], BF16) for i in range(3)]
            k = 0
            tok_dst = [("KM", 0, BF16), ("VM", 0, BF16), ("VM", 512, BF16), ("OM", 0, F32), ("OM", 512, F32), (None, 0, F32)]
            for cb in range(6):
                wb, wb_r = self.load_w(bufs, d["wtok"].ap()[:, cb * 512:(cb + 1) * 512], r["wtok"], KC, 512)
                nm, dc, dt = tok_dst[cb]
                for (r0, n) in blocks:
                    b = self.nb()
                    self.mm_tok(hT, hT_r, KC, r0, n, wb, wb_r, 0, 512, b)
                    e_t, e_r = (evf if dt == F32 else evb)[k % 3]
                    k += 1
                    cx.op("act", lambda e: e.copy(out=e_t[0:n, :], in_=self.bank[b][0:n, :]), reads=[self.bres[b]], writes=[e_r])
                    if nm is not None:
                        cx.dma("sp", d[nm].ap()[r0:r0 + n, dc:dc + 512], e_t[0:n, :], e_r, reads=[e_r], writes=[r[nm]])
                    else:
                        cx.dma("sp", d["KS"].ap()[r0:r0 + n, :], e_t[0:n, 0:256], e_r, reads=[e_r], writes=[r["KS"]])
                        cx.dma("sp", d["VS"].ap()[r0:r0 + n, :], e_t[0:n, 256:512], e_r, reads=[e_r], writes=[r["VS"]])
            feat_dst = [("QMT", 0), ("KMT", 0), ("QST", 0), ("QST", 512), ("KSTD", 0)]
            for jb in range(5):
                wb, wb_r = self.load_w(bufs, d["wfeat"].ap()[:, jb * 512:(jb + 1) * 512], r["wfeat"], KC, 512)
                nm, dr0 = feat_dst[jb]
                for jj in range(4):
                    for (c0, n) in self.sub_tiles(NTOK):
                        b = self.nb()
                        self.mm_feat(hT, hT_r, KC, c0, n, wb, wb_r, jj * 128, 128, b)
                        e_t, e_r = evb[k % 3]
                        k += 1
                        if nm == "QMT":
                            cx.op("act", lambda e: e.mul(out=e_t[:, 0:n], in_=self.bank[b][:, 0:n], mul=128 ** -0.5),
                                  reads=[self.bres[b]], writes=[e_r])
                        else:
                            cx.op("act", lambda e: e.copy(out=e_t[:, 0:n], in_=self.bank[b][:, 0:n]), reads=[self.bres[b]], writes=[e_r])
                        rr = dr0 + jj * 128
                        cx.dma("sp", d[nm].ap()[rr:rr + 128, c0:c0 + n], e_t[:, 0:n], e_r, reads=[e_r], writes=[r[nm]])
            wif_s, wif_sr = self._sb(es, tag + "wifs", [128, KC, 8], F32)
            wif, wif_r = self._sb(es, tag + "wif", [128, KC, 8], BF16)
            cx.dma("sp", wif_s[:], d["wif"].ap().rearrange("(k p) c -> p k c", p=128), wif_sr, reads=[r["wif"]], writes=[wif_sr])
            self.cast(wif[:], wif_s[:], [wif_sr], [wif_r])
            ifr, ifr_r = self._sb(es, tag + "ifr", [4, 2, NTOK], F32)
            for g in range(2):
                for (c0, n) in self.sub_tiles(NTOK):
                    b = self.nb()
                    self.mm_feat(hT, hT_r, KC, c0, n, wif, wif_r, g * 4, 4, b)
                    cx.op("act", lambda e: e.copy(out=ifr[:, g, c0:c0 + n], in_=self.bank[b][0:4, 0:n]),
                          reads=[self.bres[b]], writes=[ifr_r])
            cx.dma("sp", d["IF"].ap().rearrange("g h t -> h g t"), ifr[:], ifr_r, reads=[ifr_r], writes=[r["IF"]])
        cx.barrier()

    def mlstm_chunks(self):
        ch = [(c * 64, 64, None) for c in range(SEQ // 64)]
        ch += [(SEQ + b * DSEQ, DSEQ, b) for b in range(NB)]
        return ch

    def mlstm_scal_phase(self):
        import contextlib
        cx = self.cx
        d, r = self.dram, self.dres
        tag = "ms"
        with contextlib.ExitStack() as es:
            def T(nm, w=NTOK):
                return self._sb(es, tag + nm, [4, w], F32)
            ifr, ifr_r = self._sb(es, tag + "ifr", [4, 2, NTOK], F32)
            cx.dma("sp", ifr[:], d["IF"].ap().rearrange("g h t -> h g t"), ifr_r, reads=[r["IF"]], writes=[ifr_r])
            bi, bi_r = T("bi", 2)
            cx.dma("sp", bi[:, 0:1], d["mlstm_b_i"].ap().rearrange("o h -> h o"), bi_r, reads=[r["mlstm_b_i"]], writes=[bi_r])
            cx.dma("sp", bi[:, 1:2], d["mlstm_b_f"].ap().rearrange("o h -> h o"), bi_r, reads=[r["mlstm_b_f"]], writes=[bi_r])
            one, one_r = T("one")
            cx.op("pool", lambda e: e.memset(one[:], 1.0), writes=[one_r])
            ninf, ninf_r = T("ninf", 64)
            cx.op("pool", lambda e: e.memset(ninf[:], -1e30), writes=[ninf_r])
            li, li_r = T("li")
            x, x_r = T("x")
            ax, ax_r = T("ax")
            lf, lf_r = T("lf")
            bb, bb_r = T("b")
            a, a_r = T("a")
            mu, mu_r = T("mu")
            ws, ws_r = T("ws")
            M, M_r = T("M", SEQ // 64 + 1)
            Ms, Ms_r = T("Ms", 2 * NB)
            cx.op("dve", lambda e: e.tensor_scalar(out=li[:], in0=ifr[:, 0, :], scalar1=bi[:, 0:1], scalar2=None, op0=ALU.add),
                  reads=[ifr_r, bi_r], writes=[li_r])
            cx.op("dve", lambda e: e.tensor_scalar(out=x[:], in0=ifr[:, 1, :], scalar1=bi[:, 1:2], scalar2=None, op0=ALU.add),
                  reads=[ifr_r, bi_r], writes=[x_r])
            cx.op("act", lambda e: e.activation(out=ax[:], in_=x[:], func=AF.Abs), reads=[x_r], writes=[ax_r])
            cx.op("act", lambda e: e.activation(out=ax[:], in_=ax[:], func=AF.Exp, scale=-1.0), reads=[ax_r], writes=[ax_r])
            cx.op("act", lambda e: e.activation(out=ax[:], in_=ax[:], func=AF.Ln, bias=one[:, 0:1]), reads=[ax_r, one_r], writes=[ax_r])
            cx.op("dve", lambda e: e.tensor_scalar_min(out=lf[:], in0=x[:], scalar1=0.0), reads=[x_r], writes=[lf_r])
            cx.op("dve", lambda e: e.tensor_tensor(out=lf[:], in0=lf[:], in1=ax[:], op=ALU.subtract), reads=[lf_r, ax_r], writes=[lf_r])
            for (t0, L, sb) in self.mlstm_chunks():
                cx.op("dve", lambda e: e.tensor_tensor_scan(out=bb[:, t0:t0 + L], data0=one[:, t0:t0 + L], data1=lf[:, t0:t0 + L],
                                                            initial=0.0, op0=ALU.mult, op1=ALU.add),
                      reads=[one_r, lf_r], writes=[bb_r])
            cx.op("dve", lambda e: e.tensor_tensor(out=a[:], in0=li[:], in1=bb[:], op=ALU.subtract), reads=[li_r, bb_r], writes=[a_r])
            cx.op("pool", lambda e: e.memset(M[:], 0.0), writes=[M_r])
            cx.dma("sp", Ms[:, 0:NB], d["mm"].ap().rearrange("b h -> h b"), Ms_r, reads=[r["mm"]], writes=[Ms_r])
            for ci, (t0, L, sb) in enumerate(self.mlstm_chunks()):
                m_in = M[:, ci:ci + 1] if sb is None else Ms[:, sb:sb + 1]
                m_out = M[:, ci + 1:ci + 2] if sb is None else Ms[:, NB + sb:NB + sb + 1]
                mr = M_r if sb is None else Ms_r
                cx.op("dve", lambda e: e.tensor_tensor_scan(out=mu[:, t0:t0 + L], data0=ninf[:, 0:L], data1=a[:, t0:t0 + L],
                                                            initial=m_in, op0=ALU.max, op1=ALU.max),
                      reads=[ninf_r, a_r, mr], writes=[mu_r])
                cx.op("act", lambda e: e.activation(out=ws[:, t0:t0 + L], in_=mu[:, t0:t0 + L], func=AF.Exp, scale=-1.0, bias=m_in),
                      reads=[mu_r, mr], writes=[ws_r])
                cx.op("dve", lambda e: e.tensor_tensor(out=m_out, in0=bb[:, t0 + L - 1:t0 + L], in1=mu[:, t0 + L - 1:t0 + L], op=ALU.add),
                      reads=[bb_r, mu_r], writes=[mr])
            cx.op("dve", lambda e: e.tensor_tensor(out=bb[:], in0=bb[:], in1=mu[:], op=ALU.add), reads=[bb_r, mu_r], writes=[bb_r])
            cx.op("dve", lambda e: e.tensor_scalar_mul(out=mu[:], in0=mu[:], scalar1=-1.0), reads=[mu_r], writes=[mu_r])
            cx.dma("sp", d["NMU"].ap(), mu[:], mu_r, reads=[mu_r], writes=[r["NMU"]])
            cx.dma("sp", d["WS"].ap(), ws[:], ws_r, reads=[ws_r], writes=[r["WS"]])
            with self.nc.allow_non_contiguous_dma(reason="tiny transposed gate-scalar scratch"):
                cx.dma("sp", d["SCT"].ap()[:, 0:4].rearrange("t h -> h t"), a[:], a_r, reads=[a_r], writes=[r["SCT"]])
                cx.dma("sp", d["SCT"].ap()[:, 4:8].rearrange("t h -> h t"), bb[:], bb_r, reads=[bb_r], writes=[r["SCT"]])
                cx.dma("sp", d["mm_p"].ap().rearrange("(h o) -> h o", o=1), M[:, SEQ // 64:SEQ // 64 + 1], M_r, reads=[M_r], writes=[r["mm_p"]])
                cx.dma("sp", d["mm_s"].ap().rearrange("b h -> h b"), Ms[:, NB:2 * NB], Ms_r, reads=[Ms_r], writes=[r["mm_s"]])
        cx.barrier()

    def mlstm_phase(self):
        import contextlib
        cx = self.cx
        d, r = self.dram, self.dres
        tag = "ml"
        with contextlib.ExitStack() as es:
            S = lambda nm, shape, dt: self._sb(es, tag + nm, shape, dt)
            NBUF = 2
            qT = [S(f"qT{i}", [128, 4, 64], BF16) for i in range(NBUF)]
            kT = [S(f"kT{i}", [128, 4, 64], BF16) for i in range(NBUF)]
            kt = [S(f"kt{i}", [64, 512], BF16) for i in range(NBUF)]
            vv = [S(f"v{i}", [64, 4, 256], BF16) for i in range(NBUF)]
            nmu = [S(f"nmu{i}", [64, 4, 64], F32) for i in range(NBUF)]
            wsb = [S(f"wsb{i}", [128, 4, 64], F32) for i in range(NBUF)]
            col = [S(f"col{i}", [64, 8], F32) for i in range(NBUF)]
            WT = [S(f"WT{i}", [64, 4, 64], F32) for i in range(2)]
            ST = [S(f"ST{i}", [64, 4, 64], BF16) for i in range(2)]
            qs = [S(f"qs{i}", [128, 4, 64], BF16) for i in range(2)]
            vw = [S(f"vw{i}", [64, 4, 256], BF16) for i in range(2)]
            wlb = [S(f"wlb{i}", [64, 4], BF16) for i in range(2)]
            hm = [S(f"hm{i}", [64, 4, 256], F32) for i in range(2)]
            sm = [S(f"sm{i}", [64, 4, 4], F32) for i in range(2)]
            C, C_r = S("C", [128, 4, 256], F32)
            Cb, Cb_r = S("Cb", [128, 4, 256], BF16)
            nn, nn_r = S("n", [128, 4], F32)
            nb_, nb_r = S("nb", [128, 4], BF16)
            ones, ones_r = S("ones", [64, 1], BF16)
            cx.op("pool", lambda e: e.memset(ones[:], 1.0), writes=[ones_r])
            cx.op("pool", lambda e: e.memset(C[:], 0.0), writes=[C_r])
            cx.op("pool", lambda e: e.memset(nn[:], 0.0), writes=[nn_r])
            cx.op("pool", lambda e: e.memset(Cb[:], 0.0), writes=[Cb_r])
            cx.op("pool", lambda e: e.memset(nb_[:], 0.0), writes=[nb_r])
            chunks = self.mlstm_chunks()
            for ci, (t0, L, sb) in enumerate(chunks):
                i = ci % NBUF
                j = ci % 2
                (qT_t, qT_r), (kT_t, kT_r), (kt_t, kt_r), (v_t, v_r) = qT[i], kT[i], kt[i], vv[i]
                (nmu_t, nmu_r), (wsb_t, wsb_r), (col_t, col_r) = nmu[i], wsb[i], col[i]
                (WT_t, WT_r), (ST_t, ST_r), (qs_t, qs_r), (vw_t, vw_r) = WT[j], ST[j], qs[j], vw[j]
                (wlb_t, wlb_r), (hm_t, hm_r), (sm_t, sm_r) = wlb[j], hm[j], sm[j]
                cx.dma("sp", qT_t[:, :, 0:L], d["QMT"].ap()[:, t0:t0 + L].rearrange("(h p) t -> p h t", p=128), qT_r, reads=[r["QMT"]], writes=[qT_r])
                cx.dma("sp", kT_t[:, :, 0:L], d["KMT"].ap()[:, t0:t0 + L].rearrange("(h p) t -> p h t", p=128), kT_r, reads=[r["KMT"]], writes=[kT_r])
                cx.dma("sp", kt_t[0:L, :], d["KM"].ap()[t0:t0 + L, :], kt_r, reads=[r["KM"]], writes=[kt_r])
                cx.dma("sp", v_t[0:L], d["VM"].ap()[t0:t0 + L, :].rearrange("t (h v) -> t h v", h=4), v_r, reads=[r["VM"]], writes=[v_r])
                cx.dma("sp", nmu_t[0:L, :, 0:L], d["NMU"].ap()[:, t0:t0 + L].partition_broadcast(L), nmu_r, reads=[r["NMU"]], writes=[nmu_r])
                cx.dma("sp", wsb_t[:, :, 0:L], d["WS"].ap()[:, t0:t0 + L].partition_broadcast(128), wsb_r, reads=[r["WS"]], writes=[wsb_r])
                cx.dma("sp", col_t[0:L, :], d["SCT"].ap()[t0:t0 + L, :], col_r, reads=[r["SCT"]], writes=[col_r])
                if sb is not None:
                    cx.dma("sp", C[:], d["mC"].ap()[sb].rearrange("h k v -> k h v"), C_r, reads=[r["mC"]], writes=[C_r])
                    cx.dma("sp", nn[:], d["mn"].ap()[sb].rearrange("h k -> k h"), nn_r, reads=[r["mn"]], writes=[nn_r])
                    self.cast(Cb[:], C[:], [C_r], [Cb_r], eng="act")
                    self.cast(nb_[:], nn[:], [nn_r], [nb_r], eng="dve")
                for h in range(4):
                    cx.op("act", lambda e: e.activation(out=WT_t[0:L, h, 0:L], in_=nmu_t[0:L, h, 0:L], func=AF.Exp, bias=col_t[0:L, h:h + 1]),
                          reads=[nmu_r, col_r], writes=[WT_r])
                cx.op("pool", lambda e: e.affine_select(out=WT_t[0:L, :, 0:L], in_=WT_t[0:L, :, 0:L], pattern=[[0, 4], [1, L]],
                                                       compare_op=ALU.is_ge, fill=0.0, base=0, channel_multiplier=-1),
                      reads=[WT_r], writes=[WT_r])
                bq = self.nb()
                kq = self.bank[bq][0:L, 0:256].rearrange("s (h t) -> s h t", h=4)
                cx.group("pe", [lambda e, h=h: e.matmul(kq[:, h, 0:L], lhsT=kT_t[:, h, 0:L], rhs=qT_t[:, h, 0:L], start=True, stop=True)
                                for h in range(4)], reads=[kT_r, qT_r], writes=[self.bres[bq]])
                cx.op("dve", lambda e: e.tensor_tensor(out=ST_t[0:L, :, 0:L], in0=kq[:, :, 0:L], in1=WT_t[0:L, :, 0:L], op=ALU.mult),
                      reads=[self.bres[bq], WT_r], writes=[ST_r])
                cx.op("dve", lambda e: e.tensor_tensor(out=qs_t[:, :, 0:L], in0=qT_t[:, :, 0:L], in1=wsb_t[:, :, 0:L], op=ALU.mult),
                      reads=[qT_r, wsb_r], writes=[qs_r])
                bn = [self.nb(), self.nb()]
                bd = self.nb()
                fns = []
                for h in range(4):
                    o = self.bank[bn[h // 2]][0:L, (h % 2) * 256:(h % 2 + 1) * 256]
                    fns.append(lambda e, h=h, o=o: e.matmul(o, lhsT=ST_t[0:L, h, 0:L], rhs=v_t[0:L, h, :], start=True, stop=False))
                    fns.append(lambda e, h=h, o=o: e.matmul(o, lhsT=qs_t[:, h, 0:L], rhs=Cb[:, h, :], start=False, stop=True))
                cx.group("pe", fns, reads=[ST_r, v_r, qs_r, Cb_r], writes=[self.bres[bn[0]], self.bres[bn[1]]])
                fns = []
                for h in range(4):
                    o = self.bank[bd][0:L, h:h + 1]
                    fns.append(lambda e, h=h, o=o: e.matmul(o, lhsT=ST_t[0:L, h, 0:L], rhs=ones[0:L, :], start=True, stop=False))
                    fns.append(lambda e, h=h, o=o: e.matmul(o, lhsT=qs_t[:, h, 0:L], rhs=nb_[:, h:h + 1], start=False, stop=True))
                cx.group("pe", fns, reads=[ST_r, ones_r, qs_r, nb_r], writes=[self.bres[bd]])
                cx.op("act", lambda e: e.activation(out=sm_t[0:L, :, 0], in_=self.bank[bd][0:L, 0:4], func=AF.Abs),
                      reads=[self.bres[bd]], writes=[sm_r])
                cx.op("act", lambda e: e.activation(out=sm_t[0:L, :, 1], in_=col_t[0:L, 4:8], func=AF.Exp, scale=-1.0),
                      reads=[col_r], writes=[sm_r])
                cx.op("dve", lambda e: e.tensor_tensor(out=sm_t[0:L, :, 2], in0=sm_t[0:L, :, 0], in1=sm_t[0:L, :, 1], op=ALU.max),
                      reads=[sm_r], writes=[sm_r])
                cx.op("dve", lambda e: e.reciprocal(out=sm_t[0:L, :, 3], in_=sm_t[0:L, :, 2]), reads=[sm_r], writes=[sm_r])
                for h in range(4):
                    o = self.bank[bn[h // 2]][0:L, (h % 2) * 256:(h % 2 + 1) * 256]
                    cx.op("act", lambda e: e.activation(out=hm_t[0:L, h, :], in_=o, func=AF.Copy, scale=sm_t[0:L, h, 3:4]),
                          reads=[self.bres[bn[h // 2]], sm_r], writes=[hm_r])
                cx.dma("sp", d["HM"].ap()[t0:t0 + L, :], hm_t[0:L].rearrange("t h v -> t (h v)"), hm_r, reads=[hm_r], writes=[r["HM"]])
                for h in range(4):
                    cx.op("act", lambda e: e.activation(out=vw_t[0:L, h, :], in_=v_t[0:L, h, :], func=AF.Copy, scale=WT_t[0:L, h, L - 1:L]),
                          reads=[v_r, WT_r], writes=[vw_r])
                cx.op("dve", lambda e: e.tensor_copy(out=wlb_t[0:L, :], in_=WT_t[0:L, :, L - 1]), reads=[WT_r], writes=[wlb_r])
                bu = [self.nb(), self.nb()]
                bnu = self.nb()
                fns = []
                for h in range(4):
                    o = self.bank[bu[h // 2]][:, (h % 2) * 256:(h % 2 + 1) * 256]
                    fns.append(lambda e, h=h, o=o: e.matmul(o, lhsT=kt_t[0:L, h * 128:(h + 1) * 128], rhs=vw_t[0:L, h, :], start=True, stop=True))
                for h in range(4):
                    fns.append(lambda e, h=h: e.matmul(self.bank[bnu][:, h:h + 1], lhsT=kt_t[0:L, h * 128:(h + 1) * 128], rhs=wlb_t[0:L, h:h + 1],
                                                       start=True, stop=True))
                cx.group("pe", fns, reads=[kt_r, vw_r, wlb_r], writes=[self.bres[bu[0]], self.bres[bu[1]], self.bres[bnu]])
                for h in range(4):
                    o = self.bank[bu[h // 2]][:, (h % 2) * 256:(h % 2 + 1) * 256]
                    cx.op("dve", lambda e: e.scalar_tensor_tensor(out=C[:, h, :], in0=C[:, h, :], scalar=wsb_t[:, h, L - 1:L], in1=o,
                                                                  op0=ALU.mult, op1=ALU.add),
                          reads=[C_r, wsb_r, self.bres[bu[h // 2]]], writes=[C_r])
                cx.op("dve", lambda e: e.tensor_tensor(out=nn[:], in0=nn[:], in1=wsb_t[:, :, L - 1], op=ALU.mult), reads=[nn_r, wsb_r], writes=[nn_r])
                cx.op("dve", lambda e: e.tensor_tensor(out=nn[:], in0=nn[:], in1=self.bank[bnu][:, 0:4], op=ALU.add),
                      reads=[nn_r, self.bres[bnu]], writes=[nn_r])
                self.cast(Cb[:], C[:], [C_r], [Cb_r], eng="act")
                self.cast(nb_[:], nn[:], [nn_r], [nb_r], eng="dve")
                last_prompt = (sb is None and ci == SEQ // 64 - 1)
                if last_prompt:
                    cx.dma("sp", d["mC_p"].ap().rearrange("h k v -> k h v"), C[:], C_r, reads=[C_r], writes=[r["mC_p"]])
                    cx.dma("sp", d["mn_p"].ap().rearrange("h k -> k h"), nn[:], nn_r, reads=[nn_r], writes=[r["mn_p"]])
                if sb is not None:
                    cx.dma("sp", d["mC_s"].ap()[sb].rearrange("h k v -> k h v"), C[:], C_r, reads=[C_r], writes=[r["mC_s"]])
                    cx.dma("sp", d["mn_s"].ap()[sb].rearrange("h k -> k h"), nn[:], nn_r, reads=[nn_r], writes=[r["mn_s"]])
        cx.barrier()

    def swa_phase(self):
        import contextlib
        cx = self.cx
        d, r = self.dram, self.dres
        tag = "sw"
        NEG = -30000.0
        with contextlib.ExitStack() as es:
            S = lambda nm, shape, dt: self._sb(es, tag + nm, shape, dt)
            dist, dist_r = S("dist", [128, 256], F32)
            Mall, Mall_r = S("Mall", [128, 16, 256], F32)
            disti, disti_r = S("disti", [128, 256], mybir.dt.int32)
            cx.op("pool", lambda e: e.iota(disti[:], pattern=[[-1, 256]], base=128, channel_multiplier=1), writes=[disti_r])
            cx.op("pool", lambda e: e.tensor_copy(out=dist[:], in_=disti[:]), reads=[disti_r], writes=[dist_r])
            for h in range(16):
                slope = 2.0 ** (-8.0 * (h + 1) / 16)
                cx.op("dve", lambda e: e.tensor_scalar_mul(out=Mall[:, h, :], in0=dist[:], scalar1=-slope), reads=[dist_r], writes=[Mall_r])
            cx.op("pool", lambda e: e.affine_select(out=Mall[:], in_=Mall[:], pattern=[[0, 16], [-1, 256]], compare_op=ALU.is_ge,
                                                   fill=NEG, base=128, channel_multiplier=1), reads=[Mall_r], writes=[Mall_r])
            cx.op("pool", lambda e: e.affine_select(out=Mall[:], in_=Mall[:], pattern=[[0, 16], [1, 256]], compare_op=ALU.is_ge,
                                                   fill=NEG, base=-1, channel_multiplier=-1), reads=[Mall_r], writes=[Mall_r])
            snk, snk_r = S("snk", [128, 16], F32)
            cx.dma("sp", snk[:], d["swa_sinks"].ap()[0].partition_broadcast(128), snk_r, reads=[r["swa_sinks"]], writes=[snk_r])
            qT = [S(f"qT{i}", [128, 8, 128], BF16) for i in range(2)]
            kTd = [S(f"kTd{i}", [128, 4, 128], BF16) for i in range(2)]
            kT2 = [S(f"kT2{i}", [128, 4, 128], BF16) for i in range(2)]
            vf = [S(f"vf{i}", [128, 256], F32) for i in range(2)]
            vb = [S(f"vb{i}", [128, 256], BF16) for i in range(2)]
            v2f = [S(f"v2f{i}", [128, 256], F32) for i in range(2)]
            v2b = [S(f"v2b{i}", [128, 256], BF16) for i in range(2)]
            ckf = [S(f"ckf{i}", [128, 256], F32) for i in range(2)]
            ckd = [S(f"ckd{i}", [128, 4, 2, 64], BF16) for i in range(2)]
            sc = [S(f"sc{i}", [128, 256], F32) for i in range(2)]
            pb = [S(f"p{i}", [128, 256], BF16) for i in range(2)]
            pT = [S(f"pT{i}", [128, 2, 128], BF16) for i in range(2)]
            sm = [S(f"sm{i}", [128, 8], F32) for i in range(2)]
            hs = [S(f"hs{i}", [128, 1024], F32) for i in range(2)]
            u = 0

            def attend(nq, qT_t, qT_r, k1, k1_r, k2, k2_r, v1, v1_r, v2, v2_r, hs_t, hs_r):
                nonlocal u
                lo = 0 if k1 is not None else 128
                hi = 128 + nq

                def s1(h, uu):
                    c, chunk, half = h // 4, h // 2, h % 2
                    ps = slice(half * 64, (half + 1) * 64)
                    sc_t, sc_r = sc[uu % 2]
                    p_t, p_r = pb[uu % 2]
                    sm_t, sm_r = sm[uu % 2]
                    b = self.nb()
                    fns = []
                    rds = [qT_r, k2_r]
                    if k1 is not None:
                        fns.append(lambda e: e.matmul(self.bank[b][0:nq, 0:128], lhsT=qT_t[ps, chunk, 0:nq], rhs=k1[ps, c, :], start=True, stop=True))
                        rds.append(k1_r)
                    fns.append(lambda e: e.matmul(self.bank[b][0:nq, 128:hi], lhsT=qT_t[ps, chunk, 0:nq], rhs=k2[ps, c, 0:nq], start=True, stop=True))
                    cx.group("pe", fns, reads=rds, writes=[self.bres[b]])
                    cx.op("dve", lambda e: e.scalar_tensor_tensor(out=sc_t[0:nq, lo:hi], in0=self.bank[b][0:nq, lo:hi], scalar=0.125,
                                                                  in1=Mall[0:nq, h, lo:hi], op0=ALU.mult, op1=ALU.add),
                          reads=[self.bres[b], Mall_r], writes=[sc_r])
                    cx.op("dve", lambda e: e.reduce_max(out=sm_t[0:nq, 0:1], in_=sc_t[0:nq, lo:hi], axis=AX.X), reads=[sc_r], writes=[sm_r])
                    cx.op("dve", lambda e: e.tensor_scalar(out=sm_t[0:nq, 1:2], in0=sm_t[0:nq, 0:1], scalar1=snk[0:nq, h:h + 1], scalar2=-1.0,
                                                           op0=ALU.max, op1=ALU.mult), reads=[sm_r, snk_r], writes=[sm_r])
                    cx.op("act", lambda e: e.activation(out=p_t[0:nq, lo:hi], in_=sc_t[0:nq, lo:hi], func=AF.Exp, bias=sm_t[0:nq, 1:2],
                                                        accum_out=sm_t[0:nq, 2:3]), reads=[sc_r, sm_r], writes=[p_r, sm_r])
                    cx.op("act", lambda e: e.activation(out=sm_t[0:nq, 3:4], in_=sm_t[0:nq, 1:2], func=AF.Exp, bias=snk[0:nq, h:h + 1]),
                          reads=[sm_r, snk_r], writes=[sm_r])

                def s2(h, uu):
                    c = h // 4
                    p_t, p_r = pb[uu % 2]
                    pT_t, pT_r = pT[uu % 2]
                    sm_t, sm_r = sm[uu % 2]
                    cx.op("dve", lambda e: e.tensor_tensor(out=sm_t[0:nq, 4:5], in0=sm_t[0:nq, 2:3], in1=sm_t[0:nq, 3:4], op=ALU.add),
                          reads=[sm_r], writes=[sm_r])
                    cx.op("dve", lambda e: e.reciprocal(out=sm_t[0:nq, 5:6], in_=sm_t[0:nq, 4:5]), reads=[sm_r], writes=[sm_r])
                    bt = self.nb()
                    ptb = self.bank[bt][:].bitcast(BF16)
                    fns = []
                    if k1 is not None:
                        fns.append(lambda e: e.transpose(out=ptb[:, 0:nq], in_=p_t[0:nq, 0:128], identity=self.ident_b[0:nq, 0:nq]))
                    fns.append(lambda e: e.transpose(out=ptb[0:nq, 128:128 + nq], in_=p_t[0:nq, 128:hi], identity=self.ident_b[0:nq, 0:nq]))
                    cx.group("pe", fns, reads=[p_r, self.ident_b_r], writes=[self.bres[bt]])
                    if k1 is not None:
                        cx.op("act", lambda e: e.copy(out=pT_t[:, 0, 0:nq], in_=ptb[:, 0:nq]), reads=[self.bres[bt]], writes=[pT_r])
                    cx.op("dve", lambda e: e.tensor_copy(out=pT_t[0:nq, 1, 0:nq], in_=ptb[0:nq, 128:128 + nq]), reads=[self.bres[bt]], writes=[pT_r])
                    bo = self.nb()
                    fns = []
                    rds = [pT_r, v2_r]
                    if k1 is not None:
                        fns.append(lambda e: e.matmul(self.bank[bo][0:nq, 0:64], lhsT=pT_t[:, 0, 0:nq], rhs=v1[:, c * 64:(c + 1) * 64], start=True, stop=False))
                        rds.append(v1_r)
                    fns.append(lambda e: e.matmul(self.bank[bo][0:nq, 0:64], lhsT=pT_t[0:nq, 1, 0:nq], rhs=v2[0:nq, c * 64:(c + 1) * 64],
                                                  start=(k1 is None), stop=True))
                    cx.group("pe", fns, reads=rds, writes=[self.bres[bo]])
                    cx.op("act", lambda e: e.activation(out=hs_t[0:nq, h * 64:(h + 1) * 64], in_=self.bank[bo][0:nq, 0:64], func=AF.Copy,
                                                        scale=sm_t[0:nq, 5:6]), reads=[self.bres[bo], sm_r], writes=[hs_r])

                u0 = u
                u += 16
                s1(0, u0)
                for h in range(16):
                    if h + 1 < 16:
                        s1(h + 1, u0 + h + 1)
                    s2(h, u0 + h)

            prev = None
            for j in range(SEQ // 128):
                t0 = j * 128
                i = j % 2
                (q_t, q_r), (k_t, k_r), (vf_t, vf_r), (vb_t, vb_r), (hs_t, hs_r) = qT[i], kTd[i], vf[i], vb[i], hs[i]
                cx.dma("sp", q_t[:], d["QST"].ap()[:, t0:t0 + 128].rearrange("(c p) t -> p c t", p=128), q_r, reads=[r["QST"]], writes=[q_r])
                cx.dma("sp", k_t[:], d["KSTD"].ap()[:, t0:t0 + 128].rearrange("(c p) t -> p c t", p=128), k_r, reads=[r["KSTD"]], writes=[k_r])
                cx.dma("sp", vf_t[:], d["VS"].ap()[t0:t0 + 128, :], vf_r, reads=[r["VS"]], writes=[vf_r])
                self.cast(vb_t[:], vf_t[:], [vf_r], [vb_r])
                if prev is None:
                    attend(128, q_t, q_r, None, None, k_t, k_r, None, None, vb_t, vb_r, hs_t, hs_r)
                else:
                    attend(128, q_t, q_r, prev[0], prev[1], k_t, k_r, prev[2], prev[3], vb_t, vb_r, hs_t, hs_r)
                prev = (k_t, k_r, vb_t, vb_r)
                cx.dma("sp", d["HS"].ap()[t0:t0 + 128, :], hs_t[:], hs_r, reads=[hs_r], writes=[r["HS"]])
            for sb in range(NB):
                t0 = SEQ + sb * DSEQ
                i = sb % 2
                (q_t, q_r), (k1_t, k1_r), (k2_t, k2_r) = qT[i], kTd[i], kT2[i]
                (vf_t, vf_r), (vb_t, vb_r), (v2f_t, v2f_r), (v2b_t, v2b_r) = vf[i], vb[i], v2f[i], v2b[i]
                (ckf_t, ckf_r), (ckd_t, ckd_r), (hs_t, hs_r) = ckf[i], ckd[i], hs[i]
                cx.dma("sp", q_t[:, :, 0:DSEQ], d["QST"].ap()[:, t0:t0 + DSEQ].rearrange("(c p) t -> p c t", p=128), q_r, reads=[r["QST"]], writes=[q_r])
                cx.dma("sp", k2_t[:, :, 0:DSEQ], d["KSTD"].ap()[:, t0:t0 + DSEQ].rearrange("(c p) t -> p c t", p=128), k2_r, reads=[r["KSTD"]], writes=[k2_r])
                cx.dma("sp", ckf_t[:], d["swk"].ap()[sb], ckf_r, reads=[r["swk"]], writes=[ckf_r])
                cx.dma("sp", vf_t[:], d["swv"].ap()[sb], vf_r, reads=[r["swv"]], writes=[vf_r])
                cx.dma("sp", v2f_t[0:DSEQ, :], d["VS"].ap()[t0:t0 + DSEQ, :], v2f_r, reads=[r["VS"]], writes=[v2f_r])
                self.cast(vb_t[:], vf_t[:], [vf_r], [vb_r])
                self.cast(v2b_t[0:DSEQ, :], v2f_t[0:DSEQ, :], [v2f_r], [v2b_r])
                ck3 = ckf_t[:].rearrange("s (c d) -> s c d", c=4)
                for dup in range(2):
                    self.cast(ckd_t[:, :, dup, :], ck3, [ckf_r], [ckd_r])
                self.transpose_blk(ckd_t[:].rearrange("s c u d -> s (c u d)"), ckd_r, 128, 512, lambda c: k1_t[:, c, :], k1_r, evac="dve")
                attend(DSEQ, q_t, q_r, k1_t, k1_r, k2_t, k2_r, vb_t, vb_r, v2b_t, v2b_r, hs_t, hs_r)
                cx.dma("sp", d["HS"].ap()[t0:t0 + DSEQ, :], hs_t[0:DSEQ, :], hs_r, reads=[hs_r], writes=[r["HS"]])
            dm = cx.res("swcopy")
            cx.dma("sp", d["swk_p"].ap(), d["KS"].ap()[SEQ - 128:SEQ, :], dm, reads=[r["KS"]], writes=[r["swk_p"]])
            cx.dma("sp", d["swv_p"].ap(), d["VS"].ap()[SEQ - 128:SEQ, :], dm, reads=[r["VS"]], writes=[r["swv_p"]])
            cx.dma("sp", d["swk_s"].ap()[:, 0:128 - DSEQ, :], d["swk"].ap()[:, DSEQ:128, :], dm, reads=[r["swk"]], writes=[r["swk_s"]])
            cx.dma("sp", d["swv_s"].ap()[:, 0:128 - DSEQ, :], d["swv"].ap()[:, DSEQ:128, :], dm, reads=[r["swv"]], writes=[r["swv_s"]])
            cx.dma("sp", d["swk_s"].ap()[:, 128 - DSEQ:128, :], d["KS"].ap()[SEQ:NTOK, :].rearrange("(b t) f -> b t f", t=DSEQ), dm,
                   reads=[r["KS"]], writes=[r["swk_s"]])
            cx.dma("sp", d["swv_s"].ap()[:, 128 - DSEQ:128, :], d["VS"].ap()[SEQ:NTOK, :].rearrange("(b t) f -> b t f", t=DSEQ), dm,
                   reads=[r["VS"]], writes=[r["swv_s"]])
        cx.barrier()

    def even_out_phase(self, src_fn, dst_fn):
        import contextlib
        cx = self.cx
        d, r = self.dram, self.dres
        tag = "eo"
        blocks = blocks_of(0, SEQ) + [(SEQ, NS)]
        with contextlib.ExitStack() as es:
            S = lambda nm, shape, dt: self._sb(es, tag + nm, shape, dt)
            oT, oT_r = S("oT", [128, KC, NTOK], BF16)
            with contextlib.ExitStack() as es2:
                S2 = lambda nm, shape, dt: self._sb(es2, tag + nm, shape, dt)
                hg, hg_r = S2("hg", [128, 1024], F32)
                cx.dma("sp", hg[:], d["mlstm_head_gain"].ap().rearrange("o h v -> (o h v)").partition_broadcast(128), hg_r,
                       reads=[r["mlstm_head_gain"]], writes=[hg_r])
                hm = [S2(f"hm{i}", [128, 1024], F32) for i in range(2)]
                om = [S2(f"om{i}", [128, 1024], F32) for i in range(2)]
                hsf = [S2(f"hs{i}", [128, 1024], F32) for i in range(2)]
                sq, sq_r = S2("sq", [128, 256], BF16)
                cat = [S2(f"cat{i}", [128, D], BF16) for i in range(2)]
                st = [S2(f"st{i}", [128, 12], F32) for i in range(2)]
                for bi, (r0, n) in enumerate(blocks):
                    i = bi % 2
                    (hm_t, hm_r), (om_t, om_r), (hs_t, hs_r), (cat_t, cat_r), (st_t, st_r) = hm[i], om[i], hsf[i], cat[i], st[i]
                    cx.dma("sp", hm_t[0:n, :], d["HM"].ap()[r0:r0 + n, :], hm_r, reads=[r["HM"]], writes=[hm_r])
                    cx.dma("sp", om_t[0:n, :], d["OM"].ap()[r0:r0 + n, :], om_r, reads=[r["OM"]], writes=[om_r])
                    cx.dma("sp", hs_t[0:n, :], d["HS"].ap()[r0:r0 + n, :], hs_r, reads=[r["HS"]], writes=[hs_r])
                    for h in range(4):
                        cx.op("act", lambda e: e.activation(out=sq[0:n, :], in_=hm_t[0:n, h * 256:(h + 1) * 256], func=AF.Square,
                                                            accum_out=st_t[0:n, h:h + 1]), reads=[hm_r], writes=[sq_r, st_r])
                    cx.op("act", lambda e: e.activation(out=st_t[0:n, 4:8], in_=st_t[0:n, 0:4], func=AF.Sqrt, scale=1.0 / 256,
                                                        bias=self.eps_t[0:n, 0:1]), reads=[st_r, self.eps_r], writes=[st_r])
                    cx.op("dve", lambda e: e.reciprocal(out=st_t[0:n, 8:12], in_=st_t[0:n, 4:8]), reads=[st_r], writes=[st_r])
                    cx.op("act", lambda e: e.activation(out=om_t[0:n, :], in_=om_t[0:n, :], func=AF.Sigmoid), reads=[om_r], writes=[om_r])
                    for h in range(4):
                        cx.op("dve", lambda e: e.scalar_tensor_tensor(out=hm_t[0:n, h * 256:(h + 1) * 256], in0=hm_t[0:n, h * 256:(h + 1) * 256],
                                                                      scalar=st_t[0:n, 8 + h:9 + h], in1=hg[0:n, h * 256:(h + 1) * 256],
                                                                      op0=ALU.mult, op1=ALU.mult), reads=[hm_r, st_r, hg_r], writes=[hm_r])
                    cx.op("dve", lambda e: e.tensor_tensor(out=cat_t[0:n, 0:1024], in0=hm_t[0:n, :], in1=om_t[0:n, :], op=ALU.mult),
                          reads=[hm_r, om_r], writes=[cat_r])
                    self.cast(cat_t[0:n, 1024:2048], hs_t[0:n, :], [hs_r], [cat_r], eng="act")
                    self.transpose_into(cat_t, cat_r, n, oT, oT_r, r0)
                cx.barrier()
            self.out_proj(es, tag, oT, oT_r, KC, d["even_w_out"].ap()[0], r["even_w_out"],
                          [(r0, n, r0) for (r0, n) in blocks], src_fn, dst_fn)
        cx.barrier()

    def odd_in_phase(self, src_fn):
        import contextlib
        cx = self.cx
        d, r = self.dram, self.dres
        tag = "oi"
        blocks = blocks_of(0, SEQ) + [(SEQ, NS)]
        with contextlib.ExitStack() as es:
            hT, hT_r = self.norm_all(es, tag, src_fn, d["norm_gain"].ap()[1, 1], blocks)
            bufs = self.wbufs(es, tag, KC, 512)
            evf = [self._sb(es, f"{tag}evf{i}", [128, 512], F32) for i in range(3)]
            evb = [self._sb(es, f"{tag}evb{i}", [128, 512], BF16) for i in range(3)]
            k = 0
            for cb in range(4):
                wb, wb_r = self.load_w(bufs, d["odd_w_in"].ap()[0][:, cb * 512:(cb + 1) * 512], r["odd_w_in"], KC, 512)
                for (r0, n) in blocks:
                    b = self.nb()
                    self.mm_tok(hT, hT_r, KC, r0, n, wb, wb_r, 0, 512, b)
                    e_t, e_r = evf[k % 3]
                    k += 1
                    cx.op("act", lambda e: e.copy(out=e_t[0:n, :], in_=self.bank[b][0:n, :]), reads=[self.bres[b]], writes=[e_r])
                    cx.dma("sp", d["U"].ap()[r0:r0 + n, cb * 512:(cb + 1) * 512], e_t[0:n, :], e_r, reads=[e_r], writes=[r["U"]])
                for jj in range(4):
                    for (c0, n) in self.sub_tiles(NTOK):
                        b = self.nb()
                        self.mm_feat(hT, hT_r, KC, c0, n, wb, wb_r, jj * 128, 128, b)
                        e_t, e_r = evb[k % 3]
                        k += 1
                        cx.op("act", lambda e: e.copy(out=e_t[:, 0:n], in_=self.bank[b][:, 0:n]), reads=[self.bres[b]], writes=[e_r])
                        rr = cb * 512 + jj * 128
                        cx.dma("sp", d["UT"].ap()[rr:rr + 128, c0:c0 + n], e_t[:, 0:n], e_r, reads=[e_r], writes=[r["UT"]])
        cx.barrier()

    def _range_reduce(self, out, out_r, in_, in_r, tmpi, tmpi_r, tmpf, tmpf_r, shift):
        cx = self.cx
        TWO_PI = 2.0 * np.pi
        cx.op("dve", lambda e: e.tensor_scalar(out=tmpf, in0=in_, scalar1=shift, scalar2=1.0 / TWO_PI, op0=ALU.add, op1=ALU.mult),
              reads=[in_r], writes=[tmpf_r])
        cx.op("dve", lambda e: e.tensor_copy(out=tmpi, in_=tmpf), reads=[tmpf_r], writes=[tmpi_r])
        cx.op("dve", lambda e: e.tensor_copy(out=tmpf, in_=tmpi), reads=[tmpi_r], writes=[tmpf_r])
        cx.op("dve", lambda e: e.tensor_scalar(out=tmpf, in0=tmpf, scalar1=-TWO_PI, scalar2=shift, op0=ALU.mult, op1=ALU.add),
              reads=[tmpf_r], writes=[tmpf_r])
        cx.op("dve", lambda e: e.tensor_tensor(out=out, in0=tmpf, in1=in_, op=ALU.add), reads=[tmpf_r, in_r], writes=[out_r])
        cx.op("dve", lambda e: e.tensor_scalar(out=tmpf, in0=out, scalar1=np.pi, scalar2=-TWO_PI, op0=ALU.is_gt, op1=ALU.mult),
              reads=[out_r], writes=[tmpf_r])
        cx.op("dve", lambda e: e.tensor_tensor(out=out, in0=out, in1=tmpf, op=ALU.add), reads=[out_r, tmpf_r], writes=[out_r])
        cx.op("dve", lambda e: e.tensor_scalar(out=tmpf, in0=out, scalar1=-np.pi, scalar2=TWO_PI, op0=ALU.is_lt, op1=ALU.mult),
              reads=[out_r], writes=[tmpf_r])
        cx.op("dve", lambda e: e.tensor_tensor(out=out, in0=out, in1=tmpf, op=ALU.add), reads=[out_r, tmpf_r], writes=[out_r])

    def s5_setup_phase(self):
        import contextlib
        cx = self.cx
        d, r = self.dram, self.dres
        tag = "s5s"
        I32 = mybir.dt.int32
        with contextlib.ExitStack() as es:
            S = lambda nm, shape, dt: self._sb(es, tag + nm, shape, dt)
            are, are_r = S("are", [128, 64], F32)
            aim, aim_r = S("aim", [128, 64], F32)
            ldt, ldt_r = S("ldt", [128, 1], F32)
            cx.dma("sp", are[:], d["s5_a_re"].ap()[0], are_r, reads=[r["s5_a_re"]], writes=[are_r])
            cx.dma("sp", aim[:], d["s5_a_im"].ap()[0], aim_r, reads=[r["s5_a_im"]], writes=[aim_r])
            cx.dma("sp", ldt[:], d["s5_log_dt"].ap().rearrange("o g -> g o"), ldt_r, reads=[r["s5_log_dt"]], writes=[ldt_r])
            cx.op("act", lambda e: e.activation(out=ldt[:], in_=ldt[:], func=AF.Exp), reads=[ldt_r], writes=[ldt_r])
            lm, lm_r = S("lm", [128, 64], F32)
            th, th_r = S("th", [128, 64], F32)
            cx.op("dve", lambda e: e.tensor_scalar_mul(out=lm[:], in0=are[:], scalar1=ldt[:, 0:1]), reads=[are_r, ldt_r], writes=[lm_r])
            cx.op("dve", lambda e: e.tensor_scalar_mul(out=th[:], in0=aim[:], scalar1=ldt[:, 0:1]), reads=[aim_r, ldt_r], writes=[th_r])
            cx.dma("sp", d["LM"].ap().rearrange("(g p) -> g p", p=64), lm[:], lm_r, reads=[lm_r], writes=[r["LM"]])
            cx.dma("sp", d["TH"].ap().rearrange("(g p) -> g p", p=64), th[:], th_r, reads=[th_r], writes=[r["TH"]])
            ti, ti_r = S("ti", [128, 64], I32)
            tf, tf_r = S("tf", [128, 64], F32)
            sn, sn_r = S("sn", [128, 64], F32)
            cs, cs_r = S("cs", [128, 64], F32)
            mg, mg_r = S("mg", [128, 64], F32)
            self._range_reduce(sn[:], sn_r, th[:], th_r, ti[:], ti_r, tf[:], tf_r, 0.0)
            self._range_reduce(cs[:], cs_r, th[:], th_r, ti[:], ti_r, tf[:], tf_r, np.pi / 2)
            cx.op("act", lambda e: e.activation(out=sn[:], in_=sn[:], func=AF.Sin), reads=[sn_r], writes=[sn_r])
            cx.op("act", lambda e: e.activation(out=cs[:], in_=cs[:], func=AF.Sin), reads=[cs_r], writes=[cs_r])
            cx.op("act", lambda e: e.activation(out=mg[:], in_=lm[:], func=AF.Exp), reads=[lm_r], writes=[mg_r])
            abr, abr_r = S("abr", [128, 64], F32)
            abi, abi_r = S("abi", [128, 64], F32)
            cx.op("dve", lambda e: e.tensor_tensor(out=abr[:], in0=mg[:], in1=cs[:], op=ALU.mult), reads=[mg_r, cs_r], writes=[abr_r])
            cx.op("dve", lambda e: e.tensor_scalar_add(out=abr[:], in0=abr[:], scalar1=-1.0), reads=[abr_r], writes=[abr_r])
            cx.op("dve", lambda e: e.tensor_tensor(out=abi[:], in0=mg[:], in1=sn[:], op=ALU.mult), reads=[mg_r, sn_r], writes=[abi_r])
            den, den_r = S("den", [128, 64], F32)
            t1, t1_r = S("t1", [128, 64], F32)
            cx.op("dve", lambda e: e.tensor_tensor(out=den[:], in0=are[:], in1=are[:], op=ALU.mult), reads=[are_r], writes=[den_r])
            cx.op("dve", lambda e: e.tensor_tensor(out=t1[:], in0=aim[:], in1=aim[:], op=ALU.mult), reads=[aim_r], writes=[t1_r])
            cx.op("dve", lambda e: e.tensor_tensor(out=den[:], in0=den[:], in1=t1[:], op=ALU.add), reads=[den_r, t1_r], writes=[den_r])
            cx.op("dve", lambda e: e.reciprocal(out=den[:], in_=den[:]), reads=[den_r], writes=[den_r])
            fre, fre_r = S("fre", [128, 64], F32)
            fim, fim_r = S("fim", [128, 64], F32)
            cx.op("dve", lambda e: e.tensor_tensor(out=fre[:], in0=abr[:], in1=are[:], op=ALU.mult), reads=[abr_r, are_r], writes=[fre_r])
            cx.op("dve", lambda e: e.tensor_tensor(out=t1[:], in0=abi[:], in1=aim[:], op=ALU.mult), reads=[abi_r, aim_r], writes=[t1_r])
            cx.op("dve", lambda e: e.tensor_tensor(out=fre[:], in0=fre[:], in1=t1[:], op=ALU.add), reads=[fre_r, t1_r], writes=[fre_r])
            cx.op("dve", lambda e: e.tensor_tensor(out=fre[:], in0=fre[:], in1=den[:], op=ALU.mult), reads=[fre_r, den_r], writes=[fre_r])
            cx.op("dve", lambda e: e.tensor_tensor(out=fim[:], in0=abi[:], in1=are[:], op=ALU.mult), reads=[abi_r, are_r], writes=[fim_r])
            cx.op("dve", lambda e: e.tensor_tensor(out=t1[:], in0=abr[:], in1=aim[:], op=ALU.mult), reads=[abr_r, aim_r], writes=[t1_r])
            cx.op("dve", lambda e: e.tensor_tensor(out=fim[:], in0=fim[:], in1=t1[:], op=ALU.subtract), reads=[fim_r, t1_r], writes=[fim_r])
            cx.op("dve", lambda e: e.tensor_tensor(out=fim[:], in0=fim[:], in1=den[:], op=ALU.mult), reads=[fim_r, den_r], writes=[fim_r])
            bre, bre_r = S("bre", [128, 64, 16], F32)
            bim, bim_r = S("bim", [128, 64, 16], F32)
            cx.dma("sp", bre[:], d["s5_b_re"].ap()[0], bre_r, reads=[r["s5_b_re"]], writes=[bre_r])
            cx.dma("sp", bim[:], d["s5_b_im"].ap()[0], bim_r, reads=[r["s5_b_im"]], writes=[bim_r])
            bbo, bbo_r = S("bbo", [128, 2, 64, 16], F32)
            t3, t3_r = S("t3", [128, 64, 16], F32)
            frb = fre[:].unsqueeze(2).to_broadcast([128, 64, 16])
            fib = fim[:].unsqueeze(2).to_broadcast([128, 64, 16])
            cx.op("dve", lambda e: e.tensor_tensor(out=bbo[:, 0], in0=bre[:], in1=frb, op=ALU.mult), reads=[bre_r, fre_r], writes=[bbo_r])
            cx.op("dve", lambda e: e.tensor_tensor(out=t3[:], in0=bim[:], in1=fib, op=ALU.mult), reads=[bim_r, fim_r], writes=[t3_r])
            cx.op("dve", lambda e: e.tensor_tensor(out=bbo[:, 0], in0=bbo[:, 0], in1=t3[:], op=ALU.subtract), reads=[bbo_r, t3_r], writes=[bbo_r])
            cx.op("dve", lambda e: e.tensor_tensor(out=bbo[:, 1], in0=bim[:], in1=frb, op=ALU.mult), reads=[bim_r, fre_r], writes=[bbo_r])
            cx.op("dve", lambda e: e.tensor_tensor(out=t3[:], in0=bre[:], in1=fib, op=ALU.mult), reads=[bre_r, fim_r], writes=[t3_r])
            cx.op("dve", lambda e: e.tensor_tensor(out=bbo[:, 1], in0=bbo[:, 1], in1=t3[:], op=ALU.add), reads=[bbo_r, t3_r], writes=[bbo_r])
            cx.dma("sp", d["BBS"].ap(), bbo[:], bbo_r, reads=[bbo_r], writes=[r["BBS"]])
            cx.barrier()
        with contextlib.ExitStack() as es:
            S = lambda nm, shape, dt: self._sb(es, tag + nm, shape, dt)
            bbt, bbt_r = S("bbt", [128, 128, 16], F32)
            cx.dma("sp", bbt[:], d["BBS"].ap().rearrange("g r p c -> (r p) g c"), bbt_r, reads=[r["BBS"]], writes=[bbt_r])
            mask, mask_r = S("mask", [128, 8], F32)
            cx.op("pool", lambda e: e.memset(mask[:], 1.0), writes=[mask_r])
            cx.op("pool", lambda e: e.affine_select(out=mask[:], in_=mask[:], pattern=[[-16, 8]], compare_op=ALU.is_ge, fill=0.0, base=0,
                                                   channel_multiplier=1), reads=[mask_r], writes=[mask_r])
            cx.op("pool", lambda e: e.affine_select(out=mask[:], in_=mask[:], pattern=[[16, 8]], compare_op=ALU.is_ge, fill=0.0, base=15,
                                                   channel_multiplier=-1), reads=[mask_r], writes=[mask_r])
            dj = [S(f"dj{i}", [128, 128], F32) for i in range(2)]
            bd = [S(f"bd{i}", [128, 8, 128], BF16) for i in range(2)]
            for j in range(16):
                b = self.nb()
                dj_t, dj_r = dj[j % 2]
                bd_t, bd_r = bd[j % 2]
                cx.group("pe", [lambda e: e.transpose(out=self.bank[b][:, 0:128], in_=bbt[:, 8 * j:8 * j + 8, :].rearrange("q g c -> q (g c)"),
                                                      identity=self.ident_f[:])], reads=[bbt_r, self.ident_f_r], writes=[self.bres[b]])
                cx.op("act", lambda e: e.copy(out=dj_t[:], in_=self.bank[b][:, 0:128]), reads=[self.bres[b]], writes=[dj_r])
                cx.op("dve", lambda e: e.tensor_tensor(out=bd_t[:], in0=dj_t[:].unsqueeze(1).to_broadcast([128, 8, 128]),
                                                       in1=mask[:].unsqueeze(2).to_broadcast([128, 8, 128]), op=ALU.mult),
                      reads=[dj_r, mask_r], writes=[bd_r])
                cx.dma("sp", d["BBD"].ap()[:, j], bd_t[:], bd_r, reads=[bd_r], writes=[r["BBD"]])
            cn, cn_r = S("cn", [128, 16, 2, 64], F32)
            cx.dma("sp", cn[:, :, 0, :], d["s5_c_re"].ap()[0].rearrange("(j g) c p -> (g c) j p", g=8), cn_r, reads=[r["s5_c_re"]], writes=[cn_r])
            cx.dma("sp", cn[:, :, 1, :], d["s5_c_im"].ap()[0].rearrange("(j g) c p -> (g c) j p", g=8), cn_r, reads=[r["s5_c_im"]], writes=[cn_r])
            cm, cm_r = S("cm", [128, 16, 128], BF16)
            for j in range(16):
                b = self.nb()
                cx.group("pe", [lambda e: e.transpose(out=self.bank[b][:, 0:128], in_=cn[:, j].rearrange("q r p -> q (r p)"),
                                                      identity=self.ident_f[:])], reads=[cn_r, self.ident_f_r], writes=[self.bres[b]])
                cx.op("act", lambda e: e.copy(out=cm[0:64, j, :], in_=self.bank[b][0:64, 0:128]), reads=[self.bres[b]], writes=[cm_r])
                cx.op("act", lambda e: e.mul(out=cm[64:128, j, :], in_=self.bank[b][64:128, 0:128], mul=-1.0), reads=[self.bres[b]], writes=[cm_r])
            cx.dma("sp", d["CM"].ap(), cm[:], cm_r, reads=[cm_r], writes=[r["CM"]])
            cx.barrier()
        with contextlib.ExitStack() as es:
            S = lambda nm, shape, dt: self._sb(es, tag + nm, shape, dt)
            W = 2048
            kcol, kcol_r = S("kcol", [128, 2], F32)
            kci, kci_r = S("kci", [128, 1], I32)
            cx.op("pool", lambda e: e.iota(kci[:], pattern=[[0, 1]], base=1, channel_multiplier=1), writes=[kci_r])
            cx.op("pool", lambda e: e.tensor_copy(out=kcol[:, 0:1], in_=kci[:]), reads=[kci_r], writes=[kcol_r])
            cx.op("dve", lambda e: e.tensor_scalar_mul(out=kcol[:, 1:2], in0=kcol[:, 0:1], scalar1=-1.0), reads=[kcol_r], writes=[kcol_r])
            thb, thb_r = S("thb", [128, W], F32)
            lmb, lmb_r = S("lmb", [128, W], F32)
            ang, ang_r = S("ang", [128, W], F32)
            ti, ti_r = S("tti", [128, W], I32)
            tf, tf_r = S("ttf", [128, W], F32)
            sn, sn_r = S("tsn", [128, W], F32)
            cs, cs_r = S("tcs", [128, W], F32)
            ep, ep_r = S("tep", [128, W], F32)
            en, en_r = S("ten", [128, W], F32)
            o4 = [S(f"to{i}", [128, W], F32) for i in range(4)]
            for q in range(8192 // W):
                cs_ = slice(q * W, (q + 1) * W)
                cx.dma("sp", thb[:], d["TH"].ap()[cs_].partition_broadcast(128), thb_r, reads=[r["TH"]], writes=[thb_r])
                cx.dma("sp", lmb[:], d["LM"].ap()[cs_].partition_broadcast(128), lmb_r, reads=[r["LM"]], writes=[lmb_r])
                cx.op("dve", lambda e: e.tensor_scalar_mul(out=ang[:], in0=thb[:], scalar1=kcol[:, 0:1]), reads=[thb_r, kcol_r], writes=[ang_r])
                self._range_reduce(sn[:], sn_r, ang[:], ang_r, ti[:], ti_r, tf[:], tf_r, 0.0)
                self._range_reduce(cs[:], cs_r, ang[:], ang_r, ti[:], ti_r, tf[:], tf_r, np.pi / 2)
                cx.op("act", lambda e: e.activation(out=sn[:], in_=sn[:], func=AF.Sin), reads=[sn_r], writes=[sn_r])
                cx.op("act", lambda e: e.activation(out=cs[:], in_=cs[:], func=AF.Sin), reads=[cs_r], writes=[cs_r])
                cx.op("act", lambda e: e.activation(out=ep[:], in_=lmb[:], func=AF.Exp, scale=kcol[:, 0:1]), reads=[lmb_r, kcol_r], writes=[ep_r])
                cx.op("act", lambda e: e.activation(out=en[:], in_=lmb[:], func=AF.Exp, scale=kcol[:, 1:2]), reads=[lmb_r, kcol_r], writes=[en_r])
                (o0, o0r), (o1, o1r), (o2, o2r), (o3, o3r) = o4
                cx.op("dve", lambda e: e.tensor_tensor(out=o0[:], in0=en[:], in1=cs[:], op=ALU.mult), reads=[en_r, cs_r], writes=[o0r])
                cx.op("dve", lambda e: e.scalar_tensor_tensor(out=o1[:], in0=en[:], scalar=-1.0, in1=sn[:], op0=ALU.mult, op1=ALU.mult),
                      reads=[en_r, sn_r], writes=[o1r])
                cx.op("dve", lambda e: e.tensor_tensor(out=o2[:], in0=ep[:], in1=cs[:], op=ALU.mult), reads=[ep_r, cs_r], writes=[o2r])
                cx.op("dve", lambda e: e.tensor_tensor(out=o3[:], in0=ep[:], in1=sn[:], op=ALU.mult), reads=[ep_r, sn_r], writes=[o3r])
                for k_, (o, orr) in enumerate(o4):
                    cx.dma("sp", d["TAB"].ap()[k_, :, cs_], o[:], orr, reads=[orr], writes=[r["TAB"]])
        cx.barrier()

    def s5_scan_phase(self):
        import contextlib
        cx = self.cx
        d, r = self.dram, self.dres
        tag = "s5"
        with contextlib.ExitStack() as es:
            S = lambda nm, shape, dt: self._sb(es, tag + nm, shape, dt)
            tri, tri_r = S("tri", [128, 128], BF16)
            sel, sel_r = S("sel", [128, 128], BF16)
            tri64, tri64_r = S("tri64", [64, 16, 4], BF16)
            selb, selb_r = S("selb", [16, 16, 4], BF16)
            for t_, tr_ in ((tri, tri_r), (sel, sel_r), (tri64, tri64_r), (selb, selb_r)):
                cx.op("pool", lambda e: e.memset(t_[:], 1.0), writes=[tr_])
            AS = lambda t_, tr_, pat, base, cm: cx.op("pool", lambda e: e.affine_select(
                out=t_[:], in_=t_[:], pattern=pat, compare_op=ALU.is_ge, fill=0.0, base=base, channel_multiplier=cm), reads=[tr_], writes=[tr_])
            AS(tri, tri_r, [[1, 128]], 0, -1)
            AS(sel, sel_r, [[0, 128]], -127, 1)
            AS(tri64, tri64_r, [[4, 16], [1, 4]], 0, -1)
            AS(tri64, tri64_r, [[-4, 16], [0, 4]], 0, 1)
            AS(tri64, tri64_r, [[4, 16], [0, 4]], 3, -1)
            AS(selb, selb_r, [[-1, 16], [0, 4]], 0, 1)
            AS(selb, selb_r, [[1, 16], [0, 4]], 0, -1)
            tri64f = tri64[:].rearrange("s b t -> s (b t)")
            selbf = selb[:].rearrange("s b t -> s (b t)")
            sets = []
            for i in range(2):
                sets.append(dict(tab=S(f"tab{i}", [128, 4, 1024], F32), tabs=S(f"tabs{i}", [64, 4, 1024], F32),
                                 uT=S(f"uT{i}", [128, 2, NTOK], BF16), bbd=S(f"bbd{i}", [128, 2, 8, 128], BF16),
                                 cm=S(f"cm{i}", [128, 2, 128], BF16), x0f=S(f"x0f{i}", [16, 16, 2, 64], F32),
                                 x0b=S(f"x0b{i}", [16, 2048], BF16)))
            BH = [S(f"BH{i}", [128, 2048], BF16) for i in range(2)]
            tt = [S(f"tt{i}", [128, 1024], F32) for i in range(4)]
            tq = [S(f"tq{i}", [128, 1024], F32) for i in range(4)]
            XF = [S(f"XF{i}", [128, 2048], F32) for i in range(2)]
            Xbs = [S(f"Xb{i}", [128, 2048], BF16) for i in range(2)]
            XT = [S(f"XT{i}", [128, 16, 128], BF16) for i in range(2)]
            ys = [S(f"ys{i}", [128, 256], F32) for i in range(2)]
            ps = self.psum
            v4 = lambda ap, L: ap.rearrange("t (g r p) -> t g r p", g=16, r=2)
            v3 = lambda ap, L: ap.rearrange("t (g p) -> t g p", g=16)
            chunks = [(c * 128, 128, False) for c in range(SEQ // 128)] + [(SEQ, NS, True)]
            it = 0
            def load_eighth(e8):
                st = sets[e8 % 2]
                (tab, tab_r), (tabs, tabs_r), (uT, uT_r), (bbd, bbd_r) = st["tab"], st["tabs"], st["uT"], st["bbd"]
                (cm, cm_r), (x0f, x0f_r), (x0b, x0b_r) = st["cm"], st["x0f"], st["x0b"]
                gc = slice(e8 * 1024, (e8 + 1) * 1024)
                cx.dma("sp", tab[:], d["TAB"].ap()[:, :, gc].rearrange("k t c -> t k c"), tab_r, reads=[r["TAB"]], writes=[tab_r])
                for b_ in range(NB):
                    cx.dma("sp", tabs[4 * b_:4 * b_ + 4], d["TAB"].ap()[:, 0:4, gc].rearrange("k t c -> t k c"), tabs_r, reads=[r["TAB"]], writes=[tabs_r])
                cx.dma("sp", uT[:], d["UT"].ap()[e8 * 256:(e8 + 1) * 256, :].rearrange("(j q) t -> q j t", q=128), uT_r, reads=[r["UT"]], writes=[uT_r])
                cx.dma("sp", bbd[:], d["BBD"].ap()[:, 2 * e8:2 * e8 + 2], bbd_r, reads=[r["BBD"]], writes=[bbd_r])
                cx.dma("sp", cm[:], d["CM"].ap()[:, 2 * e8:2 * e8 + 2, :], cm_r, reads=[r["CM"]], writes=[cm_r])
                cx.dma("sp", x0f[:, :, 0, :], d["s5re"].ap()[:, e8 * 16:(e8 + 1) * 16, :], x0f_r, reads=[r["s5re"]], writes=[x0f_r])
                cx.dma("sp", x0f[:, :, 1, :], d["s5im"].ap()[:, e8 * 16:(e8 + 1) * 16, :], x0f_r, reads=[r["s5im"]], writes=[x0f_r])
                self.cast(x0b[:], x0f[:].rearrange("b g r p -> b (g r p)"), [x0f_r], [x0b_r])

            load_eighth(0)
            for e8 in range(8):
                st = sets[e8 % 2]
                (tab, tab_r), (tabs, tabs_r), (uT, uT_r), (bbd, bbd_r) = st["tab"], st["tabs"], st["uT"], st["bbd"]
                (cm, cm_r), (x0f, x0f_r), (x0b, x0b_r) = st["cm"], st["x0f"], st["x0b"]
                if e8 + 1 < 8:
                    load_eighth(e8 + 1)
                def stage_a(ci, t0, L, smp, it):
                    (BH_t, BH_r) = BH[it % 2]
                    tb_t, tb_r = (tabs, tabs_r) if smp else (tab, tab_r)
                    fns = []
                    for q in range(4):
                        fns.append(lambda e, q=q: e.matmul(ps[0:L, q * 512:(q + 1) * 512], lhsT=uT[:, q // 2, t0:t0 + L],
                                                           rhs=bbd[:, q // 2, (q % 2) * 4:(q % 2) * 4 + 4, :], start=True, stop=True))
                    cx.group("pe", fns, reads=[uT_r, bbd_r], writes=self.bres[0:4])
                    buv = v4(ps[0:L, 0:2048], L)
                    bur, bui = buv[:, :, 0, :], buv[:, :, 1, :]
                    Pr, Pi = v3(tb_t[0:L, 0, :], L), v3(tb_t[0:L, 1, :], L)
                    T = [(v3(t_[0:L, :], L), tr_) for (t_, tr_) in tt]
                    BHv = v4(BH_t[0:L, :], L)
                    rb = self.bres[0:4] + [tb_r]
                    cx.op("dve", lambda e: e.tensor_tensor(out=T[0][0], in0=bur, in1=Pr, op=ALU.mult), reads=rb, writes=[T[0][1]])
                    cx.op("dve", lambda e: e.tensor_tensor(out=T[1][0], in0=bui, in1=Pi, op=ALU.mult), reads=rb, writes=[T[1][1]])
                    cx.op("dve", lambda e: e.tensor_tensor(out=T[2][0], in0=bur, in1=Pi, op=ALU.mult), reads=rb, writes=[T[2][1]])
                    cx.op("dve", lambda e: e.tensor_tensor(out=T[3][0], in0=bui, in1=Pr, op=ALU.mult), reads=rb, writes=[T[3][1]])
                    cx.op("pool", lambda e: e.tensor_tensor(out=BHv[:, :, 0, :], in0=T[0][0], in1=T[1][0], op=ALU.subtract),
                          reads=[T[0][1], T[1][1]], writes=[BH_r])
                    cx.op("pool", lambda e: e.tensor_tensor(out=BHv[:, :, 1, :], in0=T[2][0], in1=T[3][0], op=ALU.add),
                          reads=[T[2][1], T[3][1]], writes=[BH_r])

                def stage_b1(ci, t0, L, smp, it):
                    Xb, Xb_r = Xbs[it % 2]
                    Xp, Xp_r = Xbs[(it + 1) % 2]
                    (BH_t, BH_r), (XF_t, XF_r), (XT_t, XT_r), (ys_t, ys_r) = BH[it % 2], XF[it % 2], XT[it % 2], ys[it % 2]
                    tb_t, tb_r = (tabs, tabs_r) if smp else (tab, tab_r)
                    fns = []
                    rds = [BH_r]
                    carry = smp or ci > 0
                    for q in range(4):
                        o = ps[0:L, 2048 + q * 512:2048 + (q + 1) * 512]
                        if smp:
                            fns.append(lambda e, q=q, o=o: e.matmul(o, lhsT=tri64f, rhs=BH_t[0:L, q * 512:(q + 1) * 512], start=True, stop=False))
                            fns.append(lambda e, q=q, o=o: e.matmul(o, lhsT=selbf, rhs=x0b[:, q * 512:(q + 1) * 512], start=False, stop=True))
                        else:
                            fns.append(lambda e, q=q, o=o: e.matmul(o, lhsT=tri[:], rhs=BH_t[:, q * 512:(q + 1) * 512], start=True, stop=not carry))
                            if carry:
                                fns.append(lambda e, q=q, o=o: e.matmul(o, lhsT=sel[:], rhs=Xp[:, q * 512:(q + 1) * 512], start=False, stop=True))
                    if smp:
                        rds += [tri64_r, selb_r, x0b_r]
                    else:
                        rds += [tri_r, sel_r] + ([Xp_r] if carry else [])
                    cx.group("pe", fns, reads=rds, writes=self.bres[4:8])
                    xh = v4(ps[0:L, 2048:4096], L)
                    xr, xi = xh[:, :, 0, :], xh[:, :, 1, :]
                    Qr, Qi = v3(tb_t[0:L, 2, :], L), v3(tb_t[0:L, 3, :], L)
                    Q = [(v3(t_[0:L, :], L), tr_) for (t_, tr_) in tq]
                    XFv = v4(XF_t[0:L, :], L)
                    rb = self.bres[4:8] + [tb_r]
                    cx.op("dve", lambda e: e.tensor_tensor(out=Q[0][0], in0=xr, in1=Qr, op=ALU.mult), reads=rb, writes=[Q[0][1]])
                    cx.op("dve", lambda e: e.tensor_tensor(out=Q[1][0], in0=xi, in1=Qi, op=ALU.mult), reads=rb, writes=[Q[1][1]])
                    Xbv = v4(Xb[0:L, :], L)
                    need_f32 = smp or ci == SEQ // 128 - 1
                    cx.op("pool", lambda e: e.tensor_tensor(out=Xbv[:, :, 0, :], in0=Q[0][0], in1=Q[1][0], op=ALU.subtract),
                          reads=[Q[0][1], Q[1][1]], writes=[Xb_r])
                    cx.op("dve", lambda e: e.tensor_tensor(out=Q[2][0], in0=xr, in1=Qi, op=ALU.mult), reads=rb, writes=[Q[2][1]])
                    cx.op("dve", lambda e: e.tensor_tensor(out=Q[3][0], in0=xi, in1=Qr, op=ALU.mult), reads=rb, writes=[Q[3][1]])
                    cx.op("dve", lambda e: e.tensor_tensor(out=Xbv[:, :, 1, :], in0=Q[2][0], in1=Q[3][0], op=ALU.add),
                          reads=[Q[2][1], Q[3][1]], writes=[Xb_r])
                    if need_f32:
                        cx.op("pool", lambda e: e.tensor_tensor(out=XFv[:, :, 0, :], in0=Q[0][0], in1=Q[1][0], op=ALU.subtract),
                              reads=[Q[0][1], Q[1][1]], writes=[XF_r])
                        cx.op("pool", lambda e: e.tensor_tensor(out=XFv[:, :, 1, :], in0=Q[2][0], in1=Q[3][0], op=ALU.add),
                              reads=[Q[2][1], Q[3][1]], writes=[XF_r])
                    gs = slice(e8 * 16, (e8 + 1) * 16)
                    if (not smp) and ci == SEQ // 128 - 1:
                        cx.dma("sp", d["s5re_p"].ap()[gs, :].rearrange("(o g) p -> o g p", o=1), XFv[L - 1:L, :, 0, :], XF_r, reads=[XF_r], writes=[r["s5re_p"]])
                        cx.dma("sp", d["s5im_p"].ap()[gs, :].rearrange("(o g) p -> o g p", o=1), XFv[L - 1:L, :, 1, :], XF_r, reads=[XF_r], writes=[r["s5im_p"]])
                    if smp:
                        for b_ in range(NB):
                            row = b_ * DSEQ + DSEQ - 1
                            cx.dma("sp", d["s5re_s"].ap()[b_, gs, :].rearrange("(o g) p -> o g p", o=1), XFv[row:row + 1, :, 0, :], XF_r, reads=[XF_r], writes=[r["s5re_s"]])
                            cx.dma("sp", d["s5im_s"].ap()[b_, gs, :].rearrange("(o g) p -> o g p", o=1), XFv[row:row + 1, :, 1, :], XF_r, reads=[XF_r], writes=[r["s5im_s"]])
                def stage_b2(ci, t0, L, smp, it):
                    Xb, Xb_r = Xbs[it % 2]
                    (XT_t, XT_r), (ys_t, ys_r) = XT[it % 2], ys[it % 2]
                    for k0 in (0, 8):
                        bt = self.nb()
                        pb = self.bank[bt][:].bitcast(BF16)
                        cx.group("pe", [lambda e, kc=kc: e.transpose(out=pb[:, (kc - k0) * 128:(kc - k0) * 128 + L],
                                                                     in_=Xb[0:L, kc * 128:(kc + 1) * 128], identity=self.ident_b[0:L, 0:L])
                                        for kc in range(k0, k0 + 8)], reads=[Xb_r, self.ident_b_r], writes=[self.bres[bt]])
                        src = pb.rearrange("p (k c) -> p k c", c=128)[:, :, 0:L]
                        self.cast(XT_t[:, k0:k0 + 8, 0:L], src, [self.bres[bt]], [XT_r], eng="act")
                    by = self.nb()
                    cx.group("pe", [lambda e, g=g: e.matmul(self.bank[by][0:L, g * 16:(g + 1) * 16], lhsT=XT_t[:, g, 0:L],
                                                            rhs=cm[:, g // 8, (g % 8) * 16:(g % 8 + 1) * 16], start=True, stop=True)
                                    for g in range(16)], reads=[XT_r, cm_r], writes=[self.bres[by]])
                    cx.op("act", lambda e: e.copy(out=ys_t[0:L, :], in_=self.bank[by][0:L, 0:256]), reads=[self.bres[by]], writes=[ys_r])
                    cx.dma("sp", d["Y"].ap()[t0:t0 + L, e8 * 256:(e8 + 1) * 256], ys_t[0:L, :], ys_r, reads=[ys_r], writes=[r["Y"]])

                units = [(ci, t0, L, smp, it + ci) for ci, (t0, L, smp) in enumerate(chunks)]
                it += len(chunks)
                nU = len(units)
                stage_a(*units[0])
                stage_a(*units[1])
                stage_b1(*units[0])
                for ui in range(nU):
                    if ui + 2 < nU:
                        stage_a(*units[ui + 2])
                    if ui + 1 < nU:
                        stage_b1(*units[ui + 1])
                    stage_b2(*units[ui])
        cx.barrier()

    def odd_out_phase(self, src_fn, dst_fn):
        import contextlib
        cx = self.cx
        d, r = self.dram, self.dres
        tag = "oo"
        blocks = blocks_of(0, SEQ) + [(SEQ, NS)]
        with contextlib.ExitStack() as es:
            S = lambda nm, shape, dt: self._sb(es, tag + nm, shape, dt)
            yT, yT_r = S("yT", [128, KC, NTOK], BF16)
            with contextlib.ExitStack() as es2:
                S2 = lambda nm, shape, dt: self._sb(es2, tag + nm, shape, dt)
                dsk, dsk_r = S2("dsk", [128, D], F32)
                cx.dma("sp", dsk[:], d["s5_d"].ap()[0].partition_broadcast(128), dsk_r, reads=[r["s5_d"]], writes=[dsk_r])
                yb = [S2(f"y{i}", [128, D], F32) for i in range(2)]
                ub = [S2(f"u{i}", [128, D], F32) for i in range(2)]
                gb = [S2(f"g{i}", [128, D], BF16) for i in range(2)]
                for bi, (r0, n) in enumerate(blocks):
                    (y_t, y_r), (u_t, u_r), (g_t, g_r) = yb[bi % 2], ub[bi % 2], gb[bi % 2]
                    cx.dma("sp", y_t[0:n, :], d["Y"].ap()[r0:r0 + n, :], y_r, reads=[r["Y"]], writes=[y_r])
                    cx.dma("sp", u_t[0:n, :], d["U"].ap()[r0:r0 + n, :], u_r, reads=[r["U"]], writes=[u_r])
                    cx.op("dve", lambda e: e.tensor_tensor(out=u_t[0:n, :], in0=u_t[0:n, :], in1=dsk[0:n, :], op=ALU.mult), reads=[u_r, dsk_r], writes=[u_r])
                    cx.op("dve", lambda e: e.tensor_tensor(out=y_t[0:n, :], in0=y_t[0:n, :], in1=u_t[0:n, :], op=ALU.add), reads=[y_r, u_r], writes=[y_r])
                    GC = 2.0 * (2.0 / np.pi) ** 0.5
                    cx.op("dve", lambda e: e.tensor_tensor(out=u_t[0:n, :], in0=y_t[0:n, :], in1=y_t[0:n, :], op=ALU.mult), reads=[y_r, u_r], writes=[u_r])
                    cx.op("dve", lambda e: e.tensor_scalar(out=u_t[0:n, :], in0=u_t[0:n, :], scalar1=0.044715, scalar2=1.0, op0=ALU.mult, op1=ALU.add),
                          reads=[u_r], writes=[u_r])
                    cx.op("dve", lambda e: e.tensor_tensor(out=u_t[0:n, :], in0=u_t[0:n, :], in1=y_t[0:n, :], op=ALU.mult), reads=[y_r, u_r], writes=[u_r])
                    cx.op("act", lambda e: e.activation(out=u_t[0:n, :], in_=u_t[0:n, :], func=AF.Sigmoid, scale=GC), reads=[u_r], writes=[u_r])
                    cx.op("dve", lambda e: e.tensor_tensor(out=g_t[0:n, :], in0=u_t[0:n, :], in1=y_t[0:n, :], op=ALU.mult), reads=[y_r, u_r], writes=[g_r])
                    self.transpose_into(g_t, g_r, n, yT, yT_r, r0)
                cx.barrier()
            bufs = self.wbufs(es, tag, KC, 512, nst=2, nbf=2)
            sg = [S(f"sg{i}", [128, 512], F32) for i in range(2)]
            xo = [S(f"xo{i}", [128, 512], F32) for i in range(3)]
            xw = [S(f"xw{i}", [128, 512], F32) for i in range(3)]
            W = d["odd_w_out"].ap()[0]
            ev = 0
            for cb in range(4):
                wv, wv_r = self.load_w(bufs, W[:, cb * 512:(cb + 1) * 512], r["odd_w_out"], KC, 512)
                wg, wg_r = self.load_w(bufs, W[:, D + cb * 512:D + (cb + 1) * 512], r["odd_w_out"], KC, 512)
                for (r0, n) in blocks:
                    bv, bg = self.nb(), self.nb()
                    self.mm_tok(yT, yT_r, KC, r0, n, wv, wv_r, 0, 512, bv)
                    self.mm_tok(yT, yT_r, KC, r0, n, wg, wg_r, 0, 512, bg)
                    (sg_t, sg_r), (xo_t, xo_r), (xw_t, xw_r) = sg[ev % 2], xo[ev % 3], xw[ev % 3]
                    ev += 1
                    sap, sr = src_fn(r0, n)
                    dap, dr = dst_fn(r0, n)
                    cx.dma("sp", xo_t[0:n, :], sap[:, cb * 512:(cb + 1) * 512], xo_r, reads=[sr], writes=[xo_r])
                    cx.op("act", lambda e: e.activation(out=sg_t[0:n, :], in_=self.bank[bg][0:n, :], func=AF.Sigmoid), reads=[self.bres[bg]], writes=[sg_r])
                    cx.op("dve", lambda e: e.tensor_tensor(out=sg_t[0:n, :], in0=self.bank[bv][0:n, :], in1=sg_t[0:n, :], op=ALU.mult),
                          reads=[self.bres[bv], sg_r], writes=[sg_r])
                    cx.op("dve", lambda e: e.tensor_tensor(out=xw_t[0:n, :], in0=sg_t[0:n, :], in1=xo_t[0:n, :], op=ALU.add),
                          reads=[sg_r, xo_r], writes=[xw_r])
                    cx.dma("sp", dap[:, cb * 512:(cb + 1) * 512], xw_t[0:n, :], xw_r, reads=[xw_r], writes=[dr])
        cx.barrier()

    def declare_io(self):
        di, do, ds = self.din, self.dout, self.dscr
        di("x_p", [SEQ, D]); di("x_s", [NS, D]); di("mem_p", [256, D])
        di("cmk", [2, NB, 256, 512]); di("cmv", [2, NB, 256, 512])
        di("mC", [NB, 4, 128, 256]); di("mn", [NB, 4, 128]); di("mm", [NB, 4])
        di("swk", [NB, 128, 256]); di("swv", [NB, 128, 256])
        di("s5re", [NB, 128, 64]); di("s5im", [NB, 128, 64])
        di("norm_gain", [2, 4, D]); di("final_gain", [D])
        for l in range(2):
            for i in range(2):
                di(f"wgu{l}{i}", [NF, 128, 2 * KC * 128]); di(f"wd{l}{i}", [DFF, D])
        di("wtok", [D, 3072]); di("wfeat", [D, 2560]); di("wif", [D, 8])
        di("mlstm_b_i", [1, 4]); di("mlstm_b_f", [1, 4]); di("mlstm_head_gain", [1, 4, 256]); di("swa_sinks", [1, 16])
        di("even_w_out", [1, D, D]); di("odd_w_in", [1, D, D])
        di("s5_a_re", [1, 128, 64]); di("s5_a_im", [1, 128, 64]); di("s5_log_dt", [1, 128])
        di("s5_b_re", [1, 128, 64, 16]); di("s5_b_im", [1, 128, 64, 16]); di("s5_c_re", [1, 128, 16, 64]); di("s5_c_im", [1, 128, 16, 64])
        di("s5_d", [1, D]); di("odd_w_out", [1, D, 2 * D])
        di("xwq", [2, D, 512]); di("xwk", [2, D, 512]); di("xwv", [2, D, 512]); di("xwo", [2, 512, D])
        do("y_p", [SEQ, D]); do("y_s", [NS, D]); do("mem_k_p", [2, 256, 512]); do("mem_v_p", [2, 256, 512])
        do("mC_p", [4, 128, 256]); do("mC_s", [NB, 4, 128, 256]); do("mn_p", [4, 128]); do("mn_s", [NB, 4, 128])
        do("mm_p", [4]); do("mm_s", [NB, 4])
        do("swk_p", [128, 256]); do("swk_s", [NB, 128, 256]); do("swv_p", [128, 256]); do("swv_s", [NB, 128, 256])
        do("s5re_p", [128, 64]); do("s5re_s", [NB, 128, 64]); do("s5im_p", [128, 64]); do("s5im_s", [NB, 128, 64])
        ds("XA", [NTOK, D]); ds("XB", [NTOK, D])
        ds("KM", [NTOK, 512], BF16); ds("VM", [NTOK, 1024], BF16); ds("OM", [NTOK, 1024]); ds("KS", [NTOK, 256]); ds("VS", [NTOK, 256])
        ds("QMT", [512, NTOK], BF16); ds("KMT", [512, NTOK], BF16); ds("QST", [1024, NTOK], BF16); ds("KSTD", [512, NTOK], BF16)
        ds("IF", [2, 4, NTOK]); ds("NMU", [4, NTOK]); ds("WS", [4, NTOK]); ds("SCT", [NTOK, 8])
        ds("HM", [NTOK, 1024]); ds("HS", [NTOK, 1024])
        ds("U", [NTOK, D]); ds("UT", [D, NTOK], BF16); ds("LM", [8192]); ds("TH", [8192])
        ds("BBS", [128, 2, 64, 16]); ds("BBD", [128, 16, 8, 128], BF16); ds("CM", [128, 16, 128], BF16)
        ds("TAB", [4, 128, 8192]); ds("Y", [NTOK, D])

    def xio(self, name):
        if name == "in":
            def f(r0, n):
                if r0 < SEQ:
                    return self.dram["x_p"].ap()[r0:r0 + n, :], self.dres["x_p"]
                return self.dram["x_s"].ap()[r0 - SEQ:r0 - SEQ + n, :], self.dres["x_s"]
            return f
        if name == "out":
            def f(r0, n):
                if r0 < SEQ:
                    return self.dram["y_p"].ap()[r0:r0 + n, :], self.dres["y_p"]
                return self.dram["y_s"].ap()[r0 - SEQ:r0 - SEQ + n, :], self.dres["y_s"]
            return f
        return lambda r0, n: (self.dram[name].ap()[r0:r0 + n, :], self.dres[name])

    def build(self, phases=None):
        d = self.dram
        self.declare_io()
        self.setup_consts()
        self.cx.barrier()
        tiles = [blocks_of(0, 1024), blocks_of(1024, SEQ) + [(SEQ, NS)]]
        allb = blocks_of(0, SEQ) + [(SEQ, NS)]
        ng = d["norm_gain"].ap()
        P = phases

        def on(p):
            return P is None or p in P
        if on("memkv"):
            self.memkv_phase()
        if on("ffn00"):
            self.ffn_phase("fa0", d["wgu00"].ap(), d["wd00"].ap(), ng[0, 0], tiles, self.xio("in"), self.xio("XA"))
        if on("even"):
            self.even_in_phase(self.xio("XA"))
            self.mlstm_scal_phase()
            self.mlstm_phase()
            self.swa_phase()
            self.even_out_phase(self.xio("XA"), self.xio("XB"))
        if on("xa0"):
            self.xattn_phase(0, self.xio("XB"), self.xio("XA"))
        if on("ffn01"):
            self.ffn_phase("fb0", d["wgu01"].ap(), d["wd01"].ap(), ng[0, 3], tiles, self.xio("XA"), self.xio("XB"))
        if on("ffn10"):
            self.ffn_phase("fa1", d["wgu10"].ap(), d["wd10"].ap(), ng[1, 0], tiles, self.xio("XB"), self.xio("XA"))
        if on("odd"):
            self.odd_in_phase(self.xio("XA"))
            self.s5_setup_phase()
            self.s5_scan_phase()
            self.odd_out_phase(self.xio("XA"), self.xio("XB"))
        if on("xa1"):
            self.xattn_phase(1, self.xio("XB"), self.xio("XA"))
        if on("ffn11"):
            self.ffn_phase("fb1", d["wgu11"].ap(), d["wd11"].ap(), ng[1, 3], tiles, self.xio("XA"), self.xio("XB"))
        if on("final"):
            self.final_phase(self.xio("XB"), self.xio("out"), d["final_gain"].ap(), allb)
        self.cx.finish("sp", [self.dres[n] for n in self.outputs])
        return self.nc


def _lay_wgu(wg, wu):
    a = np.stack([wg, wu], 0).reshape(2, KC, 128, NF, 128)
    return np.ascontiguousarray(a.transpose(3, 2, 0, 1, 4)).reshape(NF, 128, 2 * KC * 128)


def host_weights(inp):
    f = lambda a: np.ascontiguousarray(np.asarray(a, dtype=np.float32))
    w = {}
    for l in range(2):
        for i in range(2):
            w[f"wgu{l}{i}"] = _lay_wgu(np.asarray(inp["ffn_w_gate"][l, i]), np.asarray(inp["ffn_w_up"][l, i]))
            w[f"wd{l}{i}"] = f(inp["ffn_w_down"][l, i])
    win = np.asarray(inp["even_w_in"][0])
    q_m, k_m, v_m, o_m = win[:, 0:512], win[:, 512:1024], win[:, 1024:2048], win[:, 2048:3072]
    i_m, f_m = win[:, 3072:3076], win[:, 3076:3080]
    q_s, k_s, v_s = win[:, 3080:4104], win[:, 4104:4360], win[:, 4360:4616]
    w["wtok"] = f(np.concatenate([k_m, v_m, o_m, k_s, v_s], 1))
    ksd = np.concatenate([np.concatenate([k_s[:, c * 64:(c + 1) * 64]] * 2, 1) for c in range(4)], 1)
    w["wfeat"] = f(np.concatenate([q_m, k_m, q_s, ksd], 1))
    w["wif"] = f(np.concatenate([i_m, f_m], 1))
    for k in ("norm_gain", "final_gain", "mlstm_b_i", "mlstm_b_f", "mlstm_head_gain", "swa_sinks", "even_w_out", "odd_w_in",
              "s5_a_re", "s5_a_im", "s5_log_dt", "s5_b_re", "s5_b_im", "s5_c_re", "s5_c_im", "s5_d", "odd_w_out"):
        w[k] = f(inp[k])
    w["xwq"], w["xwk"], w["xwv"], w["xwo"] = f(inp["xattn_w_q"]), f(inp["xattn_w_k"]), f(inp["xattn_w_v"]), f(inp["xattn_w_o"])
    return w


def host_core_inputs(inp, c):
    f = lambda a: np.ascontiguousarray(np.asarray(a, dtype=np.float32))
    sb = slice(c * NB, (c + 1) * NB)
    m = {}
    m["x_p"] = f(inp["x_prompt"][c])
    m["x_s"] = f(np.asarray(inp["x_sample"][sb]).reshape(NS, D))
    m["mem_p"] = f(inp["mem_prompt"][c])
    m["cmk"] = f(np.asarray(inp["cache_mem_k"][:, sb]).reshape(2, NB, 256, 512))
    m["cmv"] = f(np.asarray(inp["cache_mem_v"][:, sb]).reshape(2, NB, 256, 512))
    m["mC"] = f(inp["state_mlstm_C"][0, sb])
    m["mn"] = f(inp["state_mlstm_n"][0, sb])
    m["mm"] = f(inp["state_mlstm_m"][0, sb])
    m["swk"] = f(np.asarray(inp["cache_swa_k"][0, sb]).reshape(NB, 128, 256))
    m["swv"] = f(np.asarray(inp["cache_swa_v"][0, sb]).reshape(NB, 128, 256))
    m["s5re"] = f(inp["state_s5_re"][0, sb])
    m["s5im"] = f(inp["state_s5_im"][0, sb])
    return m


def assemble(results, ncores=8):
    R = results
    cat = lambda k: [np.asarray(R[c][k]) for c in range(ncores)]
    y_p = np.stack(cat("y_p"), 0)
    y_s = np.concatenate([a.reshape(NB, DSEQ, D) for a in cat("y_s")], 0)
    mk = np.stack([a.reshape(2, 256, 4, 128) for a in cat("mem_k_p")], 1)
    mv = np.stack([a.reshape(2, 256, 4, 128) for a in cat("mem_v_p")], 1)
    C_p = np.stack(cat("mC_p"), 0)[None]
    C_s = np.concatenate(cat("mC_s"), 0)[None]
    n_p = np.stack(cat("mn_p"), 0)[None]
    n_s = np.concatenate(cat("mn_s"), 0)[None]
    m_p = np.stack(cat("mm_p"), 0)[None]
    m_s = np.concatenate(cat("mm_s"), 0)[None]
    k_p = np.stack([a.reshape(128, 4, 64) for a in cat("swk_p")], 0)[None]
    k_s = np.concatenate([a.reshape(NB, 128, 4, 64) for a in cat("swk_s")], 0)[None]
    v_p = np.stack([a.reshape(128, 4, 64) for a in cat("swv_p")], 0)[None]
    v_s = np.concatenate([a.reshape(NB, 128, 4, 64) for a in cat("swv_s")], 0)[None]
    sr_p = np.stack(cat("s5re_p"), 0)[None]
    sr_s = np.concatenate(cat("s5re_s"), 0)[None]
    si_p = np.stack(cat("s5im_p"), 0)[None]
    si_s = np.concatenate(cat("s5im_s"), 0)[None]
    outs = (y_p, y_s, mk, mv, C_p, C_s, n_p, n_s, m_p, m_s, k_p, k_s, v_p, v_s, sr_p, sr_s, si_p, si_s)
    return tuple(np.ascontiguousarray(o, dtype=np.float32) for o in outs)


def kernel(**inputs):
    ncores = 8
    w = host_weights(inputs)
    in_maps = []
    for c in range(ncores):
        m = host_core_inputs(inputs, c)
        m.update(w)
        in_maps.append(m)
    prog = Prog()
    with prog.nc.allow_non_contiguous_dma(reason="small strided state / scratch transfers"):
        nc = prog.build()
    res = run_bass_kernel_spmd(nc, in_maps, core_ids=list(range(ncores)))
    return assemble(res.results, ncores)
```

```python
import numpy as np
import concourse.bass as bass
import concourse.mybir as mybir
from concourse.bass_utils import run_bass_kernel_spmd

F32 = mybir.dt.float32
BF16 = mybir.dt.bfloat16
AF = mybir.ActivationFunctionType
ALU = mybir.AluOpType
AX = mybir.AxisListType

D = 2048
KC = D // 128
DFF = 5632
NF = DFF // 128
SEQ = 2048
NB = 16
DSEQ = 4
NS = NB * DSEQ
NTOK = SEQ + NS
EPS = 1e-6


class Res:
    __slots__ = ("name", "w", "rs", "dsem", "dkey")

    def __init__(self, name):
        self.name = name
        self.w = {}
        self.rs = {}
        self.dsem = None
        self.dkey = None


class Ctx:
    def __init__(self, nc):
        self.nc = nc
        self.eng = {"pe": nc.tensor, "act": nc.scalar, "dve": nc.vector, "pool": nc.gpsimd, "sp": nc.sync}
        self.sem = {}
        self.cnt = {}
        for k in ("pe", "act", "dve", "pool"):
            self.sem[k] = nc.alloc_semaphore("sem_" + k)
            self.cnt[k] = 0
        self.waited = {k: {} for k in self.eng}
        self.nres = 0
        self.n_inst = 0
        self.free_d = []
        self.phase_d = []
        self.dead = set()
        self.semG = nc.alloc_semaphore("bar_gather")
        self.semR = nc.alloc_semaphore("bar_release")
        self.bar_n = 0

    def res(self, name=None):
        self.nres += 1
        return Res(name or f"r{self.nres}")

    def sb(self, name, shape, dtype):
        t = self.nc.alloc_sbuf_tensor(name, list(shape), dtype)
        return t, self.res(name)

    def _deps(self, reads, writes, skip=None):
        d = {}
        for r in reads:
            for k, v in r.w.items():
                if d.get(k, 0) < v:
                    d[k] = v
        for w in writes:
            for k, v in w.w.items():
                if d.get(k, 0) < v:
                    d[k] = v
            for k, v in w.rs.items():
                if d.get(k, 0) < v:
                    d[k] = v
        if skip is not None:
            d.pop(skip, None)
        return d

    def _wait(self, eng, deps):
        wd = self.waited[eng]
        e = self.eng[eng]
        for k, v in deps.items():
            if k in self.dead or wd.get(k, 0) >= v:
                continue
            if k.startswith("dma:"):
                v = self.cnt[k]
            e.wait_ge(self.sem[k], v)
            wd[k] = v
            self.n_inst += 1

    def _mark(self, tok, reads, writes):
        k, v = tok
        for r in reads:
            r.rs[k] = v
        for w in writes:
            w.w[k] = v

    def op(self, eng, fn, reads=(), writes=()):
        self._wait(eng, self._deps(reads, writes))
        inst = fn(self.eng[eng])
        self.cnt[eng] += 1
        inst.then_inc(self.sem[eng], 1)
        self.n_inst += 1
        self._mark((eng, self.cnt[eng]), reads, writes)

    def group(self, eng, fns, reads=(), writes=()):
        self._wait(eng, self._deps(reads, writes, skip="pe" if eng == "pe" else None))
        inst = None
        for fn in fns:
            inst = fn(self.eng[eng])
            self.n_inst += 1
        self.cnt[eng] += 1
        inst.then_inc(self.sem[eng], 1)
        self._mark((eng, self.cnt[eng]), reads, writes)

    def dma(self, q, out, in_, sres, reads=(), writes=(), **kw):
        if sres.dsem is None:
            if self.free_d:
                sres.dsem = self.free_d.pop()
            else:
                sres.dsem = self.nc.alloc_semaphore("d_" + str(self.nres))
            sres.dkey = "dma:" + str(self.nres)
            self.nres += 1
            self.sem[sres.dkey] = sres.dsem
            self.cnt[sres.dkey] = 0
            self.phase_d.append(sres)
        self._wait(q, self._deps(reads, writes))
        inst = self.eng[q].dma_start(out=out, in_=in_, **kw)
        self.cnt[sres.dkey] += 16
        inst.then_inc(sres.dsem, 16)
        self.n_inst += 1
        self._mark((sres.dkey, self.cnt[sres.dkey]), reads, writes)

    def finish(self, eng, resources):
        d = {}
        for r in resources:
            for k, v in r.w.items():
                if d.get(k, 0) < v:
                    d[k] = v
        self._wait(eng, d)

    def barrier(self):
        allk = {k: v for k, v in self.cnt.items() if v > 0 and k not in self.dead}
        for e in ("sp", "pe", "act", "dve", "pool"):
            self._wait(e, dict(allk))
        self.bar_n += 1
        for e in ("sp", "pe", "act", "dve"):
            self.eng[e].sem_inc(self.semG, 1)
        pool = self.eng["pool"]
        pool.wait_ge(self.semG, 4 * self.bar_n)
        for res in self.phase_d:
            pool.sem_clear(res.dsem)
        pool.sem_inc(self.semR, 1)
        for e in ("sp", "pe", "act", "dve"):
            self.eng[e].wait_ge(self.semR, self.bar_n)
        for res in self.phase_d:
            self.dead.add(res.dkey)
            self.free_d.append(res.dsem)
            res.dsem = None
            res.dkey = None
        self.phase_d = []


def blocks_of(r0, r1):
    out = []
    r = r0
    while r < r1:
        n = min(128, r1 - r)
        out.append((r, n))
        r += n
    return out


class Prog:
    def __init__(self, cfg=None):
        self.cfg = cfg or {}
        self.nc = bass.Bass("TRN2", target_bir_lowering=False)
        self.cx = Ctx(self.nc)
        self.dram = {}
        self.dres = {}
        nc = self.nc
        self.bank = []
        self.bres = []
        self.psum = nc.alloc_psum_tensor("psum_all", [128, 4096], F32)
        for i in range(8):
            self.bank.append(self.psum[:, i * 512:(i + 1) * 512])
            self.bres.append(self.cx.res(f"bank{i}"))
        self.ident = None
        self.outputs = []

    def din(self, name, shape):
        t = self.nc.dram_tensor(name, list(shape), F32, kind="ExternalInput")
        self.dram[name] = t
        self.dres[name] = self.cx.res(name)
        return t

    def dout(self, name, shape):
        t = self.nc.dram_tensor(name, list(shape), F32, kind="ExternalOutput")
        self.dram[name] = t
        self.dres[name] = self.cx.res(name)
        self.outputs.append(name)
        return t

    def dscr(self, name, shape, dtype=F32):
        t = self.nc.dram_tensor(name, list(shape), dtype, kind="Internal")
        self.dram[name] = t
        self.dres[name] = self.cx.res(name)
        return t

    def setup_consts(self):
        nc, cx = self.nc, self.cx
        self.ident_f, self.ident_f_r = cx.sb("ident_f", [128, 128], F32)
        self.ident_b, self.ident_b_r = cx.sb("ident_b", [128, 128], BF16)
        cx.op("pool", lambda e: e.memset(self.ident_f[:], 1.0), writes=[self.ident_f_r])
        cx.op("pool", lambda e: e.affine_select(out=self.ident_f[:], in_=self.ident_f[:], pattern=[[-1, 128]],
                                               compare_op=ALU.is_equal, fill=0.0, base=0, channel_multiplier=1),
              reads=[self.ident_f_r], writes=[self.ident_f_r])
        cx.op("pool", lambda e: e.tensor_copy(out=self.ident_b[:], in_=self.ident_f[:]),
              reads=[self.ident_f_r], writes=[self.ident_b_r])
        self.eps_t, self.eps_r = cx.sb("eps_t", [128, 1], F32)
        cx.op("pool", lambda e: e.memset(self.eps_t[:], EPS), writes=[self.eps_r])
        self.tp_bank = 0
        self._nb = 0
        self._norm_bufs = None

    def norm_blocks(self, es, srcs, gain_ap, hT, hT_r, tag):
        nc, cx = self.nc, self.cx
        gbc, gbc_r = self._sb(es, tag + "gbc", [128, D], F32)
        cx.dma("sp", gbc[:], gain_ap.partition_broadcast(128), gbc_r, writes=[gbc_r])
        xt, xt_r = self._sb(es, tag + "xt", [128, D], F32)
        sq, sq_r = self._sb(es, tag + "sq", [128, D], BF16)
        xn, xn_r = self._sb(es, tag + "xn", [128, D], BF16)
        st, st_r = self._sb(es, tag + "st", [128, 4], F32)
        for (src, src_r, n, col0) in srcs:
            cx.dma("sp", xt[0:n, :], src, xt_r, reads=[src_r], writes=[xt_r])
            cx.op("act", lambda e: e.activation(out=sq[0:n, :], in_=xt[0:n, :], func=AF.Square,
                                                accum_out=st[0:n, 0:1]),
                  reads=[xt_r], writes=[sq_r, st_r])
            cx.op("act", lambda e: e.activation(out=st[0:n, 1:2], in_=st[0:n, 0:1], func=AF.Sqrt,
                                                scale=1.0 / D, bias=self.eps_t[0:n, 0:1]),
                  reads=[st_r, self.eps_r], writes=[st_r])
            cx.op("dve", lambda e: e.reciprocal(out=st[0:n, 2:3], in_=st[0:n, 1:2]), reads=[st_r], writes=[st_r])
            cx.op("dve", lambda e: e.scalar_tensor_tensor(out=xn[0:n, :], in0=xt[0:n, :], scalar=st[0:n, 2:3],
                                                          in1=gbc[0:n, :], op0=ALU.mult, op1=ALU.mult),
                  reads=[xt_r, st_r, gbc_r], writes=[xn_r])
            self.transpose_into(xn, xn_r, n, hT, hT_r, col0)

    def transpose_into(self, xn, xn_r, n, hT, hT_r, col0, nk=KC):
        cx = self.cx
        for half in range((nk + 7) // 8):
            k0 = half * 8
            k1 = min(nk, k0 + 8)
            b = self.nb()
            pb = self.bank[b][:].bitcast(BF16)
            fns = []
            for kc in range(k0, k1):
                fns.append(lambda e, kc=kc: e.transpose(out=pb[:, (kc - k0) * 128:(kc - k0) * 128 + n],
                                                        in_=xn[0:n, kc * 128:(kc + 1) * 128],
                                                        identity=self.ident_b[0:n, 0:n]))
            cx.group("pe", fns, reads=[xn_r, self.ident_b_r], writes=[self.bres[b]])
            src = pb.rearrange("p (k c) -> p k c", c=128)[:, 0:k1 - k0, 0:n]
            eng = "act" if half % 2 == 0 else "dve"
            if eng == "act":
                cx.op("act", lambda e: e.copy(out=hT[:, k0:k1, col0:col0 + n], in_=src),
                      reads=[self.bres[b]], writes=[hT_r])
            else:
                cx.op("dve", lambda e: e.tensor_copy(out=hT[:, k0:k1, col0:col0 + n], in_=src),
                      reads=[self.bres[b]], writes=[hT_r])

    def _sb(self, es, name, shape, dtype):
        t = es.enter_context(self.nc.sbuf_tensor(name, list(shape), dtype))
        return t, self.cx.res(name)

    def ffn_phase(self, tag, wgu, wd, gain_ap, tiles, src_fn, dst_fn):
        import contextlib
        nc, cx = self.nc, self.cx
        wgu_r = self.dres[wgu.name]
        wd_r = self.dres[wd.name]
        Tmax = max(sum(n for _, n in t) for t in tiles)
        CW = 256
        NCB = D // CW
        with contextlib.ExitStack() as es:
            aT, aT_r = self._sb(es, tag + "aT", [128, NF, Tmax], BF16)
            hT, hT_r = self._sb(es, tag + "hT", [128, KC, Tmax], BF16)
            for ti, tile in enumerate(tiles):
                cols = []
                c = 0
                for (r0, n) in tile:
                    cols.append(c)
                    c += n
                T = c
                with contextlib.ExitStack() as es2:
                    srcs = []
                    for (r0, n), c0 in zip(tile, cols):
                        ap, r = src_fn(r0, n)
                        srcs.append((ap, r, n, c0))
                    self._norm_bufs = None
                    self.norm_blocks_cached(es2, srcs, gain_ap, hT, hT_r, f"{tag}{ti}")
                    cx.barrier()
                with contextlib.ExitStack() as es2:
                    NST = 7
                    H = KC * 128
                    wst = [self._sb(es2, f"{tag}{ti}wst{i}", [128, H], F32) for i in range(NST)]
                    wbf = [self._sb(es2, f"{tag}{ti}wbf{i}", [128, 2 * KC * 128], BF16) for i in range(2)]
                    sil = [self._sb(es2, f"{tag}{ti}sil{i}", [128, 512], F32) for i in range(2)]
                    subs = self.sub_tiles(T)
                    u = 0
                    hq = 0

                    def load_chunk(f):
                        nonlocal hq
                        b_t, b_r = wbf[f % 2]
                        for half in range(2):
                            s_t, s_r = wst[hq % NST]
                            hq += 1
                            cx.dma("sp", s_t[:], wgu[f][:, half * H:(half + 1) * H], s_r, reads=[wgu_r], writes=[s_r])
                            self.cast(b_t[:, half * H:(half + 1) * H], s_t[:], [s_r], [b_r], eng=("act" if half == 0 else "dve"))

                    load_chunk(0)
                    for f in range(NF):
                        b_t, b_r = wbf[f % 2]
                        if f + 1 < NF:
                            load_chunk(f + 1)
                        for (c0, n) in subs:
                            bg = 2 * (u % 4)
                            bu = bg + 1
                            fns = []
                            for gu, bk in ((0, bg), (1, bu)):
                                for kc in range(KC):
                                    fns.append(lambda e, gu=gu, bk=bk, kc=kc: e.matmul(
                                        self.bank[bk][:, 0:n],
                                        lhsT=b_t[:, (gu * KC + kc) * 128:(gu * KC + kc + 1) * 128],
                                        rhs=hT[:, kc, c0:c0 + n], start=(kc == 0), stop=(kc == KC - 1)))
                            cx.group("pe", fns, reads=[b_r, hT_r], writes=[self.bres[bg], self.bres[bu]])
                            sl_t, sl_r = sil[u % 2]
                            cx.op("act", lambda e: e.activation(out=sl_t[:, 0:n], in_=self.bank[bg][:, 0:n], func=AF.Silu),
                                  reads=[self.bres[bg]], writes=[sl_r])
                            cx.op("dve", lambda e: e.tensor_tensor(out=aT[:, f, c0:c0 + n], in0=sl_t[:, 0:n],
                                                                   in1=self.bank[bu][:, 0:n], op=ALU.mult),
                                  reads=[sl_r, self.bres[bu]], writes=[aT_r])
                            u += 1
                    cx.barrier()
                if self.cfg.get('skipD'):
                    continue
                with contextlib.ExitStack() as es2:
                    wres = [self._sb(es2, f"{tag}{ti}wres{i}", [128, NF, CW], BF16) for i in range(2)]
                    dst_ = [self._sb(es2, f"{tag}{ti}dst{i}", [128, 11, CW], F32) for i in range(2)]
                    xo = [self._sb(es2, f"{tag}{ti}xo{i}", [128, CW], F32) for i in range(4)]
                    xw = [self._sb(es2, f"{tag}{ti}xw{i}", [128, CW], F32) for i in range(4)]
                    g = 0
                    ev = 0
                    hb = 0
                    hres = [cx.res(f"{tag}{ti}hb{i}") for i in range(16)]
                    def load_wd(cb):
                        nonlocal g
                        w_t, w_r = wres[cb % 2]
                        for q in range(4):
                            s_t, s_r = dst_[g % 2]
                            g += 1
                            src = wd[q * 11 * 128:(q + 1) * 11 * 128, cb * CW:(cb + 1) * CW].rearrange("(j p) c -> p j c", p=128)
                            cx.dma("sp", s_t[:], src, s_r, reads=[wd_r], writes=[s_r])
                            self.cast(w_t[:, q * 11:(q + 1) * 11, :], s_t[:], [s_r], [w_r])

                    load_wd(0)
                    for cb in range(NCB):
                        w_t, w_r = wres[cb % 2]
                        if cb + 1 < NCB:
                            load_wd(cb + 1)
                        for bi, ((r0, n), c0) in enumerate(zip(tile, cols)):
                            bk, half = hb % 8, hb // 8
                            h_r = hres[hb % 8]
                            hb = (hb + 1) % 16
                            o = self.bank[bk][0:n, half * CW:(half + 1) * CW]
                            fns = [lambda e, f=f: e.matmul(o, lhsT=aT[:, f, c0:c0 + n], rhs=w_t[:, f, :],
                                                           start=(f == 0), stop=(f == NF - 1)) for f in range(NF)]
                            cx.group("pe", fns, reads=[w_r, aT_r], writes=[h_r])
                            xo_t, xo_r = xo[ev % 4]
                            xw_t, xw_r = xw[ev % 4]
                            ev += 1
                            sap, sr = src_fn(r0, n)
                            dap, dr = dst_fn(r0, n)
                            cx.dma("sp", xo_t[0:n, :], sap[:, cb * CW:(cb + 1) * CW], xo_r, reads=[sr], writes=[xo_r])
                            cx.op("dve", lambda e: e.scalar_tensor_tensor(out=xw_t[0:n, :], in0=o, scalar=0.5,
                                                                          in1=xo_t[0:n, :], op0=ALU.mult, op1=ALU.add),
                                  reads=[h_r, xo_r], writes=[xw_r])
                            cx.dma("sp", dap[:, cb * CW:(cb + 1) * CW], xw_t[0:n, :], xw_r, reads=[xw_r], writes=[dr])
                    cx.barrier()

    def norm_blocks_cached(self, es, srcs, gain_ap, hT, hT_r, tag):
        cx = self.cx
        if self._norm_bufs is None:
            gbc, gbc_r = self._sb(es, tag + "gbc", [128, D], F32)
            cx.dma("sp", gbc[:], gain_ap.partition_broadcast(128), gbc_r, writes=[gbc_r])
            xt = [self._sb(es, f"{tag}xt{i}", [128, D], F32) for i in range(2)]
            sq = [self._sb(es, f"{tag}sq{i}", [128, D], BF16) for i in range(2)]
            xn = [self._sb(es, f"{tag}xn{i}", [128, D], BF16) for i in range(2)]
            st = [self._sb(es, f"{tag}st{i}", [128, 4], F32) for i in range(2)]
            self._norm_bufs = (gbc, gbc_r, xt, sq, xn, st)
        gbc, gbc_r, xts, sqs, xns, sts = self._norm_bufs
        for bi, (src, src_r, n, col0) in enumerate(srcs):
            (xt, xt_r), (sq, sq_r), (xn, xn_r), (st, st_r) = xts[bi % 2], sqs[bi % 2], xns[bi % 2], sts[bi % 2]
            cx.dma("sp", xt[0:n, :], src, xt_r, reads=[src_r], writes=[xt_r])
            cx.op("act", lambda e: e.activation(out=sq[0:n, :], in_=xt[0:n, :], func=AF.Square,
                                                accum_out=st[0:n, 0:1]),
                  reads=[xt_r], writes=[sq_r, st_r])
            cx.op("act", lambda e: e.activation(out=st[0:n, 1:2], in_=st[0:n, 0:1], func=AF.Sqrt,
                                                scale=1.0 / D, bias=self.eps_t[0:n, 0:1]),
                  reads=[st_r, self.eps_r], writes=[st_r])
            cx.op("dve", lambda e: e.reciprocal(out=st[0:n, 2:3], in_=st[0:n, 1:2]), reads=[st_r], writes=[st_r])
            cx.op("dve", lambda e: e.scalar_tensor_tensor(out=xn[0:n, :], in0=xt[0:n, :], scalar=st[0:n, 2:3],
                                                          in1=gbc[0:n, :], op0=ALU.mult, op1=ALU.mult),
                  reads=[xt_r, st_r, gbc_r], writes=[xn_r])
            self.transpose_into(xn, xn_r, n, hT, hT_r, col0)

    def cast(self, out, in_, reads, writes, eng=None):
        if eng is None:
            self._ce = 1 - getattr(self, "_ce", 0)
            eng = "act" if self._ce else "dve"
        if eng == "act":
            self.cx.op("act", lambda e: e.copy(out=out, in_=in_), reads=reads, writes=writes)
        else:
            self.cx.op("dve", lambda e: e.tensor_copy(out=out, in_=in_), reads=reads, writes=writes)

    def nb(self):
        b = self._nb
        self._nb = (self._nb + 1) % 8
        return b

    def sub_tiles(self, T, w=512):
        out = []
        c = 0
        while c < T:
            n = min(w, T - c)
            out.append((c, n))
            c += n
        return out

    def wbufs(self, es, tag, nk, w, nst=2, nbf=2):
        return {"st": [self._sb(es, f"{tag}wS{i}", [128, nk, w], F32) for i in range(nst)],
                "bf": [self._sb(es, f"{tag}wB{i}", [128, nk, w], BF16) for i in range(nbf)], "i": 0}

    def load_w(self, bufs, src_ap, src_r, nk, w):
        cx = self.cx
        i = bufs["i"]
        bufs["i"] += 1
        st, st_r = bufs["st"][i % len(bufs["st"])]
        bf, bf_r = bufs["bf"][i % len(bufs["bf"])]
        cx.dma("sp", st[:, 0:nk, 0:w], src_ap.rearrange("(k p) c -> p k c", p=128), st_r, reads=[src_r], writes=[st_r])
        h = max(1, nk // 2)
        self.cast(bf[:, 0:h, 0:w], st[:, 0:h, 0:w], [st_r], [bf_r], eng="act")
        if nk > h:
            self.cast(bf[:, h:nk, 0:w], st[:, h:nk, 0:w], [st_r], [bf_r], eng="dve")
        return bf, bf_r

    def load_w_resident(self, bufs, W_ap, W_r, nk, N, dst, dst_r):
        cx = self.cx
        for k0 in range(0, nk, 4):
            k1 = min(nk, k0 + 4)
            for c0 in range(0, N, 512):
                c1 = min(N, c0 + 512)
                i = bufs["i"]
                bufs["i"] += 1
                st, st_r = bufs["st"][i % len(bufs["st"])]
                cx.dma("sp", st[:, 0:k1 - k0, 0:c1 - c0],
                       W_ap[k0 * 128:k1 * 128, c0:c1].rearrange("(k p) c -> p k c", p=128), st_r,
                       reads=[W_r], writes=[st_r])
                self.cast(dst[:, k0:k1, c0:c1], st[:, 0:k1 - k0, 0:c1 - c0], [st_r], [dst_r])

    def norm_all(self, es, tag, src_fn, gain_ap, blocks):
        T = max(r0 + n for r0, n in blocks)
        hT, hT_r = self._sb(es, tag + "hT", [128, KC, T], BF16)
        self._norm_bufs = None
        srcs = []
        for (r0, n) in blocks:
            ap, r = src_fn(r0, n)
            srcs.append((ap, r, n, r0))
        import contextlib
        with contextlib.ExitStack() as es2:
            self.norm_blocks_cached(es2, srcs, gain_ap, hT, hT_r, tag)
            self.cx.barrier()
        return hT, hT_r

    def mm_tok(self, hT, hT_r, nk, col0, n, wb, wb_r, wc0, w, bank):
        fns = [lambda e, kc=kc: e.matmul(self.bank[bank][0:n, 0:w], lhsT=hT[:, kc, col0:col0 + n],
                                         rhs=wb[:, kc, wc0:wc0 + w], start=(kc == 0), stop=(kc == nk - 1))
               for kc in range(nk)]
        self.cx.group("pe", fns, reads=[hT_r, wb_r], writes=[self.bres[bank]])

    def mm_feat(self, hT, hT_r, nk, c0, n, wb, wb_r, wc0, m, bank, bcol=0):
        fns = [lambda e, kc=kc: e.matmul(self.bank[bank][0:m, bcol:bcol + n], lhsT=wb[:, kc, wc0:wc0 + m],
                                         rhs=hT[:, kc, c0:c0 + n], start=(kc == 0), stop=(kc == nk - 1))
               for kc in range(nk)]
        self.cx.group("pe", fns, reads=[hT_r, wb_r], writes=[self.bres[bank]])

    def out_proj(self, es, tag, oT, oT_r, nk, W_ap, W_r, blocks, src_fn, dst_fn):
        cx = self.cx
        wres, wres_r = self._sb(es, tag + "Wres", [128, nk, D], BF16)
        bufs = self.wbufs(es, tag + "op", 4, 512, nst=2, nbf=0)
        self.load_w_resident(bufs, W_ap, W_r, nk, D, wres, wres_r)
        xo = [self._sb(es, f"{tag}oxo{i}", [128, 512], F32) for i in range(3)]
        xw = [self._sb(es, f"{tag}oxw{i}", [128, 512], F32) for i in range(3)]
        ev = 0
        for (r0, n, col0) in blocks:
            sap, sr = src_fn(r0, n)
            dap, dr = dst_fn(r0, n)
            for cb in range(4):
                b = self.nb()
                self.mm_tok(oT, oT_r, nk, col0, n, wres, wres_r, cb * 512, 512, b)
                xo_t, xo_r = xo[ev % 3]
                xw_t, xw_r = xw[ev % 3]
                ev += 1
                cx.dma("sp", xo_t[0:n, :], sap[:, cb * 512:(cb + 1) * 512], xo_r, reads=[sr], writes=[xo_r])
                cx.op("dve", lambda e: e.tensor_tensor(out=xw_t[0:n, :], in0=self.bank[b][0:n, :], in1=xo_t[0:n, :], op=ALU.add),
                      reads=[self.bres[b], xo_r], writes=[xw_r])
                cx.dma("sp", dap[:, cb * 512:(cb + 1) * 512], xw_t[0:n, :], xw_r, reads=[xw_r], writes=[dr])

    def transpose_blk(self, src, src_r, n, ncols, dst_fn, dst_r, dt=BF16, evac="act", scale=None):
        cx = self.cx
        nk = ncols // 128
        per = 8 if dt == BF16 else 4
        ident, ident_r = (self.ident_b, self.ident_b_r) if dt == BF16 else (self.ident_f, self.ident_f_r)
        for k0 in range(0, nk, per):
            k1 = min(nk, k0 + per)
            b = self.nb()
            pb = self.bank[b][:].bitcast(BF16) if dt == BF16 else self.bank[b][:]
            fns = [lambda e, kc=kc: e.transpose(out=pb[:, (kc - k0) * 128:(kc - k0) * 128 + n],
                                                in_=src[0:n, kc * 128:(kc + 1) * 128], identity=ident[0:n, 0:n])
                   for kc in range(k0, k1)]
            cx.group("pe", fns, reads=[src_r, ident_r], writes=[self.bres[b]])
            for kc in range(k0, k1):
                o = dst_fn(kc)
                i_ = pb[:, (kc - k0) * 128:(kc - k0) * 128 + n]
                if evac == "act":
                    cx.op("act", lambda e: e.copy(out=o, in_=i_), reads=[self.bres[b]], writes=[dst_r])
                else:
                    cx.op("dve", lambda e: e.tensor_copy(out=o, in_=i_), reads=[self.bres[b]], writes=[dst_r])

    def memkv_phase(self):
        import contextlib
        cx = self.cx
        mem = self.dram["mem_p"]
        with contextlib.ExitStack() as es:
            mT, mT_r = self._sb(es, "mkT", [128, KC, 256], BF16)
            xt, xt_r = self._sb(es, "mkx", [128, D], F32)
            xb, xb_r = self._sb(es, "mkxb", [128, D], BF16)
            for blk in range(2):
                cx.dma("sp", xt[:], mem.ap()[blk * 128:(blk + 1) * 128, :], xt_r, reads=[self.dres["mem_p"]], writes=[xt_r])
                cx.op("act", lambda e: e.copy(out=xb[:], in_=xt[:]), reads=[xt_r], writes=[xb_r])
                self.transpose_into(xb, xb_r, 128, mT, mT_r, blk * 128)
            bufs = self.wbufs(es, "mk", KC, 512)
            ev = [self._sb(es, f"mkev{i}", [128, 512], F32) for i in range(2)]
            k = 0
            for l in range(2):
                for nm, wn in (("mem_k_p", "xwk"), ("mem_v_p", "xwv")):
                    wb, wb_r = self.load_w(bufs, self.dram[wn].ap()[l], self.dres[wn], KC, 512)
                    for blk in range(2):
                        b = self.nb()
                        self.mm_tok(mT, mT_r, KC, blk * 128, 128, wb, wb_r, 0, 512, b)
                        e_t, e_r = ev[k % 2]
                        k += 1
                        cx.op("act", lambda e: e.copy(out=e_t[:], in_=self.bank[b][:]), reads=[self.bres[b]], writes=[e_r])
                        cx.dma("sp", self.dram[nm].ap()[l, blk * 128:(blk + 1) * 128, :], e_t[:], e_r,
                               reads=[e_r], writes=[self.dres[nm]])
        cx.barrier()

    def final_phase(self, src_fn, dst_fn, gain_ap, blocks):
        import contextlib
        cx = self.cx
        with contextlib.ExitStack() as es:
            gbc, gbc_r = self._sb(es, "fgbc", [128, D], F32)
            cx.dma("sp", gbc[:], gain_ap.partition_broadcast(128), gbc_r, writes=[gbc_r])
            xt = [self._sb(es, f"fx{i}", [128, D], F32) for i in range(2)]
            sq, sq_r = self._sb(es, "fsq", [128, D], BF16)
            yo = [self._sb(es, f"fy{i}", [128, D], F32) for i in range(2)]
            st, st_r = self._sb(es, "fst", [128, 4], F32)
            for i, (r0, n) in enumerate(blocks):
                x_t, x_r = xt[i % 2]
                y_t, y_r = yo[i % 2]
                sap, sr = src_fn(r0, n)
                dap, dr = dst_fn(r0, n)
                cx.dma("sp", x_t[0:n, :], sap, x_r, reads=[sr], writes=[x_r])
                cx.op("act", lambda e: e.activation(out=sq[0:n, :], in_=x_t[0:n, :], func=AF.Square, accum_out=st[0:n, 0:1]),
                      reads=[x_r], writes=[sq_r, st_r])
                cx.op("act", lambda e: e.activation(out=st[0:n, 1:2], in_=st[0:n, 0:1], func=AF.Sqrt, scale=1.0 / D,
                                                    bias=self.eps_t[0:n, 0:1]), reads=[st_r, self.eps_r], writes=[st_r])
                cx.op("dve", lambda e: e.reciprocal(out=st[0:n, 2:3], in_=st[0:n, 1:2]), reads=[st_r], writes=[st_r])
                cx.op("dve", lambda e: e.scalar_tensor_tensor(out=y_t[0:n, :], in0=x_t[0:n, :], scalar=st[0:n, 2:3],
                                                              in1=gbc[0:n, :], op0=ALU.mult, op1=ALU.mult),
                      reads=[x_r, st_r, gbc_r], writes=[y_r])
                cx.dma("sp", dap, y_t[0:n, :], y_r, reads=[y_r], writes=[dr])
        cx.barrier()

    def xattn_phase(self, l, src_fn, dst_fn):
        import contextlib
        cx = self.cx
        tag = f"xa{l}"
        blocks = blocks_of(0, SEQ) + [(SEQ, NS)]
        scale = 128 ** -0.5
        with contextlib.ExitStack() as es:
            hT, hT_r = self.norm_all(es, tag, src_fn, self.dram["norm_gain"].ap()[l, 2], blocks)
            qT, qT_r = self._sb(es, tag + "qT", [128, 4, NTOK], BF16)
            oT, oT_r = self._sb(es, tag + "oT", [128, 4, NTOK], BF16)
            with contextlib.ExitStack() as es2:
                bufs = self.wbufs(es2, tag + "q", KC, 512, nst=1, nbf=1)
                wb, wb_r = self.load_w(bufs, self.dram["xwq"].ap()[l], self.dres["xwq"], KC, 512)
                for hd in range(4):
                    for (c0, n) in self.sub_tiles(NTOK):
                        b = self.nb()
                        self.mm_feat(hT, hT_r, KC, c0, n, wb, wb_r, hd * 128, 128, b)
                        cx.op("act", lambda e: e.copy(out=qT[:, hd, c0:c0 + n], in_=self.bank[b][:, 0:n]),
                              reads=[self.bres[b]], writes=[qT_r])
                cx.barrier()
            kf = [self._sb(es, f"{tag}kf{i}", [128, 2, 512], F32) for i in range(2)]
            vf = [self._sb(es, f"{tag}vf{i}", [128, 2, 512], F32) for i in range(2)]
            kb = [self._sb(es, f"{tag}kb{i}", [128, 2, 512], BF16) for i in range(2)]
            vb = [self._sb(es, f"{tag}vb{i}", [128, 2, 512], BF16) for i in range(2)]
            kT = [self._sb(es, f"{tag}kT{i}", [128, 4, 256], BF16) for i in range(2)]
            pp = [self._sb(es, f"{tag}p{i}", [128, 256], F32) for i in range(2)]
            pn = [self._sb(es, f"{tag}pn{i}", [128, 256], BF16) for i in range(2)]
            pT = [self._sb(es, f"{tag}pT{i}", [128, 2, 128], BF16) for i in range(2)]
            sm = [self._sb(es, f"{tag}sm{i}", [128, 4], F32) for i in range(2)]
            u = 0

            def load_kv(i, k_ap, k_r, v_ap, v_r):
                kf_t, kf_r = kf[i % 2]
                vf_t, vf_r = vf[i % 2]
                kb_t, kb_r = kb[i % 2]
                vb_t, vb_r = vb[i % 2]
                kT_t, kT_r = kT[i % 2]
                cx.dma("sp", kf_t[:], k_ap.rearrange("(c p) f -> p c f", p=128), kf_r, reads=[k_r], writes=[kf_r])
                cx.dma("sp", vf_t[:], v_ap.rearrange("(c p) f -> p c f", p=128), vf_r, reads=[v_r], writes=[vf_r])
                self.cast(kb_t[:], kf_t[:], [kf_r], [kb_r], eng="act")
                self.cast(vb_t[:], vf_t[:], [vf_r], [vb_r], eng="dve")
                for mc in range(2):
                    self.transpose_blk(kb_t[:, mc, :], kb_r, 128, 512,
                                       lambda hd: kT_t[:, hd, mc * 128:(mc + 1) * 128], kT_r, evac="dve")
                return kT_t, kT_r, vb_t, vb_r

            def s1(c0, n, kT_t, kT_r, vb_t, vb_r, hd, uu):
                p_t, p_r = pp[uu % 2]
                sm_t, sm_r = sm[uu % 2]
                b = self.nb()
                cx.group("pe", [lambda e: e.matmul(self.bank[b][0:n, 0:256], lhsT=qT[:, hd, c0:c0 + n],
                                                   rhs=kT_t[:, hd, :], start=True, stop=True)],
                         reads=[qT_r, kT_r], writes=[self.bres[b]])
                cx.op("dve", lambda e: e.reduce_max(out=sm_t[0:n, 0:1], in_=self.bank[b][0:n, 0:256], axis=AX.X),
                      reads=[self.bres[b]], writes=[sm_r])
                cx.op("dve", lambda e: e.tensor_scalar_mul(out=sm_t[0:n, 1:2], in0=sm_t[0:n, 0:1], scalar1=-scale),
                      reads=[sm_r], writes=[sm_r])
                cx.op("act", lambda e: e.activation(out=p_t[0:n, :], in_=self.bank[b][0:n, 0:256], func=AF.Exp,
                                                    scale=scale, bias=sm_t[0:n, 1:2], accum_out=sm_t[0:n, 2:3]),
                      reads=[self.bres[b], sm_r], writes=[p_r, sm_r])

            def s2(c0, n, kT_t, kT_r, vb_t, vb_r, hd, uu):
                p_t, p_r = pp[uu % 2]
                pn_t, pn_r = pn[uu % 2]
                pT_t, pT_r = pT[uu % 2]
                sm_t, sm_r = sm[uu % 2]
                cx.op("dve", lambda e: e.reciprocal(out=sm_t[0:n, 3:4], in_=sm_t[0:n, 2:3]), reads=[sm_r], writes=[sm_r])
                cx.op("dve", lambda e: e.tensor_scalar_mul(out=pn_t[0:n, :], in0=p_t[0:n, :], scalar1=sm_t[0:n, 3:4]),
                      reads=[p_r, sm_r], writes=[pn_r])
                self.transpose_blk(pn_t, pn_r, n, 256, lambda mc: pT_t[:, mc, 0:n], pT_r, evac="act")
                b2 = self.nb()
                cx.group("pe", [lambda e, mc=mc: e.matmul(self.bank[b2][:, 0:n], lhsT=vb_t[:, mc, hd * 128:(hd + 1) * 128],
                                                          rhs=pT_t[:, mc, 0:n], start=(mc == 0), stop=(mc == 1))
                                for mc in range(2)],
                         reads=[vb_r, pT_r], writes=[self.bres[b2]])
                cx.op("act", lambda e: e.copy(out=oT[:, hd, c0:c0 + n], in_=self.bank[b2][:, 0:n]),
                      reads=[self.bres[b2]], writes=[oT_r])

            pend = [None]

            def unit(c0, n, kT_t, kT_r, vb_t, vb_r):
                nonlocal u
                for hd in range(4):
                    item = (c0, n, kT_t, kT_r, vb_t, vb_r, hd, u)
                    u += 1
                    s1(*item)
                    if pend[0] is not None:
                        s2(*pend[0])
                    pend[0] = item

            kvp = load_kv(0, self.dram["mem_k_p"].ap()[l], self.dres["mem_k_p"], self.dram["mem_v_p"].ap()[l], self.dres["mem_v_p"])
            for (r0, n) in blocks_of(0, SEQ):
                unit(r0, n, *kvp)
            for bi in range(NB):
                kv = load_kv(bi + 1, self.dram["cmk"].ap()[l, bi], self.dres["cmk"], self.dram["cmv"].ap()[l, bi], self.dres["cmv"])
                unit(SEQ + bi * DSEQ, DSEQ, *kv)
            s2(*pend[0])
            self.out_proj(es, tag, oT, oT_r, 4, self.dram["xwo"].ap()[l], self.dres["xwo"],
                          [(r0, n, r0) for (r0, n) in blocks], src_fn, dst_fn)
        cx.barrier()

    def even_in_phase(self, src_fn):
        import contextlib
        cx = self.cx
        tag = "ei"
        blocks = blocks_of(0, SEQ) + [(SEQ, NS)]
        d, r = self.dram, self.dres
        with contextlib.ExitStack() as es:
            hT, hT_r = self.norm_all(es, tag, src_fn, d["norm_gain"].ap()[0, 1], blocks)
            bufs = self.wbufs(es, tag, KC, 512)
            evf = [self._sb(es, f"{tag}evf{i}", [128, 512], F32) for i in range(3)]
            evb = [self._sb(es, f"{tag}evb{i}", [128, 512], BF16) for i in range(3)]
            k = 0
            tok_dst = [("KM", 0, BF16), ("VM", 0, BF16), ("VM", 512, BF16), ("OM", 0, F32), ("OM", 512, F32), (None, 0, F32)]
            for cb in range(6):
                wb, wb_r = self.load_w(bufs, d["wtok"].ap()[:, cb * 512:(cb + 1) * 512], r["wtok"], KC, 512)
                nm, dc, dt = tok_dst[cb]
                for (r0, n) in blocks:
                    b = self.nb()
                    self.mm_tok(hT, hT_r, KC, r0, n, wb, wb_r, 0, 512, b)
                    e_t, e_r = (evf if dt == F32 else evb)[k % 3]
                    k += 1
                    cx.op("act", lambda e: e.copy(out=e_t[0:n, :], in_=self.bank[b][0:n, :]), reads=[self.bres[b]], writes=[e_r])
                    if nm is not None:
                        cx.dma("sp", d[nm].ap()[r0:r0 + n, dc:dc + 512], e_t[0:n, :], e_r, reads=[e_r], writes=[r[nm]])
                    else:
                        cx.dma("sp", d["KS"].ap()[r0:r0 + n, :], e_t[0:n, 0:256], e_r, reads=[e_r], writes=[r["KS"]])
                        cx.dma("sp", d["VS"].ap()[r0:r0 + n, :], e_t[0:n, 256:512], e_r, reads=[e_r], writes=[r["VS"]])
            feat_dst = [("QMT", 0), ("KMT", 0), ("QST", 0), ("QST", 512), ("KSTD", 0)]
            for jb in range(5):
                wb, wb_r = self.load_w(bufs, d["wfeat"].ap()[:, jb * 512:(jb + 1) * 512], r["wfeat"], KC, 512)
                nm, dr0 = feat_dst[jb]
                for jj in range(4):
                    for (c0, n) in self.sub_tiles(NTOK):
                        b = self.nb()
                        self.mm_feat(hT, hT_r, KC, c0, n, wb, wb_r, jj * 128, 128, b)
                        e_t, e_r = evb[k % 3]
                        k += 1
                        if nm == "QMT":
                            cx.op("act", lambda e: e.mul(out=e_t[:, 0:n], in_=self.bank[b][:, 0:n], mul=128 ** -0.5),
                                  reads=[self.bres[b]], writes=[e_r])
                        else:
                            cx.op("act", lambda e: e.copy(out=e_t[:, 0:n], in_=self.bank[b][:, 0:n]), reads=[self.bres[b]], writes=[e_r])
                        rr = dr0 + jj * 128
                        cx.dma("sp", d[nm].ap()[rr:rr + 128, c0:c0 + n], e_t[:, 0:n], e_r, reads=[e_r], writes=[r[nm]])
            wif_s, wif_sr = self._sb(es, tag + "wifs", [128, KC, 8], F32)
            wif, wif_r = self._sb(es, tag + "wif", [128, KC, 8], BF16)
            cx.dma("sp", wif_s[:], d["wif"].ap().rearrange("(k p) c -> p k c", p=128), wif_sr, reads=[r["wif"]], writes=[wif_sr])
            self.cast(wif[:], wif_s[:], [wif_sr], [wif_r])
            ifr, ifr_r = self._sb(es, tag + "ifr", [4, 2, NTOK], F32)
            for g in range(2):
                for (c0, n) in self.sub_tiles(NTOK):
                    b = self.nb()
                    self.mm_feat(hT, hT_r, KC, c0, n, wif, wif_r, g * 4, 4, b)
                    cx.op("act", lambda e: e.copy(out=ifr[:, g, c0:c0 + n], in_=self.bank[b][0:4, 0:n]),
                          reads=[self.bres[b]], writes=[ifr_r])
            cx.dma("sp", d["IF"].ap().rearrange("g h t -> h g t"), ifr[:], ifr_r, reads=[ifr_r], writes=[r["IF"]])
        cx.barrier()

    def mlstm_chunks(self):
        ch = [(c * 64, 64, None) for c in range(SEQ // 64)]
        ch += [(SEQ + b * DSEQ, DSEQ, b) for b in range(NB)]
        return ch

    def mlstm_scal_phase(self):
        import contextlib
        cx = self.cx
        d, r = self.dram, self.dres
        tag = "ms"
        with contextlib.ExitStack() as es:
            def T(nm, w=NTOK):
                return self._sb(es, tag + nm, [4, w], F32)
            ifr, ifr_r = self._sb(es, tag + "ifr", [4, 2, NTOK], F32)
            cx.dma("sp", ifr[:], d["IF"].ap().rearrange("g h t -> h g t"), ifr_r, reads=[r["IF"]], writes=[ifr_r])
            bi, bi_r = T("bi", 2)
            cx.dma("sp", bi[:, 0:1], d["mlstm_b_i"].ap().rearrange("o h -> h o"), bi_r, reads=[r["mlstm_b_i"]], writes=[bi_r])
            cx.dma("sp", bi[:, 1:2], d["mlstm_b_f"].ap().rearrange("o h -> h o"), bi_r, reads=[r["mlstm_b_f"]], writes=[bi_r])
            one, one_r = T("one")
            cx.op("pool", lambda e: e.memset(one[:], 1.0), writes=[one_r])
            ninf, ninf_r = T("ninf", 64)
            cx.op("pool", lambda e: e.memset(ninf[:], -1e30), writes=[ninf_r])
            li, li_r = T("li")
            x, x_r = T("x")
            ax, ax_r = T("ax")
            lf, lf_r = T("lf")
            bb, bb_r = T("b")
            a, a_r = T("a")
            mu, mu_r = T("mu")
            ws, ws_r = T("ws")
            M, M_r = T("M", SEQ // 64 + 1)
            Ms, Ms_r = T("Ms", 2 * NB)
            cx.op("dve", lambda e: e.tensor_scalar(out=li[:], in0=ifr[:, 0, :], scalar1=bi[:, 0:1], scalar2=None, op0=ALU.add),
                  reads=[ifr_r, bi_r], writes=[li_r])
            cx.op("dve", lambda e: e.tensor_scalar(out=x[:], in0=ifr[:, 1, :], scalar1=bi[:, 1:2], scalar2=None, op0=ALU.add),
                  reads=[ifr_r, bi_r], writes=[x_r])
            cx.op("act", lambda e: e.activation(out=ax[:], in_=x[:], func=AF.Abs), reads=[x_r], writes=[ax_r])
            cx.op("act", lambda e: e.activation(out=ax[:], in_=ax[:], func=AF.Exp, scale=-1.0), reads=[ax_r], writes=[ax_r])
            cx.op("act", lambda e: e.activation(out=ax[:], in_=ax[:], func=AF.Ln, bias=one[:, 0:1]), reads=[ax_r, one_r], writes=[ax_r])
            cx.op("dve", lambda e: e.tensor_scalar_min(out=lf[:], in0=x[:], scalar1=0.0), reads=[x_r], writes=[lf_r])
            cx.op("dve", lambda e: e.tensor_tensor(out=lf[:], in0=lf[:], in1=ax[:], op=ALU.subtract), reads=[lf_r, ax_r], writes=[lf_r])
            for (t0, L, sb) in self.mlstm_chunks():
                cx.op("dve", lambda e: e.tensor_tensor_scan(out=bb[:, t0:t0 + L], data0=one[:, t0:t0 + L], data1=lf[:, t0:t0 + L],
                                                            initial=0.0, op0=ALU.mult, op1=ALU.add),
                      reads=[one_r, lf_r], writes=[bb_r])
            cx.op("dve", lambda e: e.tensor_tensor(out=a[:], in0=li[:], in1=bb[:], op=ALU.subtract), reads=[li_r, bb_r], writes=[a_r])
            cx.op("pool", lambda e: e.memset(M[:], 0.0), writes=[M_r])
            cx.dma("sp", Ms[:, 0:NB], d["mm"].ap().rearrange("b h -> h b"), Ms_r, reads=[r["mm"]], writes=[Ms_r])
            for ci, (t0, L, sb) in enumerate(self.mlstm_chunks()):
                m_in = M[:, ci:ci + 1] if sb is None else Ms[:, sb:sb + 1]
                m_out = M[:, ci + 1:ci + 2] if sb is None else Ms[:, NB + sb:NB + sb + 1]
                mr = M_r if sb is None else Ms_r
                cx.op("dve", lambda e: e.tensor_tensor_scan(out=mu[:, t0:t0 + L], data0=ninf[:, 0:L], data1=a[:, t0:t0 + L],
                                                            initial=m_in, op0=ALU.max, op1=ALU.max),
                      reads=[ninf_r, a_r, mr], writes=[mu_r])
                cx.op("act", lambda e: e.activation(out=ws[:, t0:t0 + L], in_=mu[:, t0:t0 + L], func=AF.Exp, scale=-1.0, bias=m_in),
                      reads=[mu_r, mr], writes=[ws_r])
                cx.op("dve", lambda e: e.tensor_tensor(out=m_out, in0=bb[:, t0 + L - 1:t0 + L], in1=mu[:, t0 + L - 1:t0 + L], op=ALU.add),
                      reads=[bb_r, mu_r], writes=[mr])
            cx.op("dve", lambda e: e.tensor_tensor(out=bb[:], in0=bb[:], in1=mu[:], op=ALU.add), reads=[bb_r, mu_r], writes=[bb_r])
            cx.op("dve", lambda e: e.tensor_scalar_mul(out=mu[:], in0=mu[:], scalar1=-1.0), reads=[mu_r], writes=[mu_r])
            cx.dma("sp", d["NMU"].ap(), mu[:], mu_r, reads=[mu_r], writes=[r["NMU"]])
            cx.dma("sp", d["WS"].ap(), ws[:], ws_r, reads=[ws_r], writes=[r["WS"]])
            with self.nc.allow_non_contiguous_dma(reason="tiny transposed gate-scalar scratch"):
                cx.dma("sp", d["SCT"].ap()[:, 0:4].rearrange("t h -> h t"), a[:], a_r, reads=[a_r], writes=[r["SCT"]])
                cx.dma("sp", d["SCT"].ap()[:, 4:8].rearrange("t h -> h t"), bb[:], bb_r, reads=[bb_r], writes=[r["SCT"]])
                cx.dma("sp", d["mm_p"].ap().rearrange("(h o) -> h o", o=1), M[:, SEQ // 64:SEQ // 64 + 1], M_r, reads=[M_r], writes=[r["mm_p"]])
                cx.dma("sp", d["mm_s"].ap().rearrange("b h -> h b"), Ms[:, NB:2 * NB], Ms_r, reads=[Ms_r], writes=[r["mm_s"]])
        cx.barrier()

    def mlstm_phase(self):
        import contextlib
        cx = self.cx
        d, r = self.dram, self.dres
        tag = "ml"
        with contextlib.ExitStack() as es:
            S = lambda nm, shape, dt: self._sb(es, tag + nm, shape, dt)
            NBUF = 2
            qT = [S(f"qT{i}", [128, 4, 64], BF16) for i in range(NBUF)]
            kT = [S(f"kT{i}", [128, 4, 64], BF16) for i in range(NBUF)]
            kt = [S(f"kt{i}", [64, 512], BF16) for i in range(NBUF)]
            vv = [S(f"v{i}", [64, 4, 256], BF16) for i in range(NBUF)]
            nmu = [S(f"nmu{i}", [64, 4, 64], F32) for i in range(NBUF)]
            wsb = [S(f"wsb{i}", [128, 4, 64], F32) for i in range(NBUF)]
            col = [S(f"col{i}", [64, 8], F32) for i in range(NBUF)]
            WT = [S(f"WT{i}", [64, 4, 64], F32) for i in range(2)]
            ST = [S(f"ST{i}", [64, 4, 64], BF16) for i in range(2)]
            qs = [S(f"qs{i}", [128, 4, 64], BF16) for i in range(2)]
            vw = [S(f"vw{i}", [64, 4, 256], BF16) for i in range(2)]
            wlb = [S(f"wlb{i}", [64, 4], BF16) for i in range(2)]
            hm = [S(f"hm{i}", [64, 4, 256], F32) for i in range(2)]
            sm = [S(f"sm{i}", [64, 4, 4], F32) for i in range(2)]
            C, C_r = S("C", [128, 4, 256], F32)
            Cb, Cb_r = S("Cb", [128, 4, 256], BF16)
            nn, nn_r = S("n", [128, 4], F32)
            nb_, nb_r = S("nb", [128, 4], BF16)
            ones, ones_r = S("ones", [64, 1], BF16)
            cx.op("pool", lambda e: e.memset(ones[:], 1.0), writes=[ones_r])
            cx.op("pool", lambda e: e.memset(C[:], 0.0), writes=[C_r])
            cx.op("pool", lambda e: e.memset(nn[:], 0.0), writes=[nn_r])
            cx.op("pool", lambda e: e.memset(Cb[:], 0.0), writes=[Cb_r])
            cx.op("pool", lambda e: e.memset(nb_[:], 0.0), writes=[nb_r])
            chunks = self.mlstm_chunks()
            Us = [S(f"Us{i}", [128, 4, 256], F32) for i in range(2)]
            nUs = [S(f"nUs{i}", [128, 4], F32) for i in range(2)]

            def handles(ci):
                t0, L, sb = chunks[ci]
                i = ci % NBUF
                j = ci % 2
                return (t0, L, sb, qT[i], kT[i], kt[i], vv[i], nmu[i], wsb[i], col[i], WT[j], ST[j], qs[j], vw[j], wlb[j], hm[j], sm[j], Us[j], nUs[j])

            def part1(ci):
                (t0, L, sb, (qT_t, qT_r), (kT_t, kT_r), (kt_t, kt_r), (v_t, v_r), (nmu_t, nmu_r), (wsb_t, wsb_r), (col_t, col_r),
                 (WT_t, WT_r), (ST_t, ST_r), (qs_t, qs_r), (vw_t, vw_r), (wlb_t, wlb_r), (hm_t, hm_r), (sm_t, sm_r), (Us_t, Us_r), (nUs_t, nUs_r)) = handles(ci)
                cx.dma("sp", qT_t[:, :, 0:L], d["QMT"].ap()[:, t0:t0 + L].rearrange("(h p) t -> p h t", p=128), qT_r, reads=[r["QMT"]], writes=[qT_r])
                cx.dma("sp", kT_t[:, :, 0:L], d["KMT"].ap()[:, t0:t0 + L].rearrange("(h p) t -> p h t", p=128), kT_r, reads=[r["KMT"]], writes=[kT_r])
                cx.dma("sp", kt_t[0:L, :], d["KM"].ap()[t0:t0 + L, :], kt_r, reads=[r["KM"]], writes=[kt_r])
                cx.dma("sp", v_t[0:L], d["VM"].ap()[t0:t0 + L, :].rearrange("t (h v) -> t h v", h=4), v_r, reads=[r["VM"]], writes=[v_r])
                cx.dma("sp", nmu_t[0:L, :, 0:L], d["NMU"].ap()[:, t0:t0 + L].partition_broadcast(L), nmu_r, reads=[r["NMU"]], writes=[nmu_r])
                cx.dma("sp", wsb_t[:, :, 0:L], d["WS"].ap()[:, t0:t0 + L].partition_broadcast(128), wsb_r, reads=[r["WS"]], writes=[wsb_r])
                cx.dma("sp", col_t[0:L, :], d["SCT"].ap()[t0:t0 + L, :], col_r, reads=[r["SCT"]], writes=[col_r])
                for h in range(4):
                    cx.op("act", lambda e: e.activation(out=WT_t[0:L, h, 0:L], in_=nmu_t[0:L, h, 0:L], func=AF.Exp, bias=col_t[0:L, h:h + 1]),
                          reads=[nmu_r, col_r], writes=[WT_r])
                cx.op("pool", lambda e: e.affine_select(out=WT_t[0:L, :, 0:L], in_=WT_t[0:L, :, 0:L], pattern=[[0, 4], [1, L]],
                                                       compare_op=ALU.is_ge, fill=0.0, base=0, channel_multiplier=-1),
                      reads=[WT_r], writes=[WT_r])
                bq = self.nb()
                kq = self.bank[bq][0:L, 0:256].rearrange("s (h t) -> s h t", h=4)
                cx.group("pe", [lambda e, h=h: e.matmul(kq[:, h, 0:L], lhsT=kT_t[:, h, 0:L], rhs=qT_t[:, h, 0:L], start=True, stop=True)
                                for h in range(4)], reads=[kT_r, qT_r], writes=[self.bres[bq]])
                cx.op("dve", lambda e: e.tensor_tensor(out=ST_t[0:L, :, 0:L], in0=kq[:, :, 0:L], in1=WT_t[0:L, :, 0:L], op=ALU.mult),
                      reads=[self.bres[bq], WT_r], writes=[ST_r])
                cx.op("dve", lambda e: e.tensor_tensor(out=qs_t[:, :, 0:L], in0=qT_t[:, :, 0:L], in1=wsb_t[:, :, 0:L], op=ALU.mult),
                      reads=[qT_r, wsb_r], writes=[qs_r])
                for h in range(4):
                    cx.op("act", lambda e: e.activation(out=vw_t[0:L, h, :], in_=v_t[0:L, h, :], func=AF.Copy, scale=WT_t[0:L, h, L - 1:L]),
                          reads=[v_r, WT_r], writes=[vw_r])
                cx.op("dve", lambda e: e.tensor_copy(out=wlb_t[0:L, :], in_=WT_t[0:L, :, L - 1]), reads=[WT_r], writes=[wlb_r])
                bu = [self.nb(), self.nb()]
                bnu = self.nb()
                fns = []
                for h in range(4):
                    o = self.bank[bu[h // 2]][:, (h % 2) * 256:(h % 2 + 1) * 256]
                    fns.append(lambda e, h=h, o=o: e.matmul(o, lhsT=kt_t[0:L, h * 128:(h + 1) * 128], rhs=vw_t[0:L, h, :], start=True, stop=True))
                for h in range(4):
                    fns.append(lambda e, h=h: e.matmul(self.bank[bnu][:, h:h + 1], lhsT=kt_t[0:L, h * 128:(h + 1) * 128], rhs=wlb_t[0:L, h:h + 1],
                                                       start=True, stop=True))
                cx.group("pe", fns, reads=[kt_r, vw_r, wlb_r], writes=[self.bres[bu[0]], self.bres[bu[1]], self.bres[bnu]])
                for hh in range(2):
                    cx.op("act", lambda e: e.copy(out=Us_t[:, 2 * hh:2 * hh + 2, :], in_=self.bank[bu[hh]][:, 0:512].rearrange("p (h v) -> p h v", h=2)),
                          reads=[self.bres[bu[hh]]], writes=[Us_r])
                cx.op("dve", lambda e: e.tensor_copy(out=nUs_t[:], in_=self.bank[bnu][:, 0:4]), reads=[self.bres[bnu]], writes=[nUs_r])

            def part2(ci):
                (t0, L, sb, (qT_t, qT_r), (kT_t, kT_r), (kt_t, kt_r), (v_t, v_r), (nmu_t, nmu_r), (wsb_t, wsb_r), (col_t, col_r),
                 (WT_t, WT_r), (ST_t, ST_r), (qs_t, qs_r), (vw_t, vw_r), (wlb_t, wlb_r), (hm_t, hm_r), (sm_t, sm_r), (Us_t, Us_r), (nUs_t, nUs_r)) = handles(ci)
                if sb is not None:
                    cx.dma("sp", C[:], d["mC"].ap()[sb].rearrange("h k v -> k h v"), C_r, reads=[r["mC"]], writes=[C_r])
                    cx.dma("sp", nn[:], d["mn"].ap()[sb].rearrange("h k -> k h"), nn_r, reads=[r["mn"]], writes=[nn_r])
                    self.cast(Cb[:], C[:], [C_r], [Cb_r], eng="act")
                    self.cast(nb_[:], nn[:], [nn_r], [nb_r], eng="dve")
                bn = [self.nb(), self.nb()]
                bd = self.nb()
                fns = []
                for h in range(4):
                    o = self.bank[bn[h // 2]][0:L, (h % 2) * 256:(h % 2 + 1) * 256]
                    fns.append(lambda e, h=h, o=o: e.matmul(o, lhsT=ST_t[0:L, h, 0:L], rhs=v_t[0:L, h, :], start=True, stop=False))
                    fns.append(lambda e, h=h, o=o: e.matmul(o, lhsT=qs_t[:, h, 0:L], rhs=Cb[:, h, :], start=False, stop=True))
                cx.group("pe", fns, reads=[ST_r, v_r, qs_r, Cb_r], writes=[self.bres[bn[0]], self.bres[bn[1]]])
                fns = []
                for h in range(4):
                    o = self.bank[bd][0:L, h:h + 1]
                    fns.append(lambda e, h=h, o=o: e.matmul(o, lhsT=ST_t[0:L, h, 0:L], rhs=ones[0:L, :], start=True, stop=False))
                    fns.append(lambda e, h=h, o=o: e.matmul(o, lhsT=qs_t[:, h, 0:L], rhs=nb_[:, h:h + 1], start=False, stop=True))
                cx.group("pe", fns, reads=[ST_r, ones_r, qs_r, nb_r], writes=[self.bres[bd]])
                cx.op("act", lambda e: e.activation(out=sm_t[0:L, :, 0], in_=self.bank[bd][0:L, 0:4], func=AF.Abs),
                      reads=[self.bres[bd]], writes=[sm_r])
                cx.op("act", lambda e: e.activation(out=sm_t[0:L, :, 1], in_=col_t[0:L, 4:8], func=AF.Exp, scale=-1.0),
                      reads=[col_r], writes=[sm_r])
                cx.op("dve", lambda e: e.tensor_tensor(out=sm_t[0:L, :, 2], in0=sm_t[0:L, :, 0], in1=sm_t[0:L, :, 1], op=ALU.max),
                      reads=[sm_r], writes=[sm_r])
                cx.op("dve", lambda e: e.reciprocal(out=sm_t[0:L, :, 3], in_=sm_t[0:L, :, 2]), reads=[sm_r], writes=[sm_r])
                for h in range(4):
                    o = self.bank[bn[h // 2]][0:L, (h % 2) * 256:(h % 2 + 1) * 256]
                    cx.op("act", lambda e: e.activation(out=hm_t[0:L, h, :], in_=o, func=AF.Copy, scale=sm_t[0:L, h, 3:4]),
                          reads=[self.bres[bn[h // 2]], sm_r], writes=[hm_r])
                cx.dma("sp", d["HM"].ap()[t0:t0 + L, :], hm_t[0:L].rearrange("t h v -> t (h v)"), hm_r, reads=[hm_r], writes=[r["HM"]])
                for h in range(4):
                    cx.op("dve", lambda e: e.scalar_tensor_tensor(out=C[:, h, :], in0=C[:, h, :], scalar=wsb_t[:, h, L - 1:L], in1=Us_t[:, h, :],
                                                                  op0=ALU.mult, op1=ALU.add),
                          reads=[C_r, wsb_r, Us_r], writes=[C_r])
                cx.op("dve", lambda e: e.tensor_tensor(out=nn[:], in0=nn[:], in1=wsb_t[:, :, L - 1], op=ALU.mult), reads=[nn_r, wsb_r], writes=[nn_r])
                cx.op("dve", lambda e: e.tensor_tensor(out=nn[:], in0=nn[:], in1=nUs_t[:], op=ALU.add),
                      reads=[nn_r, nUs_r], writes=[nn_r])
                self.cast(Cb[:], C[:], [C_r], [Cb_r], eng="act")
                self.cast(nb_[:], nn[:], [nn_r], [nb_r], eng="dve")
                last_prompt = (sb is None and ci == SEQ // 64 - 1)
                if last_prompt:
                    cx.dma("sp", d["mC_p"].ap().rearrange("h k v -> k h v"), C[:], C_r, reads=[C_r], writes=[r["mC_p"]])
                    cx.dma("sp", d["mn_p"].ap().rearrange("h k -> k h"), nn[:], nn_r, reads=[nn_r], writes=[r["mn_p"]])
                if sb is not None:
                    cx.dma("sp", d["mC_s"].ap()[sb].rearrange("h k v -> k h v"), C[:], C_r, reads=[C_r], writes=[r["mC_s"]])
                    cx.dma("sp", d["mn_s"].ap()[sb].rearrange("h k -> k h"), nn[:], nn_r, reads=[nn_r], writes=[r["mn_s"]])

            part1(0)
            for ci in range(len(chunks)):
                if ci + 1 < len(chunks):
                    part1(ci + 1)
                part2(ci)
        cx.barrier()

    def swa_phase(self):
        import contextlib
        cx = self.cx
        d, r = self.dram, self.dres
        tag = "sw"
        NEG = -30000.0
        with contextlib.ExitStack() as es:
            S = lambda nm, shape, dt: self._sb(es, tag + nm, shape, dt)
            dist, dist_r = S("dist", [128, 256], F32)
            Mall, Mall_r = S("Mall", [128, 16, 256], F32)
            disti, disti_r = S("disti", [128, 256], mybir.dt.int32)
            cx.op("pool", lambda e: e.iota(disti[:], pattern=[[-1, 256]], base=128, channel_multiplier=1), writes=[disti_r])
            cx.op("pool", lambda e: e.tensor_copy(out=dist[:], in_=disti[:]), reads=[disti_r], writes=[dist_r])
            for h in range(16):
                slope = 2.0 ** (-8.0 * (h + 1) / 16)
                cx.op("dve", lambda e: e.tensor_scalar_mul(out=Mall[:, h, :], in0=dist[:], scalar1=-slope), reads=[dist_r], writes=[Mall_r])
            cx.op("pool", lambda e: e.affine_select(out=Mall[:], in_=Mall[:], pattern=[[0, 16], [-1, 256]], compare_op=ALU.is_ge,
                                                   fill=NEG, base=128, channel_multiplier=1), reads=[Mall_r], writes=[Mall_r])
            cx.op("pool", lambda e: e.affine_select(out=Mall[:], in_=Mall[:], pattern=[[0, 16], [1, 256]], compare_op=ALU.is_ge,
                                                   fill=NEG, base=-1, channel_multiplier=-1), reads=[Mall_r], writes=[Mall_r])
            snk, snk_r = S("snk", [128, 16], F32)
            cx.dma("sp", snk[:], d["swa_sinks"].ap()[0].partition_broadcast(128), snk_r, reads=[r["swa_sinks"]], writes=[snk_r])
            qT = [S(f"qT{i}", [128, 8, 128], BF16) for i in range(2)]
            kTd = [S(f"kTd{i}", [128, 4, 128], BF16) for i in range(2)]
            kT2 = [S(f"kT2{i}", [128, 4, 128], BF16) for i in range(2)]
            vf = [S(f"vf{i}", [128, 256], F32) for i in range(2)]
            vb = [S(f"vb{i}", [128, 256], BF16) for i in range(2)]
            v2f = [S(f"v2f{i}", [128, 256], F32) for i in range(2)]
            v2b = [S(f"v2b{i}", [128, 256], BF16) for i in range(2)]
            ckf = [S(f"ckf{i}", [128, 256], F32) for i in range(2)]
            ckd = [S(f"ckd{i}", [128, 4, 2, 64], BF16) for i in range(2)]
            sc = [S(f"sc{i}", [128, 256], F32) for i in range(2)]
            pb = [S(f"p{i}", [128, 256], BF16) for i in range(2)]
            pT = [S(f"pT{i}", [128, 2, 128], BF16) for i in range(2)]
            sm = [S(f"sm{i}", [128, 8], F32) for i in range(2)]
            hs = [S(f"hs{i}", [128, 1024], F32) for i in range(2)]
            u = 0

            def attend(nq, qT_t, qT_r, k1, k1_r, k2, k2_r, v1, v1_r, v2, v2_r, hs_t, hs_r):
                nonlocal u
                lo = 0 if k1 is not None else 128
                hi = 128 + nq

                def s1(h, uu):
                    c, chunk, half = h // 4, h // 2, h % 2
                    ps = slice(half * 64, (half + 1) * 64)
                    sc_t, sc_r = sc[uu % 2]
                    p_t, p_r = pb[uu % 2]
                    sm_t, sm_r = sm[uu % 2]
                    b = self.nb()
                    fns = []
                    rds = [qT_r, k2_r]
                    if k1 is not None:
                        fns.append(lambda e: e.matmul(self.bank[b][0:nq, 0:128], lhsT=qT_t[ps, chunk, 0:nq], rhs=k1[ps, c, :], start=True, stop=True))
                        rds.append(k1_r)
                    fns.append(lambda e: e.matmul(self.bank[b][0:nq, 128:hi], lhsT=qT_t[ps, chunk, 0:nq], rhs=k2[ps, c, 0:nq], start=True, stop=True))
                    cx.group("pe", fns, reads=rds, writes=[self.bres[b]])
                    cx.op("dve", lambda e: e.scalar_tensor_tensor(out=sc_t[0:nq, lo:hi], in0=self.bank[b][0:nq, lo:hi], scalar=0.125,
                                                                  in1=Mall[0:nq, h, lo:hi], op0=ALU.mult, op1=ALU.add),
                          reads=[self.bres[b], Mall_r], writes=[sc_r])
                    cx.op("dve", lambda e: e.reduce_max(out=sm_t[0:nq, 0:1], in_=sc_t[0:nq, lo:hi], axis=AX.X), reads=[sc_r], writes=[sm_r])
                    cx.op("dve", lambda e: e.tensor_scalar(out=sm_t[0:nq, 1:2], in0=sm_t[0:nq, 0:1], scalar1=snk[0:nq, h:h + 1], scalar2=-1.0,
                                                           op0=ALU.max, op1=ALU.mult), reads=[sm_r, snk_r], writes=[sm_r])
                    cx.op("act", lambda e: e.activation(out=p_t[0:nq, lo:hi], in_=sc_t[0:nq, lo:hi], func=AF.Exp, bias=sm_t[0:nq, 1:2],
                                                        accum_out=sm_t[0:nq, 2:3]), reads=[sc_r, sm_r], writes=[p_r, sm_r])
                    cx.op("act", lambda e: e.activation(out=sm_t[0:nq, 3:4], in_=sm_t[0:nq, 1:2], func=AF.Exp, bias=snk[0:nq, h:h + 1]),
                          reads=[sm_r, snk_r], writes=[sm_r])

                def s2(h, uu):
                    c = h // 4
                    p_t, p_r = pb[uu % 2]
                    pT_t, pT_r = pT[uu % 2]
                    sm_t, sm_r = sm[uu % 2]
                    cx.op("dve", lambda e: e.tensor_tensor(out=sm_t[0:nq, 4:5], in0=sm_t[0:nq, 2:3], in1=sm_t[0:nq, 3:4], op=ALU.add),
                          reads=[sm_r], writes=[sm_r])
                    cx.op("dve", lambda e: e.reciprocal(out=sm_t[0:nq, 5:6], in_=sm_t[0:nq, 4:5]), reads=[sm_r], writes=[sm_r])
                    bt = self.nb()
                    ptb = self.bank[bt][:].bitcast(BF16)
                    fns = []
                    if k1 is not None:
                        fns.append(lambda e: e.transpose(out=ptb[:, 0:nq], in_=p_t[0:nq, 0:128], identity=self.ident_b[0:nq, 0:nq]))
                    fns.append(lambda e: e.transpose(out=ptb[0:nq, 128:128 + nq], in_=p_t[0:nq, 128:hi], identity=self.ident_b[0:nq, 0:nq]))
                    cx.group("pe", fns, reads=[p_r, self.ident_b_r], writes=[self.bres[bt]])
                    if k1 is not None:
                        cx.op("act", lambda e: e.copy(out=pT_t[:, 0, 0:nq], in_=ptb[:, 0:nq]), reads=[self.bres[bt]], writes=[pT_r])
                    cx.op("dve", lambda e: e.tensor_copy(out=pT_t[0:nq, 1, 0:nq], in_=ptb[0:nq, 128:128 + nq]), reads=[self.bres[bt]], writes=[pT_r])
                    bo = self.nb()
                    fns = []
                    rds = [pT_r, v2_r]
                    if k1 is not None:
                        fns.append(lambda e: e.matmul(self.bank[bo][0:nq, 0:64], lhsT=pT_t[:, 0, 0:nq], rhs=v1[:, c * 64:(c + 1) * 64], start=True, stop=False))
                        rds.append(v1_r)
                    fns.append(lambda e: e.matmul(self.bank[bo][0:nq, 0:64], lhsT=pT_t[0:nq, 1, 0:nq], rhs=v2[0:nq, c * 64:(c + 1) * 64],
                                                  start=(k1 is None), stop=True))
                    cx.group("pe", fns, reads=rds, writes=[self.bres[bo]])
                    cx.op("act", lambda e: e.activation(out=hs_t[0:nq, h * 64:(h + 1) * 64], in_=self.bank[bo][0:nq, 0:64], func=AF.Copy,
                                                        scale=sm_t[0:nq, 5:6]), reads=[self.bres[bo], sm_r], writes=[hs_r])

                u0 = u
                u += 16
                s1(0, u0)
                for h in range(16):
                    if h + 1 < 16:
                        s1(h + 1, u0 + h + 1)
                    s2(h, u0 + h)

            prev = None
            for j in range(SEQ // 128):
                t0 = j * 128
                i = j % 2
                (q_t, q_r), (k_t, k_r), (vf_t, vf_r), (vb_t, vb_r), (hs_t, hs_r) = qT[i], kTd[i], vf[i], vb[i], hs[i]
                cx.dma("sp", q_t[:], d["QST"].ap()[:, t0:t0 + 128].rearrange("(c p) t -> p c t", p=128), q_r, reads=[r["QST"]], writes=[q_r])
                cx.dma("sp", k_t[:], d["KSTD"].ap()[:, t0:t0 + 128].rearrange("(c p) t -> p c t", p=128), k_r, reads=[r["KSTD"]], writes=[k_r])
                cx.dma("sp", vf_t[:], d["VS"].ap()[t0:t0 + 128, :], vf_r, reads=[r["VS"]], writes=[vf_r])
                self.cast(vb_t[:], vf_t[:], [vf_r], [vb_r])
                if prev is None:
                    attend(128, q_t, q_r, None, None, k_t, k_r, None, None, vb_t, vb_r, hs_t, hs_r)
                else:
                    attend(128, q_t, q_r, prev[0], prev[1], k_t, k_r, prev[2], prev[3], vb_t, vb_r, hs_t, hs_r)
                prev = (k_t, k_r, vb_t, vb_r)
                cx.dma("sp", d["HS"].ap()[t0:t0 + 128, :], hs_t[:], hs_r, reads=[hs_r], writes=[r["HS"]])
            for sb in range(NB):
                t0 = SEQ + sb * DSEQ
                i = sb % 2
                (q_t, q_r), (k1_t, k1_r), (k2_t, k2_r) = qT[i], kTd[i], kT2[i]
                (vf_t, vf_r), (vb_t, vb_r), (v2f_t, v2f_r), (v2b_t, v2b_r) = vf[i], vb[i], v2f[i], v2b[i]
                (ckf_t, ckf_r), (ckd_t, ckd_r), (hs_t, hs_r) = ckf[i], ckd[i], hs[i]
                cx.dma("sp", q_t[:, :, 0:DSEQ], d["QST"].ap()[:, t0:t0 + DSEQ].rearrange("(c p) t -> p c t", p=128), q_r, reads=[r["QST"]], writes=[q_r])
                cx.dma("sp", k2_t[:, :, 0:DSEQ], d["KSTD"].ap()[:, t0:t0 + DSEQ].rearrange("(c p) t -> p c t", p=128), k2_r, reads=[r["KSTD"]], writes=[k2_r])
                cx.dma("sp", ckf_t[:], d["swk"].ap()[sb], ckf_r, reads=[r["swk"]], writes=[ckf_r])
                cx.dma("sp", vf_t[:], d["swv"].ap()[sb], vf_r, reads=[r["swv"]], writes=[vf_r])
                cx.dma("sp", v2f_t[0:DSEQ, :], d["VS"].ap()[t0:t0 + DSEQ, :], v2f_r, reads=[r["VS"]], writes=[v2f_r])
                self.cast(vb_t[:], vf_t[:], [vf_r], [vb_r])
                self.cast(v2b_t[0:DSEQ, :], v2f_t[0:DSEQ, :], [v2f_r], [v2b_r])
                ck3 = ckf_t[:].rearrange("s (c d) -> s c d", c=4)
                for dup in range(2):
                    self.cast(ckd_t[:, :, dup, :], ck3, [ckf_r], [ckd_r])
                self.transpose_blk(ckd_t[:].rearrange("s c u d -> s (c u d)"), ckd_r, 128, 512, lambda c: k1_t[:, c, :], k1_r, evac="dve")
                attend(DSEQ, q_t, q_r, k1_t, k1_r, k2_t, k2_r, vb_t, vb_r, v2b_t, v2b_r, hs_t, hs_r)
                cx.dma("sp", d["HS"].ap()[t0:t0 + DSEQ, :], hs_t[0:DSEQ, :], hs_r, reads=[hs_r], writes=[r["HS"]])
            dm = cx.res("swcopy")
            cx.dma("sp", d["swk_p"].ap(), d["KS"].ap()[SEQ - 128:SEQ, :], dm, reads=[r["KS"]], writes=[r["swk_p"]])
            cx.dma("sp", d["swv_p"].ap(), d["VS"].ap()[SEQ - 128:SEQ, :], dm, reads=[r["VS"]], writes=[r["swv_p"]])
            cx.dma("sp", d["swk_s"].ap()[:, 0:128 - DSEQ, :], d["swk"].ap()[:, DSEQ:128, :], dm, reads=[r["swk"]], writes=[r["swk_s"]])
            cx.dma("sp", d["swv_s"].ap()[:, 0:128 - DSEQ, :], d["swv"].ap()[:, DSEQ:128, :], dm, reads=[r["swv"]], writes=[r["swv_s"]])
            cx.dma("sp", d["swk_s"].ap()[:, 128 - DSEQ:128, :], d["KS"].ap()[SEQ:NTOK, :].rearrange("(b t) f -> b t f", t=DSEQ), dm,
                   reads=[r["KS"]], writes=[r["swk_s"]])
            cx.dma("sp", d["swv_s"].ap()[:, 128 - DSEQ:128, :], d["VS"].ap()[SEQ:NTOK, :].rearrange("(b t) f -> b t f", t=DSEQ), dm,
                   reads=[r["VS"]], writes=[r["swv_s"]])
        cx.barrier()

    def even_out_phase(self, src_fn, dst_fn):
        import contextlib
        cx = self.cx
        d, r = self.dram, self.dres
        tag = "eo"
        blocks = blocks_of(0, SEQ) + [(SEQ, NS)]
        with contextlib.ExitStack() as es:
            S = lambda nm, shape, dt: self._sb(es, tag + nm, shape, dt)
            oT, oT_r = S("oT", [128, KC, NTOK], BF16)
            with contextlib.ExitStack() as es2:
                S2 = lambda nm, shape, dt: self._sb(es2, tag + nm, shape, dt)
                hg, hg_r = S2("hg", [128, 1024], F32)
                cx.dma("sp", hg[:], d["mlstm_head_gain"].ap().rearrange("o h v -> (o h v)").partition_broadcast(128), hg_r,
                       reads=[r["mlstm_head_gain"]], writes=[hg_r])
                hm = [S2(f"hm{i}", [128, 1024], F32) for i in range(2)]
                om = [S2(f"om{i}", [128, 1024], F32) for i in range(2)]
                hsf = [S2(f"hs{i}", [128, 1024], F32) for i in range(2)]
                sq, sq_r = S2("sq", [128, 256], BF16)
                cat = [S2(f"cat{i}", [128, D], BF16) for i in range(2)]
                st = [S2(f"st{i}", [128, 12], F32) for i in range(2)]
                for bi, (r0, n) in enumerate(blocks):
                    i = bi % 2
                    (hm_t, hm_r), (om_t, om_r), (hs_t, hs_r), (cat_t, cat_r), (st_t, st_r) = hm[i], om[i], hsf[i], cat[i], st[i]
                    cx.dma("sp", hm_t[0:n, :], d["HM"].ap()[r0:r0 + n, :], hm_r, reads=[r["HM"]], writes=[hm_r])
                    cx.dma("sp", om_t[0:n, :], d["OM"].ap()[r0:r0 + n, :], om_r, reads=[r["OM"]], writes=[om_r])
                    cx.dma("sp", hs_t[0:n, :], d["HS"].ap()[r0:r0 + n, :], hs_r, reads=[r["HS"]], writes=[hs_r])
                    for h in range(4):
                        cx.op("act", lambda e: e.activation(out=sq[0:n, :], in_=hm_t[0:n, h * 256:(h + 1) * 256], func=AF.Square,
                                                            accum_out=st_t[0:n, h:h + 1]), reads=[hm_r], writes=[sq_r, st_r])
                    cx.op("act", lambda e: e.activation(out=st_t[0:n, 4:8], in_=st_t[0:n, 0:4], func=AF.Sqrt, scale=1.0 / 256,
                                                        bias=self.eps_t[0:n, 0:1]), reads=[st_r, self.eps_r], writes=[st_r])
                    cx.op("dve", lambda e: e.reciprocal(out=st_t[0:n, 8:12], in_=st_t[0:n, 4:8]), reads=[st_r], writes=[st_r])
                    cx.op("act", lambda e: e.activation(out=om_t[0:n, :], in_=om_t[0:n, :], func=AF.Sigmoid), reads=[om_r], writes=[om_r])
                    for h in range(4):
                        cx.op("dve", lambda e: e.scalar_tensor_tensor(out=hm_t[0:n, h * 256:(h + 1) * 256], in0=hm_t[0:n, h * 256:(h + 1) * 256],
                                                                      scalar=st_t[0:n, 8 + h:9 + h], in1=hg[0:n, h * 256:(h + 1) * 256],
                                                                      op0=ALU.mult, op1=ALU.mult), reads=[hm_r, st_r, hg_r], writes=[hm_r])
                    cx.op("dve", lambda e: e.tensor_tensor(out=cat_t[0:n, 0:1024], in0=hm_t[0:n, :], in1=om_t[0:n, :], op=ALU.mult),
                          reads=[hm_r, om_r], writes=[cat_r])
                    self.cast(cat_t[0:n, 1024:2048], hs_t[0:n, :], [hs_r], [cat_r], eng="act")
                    self.transpose_into(cat_t, cat_r, n, oT, oT_r, r0)
                cx.barrier()
            self.out_proj(es, tag, oT, oT_r, KC, d["even_w_out"].ap()[0], r["even_w_out"],
                          [(r0, n, r0) for (r0, n) in blocks], src_fn, dst_fn)
        cx.barrier()

    def odd_in_phase(self, src_fn):
        import contextlib
        cx = self.cx
        d, r = self.dram, self.dres
        tag = "oi"
        blocks = blocks_of(0, SEQ) + [(SEQ, NS)]
        with contextlib.ExitStack() as es:
            hT, hT_r = self.norm_all(es, tag, src_fn, d["norm_gain"].ap()[1, 1], blocks)
            bufs = self.wbufs(es, tag, KC, 512)
            evf = [self._sb(es, f"{tag}evf{i}", [128, 512], F32) for i in range(3)]
            evb = [self._sb(es, f"{tag}evb{i}", [128, 512], BF16) for i in range(3)]
            k = 0
            for cb in range(4):
                wb, wb_r = self.load_w(bufs, d["odd_w_in"].ap()[0][:, cb * 512:(cb + 1) * 512], r["odd_w_in"], KC, 512)
                for (r0, n) in blocks:
                    b = self.nb()
                    self.mm_tok(hT, hT_r, KC, r0, n, wb, wb_r, 0, 512, b)
                    e_t, e_r = evf[k % 3]
                    k += 1
                    cx.op("act", lambda e: e.copy(out=e_t[0:n, :], in_=self.bank[b][0:n, :]), reads=[self.bres[b]], writes=[e_r])
                    cx.dma("sp", d["U"].ap()[r0:r0 + n, cb * 512:(cb + 1) * 512], e_t[0:n, :], e_r, reads=[e_r], writes=[r["U"]])
                for jj in range(4):
                    for (c0, n) in self.sub_tiles(NTOK):
                        b = self.nb()
                        self.mm_feat(hT, hT_r, KC, c0, n, wb, wb_r, jj * 128, 128, b)
                        e_t, e_r = evb[k % 3]
                        k += 1
                        cx.op("act", lambda e: e.copy(out=e_t[:, 0:n], in_=self.bank[b][:, 0:n]), reads=[self.bres[b]], writes=[e_r])
                        rr = cb * 512 + jj * 128
                        cx.dma("sp", d["UT"].ap()[rr:rr + 128, c0:c0 + n], e_t[:, 0:n], e_r, reads=[e_r], writes=[r["UT"]])
        cx.barrier()

    def _range_reduce(self, out, out_r, in_, in_r, tmpi, tmpi_r, tmpf, tmpf_r, shift):
        cx = self.cx
        TWO_PI = 2.0 * np.pi
        cx.op("dve", lambda e: e.tensor_scalar(out=tmpf, in0=in_, scalar1=shift, scalar2=1.0 / TWO_PI, op0=ALU.add, op1=ALU.mult),
              reads=[in_r], writes=[tmpf_r])
        cx.op("dve", lambda e: e.tensor_copy(out=tmpi, in_=tmpf), reads=[tmpf_r], writes=[tmpi_r])
        cx.op("dve", lambda e: e.tensor_copy(out=tmpf, in_=tmpi), reads=[tmpi_r], writes=[tmpf_r])
        cx.op("dve", lambda e: e.tensor_scalar(out=tmpf, in0=tmpf, scalar1=-TWO_PI, scalar2=shift, op0=ALU.mult, op1=ALU.add),
              reads=[tmpf_r], writes=[tmpf_r])
        cx.op("dve", lambda e: e.tensor_tensor(out=out, in0=tmpf, in1=in_, op=ALU.add), reads=[tmpf_r, in_r], writes=[out_r])
        cx.op("dve", lambda e: e.tensor_scalar(out=tmpf, in0=out, scalar1=np.pi, scalar2=-TWO_PI, op0=ALU.is_gt, op1=ALU.mult),
              reads=[out_r], writes=[tmpf_r])
        cx.op("dve", lambda e: e.tensor_tensor(out=out, in0=out, in1=tmpf, op=ALU.add), reads=[out_r, tmpf_r], writes=[out_r])
        cx.op("dve", lambda e: e.tensor_scalar(out=tmpf, in0=out, scalar1=-np.pi, scalar2=TWO_PI, op0=ALU.is_lt, op1=ALU.mult),
              reads=[out_r], writes=[tmpf_r])
        cx.op("dve", lambda e: e.tensor_tensor(out=out, in0=out, in1=tmpf, op=ALU.add), reads=[out_r, tmpf_r], writes=[out_r])

    def s5_setup_phase(self):
        import contextlib
        cx = self.cx
        d, r = self.dram, self.dres
        tag = "s5s"
        I32 = mybir.dt.int32
        with contextlib.ExitStack() as es:
            S = lambda nm, shape, dt: self._sb(es, tag + nm, shape, dt)
            are, are_r = S("are", [128, 64], F32)
            aim, aim_r = S("aim", [128, 64], F32)
            ldt, ldt_r = S("ldt", [128, 1], F32)
            cx.dma("sp", are[:], d["s5_a_re"].ap()[0], are_r, reads=[r["s5_a_re"]], writes=[are_r])
            cx.dma("sp", aim[:], d["s5_a_im"].ap()[0], aim_r, reads=[r["s5_a_im"]], writes=[aim_r])
            cx.dma("sp", ldt[:], d["s5_log_dt"].ap().rearrange("o g -> g o"), ldt_r, reads=[r["s5_log_dt"]], writes=[ldt_r])
            cx.op("act", lambda e: e.activation(out=ldt[:], in_=ldt[:], func=AF.Exp), reads=[ldt_r], writes=[ldt_r])
            lm, lm_r = S("lm", [128, 64], F32)
            th, th_r = S("th", [128, 64], F32)
            cx.op("dve", lambda e: e.tensor_scalar_mul(out=lm[:], in0=are[:], scalar1=ldt[:, 0:1]), reads=[are_r, ldt_r], writes=[lm_r])
            cx.op("dve", lambda e: e.tensor_scalar_mul(out=th[:], in0=aim[:], scalar1=ldt[:, 0:1]), reads=[aim_r, ldt_r], writes=[th_r])
            cx.dma("sp", d["LM"].ap().rearrange("(g p) -> g p", p=64), lm[:], lm_r, reads=[lm_r], writes=[r["LM"]])
            cx.dma("sp", d["TH"].ap().rearrange("(g p) -> g p", p=64), th[:], th_r, reads=[th_r], writes=[r["TH"]])
            ti, ti_r = S("ti", [128, 64], I32)
            tf, tf_r = S("tf", [128, 64], F32)
            sn, sn_r = S("sn", [128, 64], F32)
            cs, cs_r = S("cs", [128, 64], F32)
            mg, mg_r = S("mg", [128, 64], F32)
            self._range_reduce(sn[:], sn_r, th[:], th_r, ti[:], ti_r, tf[:], tf_r, 0.0)
            self._range_reduce(cs[:], cs_r, th[:], th_r, ti[:], ti_r, tf[:], tf_r, np.pi / 2)
            cx.op("act", lambda e: e.activation(out=sn[:], in_=sn[:], func=AF.Sin), reads=[sn_r], writes=[sn_r])
            cx.op("act", lambda e: e.activation(out=cs[:], in_=cs[:], func=AF.Sin), reads=[cs_r], writes=[cs_r])
            cx.op("act", lambda e: e.activation(out=mg[:], in_=lm[:], func=AF.Exp), reads=[lm_r], writes=[mg_r])
            abr, abr_r = S("abr", [128, 64], F32)
            abi, abi_r = S("abi", [128, 64], F32)
            cx.op("dve", lambda e: e.tensor_tensor(out=abr[:], in0=mg[:], in1=cs[:], op=ALU.mult), reads=[mg_r, cs_r], writes=[abr_r])
            cx.op("dve", lambda e: e.tensor_scalar_add(out=abr[:], in0=abr[:], scalar1=-1.0), reads=[abr_r], writes=[abr_r])
            cx.op("dve", lambda e: e.tensor_tensor(out=abi[:], in0=mg[:], in1=sn[:], op=ALU.mult), reads=[mg_r, sn_r], writes=[abi_r])
            den, den_r = S("den", [128, 64], F32)
            t1, t1_r = S("t1", [128, 64], F32)
            cx.op("dve", lambda e: e.tensor_tensor(out=den[:], in0=are[:], in1=are[:], op=ALU.mult), reads=[are_r], writes=[den_r])
            cx.op("dve", lambda e: e.tensor_tensor(out=t1[:], in0=aim[:], in1=aim[:], op=ALU.mult), reads=[aim_r], writes=[t1_r])
            cx.op("dve", lambda e: e.tensor_tensor(out=den[:], in0=den[:], in1=t1[:], op=ALU.add), reads=[den_r, t1_r], writes=[den_r])
            cx.op("dve", lambda e: e.reciprocal(out=den[:], in_=den[:]), reads=[den_r], writes=[den_r])
            fre, fre_r = S("fre", [128, 64], F32)
            fim, fim_r = S("fim", [128, 64], F32)
            cx.op("dve", lambda e: e.tensor_tensor(out=fre[:], in0=abr[:], in1=are[:], op=ALU.mult), reads=[abr_r, are_r], writes=[fre_r])
            cx.op("dve", lambda e: e.tensor_tensor(out=t1[:], in0=abi[:], in1=aim[:], op=ALU.mult), reads=[abi_r, aim_r], writes=[t1_r])
            cx.op("dve", lambda e: e.tensor_tensor(out=fre[:], in0=fre[:], in1=t1[:], op=ALU.add), reads=[fre_r, t1_r], writes=[fre_r])
            cx.op("dve", lambda e: e.tensor_tensor(out=fre[:], in0=fre[:], in1=den[:], op=ALU.mult), reads=[fre_r, den_r], writes=[fre_r])
            cx.op("dve", lambda e: e.tensor_tensor(out=fim[:], in0=abi[:], in1=are[:], op=ALU.mult), reads=[abi_r, are_r], writes=[fim_r])
            cx.op("dve", lambda e: e.tensor_tensor(out=t1[:], in0=abr[:], in1=aim[:], op=ALU.mult), reads=[abr_r, aim_r], writes=[t1_r])
            cx.op("dve", lambda e: e.tensor_tensor(out=fim[:], in0=fim[:], in1=t1[:], op=ALU.subtract), reads=[fim_r, t1_r], writes=[fim_r])
            cx.op("dve", lambda e: e.tensor_tensor(out=fim[:], in0=fim[:], in1=den[:], op=ALU.mult), reads=[fim_r, den_r], writes=[fim_r])
            bre, bre_r = S("bre", [128, 64, 16], F32)
            bim, bim_r = S("bim", [128, 64, 16], F32)
            cx.dma("sp", bre[:], d["s5_b_re"].ap()[0], bre_r, reads=[r["s5_b_re"]], writes=[bre_r])
            cx.dma("sp", bim[:], d["s5_b_im"].ap()[0], bim_r, reads=[r["s5_b_im"]], writes=[bim_r])
            bbo, bbo_r = S("bbo", [128, 2, 64, 16], F32)
            t3, t3_r = S("t3", [128, 64, 16], F32)
            frb = fre[:].unsqueeze(2).to_broadcast([128, 64, 16])
            fib = fim[:].unsqueeze(2).to_broadcast([128, 64, 16])
            cx.op("dve", lambda e: e.tensor_tensor(out=bbo[:, 0], in0=bre[:], in1=frb, op=ALU.mult), reads=[bre_r, fre_r], writes=[bbo_r])
            cx.op("dve", lambda e: e.tensor_tensor(out=t3[:], in0=bim[:], in1=fib, op=ALU.mult), reads=[bim_r, fim_r], writes=[t3_r])
            cx.op("dve", lambda e: e.tensor_tensor(out=bbo[:, 0], in0=bbo[:, 0], in1=t3[:], op=ALU.subtract), reads=[bbo_r, t3_r], writes=[bbo_r])
            cx.op("dve", lambda e: e.tensor_tensor(out=bbo[:, 1], in0=bim[:], in1=frb, op=ALU.mult), reads=[bim_r, fre_r], writes=[bbo_r])
            cx.op("dve", lambda e: e.tensor_tensor(out=t3[:], in0=bre[:], in1=fib, op=ALU.mult), reads=[bre_r, fim_r], writes=[t3_r])
            cx.op("dve", lambda e: e.tensor_tensor(out=bbo[:, 1], in0=bbo[:, 1], in1=t3[:], op=ALU.add), reads=[bbo_r, t3_r], writes=[bbo_r])
            cx.dma("sp", d["BBS"].ap(), bbo[:], bbo_r, reads=[bbo_r], writes=[r["BBS"]])
            cx.barrier()
        with contextlib.ExitStack() as es:
            S = lambda nm, shape, dt: self._sb(es, tag + nm, shape, dt)
            bbt, bbt_r = S("bbt", [128, 128, 16], F32)
            cx.dma("sp", bbt[:], d["BBS"].ap().rearrange("g r p c -> (r p) g c"), bbt_r, reads=[r["BBS"]], writes=[bbt_r])
            mask, mask_r = S("mask", [128, 8], F32)
            cx.op("pool", lambda e: e.memset(mask[:], 1.0), writes=[mask_r])
            cx.op("pool", lambda e: e.affine_select(out=mask[:], in_=mask[:], pattern=[[-16, 8]], compare_op=ALU.is_ge, fill=0.0, base=0,
                                                   channel_multiplier=1), reads=[mask_r], writes=[mask_r])
            cx.op("pool", lambda e: e.affine_select(out=mask[:], in_=mask[:], pattern=[[16, 8]], compare_op=ALU.is_ge, fill=0.0, base=15,
                                                   channel_multiplier=-1), reads=[mask_r], writes=[mask_r])
            dj = [S(f"dj{i}", [128, 128], F32) for i in range(2)]
            bd = [S(f"bd{i}", [128, 8, 128], BF16) for i in range(2)]
            for j in range(16):
                b = self.nb()
                dj_t, dj_r = dj[j % 2]
                bd_t, bd_r = bd[j % 2]
                cx.group("pe", [lambda e: e.transpose(out=self.bank[b][:, 0:128], in_=bbt[:, 8 * j:8 * j + 8, :].rearrange("q g c -> q (g c)"),
                                                      identity=self.ident_f[:])], reads=[bbt_r, self.ident_f_r], writes=[self.bres[b]])
                cx.op("act", lambda e: e.copy(out=dj_t[:], in_=self.bank[b][:, 0:128]), reads=[self.bres[b]], writes=[dj_r])
                cx.op("dve", lambda e: e.tensor_tensor(out=bd_t[:], in0=dj_t[:].unsqueeze(1).to_broadcast([128, 8, 128]),
                                                       in1=mask[:].unsqueeze(2).to_broadcast([128, 8, 128]), op=ALU.mult),
                      reads=[dj_r, mask_r], writes=[bd_r])
                cx.dma("sp", d["BBD"].ap()[:, j], bd_t[:], bd_r, reads=[bd_r], writes=[r["BBD"]])
            cn, cn_r = S("cn", [128, 16, 2, 64], F32)
            cx.dma("sp", cn[:, :, 0, :], d["s5_c_re"].ap()[0].rearrange("(j g) c p -> (g c) j p", g=8), cn_r, reads=[r["s5_c_re"]], writes=[cn_r])
            cx.dma("sp", cn[:, :, 1, :], d["s5_c_im"].ap()[0].rearrange("(j g) c p -> (g c) j p", g=8), cn_r, reads=[r["s5_c_im"]], writes=[cn_r])
            cm, cm_r = S("cm", [128, 16, 128], BF16)
            for j in range(16):
                b = self.nb()
                cx.group("pe", [lambda e: e.transpose(out=self.bank[b][:, 0:128], in_=cn[:, j].rearrange("q r p -> q (r p)"),
                                                      identity=self.ident_f[:])], reads=[cn_r, self.ident_f_r], writes=[self.bres[b]])
                cx.op("act", lambda e: e.copy(out=cm[0:64, j, :], in_=self.bank[b][0:64, 0:128]), reads=[self.bres[b]], writes=[cm_r])
                cx.op("act", lambda e: e.mul(out=cm[64:128, j, :], in_=self.bank[b][64:128, 0:128], mul=-1.0), reads=[self.bres[b]], writes=[cm_r])
            cx.dma("sp", d["CM"].ap(), cm[:], cm_r, reads=[cm_r], writes=[r["CM"]])
            cx.barrier()
        with contextlib.ExitStack() as es:
            S = lambda nm, shape, dt: self._sb(es, tag + nm, shape, dt)
            W = 2048
            kcol, kcol_r = S("kcol", [128, 2], F32)
            kci, kci_r = S("kci", [128, 1], I32)
            cx.op("pool", lambda e: e.iota(kci[:], pattern=[[0, 1]], base=1, channel_multiplier=1), writes=[kci_r])
            cx.op("pool", lambda e: e.tensor_copy(out=kcol[:, 0:1], in_=kci[:]), reads=[kci_r], writes=[kcol_r])
            cx.op("dve", lambda e: e.tensor_scalar_mul(out=kcol[:, 1:2], in0=kcol[:, 0:1], scalar1=-1.0), reads=[kcol_r], writes=[kcol_r])
            thb, thb_r = S("thb", [128, W], F32)
            lmb, lmb_r = S("lmb", [128, W], F32)
            ang, ang_r = S("ang", [128, W], F32)
            ti, ti_r = S("tti", [128, W], I32)
            tf, tf_r = S("ttf", [128, W], F32)
            sn, sn_r = S("tsn", [128, W], F32)
            cs, cs_r = S("tcs", [128, W], F32)
            ep, ep_r = S("tep", [128, W], F32)
            en, en_r = S("ten", [128, W], F32)
            o4 = [S(f"to{i}", [128, W], F32) for i in range(4)]
            for q in range(8192 // W):
                cs_ = slice(q * W, (q + 1) * W)
                cx.dma("sp", thb[:], d["TH"].ap()[cs_].partition_broadcast(128), thb_r, reads=[r["TH"]], writes=[thb_r])
                cx.dma("sp", lmb[:], d["LM"].ap()[cs_].partition_broadcast(128), lmb_r, reads=[r["LM"]], writes=[lmb_r])
                cx.op("dve", lambda e: e.tensor_scalar_mul(out=ang[:], in0=thb[:], scalar1=kcol[:, 0:1]), reads=[thb_r, kcol_r], writes=[ang_r])
                self._range_reduce(sn[:], sn_r, ang[:], ang_r, ti[:], ti_r, tf[:], tf_r, 0.0)
                self._range_reduce(cs[:], cs_r, ang[:], ang_r, ti[:], ti_r, tf[:], tf_r, np.pi / 2)
                cx.op("act", lambda e: e.activation(out=sn[:], in_=sn[:], func=AF.Sin), reads=[sn_r], writes=[sn_r])
                cx.op("act", lambda e: e.activation(out=cs[:], in_=cs[:], func=AF.Sin), reads=[cs_r], writes=[cs_r])
                cx.op("act", lambda e: e.activation(out=ep[:], in_=lmb[:], func=AF.Exp, scale=kcol[:, 0:1]), reads=[lmb_r, kcol_r], writes=[ep_r])
                cx.op("act", lambda e: e.activation(out=en[:], in_=lmb[:], func=AF.Exp, scale=kcol[:, 1:2]), reads=[lmb_r, kcol_r], writes=[en_r])
                (o0, o0r), (o1, o1r), (o2, o2r), (o3, o3r) = o4
                cx.op("dve", lambda e: e.tensor_tensor(out=o0[:], in0=en[:], in1=cs[:], op=ALU.mult), reads=[en_r, cs_r], writes=[o0r])
                cx.op("dve", lambda e: e.scalar_tensor_tensor(out=o1[:], in0=en[:], scalar=-1.0, in1=sn[:], op0=ALU.mult, op1=ALU.mult),
                      reads=[en_r, sn_r], writes=[o1r])
                cx.op("dve", lambda e: e.tensor_tensor(out=o2[:], in0=ep[:], in1=cs[:], op=ALU.mult), reads=[ep_r, cs_r], writes=[o2r])
                cx.op("dve", lambda e: e.tensor_tensor(out=o3[:], in0=ep[:], in1=sn[:], op=ALU.mult), reads=[ep_r, sn_r], writes=[o3r])
                for k_, (o, orr) in enumerate(o4):
                    cx.dma("sp", d["TAB"].ap()[k_, :, cs_], o[:], orr, reads=[orr], writes=[r["TAB"]])
        cx.barrier()

    def s5_scan_phase(self):
        import contextlib
        cx = self.cx
        d, r = self.dram, self.dres
        tag = "s5"
        with contextlib.ExitStack() as es:
            S = lambda nm, shape, dt: self._sb(es, tag + nm, shape, dt)
            tri, tri_r = S("tri", [128, 128], BF16)
            sel, sel_r = S("sel", [128, 128], BF16)
            tri64, tri64_r = S("tri64", [64, 16, 4], BF16)
            selb, selb_r = S("selb", [16, 16, 4], BF16)
            for t_, tr_ in ((tri, tri_r), (sel, sel_r), (tri64, tri64_r), (selb, selb_r)):
                cx.op("pool", lambda e: e.memset(t_[:], 1.0), writes=[tr_])
            AS = lambda t_, tr_, pat, base, cm: cx.op("pool", lambda e: e.affine_select(
                out=t_[:], in_=t_[:], pattern=pat, compare_op=ALU.is_ge, fill=0.0, base=base, channel_multiplier=cm), reads=[tr_], writes=[tr_])
            AS(tri, tri_r, [[1, 128]], 0, -1)
            AS(sel, sel_r, [[0, 128]], -127, 1)
            AS(tri64, tri64_r, [[4, 16], [1, 4]], 0, -1)
            AS(tri64, tri64_r, [[-4, 16], [0, 4]], 0, 1)
            AS(tri64, tri64_r, [[4, 16], [0, 4]], 3, -1)
            AS(selb, selb_r, [[-1, 16], [0, 4]], 0, 1)
            AS(selb, selb_r, [[1, 16], [0, 4]], 0, -1)
            tri64f = tri64[:].rearrange("s b t -> s (b t)")
            selbf = selb[:].rearrange("s b t -> s (b t)")
            sets = []
            for i in range(2):
                sets.append(dict(tab=S(f"tab{i}", [128, 4, 1024], F32), tabs=S(f"tabs{i}", [64, 4, 1024], F32),
                                 uT=S(f"uT{i}", [128, 2, NTOK], BF16), bbd=S(f"bbd{i}", [128, 2, 8, 128], BF16),
                                 cm=S(f"cm{i}", [128, 2, 128], BF16), x0f=S(f"x0f{i}", [16, 16, 2, 64], F32),
                                 x0b=S(f"x0b{i}", [16, 2048], BF16)))
            BH = [S(f"BH{i}", [128, 2048], BF16) for i in range(2)]
            tt = [S(f"tt{i}", [128, 1024], F32) for i in range(4)]
            tq = [S(f"tq{i}", [128, 1024], F32) for i in range(4)]
            XF = [S(f"XF{i}", [128, 2048], F32) for i in range(2)]
            Xbs = [S(f"Xb{i}", [128, 2048], BF16) for i in range(2)]
            XT = [S(f"XT{i}", [128, 16, 128], BF16) for i in range(2)]
            ys = [S(f"ys{i}", [128, 256], F32) for i in range(2)]
            ps = self.psum
            v4 = lambda ap, L: ap.rearrange("t (g r p) -> t g r p", g=16, r=2)
            v3 = lambda ap, L: ap.rearrange("t (g p) -> t g p", g=16)
            chunks = [(c * 128, 128, False) for c in range(SEQ // 128)] + [(SEQ, NS, True)]
            it = 0
            def load_eighth(e8):
                st = sets[e8 % 2]
                (tab, tab_r), (tabs, tabs_r), (uT, uT_r), (bbd, bbd_r) = st["tab"], st["tabs"], st["uT"], st["bbd"]
                (cm, cm_r), (x0f, x0f_r), (x0b, x0b_r) = st["cm"], st["x0f"], st["x0b"]
                gc = slice(e8 * 1024, (e8 + 1) * 1024)
                cx.dma("sp", tab[:], d["TAB"].ap()[:, :, gc].rearrange("k t c -> t k c"), tab_r, reads=[r["TAB"]], writes=[tab_r])
                for b_ in range(NB):
                    cx.dma("sp", tabs[4 * b_:4 * b_ + 4], d["TAB"].ap()[:, 0:4, gc].rearrange("k t c -> t k c"), tabs_r, reads=[r["TAB"]], writes=[tabs_r])
                cx.dma("sp", uT[:], d["UT"].ap()[e8 * 256:(e8 + 1) * 256, :].rearrange("(j q) t -> q j t", q=128), uT_r, reads=[r["UT"]], writes=[uT_r])
                cx.dma("sp", bbd[:], d["BBD"].ap()[:, 2 * e8:2 * e8 + 2], bbd_r, reads=[r["BBD"]], writes=[bbd_r])
                cx.dma("sp", cm[:], d["CM"].ap()[:, 2 * e8:2 * e8 + 2, :], cm_r, reads=[r["CM"]], writes=[cm_r])
                cx.dma("sp", x0f[:, :, 0, :], d["s5re"].ap()[:, e8 * 16:(e8 + 1) * 16, :], x0f_r, reads=[r["s5re"]], writes=[x0f_r])
                cx.dma("sp", x0f[:, :, 1, :], d["s5im"].ap()[:, e8 * 16:(e8 + 1) * 16, :], x0f_r, reads=[r["s5im"]], writes=[x0f_r])
                self.cast(x0b[:], x0f[:].rearrange("b g r p -> b (g r p)"), [x0f_r], [x0b_r])

            load_eighth(0)
            for e8 in range(8):
                st = sets[e8 % 2]
                (tab, tab_r), (tabs, tabs_r), (uT, uT_r), (bbd, bbd_r) = st["tab"], st["tabs"], st["uT"], st["bbd"]
                (cm, cm_r), (x0f, x0f_r), (x0b, x0b_r) = st["cm"], st["x0f"], st["x0b"]
                if e8 + 1 < 8:
                    load_eighth(e8 + 1)
                def stage_a(ci, t0, L, smp, it):
                    (BH_t, BH_r) = BH[it % 2]
                    tb_t, tb_r = (tabs, tabs_r) if smp else (tab, tab_r)
                    fns = []
                    for q in range(4):
                        fns.append(lambda e, q=q: e.matmul(ps[0:L, q * 512:(q + 1) * 512], lhsT=uT[:, q // 2, t0:t0 + L],
                                                           rhs=bbd[:, q // 2, (q % 2) * 4:(q % 2) * 4 + 4, :], start=True, stop=True))
                    cx.group("pe", fns, reads=[uT_r, bbd_r], writes=self.bres[0:4])
                    buv = v4(ps[0:L, 0:2048], L)
                    bur, bui = buv[:, :, 0, :], buv[:, :, 1, :]
                    Pr, Pi = v3(tb_t[0:L, 0, :], L), v3(tb_t[0:L, 1, :], L)
                    T = [(v3(t_[0:L, :], L), tr_) for (t_, tr_) in tt]
                    BHv = v4(BH_t[0:L, :], L)
                    rb = self.bres[0:4] + [tb_r]
                    cx.op("dve", lambda e: e.tensor_tensor(out=T[0][0], in0=bur, in1=Pr, op=ALU.mult), reads=rb, writes=[T[0][1]])
                    cx.op("dve", lambda e: e.tensor_tensor(out=T[1][0], in0=bui, in1=Pi, op=ALU.mult), reads=rb, writes=[T[1][1]])
                    cx.op("dve", lambda e: e.tensor_tensor(out=T[2][0], in0=bur, in1=Pi, op=ALU.mult), reads=rb, writes=[T[2][1]])
                    cx.op("dve", lambda e: e.tensor_tensor(out=T[3][0], in0=bui, in1=Pr, op=ALU.mult), reads=rb, writes=[T[3][1]])
                    cx.op("pool", lambda e: e.tensor_tensor(out=BHv[:, :, 0, :], in0=T[0][0], in1=T[1][0], op=ALU.subtract),
                          reads=[T[0][1], T[1][1]], writes=[BH_r])
                    cx.op("pool", lambda e: e.tensor_tensor(out=BHv[:, :, 1, :], in0=T[2][0], in1=T[3][0], op=ALU.add),
                          reads=[T[2][1], T[3][1]], writes=[BH_r])

                def stage_b1(ci, t0, L, smp, it):
                    Xb, Xb_r = Xbs[it % 2]
                    Xp, Xp_r = Xbs[(it + 1) % 2]
                    (BH_t, BH_r), (XF_t, XF_r), (XT_t, XT_r), (ys_t, ys_r) = BH[it % 2], XF[it % 2], XT[it % 2], ys[it % 2]
                    tb_t, tb_r = (tabs, tabs_r) if smp else (tab, tab_r)
                    fns = []
                    rds = [BH_r]
                    carry = smp or ci > 0
                    for q in range(4):
                        o = ps[0:L, 2048 + q * 512:2048 + (q + 1) * 512]
                        if smp:
                            fns.append(lambda e, q=q, o=o: e.matmul(o, lhsT=tri64f, rhs=BH_t[0:L, q * 512:(q + 1) * 512], start=True, stop=False))
                            fns.append(lambda e, q=q, o=o: e.matmul(o, lhsT=selbf, rhs=x0b[:, q * 512:(q + 1) * 512], start=False, stop=True))
                        else:
                            fns.append(lambda e, q=q, o=o: e.matmul(o, lhsT=tri[:], rhs=BH_t[:, q * 512:(q + 1) * 512], start=True, stop=not carry))
                            if carry:
                                fns.append(lambda e, q=q, o=o: e.matmul(o, lhsT=sel[:], rhs=Xp[:, q * 512:(q + 1) * 512], start=False, stop=True))
                    if smp:
                        rds += [tri64_r, selb_r, x0b_r]
                    else:
                        rds += [tri_r, sel_r] + ([Xp_r] if carry else [])
                    cx.group("pe", fns, reads=rds, writes=self.bres[4:8])
                    xh = v4(ps[0:L, 2048:4096], L)
                    xr, xi = xh[:, :, 0, :], xh[:, :, 1, :]
                    Qr, Qi = v3(tb_t[0:L, 2, :], L), v3(tb_t[0:L, 3, :], L)
                    Q = [(v3(t_[0:L, :], L), tr_) for (t_, tr_) in tq]
                    XFv = v4(XF_t[0:L, :], L)
                    rb = self.bres[4:8] + [tb_r]
                    cx.op("dve", lambda e: e.tensor_tensor(out=Q[0][0], in0=xr, in1=Qr, op=ALU.mult), reads=rb, writes=[Q[0][1]])
                    cx.op("dve", lambda e: e.tensor_tensor(out=Q[1][0], in0=xi, in1=Qi, op=ALU.mult), reads=rb, writes=[Q[1][1]])
                    Xbv = v4(Xb[0:L, :], L)
                    need_f32 = smp or ci == SEQ // 128 - 1
                    cx.op("pool", lambda e: e.tensor_tensor(out=Xbv[:, :, 0, :], in0=Q[0][0], in1=Q[1][0], op=ALU.subtract),
                          reads=[Q[0][1], Q[1][1]], writes=[Xb_r])
                    cx.op("dve", lambda e: e.tensor_tensor(out=Q[2][0], in0=xr, in1=Qi, op=ALU.mult), reads=rb, writes=[Q[2][1]])
                    cx.op("dve", lambda e: e.tensor_tensor(out=Q[3][0], in0=xi, in1=Qr, op=ALU.mult), reads=rb, writes=[Q[3][1]])
                    cx.op("dve", lambda e: e.tensor_tensor(out=Xbv[:, :, 1, :], in0=Q[2][0], in1=Q[3][0], op=ALU.add),
                          reads=[Q[2][1], Q[3][1]], writes=[Xb_r])
                    if need_f32:
                        cx.op("pool", lambda e: e.tensor_tensor(out=XFv[:, :, 0, :], in0=Q[0][0], in1=Q[1][0], op=ALU.subtract),
                              reads=[Q[0][1], Q[1][1]], writes=[XF_r])
                        cx.op("pool", lambda e: e.tensor_tensor(out=XFv[:, :, 1, :], in0=Q[2][0], in1=Q[3][0], op=ALU.add),
                              reads=[Q[2][1], Q[3][1]], writes=[XF_r])
                    gs = slice(e8 * 16, (e8 + 1) * 16)
                    if (not smp) and ci == SEQ // 128 - 1:
                        cx.dma("sp", d["s5re_p"].ap()[gs, :].rearrange("(o g) p -> o g p", o=1), XFv[L - 1:L, :, 0, :], XF_r, reads=[XF_r], writes=[r["s5re_p"]])
                        cx.dma("sp", d["s5im_p"].ap()[gs, :].rearrange("(o g) p -> o g p", o=1), XFv[L - 1:L, :, 1, :], XF_r, reads=[XF_r], writes=[r["s5im_p"]])
                    if smp:
                        for b_ in range(NB):
                            row = b_ * DSEQ + DSEQ - 1
                            cx.dma("sp", d["s5re_s"].ap()[b_, gs, :].rearrange("(o g) p -> o g p", o=1), XFv[row:row + 1, :, 0, :], XF_r, reads=[XF_r], writes=[r["s5re_s"]])
                            cx.dma("sp", d["s5im_s"].ap()[b_, gs, :].rearrange("(o g) p -> o g p", o=1), XFv[row:row + 1, :, 1, :], XF_r, reads=[XF_r], writes=[r["s5im_s"]])
                def stage_b2(ci, t0, L, smp, it):
                    Xb, Xb_r = Xbs[it % 2]
                    (XT_t, XT_r), (ys_t, ys_r) = XT[it % 2], ys[it % 2]
                    for k0 in (0, 8):
                        bt = self.nb()
                        pb = self.bank[bt][:].bitcast(BF16)
                        cx.group("pe", [lambda e, kc=kc: e.transpose(out=pb[:, (kc - k0) * 128:(kc - k0) * 128 + L],
                                                                     in_=Xb[0:L, kc * 128:(kc + 1) * 128], identity=self.ident_b[0:L, 0:L])
                                        for kc in range(k0, k0 + 8)], reads=[Xb_r, self.ident_b_r], writes=[self.bres[bt]])
                        src = pb.rearrange("p (k c) -> p k c", c=128)[:, :, 0:L]
                        self.cast(XT_t[:, k0:k0 + 8, 0:L], src, [self.bres[bt]], [XT_r], eng="act")
                    by = self.nb()
                    cx.group("pe", [lambda e, g=g: e.matmul(self.bank[by][0:L, g * 16:(g + 1) * 16], lhsT=XT_t[:, g, 0:L],
                                                            rhs=cm[:, g // 8, (g % 8) * 16:(g % 8 + 1) * 16], start=True, stop=True)
                                    for g in range(16)], reads=[XT_r, cm_r], writes=[self.bres[by]])
                    cx.op("act", lambda e: e.copy(out=ys_t[0:L, :], in_=self.bank[by][0:L, 0:256]), reads=[self.bres[by]], writes=[ys_r])
                    cx.dma("sp", d["Y"].ap()[t0:t0 + L, e8 * 256:(e8 + 1) * 256], ys_t[0:L, :], ys_r, reads=[ys_r], writes=[r["Y"]])

                units = [(ci, t0, L, smp, it + ci) for ci, (t0, L, smp) in enumerate(chunks)]
                it += len(chunks)
                nU = len(units)
                stage_a(*units[0])
                stage_a(*units[1])
                stage_b1(*units[0])
                for ui in range(nU):
                    if ui + 2 < nU:
                        stage_a(*units[ui + 2])
                    if ui + 1 < nU:
                        stage_b1(*units[ui + 1])
                    stage_b2(*units[ui])
        cx.barrier()

    def odd_out_phase(self, src_fn, dst_fn):
        import contextlib
        cx = self.cx
        d, r = self.dram, self.dres
        tag = "oo"
        blocks = blocks_of(0, SEQ) + [(SEQ, NS)]
        with contextlib.ExitStack() as es:
            S = lambda nm, shape, dt: self._sb(es, tag + nm, shape, dt)
            yT, yT_r = S("yT", [128, KC, NTOK], BF16)
            with contextlib.ExitStack() as es2:
                S2 = lambda nm, shape, dt: self._sb(es2, tag + nm, shape, dt)
                dsk, dsk_r = S2("dsk", [128, D], F32)
                cx.dma("sp", dsk[:], d["s5_d"].ap()[0].partition_broadcast(128), dsk_r, reads=[r["s5_d"]], writes=[dsk_r])
                yb = [S2(f"y{i}", [128, D], F32) for i in range(2)]
                ub = [S2(f"u{i}", [128, D], F32) for i in range(2)]
                gb = [S2(f"g{i}", [128, D], BF16) for i in range(2)]
                for bi, (r0, n) in enumerate(blocks):
                    (y_t, y_r), (u_t, u_r), (g_t, g_r) = yb[bi % 2], ub[bi % 2], gb[bi % 2]
                    cx.dma("sp", y_t[0:n, :], d["Y"].ap()[r0:r0 + n, :], y_r, reads=[r["Y"]], writes=[y_r])
                    cx.dma("sp", u_t[0:n, :], d["U"].ap()[r0:r0 + n, :], u_r, reads=[r["U"]], writes=[u_r])
                    cx.op("dve", lambda e: e.tensor_tensor(out=u_t[0:n, :], in0=u_t[0:n, :], in1=dsk[0:n, :], op=ALU.mult), reads=[u_r, dsk_r], writes=[u_r])
                    cx.op("dve", lambda e: e.tensor_tensor(out=y_t[0:n, :], in0=y_t[0:n, :], in1=u_t[0:n, :], op=ALU.add), reads=[y_r, u_r], writes=[y_r])
                    GC = 2.0 * (2.0 / np.pi) ** 0.5
                    cx.op("dve", lambda e: e.tensor_tensor(out=u_t[0:n, :], in0=y_t[0:n, :], in1=y_t[0:n, :], op=ALU.mult), reads=[y_r, u_r], writes=[u_r])
                    cx.op("dve", lambda e: e.tensor_scalar(out=u_t[0:n, :], in0=u_t[0:n, :], scalar1=0.044715, scalar2=1.0, op0=ALU.mult, op1=ALU.add),
                          reads=[u_r], writes=[u_r])
                    cx.op("dve", lambda e: e.tensor_tensor(out=u_t[0:n, :], in0=u_t[0:n, :], in1=y_t[0:n, :], op=ALU.mult), reads=[y_r, u_r], writes=[u_r])
                    cx.op("act", lambda e: e.activation(out=u_t[0:n, :], in_=u_t[0:n, :], func=AF.Sigmoid, scale=GC), reads=[u_r], writes=[u_r])
                    cx.op("dve", lambda e: e.tensor_tensor(out=g_t[0:n, :], in0=u_t[0:n, :], in1=y_t[0:n, :], op=ALU.mult), reads=[y_r, u_r], writes=[g_r])
                    self.transpose_into(g_t, g_r, n, yT, yT_r, r0)
                cx.barrier()
            bufs = self.wbufs(es, tag, KC, 512, nst=2, nbf=2)
            sg = [S(f"sg{i}", [128, 512], F32) for i in range(2)]
            xo = [S(f"xo{i}", [128, 512], F32) for i in range(3)]
            xw = [S(f"xw{i}", [128, 512], F32) for i in range(3)]
            W = d["odd_w_out"].ap()[0]
            ev = 0
            for cb in range(4):
                wv, wv_r = self.load_w(bufs, W[:, cb * 512:(cb + 1) * 512], r["odd_w_out"], KC, 512)
                wg, wg_r = self.load_w(bufs, W[:, D + cb * 512:D + (cb + 1) * 512], r["odd_w_out"], KC, 512)
                for (r0, n) in blocks:
                    bv, bg = self.nb(), self.nb()
                    self.mm_tok(yT, yT_r, KC, r0, n, wv, wv_r, 0, 512, bv)
                    self.mm_tok(yT, yT_r, KC, r0, n, wg, wg_r, 0, 512, bg)
                    (sg_t, sg_r), (xo_t, xo_r), (xw_t, xw_r) = sg[ev % 2], xo[ev % 3], xw[ev % 3]
                    ev += 1
                    sap, sr = src_fn(r0, n)
                    dap, dr = dst_fn(r0, n)
                    cx.dma("sp", xo_t[0:n, :], sap[:, cb * 512:(cb + 1) * 512], xo_r, reads=[sr], writes=[xo_r])
                    cx.op("act", lambda e: e.activation(out=sg_t[0:n, :], in_=self.bank[bg][0:n, :], func=AF.Sigmoid), reads=[self.bres[bg]], writes=[sg_r])
                    cx.op("dve", lambda e: e.tensor_tensor(out=sg_t[0:n, :], in0=self.bank[bv][0:n, :], in1=sg_t[0:n, :], op=ALU.mult),
                          reads=[self.bres[bv], sg_r], writes=[sg_r])
                    cx.op("dve", lambda e: e.tensor_tensor(out=xw_t[0:n, :], in0=sg_t[0:n, :], in1=xo_t[0:n, :], op=ALU.add),
                          reads=[sg_r, xo_r], writes=[xw_r])
                    cx.dma("sp", dap[:, cb * 512:(cb + 1) * 512], xw_t[0:n, :], xw_r, reads=[xw_r], writes=[dr])
        cx.barrier()

    def declare_io(self):
        di, do, ds = self.din, self.dout, self.dscr
        di("x_p", [SEQ, D]); di("x_s", [NS, D]); di("mem_p", [256, D])
        di("cmk", [2, NB, 256, 512]); di("cmv", [2, NB, 256, 512])
        di("mC", [NB, 4, 128, 256]); di("mn", [NB, 4, 128]); di("mm", [NB, 4])
        di("swk", [NB, 128, 256]); di("swv", [NB, 128, 256])
        di("s5re", [NB, 128, 64]); di("s5im", [NB, 128, 64])
        di("norm_gain", [2, 4, D]); di("final_gain", [D])
        for l in range(2):
            for i in range(2):
                di(f"wgu{l}{i}", [NF, 128, 2 * KC * 128]); di(f"wd{l}{i}", [DFF, D])
        di("wtok", [D, 3072]); di("wfeat", [D, 2560]); di("wif", [D, 8])
        di("mlstm_b_i", [1, 4]); di("mlstm_b_f", [1, 4]); di("mlstm_head_gain", [1, 4, 256]); di("swa_sinks", [1, 16])
        di("even_w_out", [1, D, D]); di("odd_w_in", [1, D, D])
        di("s5_a_re", [1, 128, 64]); di("s5_a_im", [1, 128, 64]); di("s5_log_dt", [1, 128])
        di("s5_b_re", [1, 128, 64, 16]); di("s5_b_im", [1, 128, 64, 16]); di("s5_c_re", [1, 128, 16, 64]); di("s5_c_im", [1, 128, 16, 64])
        di("s5_d", [1, D]); di("odd_w_out", [1, D, 2 * D])
        di("xwq", [2, D, 512]); di("xwk", [2, D, 512]); di("xwv", [2, D, 512]); di("xwo", [2, 512, D])
        do("y_p", [SEQ, D]); do("y_s", [NS, D]); do("mem_k_p", [2, 256, 512]); do("mem_v_p", [2, 256, 512])
        do("mC_p", [4, 128, 256]); do("mC_s", [NB, 4, 128, 256]); do("mn_p", [4, 128]); do("mn_s", [NB, 4, 128])
        do("mm_p", [4]); do("mm_s", [NB, 4])
        do("swk_p", [128, 256]); do("swk_s", [NB, 128, 256]); do("swv_p", [128, 256]); do("swv_s", [NB, 128, 256])
        do("s5re_p", [128, 64]); do("s5re_s", [NB, 128, 64]); do("s5im_p", [128, 64]); do("s5im_s", [NB, 128, 64])
        ds("XA", [NTOK, D]); ds("XB", [NTOK, D])
        ds("KM", [NTOK, 512], BF16); ds("VM", [NTOK, 1024], BF16); ds("OM", [NTOK, 1024]); ds("KS", [NTOK, 256]); ds("VS", [NTOK, 256])
        ds("QMT", [512, NTOK], BF16); ds("KMT", [512, NTOK], BF16); ds("QST", [1024, NTOK], BF16); ds("KSTD", [512, NTOK], BF16)
        ds("IF", [2, 4, NTOK]); ds("NMU", [4, NTOK]); ds("WS", [4, NTOK]); ds("SCT", [NTOK, 8])
        ds("HM", [NTOK, 1024]); ds("HS", [NTOK, 1024])
        ds("U", [NTOK, D]); ds("UT", [D, NTOK], BF16); ds("LM", [8192]); ds("TH", [8192])
        ds("BBS", [128, 2, 64, 16]); ds("BBD", [128, 16, 8, 128], BF16); ds("CM", [128, 16, 128], BF16)
        ds("TAB", [4, 128, 8192]); ds("Y", [NTOK, D])

    def xio(self, name):
        if name == "in":
            def f(r0, n):
                if r0 < SEQ:
                    return self.dram["x_p"].ap()[r0:r0 + n, :], self.dres["x_p"]
                return self.dram["x_s"].ap()[r0 - SEQ:r0 - SEQ + n, :], self.dres["x_s"]
            return f
        if name == "out":
            def f(r0, n):
                if r0 < SEQ:
                    return self.dram["y_p"].ap()[r0:r0 + n, :], self.dres["y_p"]
                return self.dram["y_s"].ap()[r0 - SEQ:r0 - SEQ + n, :], self.dres["y_s"]
            return f
        return lambda r0, n: (self.dram[name].ap()[r0:r0 + n, :], self.dres[name])

    def build(self, phases=None):
        d = self.dram
        self.declare_io()
        self.setup_consts()
        self.cx.barrier()
        tiles = [blocks_of(0, 1024), blocks_of(1024, SEQ) + [(SEQ, NS)]]
        allb = blocks_of(0, SEQ) + [(SEQ, NS)]
        ng = d["norm_gain"].ap()
        P = phases

        def on(p):
            return P is None or p in P
        if on("memkv"):
            self.memkv_phase()
        if on("ffn00"):
            self.ffn_phase("fa0", d["wgu00"].ap(), d["wd00"].ap(), ng[0, 0], tiles, self.xio("in"), self.xio("XA"))
        if on("even"):
            self.even_in_phase(self.xio("XA"))
            self.mlstm_scal_phase()
            self.mlstm_phase()
            self.swa_phase()
            self.even_out_phase(self.xio("XA"), self.xio("XB"))
        if on("xa0"):
            self.xattn_phase(0, self.xio("XB"), self.xio("XA"))
        if on("ffn01"):
            self.ffn_phase("fb0", d["wgu01"].ap(), d["wd01"].ap(), ng[0, 3], tiles, self.xio("XA"), self.xio("XB"))
        if on("ffn10"):
            self.ffn_phase("fa1", d["wgu10"].ap(), d["wd10"].ap(), ng[1, 0], tiles, self.xio("XB"), self.xio("XA"))
        if on("odd"):
            self.odd_in_phase(self.xio("XA"))
            self.s5_setup_phase()
            self.s5_scan_phase()
            self.odd_out_phase(self.xio("XA"), self.xio("XB"))
        if on("xa1"):
            self.xattn_phase(1, self.xio("XB"), self.xio("XA"))
        if on("ffn11"):
            self.ffn_phase("fb1", d["wgu11"].ap(), d["wd11"].ap(), ng[1, 3], tiles, self.xio("XA"), self.xio("XB"))
        if on("final"):
            self.final_phase(self.xio("XB"), self.xio("out"), d["final_gain"].ap(), allb)
        self.cx.finish("sp", [self.dres[n] for n in self.outputs])
        return self.nc


def _lay_wgu(wg, wu):
    a = np.stack([wg, wu], 0).reshape(2, KC, 128, NF, 128)
    return np.ascontiguousarray(a.transpose(3, 2, 0, 1, 4)).reshape(NF, 128, 2 * KC * 128)


def host_weights(inp):
    f = lambda a: np.ascontiguousarray(np.asarray(a, dtype=np.float32))
    w = {}
    for l in range(2):
        for i in range(2):
            w[f"wgu{l}{i}"] = _lay_wgu(np.asarray(inp["ffn_w_gate"][l, i]), np.asarray(inp["ffn_w_up"][l, i]))
            w[f"wd{l}{i}"] = f(inp["ffn_w_down"][l, i])
    win = np.asarray(inp["even_w_in"][0])
    q_m, k_m, v_m, o_m = win[:, 0:512], win[:, 512:1024], win[:, 1024:2048], win[:, 2048:3072]
    i_m, f_m = win[:, 3072:3076], win[:, 3076:3080]
    q_s, k_s, v_s = win[:, 3080:4104], win[:, 4104:4360], win[:, 4360:4616]
    w["wtok"] = f(np.concatenate([k_m, v_m, o_m, k_s, v_s], 1))
    ksd = np.concatenate([np.concatenate([k_s[:, c * 64:(c + 1) * 64]] * 2, 1) for c in range(4)], 1)
    w["wfeat"] = f(np.concatenate([q_m, k_m, q_s, ksd], 1))
    w["wif"] = f(np.concatenate([i_m, f_m], 1))
    for k in ("norm_gain", "final_gain", "mlstm_b_i", "mlstm_b_f", "mlstm_head_gain", "swa_sinks", "even_w_out", "odd_w_in",
              "s5_a_re", "s5_a_im", "s5_log_dt", "s5_b_re", "s5_b_im", "s5_c_re", "s5_c_im", "s5_d", "odd_w_out"):
        w[k] = f(inp[k])
    w["xwq"], w["xwk"], w["xwv"], w["xwo"] = f(inp["xattn_w_q"]), f(inp["xattn_w_k"]), f(inp["xattn_w_v"]), f(inp["xattn_w_o"])
    return w


def host_core_inputs(inp, c):
    f = lambda a: np.ascontiguousarray(np.asarray(a, dtype=np.float32))
    sb = slice(c * NB, (c + 1) * NB)
    m = {}
    m["x_p"] = f(inp["x_prompt"][c])
    m["x_s"] = f(np.asarray(inp["x_sample"][sb]).reshape(NS, D))
    m["mem_p"] = f(inp["mem_prompt"][c])
    m["cmk"] = f(np.asarray(inp["cache_mem_k"][:, sb]).reshape(2, NB, 256, 512))
    m["cmv"] = f(np.asarray(inp["cache_mem_v"][:, sb]).reshape(2, NB, 256, 512))
    m["mC"] = f(inp["state_mlstm_C"][0, sb])
    m["mn"] = f(inp["state_mlstm_n"][0, sb])
    m["mm"] = f(inp["state_mlstm_m"][0, sb])
    m["swk"] = f(np.asarray(inp["cache_swa_k"][0, sb]).reshape(NB, 128, 256))
    m["swv"] = f(np.asarray(inp["cache_swa_v"][0, sb]).reshape(NB, 128, 256))
    m["s5re"] = f(inp["state_s5_re"][0, sb])
    m["s5im"] = f(inp["state_s5_im"][0, sb])
    return m


def assemble(results, ncores=8):
    R = results
    cat = lambda k: [np.asarray(R[c][k]) for c in range(ncores)]
    y_p = np.stack(cat("y_p"), 0)
    y_s = np.concatenate([a.reshape(NB, DSEQ, D) for a in cat("y_s")], 0)
    mk = np.stack([a.reshape(2, 256, 4, 128) for a in cat("mem_k_p")], 1)
    mv = np.stack([a.reshape(2, 256, 4, 128) for a in cat("mem_v_p")], 1)
    C_p = np.stack(cat("mC_p"), 0)[None]
    C_s = np.concatenate(cat("mC_s"), 0)[None]
    n_p = np.stack(cat("mn_p"), 0)[None]
    n_s = np.concatenate(cat("mn_s"), 0)[None]
    m_p = np.stack(cat("mm_p"), 0)[None]
    m_s = np.concatenate(cat("mm_s"), 0)[None]
    k_p = np.stack([a.reshape(128, 4, 64) for a in cat("swk_p")], 0)[None]
    k_s = np.concatenate([a.reshape(NB, 128, 4, 64) for a in cat("swk_s")], 0)[None]
    v_p = np.stack([a.reshape(128, 4, 64) for a in cat("swv_p")], 0)[None]
    v_s = np.concatenate([a.reshape(NB, 128, 4, 64) for a in cat("swv_s")], 0)[None]
    sr_p = np.stack(cat("s5re_p"), 0)[None]
    sr_s = np.concatenate(cat("s5re_s"), 0)[None]
    si_p = np.stack(cat("s5im_p"), 0)[None]
    si_s = np.concatenate(cat("s5im_s"), 0)[None]
    outs = (y_p, y_s, mk, mv, C_p, C_s, n_p, n_s, m_p, m_s, k_p, k_s, v_p, v_s, sr_p, sr_s, si_p, si_s)
    return tuple(np.ascontiguousarray(o, dtype=np.float32) for o in outs)


def kernel(**inputs):
    ncores = 8
    w = host_weights(inputs)
    in_maps = []
    for c in range(ncores):
        m = host_core_inputs(inputs, c)
        m.update(w)
        in_maps.append(m)
    prog = Prog()
    with prog.nc.allow_non_contiguous_dma(reason="small strided state / scratch transfers"):
        nc = prog.build()
    res = run_bass_kernel_spmd(nc, in_maps, core_ids=list(range(ncores)))
    return assemble(res.results, ncores)
```
